# Optimizing a Trainium2 kernel written in Bass

```python
import math
import jax, jax.numpy as jnp
from jax import lax
import numpy as np

D_MODEL = 1024
BATCH = 8
SEQ = 2048
DEPTH = 4

N_A_LAYERS = DEPTH // 2
N_B_LAYERS = DEPTH - N_A_LAYERS
S5_GROUP = 16
S5_GROUPS = D_MODEL // S5_GROUP
S5_STATE = 64
DT_MIN = 1e-3
DT_MAX = 1e-1
HEAD_DIM = 128
N_HEADS = D_MODEL // HEAD_DIM
MOBA_BLOCK = 256
MOBA_TOPK = 3
QUERY_CHUNK = 16
D_FF = 128 * ((8 * D_MODEL // 3 + 127) // 128)
CONV_WIDTH = 3
ROPE_THETA = 10000.0
EPS = 1e-6

kernel_name = "yoco_s5_moba_convffn_trunk"


def rmsnorm(x, g):
    xf = x.astype(jnp.float32)
    y = xf * lax.rsqrt(jnp.mean(xf * xf, axis=-1, keepdims=True) + EPS) * g.astype(jnp.float32)
    return y.astype(x.dtype)


def rope_tables(length):
    half = HEAD_DIM // 2
    inv_freq = ROPE_THETA ** (-jnp.arange(half, dtype=jnp.float32) * 2.0 / HEAD_DIM)
    ang = jnp.arange(length, dtype=jnp.float32)[:, None] * inv_freq[None, :]
    return jnp.cos(ang), jnp.sin(ang)


def apply_rope(x, cos, sin):
    half = HEAD_DIM // 2
    xf = x.astype(jnp.float32)
    x1, x2 = xf[..., :half], xf[..., half:]
    c, s = cos[None, :, None, :], sin[None, :, None, :]
    return jnp.concatenate([x1 * c - x2 * s, x1 * s + x2 * c], axis=-1).astype(x.dtype)


def s5_mixer(xn, A_re, A_im, log_dt, B_re, B_im, C_re, C_im, D_skip, w_glu, b_glu):
    bsz, length, d = xn.shape
    u = xn.astype(jnp.float32)
    lam = lax.complex(A_re.astype(jnp.float32), A_im.astype(jnp.float32))
    dt = jnp.exp(log_dt.astype(jnp.float32))[:, None]
    lam_bar = jnp.exp(lam * dt)
    b_mat = lax.complex(B_re.astype(jnp.float32), B_im.astype(jnp.float32))
    b_bar = ((lam_bar - 1.0) / lam)[..., None] * b_mat
    ug = u.reshape(bsz, length, S5_GROUPS, S5_GROUP).astype(jnp.complex64)
    bu = jnp.einsum('blgc,gpc->blgp', ug, b_bar)
    a = jnp.broadcast_to(lam_bar, bu.shape)

    def combine(left, right):
        a_l, b_l = left
        a_r, b_r = right
        return a_r * a_l, a_r * b_l + b_r

    _, states = lax.associative_scan(combine, (a, bu), axis=1)
    c_mat = lax.complex(C_re.astype(jnp.float32), C_im.astype(jnp.float32))
    y = jnp.real(jnp.einsum('blgp,gcp->blgc', states, c_mat)).reshape(bsz, length, d)
    y = y + D_skip.astype(jnp.float32) * u
    g = jax.nn.gelu(y).astype(xn.dtype)
    z = g @ w_glu + b_glu
    za, zb = jnp.split(z, 2, axis=-1)
    return za * jax.nn.sigmoid(zb)


def conv_ffn(xn, w_up, conv_w, conv_b, w_down):
    h = xn @ w_up
    ch = h.shape[-1]
    h = lax.conv_general_dilated(h, conv_w[:, None, :], window_strides=(1,),
                                 padding=[(CONV_WIDTH - 1, 0)],
                                 dimension_numbers=('NWC', 'WIO', 'NWC'),
                                 feature_group_count=ch) + conv_b
    gate, val = jnp.split(h, 2, axis=-1)
    return (jax.nn.silu(gate) * val) @ w_down


def pad_to_blocks(t, n_blocks):
    length = t.shape[1]
    t = jnp.pad(t, ((0, 0), (0, n_blocks * MOBA_BLOCK - length), (0, 0), (0, 0)))
    return t.transpose(0, 2, 1, 3)


def shared_kv(h, kv_norm, w_kv, k_norm, cos, sin):
    bsz, length, _ = h.shape
    n_blocks = -(-length // MOBA_BLOCK)
    kv = rmsnorm(h, kv_norm) @ w_kv
    k, v = jnp.split(kv, 2, axis=-1)
    k = apply_rope(rmsnorm(k.reshape(bsz, length, N_HEADS, HEAD_DIM), k_norm), cos, sin)
    v = v.reshape(bsz, length, N_HEADS, HEAD_DIM)
    kb = pad_to_blocks(k, n_blocks).reshape(bsz, N_HEADS, n_blocks, MOBA_BLOCK, HEAD_DIM)
    vb = pad_to_blocks(v, n_blocks).reshape(bsz, N_HEADS, n_blocks, MOBA_BLOCK, HEAD_DIM)
    kmean = jnp.mean(kb.astype(jnp.float32), axis=3)
    return kb, vb, kmean


def moba_attention(q, kb, vb, kmean):
    bsz, nh, n_blocks, blk, dh = kb.shape
    n_chunks = n_blocks * blk // QUERY_CHUNK
    k_sel = min(MOBA_TOPK, n_blocks)
    scale = dh ** -0.5
    kflat = kb.reshape(bsz, nh, n_blocks * blk, dh)
    vflat = vb.reshape(bsz, nh, n_blocks * blk, dh)
    qc = q.reshape(bsz, nh, n_chunks, QUERY_CHUNK, dh).transpose(2, 0, 1, 3, 4)
    bidx = jnp.arange(bsz)[:, None, None, None]
    hidx = jnp.arange(nh)[None, :, None, None]

    def one_chunk(args):
        qi, ci = args
        start = ci * QUERY_CHUNK
        cur = start // blk
        qpos = start + jnp.arange(QUERY_CHUNK)
        gate = jnp.einsum('bhcd,bhnd->bhcn', qi.astype(jnp.float32), kmean)
        gate = jnp.where(jnp.arange(n_blocks) < cur, gate, -jnp.inf)
        _, sel = lax.top_k(gate, k_sel)
        sel_valid = sel < cur
        kg = kb[bidx, hidx, sel]
        vg = vb[bidx, hidx, sel]
        s_sel = jnp.einsum('bhcd,bhcknd->bhckn', qi, kg).astype(jnp.float32) * scale
        s_sel = jnp.where(sel_valid[..., None], s_sel, -jnp.inf)
        s_sel = s_sel.reshape(bsz, nh, QUERY_CHUNK, k_sel * blk)
        kown = lax.dynamic_slice_in_dim(kflat, cur * blk, blk, axis=2)
        vown = lax.dynamic_slice_in_dim(vflat, cur * blk, blk, axis=2)
        kpos = cur * blk + jnp.arange(blk)
        s_own = jnp.einsum('bhcd,bhnd->bhcn', qi, kown).astype(jnp.float32) * scale
        s_own = jnp.where(kpos[None, :] <= qpos[:, None], s_own, -jnp.inf)
        p = jax.nn.softmax(jnp.concatenate([s_sel, s_own], axis=-1), axis=-1)
        p_sel = p[..., :k_sel * blk].reshape(bsz, nh, QUERY_CHUNK, k_sel, blk).astype(vb.dtype)
        p_own = p[..., k_sel * blk:].astype(vb.dtype)
        return (jnp.einsum('bhckn,bhcknd->bhcd', p_sel, vg)
                + jnp.einsum('bhcn,bhnd->bhcd', p_own, vown))

    outs = lax.map(one_chunk, (qc, jnp.arange(n_chunks)))
    return outs.transpose(1, 2, 0, 3, 4).reshape(bsz, nh, n_chunks * QUERY_CHUNK, dh)


def moba_mixer(hn, w_q, q_norm, w_o, kb, vb, kmean, cos, sin):
    bsz, length, d = hn.shape
    n_blocks = kb.shape[2]
    q = (hn @ w_q).reshape(bsz, length, N_HEADS, HEAD_DIM)
    q = apply_rope(rmsnorm(q, q_norm), cos, sin)
    q = pad_to_blocks(q, n_blocks)
    o = moba_attention(q, kb, vb, kmean)[:, :, :length]
    o = o.transpose(0, 2, 1, 3).reshape(bsz, length, d)
    return o @ w_o


def setup_inputs(seed: int = 0) -> dict:
    key = jax.random.key(seed)
    ks = iter(jax.random.split(key, 32))
    nrm = lambda shape, s: jax.random.normal(next(ks), shape, jnp.float32) * s
    na, nb, G, P, C = N_A_LAYERS, N_B_LAYERS, S5_GROUPS, S5_STATE, S5_GROUP
    x = nrm((BATCH, SEQ, D_MODEL), 1.0)
    a_norm = 1.0 + nrm((na, D_MODEL), 0.02)
    s5_A_re = -0.5 + nrm((na, G, P), 0.01)
    s5_A_im = math.pi * jnp.arange(P, dtype=jnp.float32)[None, None, :] + nrm((na, G, P), 0.01)
    s5_log_dt = jax.random.uniform(next(ks), (na, G), jnp.float32, math.log(DT_MIN), math.log(DT_MAX))
    s5_B_re = nrm((na, G, P, C), (2 * C) ** -0.5)
    s5_B_im = nrm((na, G, P, C), (2 * C) ** -0.5)
    s5_C_re = nrm((na, G, C, P), P ** -0.5)
    s5_C_im = nrm((na, G, C, P), P ** -0.5)
    s5_D = 1.0 + nrm((na, D_MODEL), 0.1)
    w_glu = nrm((na, D_MODEL, 2 * D_MODEL), D_MODEL ** -0.5)
    b_glu = nrm((na, 2 * D_MODEL), 0.01)
    kv_norm = 1.0 + nrm((D_MODEL,), 0.02)
    w_kv = nrm((D_MODEL, 2 * N_HEADS * HEAD_DIM), D_MODEL ** -0.5)
    k_norm = 1.0 + nrm((HEAD_DIM,), 0.02)
    b_norm = 1.0 + nrm((nb, D_MODEL), 0.02)
    w_q = nrm((nb, D_MODEL, N_HEADS * HEAD_DIM), D_MODEL ** -0.5)
    q_norm = 1.0 + nrm((nb, HEAD_DIM), 0.02)
    w_o = nrm((nb, N_HEADS * HEAD_DIM, D_MODEL), (N_HEADS * HEAD_DIM) ** -0.5)
    ffn_norm = 1.0 + nrm((DEPTH, D_MODEL), 0.02)
    w_up = nrm((DEPTH, D_MODEL, 2 * D_FF), D_MODEL ** -0.5)
    conv_w = nrm((DEPTH, CONV_WIDTH, 2 * D_FF), CONV_WIDTH ** -0.5)
    conv_b = nrm((DEPTH, 2 * D_FF), 0.01)
    w_down = nrm((DEPTH, D_FF, D_MODEL), D_FF ** -0.5)
    return {"x": x, "a_norm": a_norm, "s5_A_re": s5_A_re, "s5_A_im": s5_A_im,
            "s5_log_dt": s5_log_dt, "s5_B_re": s5_B_re, "s5_B_im": s5_B_im,
            "s5_C_re": s5_C_re, "s5_C_im": s5_C_im, "s5_D": s5_D, "w_glu": w_glu,
            "b_glu": b_glu, "kv_norm": kv_norm, "w_kv": w_kv, "k_norm": k_norm,
            "b_norm": b_norm, "w_q": w_q, "q_norm": q_norm, "w_o": w_o,
            "ffn_norm": ffn_norm, "w_up": w_up, "conv_w": conv_w, "conv_b": conv_b,
            "w_down": w_down}


def reference(x, a_norm, s5_A_re, s5_A_im, s5_log_dt, s5_B_re, s5_B_im, s5_C_re, s5_C_im,
              s5_D, w_glu, b_glu, kv_norm, w_kv, k_norm, b_norm, w_q, q_norm, w_o,
              ffn_norm, w_up, conv_w, conv_b, w_down):
    length = x.shape[1]
    cos, sin = rope_tables(length)
    h = x
    kb = vb = kmean = None
    for layer in range(DEPTH):
        if layer < N_A_LAYERS:
            i = layer
            h = h + s5_mixer(rmsnorm(h, a_norm[i]), s5_A_re[i], s5_A_im[i], s5_log_dt[i],
                             s5_B_re[i], s5_B_im[i], s5_C_re[i], s5_C_im[i], s5_D[i],
                             w_glu[i], b_glu[i]).astype(h.dtype)
        else:
            if layer == N_A_LAYERS:
                kb, vb, kmean = shared_kv(h, kv_norm, w_kv, k_norm, cos, sin)
            j = layer - N_A_LAYERS
            h = h + moba_mixer(rmsnorm(h, b_norm[j]), w_q[j], q_norm[j], w_o[j],
                               kb, vb, kmean, cos, sin).astype(h.dtype)
        h = h + conv_ffn(rmsnorm(h, ffn_norm[layer]), w_up[layer], conv_w[layer],
                         conv_b[layer], w_down[layer]).astype(h.dtype)
    return h
```

```python
import math
from contextlib import ExitStack

import numpy as np
import concourse.bass as bass
import concourse.mybir as mybir
from concourse.bass_utils import run_bass_kernel_spmd

F32 = mybir.dt.float32
BF16 = mybir.dt.bfloat16
ALU = mybir.AluOpType
AF = mybir.ActivationFunctionType
AX = mybir.AxisListType

L = 2048
D = 1024
NFC = 8
DFF = 2816
NUPC = 44
NACT = 22
NH = 8
EPS = 1e-6
MAGIC = 12582912.0
TWO_PI = 2.0 * math.pi
NEG = -30000.0
ENGS = ("pe", "act", "dve", "pool", "sp")

VC = {}
_c = 0
for _l in range(2):
    VC[("a_norm", _l)] = _c; _c += 8
    VC[("s5_D", _l)] = _c; _c += 8
    VC[("b_glu", _l)] = _c; _c += 16
VC["kv_norm"] = _c; _c += 8
VC["k_norm"] = _c; _c += 1
for _j in range(2):
    VC[("b_norm", _j)] = _c; _c += 8
    VC[("q_norm", _j)] = _c; _c += 1
for _l in range(4):
    VC[("ffn_norm", _l)] = _c; _c += 8
    VC[("conv_w", _l)] = _c; _c += 3 * NUPC
    VC[("conv_b", _l)] = _c; _c += NUPC
NV = _c
CC = {}
_c = 0
for _n, _w in (("ident", 128), ("pswap", 128), ("maskQ", 128), ("onesD", 128), ("onesH", 128),
               ("cmask", 128), ("futmask", 128), ("rowmask", 4), ("kk", 9), ("ownmask", 128)):
    CC[_n] = _c; _c += _w
NCST = _c


class _Op:
    __slots__ = ("eng", "fn", "deps", "dma", "sig", "cnt", "sem", "semval", "semprev", "idx")


class Sched:
    def __init__(self, nc):
        self.nc = nc
        self.ops = []
        self.res = {}

    def op(self, eng, fn, reads=(), writes=(), dma=False):
        o = _Op()
        o.eng = eng; o.fn = fn; o.dma = dma; o.sig = False; o.idx = len(self.ops)
        deps = set()
        for r in reads:
            st = self.res.get(r)
            if st is not None and st[0] is not None:
                deps.add(st[0])
        for w in writes:
            st = self.res.get(w)
            if st is not None:
                if st[0] is not None:
                    deps.add(st[0])
                deps.update(st[1])
        for r in reads:
            st = self.res.setdefault(r, [None, []])
            st[1].append(o.idx)
        for w in writes:
            self.res[w] = [o.idx, []]
        deps.discard(o.idx)
        o.deps = deps
        self.ops.append(o)
        return o.idx

    def dma(self, eng, out, in_, reads=(), writes=(), **kw):
        return self.op(eng, lambda e: e.dma_start(out=out, in_=in_, **kw), reads, writes, dma=True)

    def emit(self, sems, dma_sems):
        ops = self.ops
        for o in ops:
            latest = {}
            ddeps = []
            for d in o.deps:
                od = ops[d]
                if od.dma:
                    ddeps.append(d)
                elif od.eng != o.eng or o.dma or o.eng != "pe":
                    if od.eng not in latest or latest[od.eng] < d:
                        latest[od.eng] = d
            o.deps = (latest, ddeps)
            for d in latest.values():
                ops[d].sig = True
        cnt = {e: 0 for e in ENGS}
        for o in ops:
            if o.dma:
                continue
            if o.sig:
                cnt[o.eng] += 1
            o.cnt = cnt[o.eng]
        semcount = [0] * len(dma_sems)
        k = 0
        for o in ops:
            if o.dma:
                o.sem = k % len(dma_sems)
                o.semprev = semcount[o.sem]
                semcount[o.sem] += 16
                o.semval = semcount[o.sem]
                k += 1
        per_eng = {e: [o for o in ops if o.eng == e] for e in ENGS}

        def run_engine(ename, eng):
            waited = {e: 0 for e in ENGS}
            dwaited = [0] * len(dma_sems)
            for o in per_eng[ename]:
                latest, ddeps = o.deps
                for d in sorted(ddeps):
                    od = ops[d]
                    if dwaited[od.sem] < od.semval:
                        eng.wait_ge(dma_sems[od.sem], od.semval)
                        dwaited[od.sem] = od.semval
                for en_, d in latest.items():
                    od = ops[d]
                    if waited[en_] < od.cnt:
                        eng.wait_ge(sems[en_], od.cnt)
                        waited[en_] = od.cnt
                if o.dma:
                    if o.semprev > 0 and dwaited[o.sem] < o.semprev:
                        eng.wait_ge(dma_sems[o.sem], o.semprev)
                        dwaited[o.sem] = o.semprev
                    o.fn(eng).then_inc(dma_sems[o.sem], 16)
                else:
                    ins = o.fn(eng)
                    if o.sig:
                        ins.then_inc(sems[ename], 1)
            last = {}
            for o in per_eng[ename]:
                if o.dma:
                    last[o.sem] = max(last.get(o.sem, 0), o.semval)
            for s_, v in last.items():
                if dwaited[s_] < v:
                    eng.wait_ge(dma_sems[s_], v)

        with self.nc.Block() as block:
            @block.tensor
            def _(e):
                run_engine("pe", e)

            @block.scalar
            def _(e):
                run_engine("act", e)

            @block.vector
            def _(e):
                run_engine("dve", e)

            @block.gpsimd
            def _(e):
                run_engine("pool", e)

            @block.sync
            def _(e):
                run_engine("sp", e)


class Builder:
    def __init__(self, stop=None):
        self.stop = stop
        self.nc = bass.Bass("TRN2", target_bir_lowering=False)
        self.S = Sched(self.nc)
        self.wslot = 0
        self.pshalf = 0

    def declare(self, es):
        nc = self.nc
        di = lambda n, s, dt=F32: nc.dram_tensor(n, s, dt, kind="ExternalInput").ap()
        self.xT_d = di("xT", [128, NFC * L])
        self.vec_d = di("vec", [128, NV])
        self.cst_d = di("cst", [128, NCST])
        self.rope_d = di("rope", [128, 2 * L])
        self.s5p_d = di("s5p", [2, 128, 96 + 4096])
        self.w_glu_d = di("w_glu", [2, D, 2 * D])
        self.w_kv_d = di("w_kv", [D, 2 * D])
        self.w_q_d = di("w_q", [2, D, D])
        self.w_o_d = di("w_o", [2, D, D])
        self.w_up_d = di("w_up", [4, D, 2 * DFF])
        self.w_down_d = di("w_down", [4, DFF, D])
        self.out_d = nc.dram_tensor("outT", [128, NFC * L], F32, kind="ExternalOutput").ap()
        self.kT_s = nc.dram_tensor("kT_s", [NH, 128, L], BF16, kind="Internal").ap()
        self.V_s = nc.dram_tensor("V_s", [NH, 128, 16 * 129], BF16, kind="Internal").ap()
        self.OT_s = nc.dram_tensor("OT_s", [NH, 128, L], BF16, kind="Internal").ap()
        self.Q_s = nc.dram_tensor("Q_s", [NH, 128, L], BF16, kind="Internal").ap()
        self.NS_s = nc.dram_tensor("NS_s", [NH, 8, L], BF16, kind="Internal").ap()

        sb = lambda n, s, dt=F32: es.enter_context(nc.sbuf_tensor(n, s, dt))
        self.HT = sb("HT", [128, NFC * L])
        self.BIGA = sb("BIGA", [128, NFC * L], BF16)
        self.BIGB = sb("BIGB", [128, 16512], BF16)
        self.FW = sb("FW", [128, 3 * L])
        self.TMP = sb("TMP", [128, 1024])
        self.TMP2 = sb("TMP2", [128, 1024])
        self.UNI = sb("UNI", [128, 6656])
        self.VEC = sb("VEC", [128, NV])
        self.CST = sb("CST", [128, NCST])
        self.CSTB = sb("CSTB", [128, 4 * 128], BF16)
        self.S5S = sb("S5S", [128, 2048])
        self.SMALL = sb("SMALL", [128, 1024])
        self.PS = es.enter_context(nc.psum_tensor("PS", [128, 4096], F32))
        self.sems = {e: es.enter_context(nc.semaphore("s_" + e)) for e in ENGS}
        self.dsems = [es.enter_context(nc.semaphore("d%d" % i)) for i in range(40)]

    def FA(self):
        return self.FW[:, 0:L]

    def FB(self):
        return self.FW[:, L:2 * L]

    def FC(self):
        return self.FW[:, 2 * L:3 * L]

    def cst(self, name, w=128):
        c = CC[name]
        return self.CST[:, c:c + w]

    def vcol(self, key, i=0):
        c = VC[key] + i
        return self.VEC[:, c:c + 1]

    NSLOT = 8

    def WBF(self, i):
        return self.UNI[:, i * 512:(i + 1) * 512].bitcast(BF16)

    @staticmethod
    def wres(i):
        return ["h%d" % i]

    @staticmethod
    def bres(lo, hi):
        return [("B", k) for k in range(lo // 1024, (hi - 1) // 1024 + 1)]

    def prologue(self):
        S = self.S
        S.dma("sp", self.CST[:], self.cst_d, writes=["CST"])
        S.dma("sp", self.VEC[:], self.vec_d, writes=["VEC"])
        for fc in range(NFC):
            S.dma("act" if fc % 2 else "sp", self.HT[:, fc * L:(fc + 1) * L], self.xT_d[:, fc * L:(fc + 1) * L],
                  writes=[("H", fc)])
        for i, n in enumerate(("ident", "onesD", "onesH", "cmask")):
            src = self.cst(n)
            dst = self.CSTB[:, i * 128:(i + 1) * 128]
            S.op("dve", lambda e, d=dst, s=src: e.tensor_copy(d, s), reads=["CST"], writes=["CSTB"])

    def identB(self):
        return self.CSTB[:, 0:128]

    def onesD(self):
        return self.CSTB[:, 128:256]

    def onesH(self):
        return self.CSTB[:, 256:384]

    def cmaskB(self):
        return self.CSTB[:, 384:512]

    def epilogue(self):
        S = self.S
        for fc in range(NFC):
            S.dma("act" if fc % 2 else "sp", self.out_d[:, fc * L:(fc + 1) * L], self.HT[:, fc * L:(fc + 1) * L],
                  reads=[("H", fc)], writes=[("OUT", fc)])

    def rms_rstd(self, dst, dst_res):
        S = self.S

        def stA(tt):
            sq = self.BIGB[:, (tt % 2) * 4096:(tt % 2 + 1) * 4096]
            sqres = self.bres((tt % 2) * 4096, (tt % 2 + 1) * 4096)
            hin = self.HT[:].rearrange("p (f t) -> p f t", t=L)[:, :, tt * 512:(tt + 1) * 512]
            S.op("act", lambda e, o=sq, i=hin: e.activation(out=o.rearrange("p (f t) -> p f t", t=512), in_=i, func=AF.Square),
                 reads=[("H", fc) for fc in range(NFC)], writes=sqres)

        def stB(tt):
            sq = self.BIGB[:, (tt % 2) * 4096:(tt % 2 + 1) * 4096]
            sqres = self.bres((tt % 2) * 4096, (tt % 2 + 1) * 4096)
            bank = 4 + (tt % 2)
            ps = self.PS[:, bank * 512:(bank + 1) * 512]
            for fc in range(NFC):
                S.op("pe", lambda e, o=ps, r=sq[:, fc * 512:(fc + 1) * 512], a=(fc == 0), z=(fc == NFC - 1):
                     e.matmul(o, self.onesD(), r, start=a, stop=z),
                     reads=sqres + ["CSTB"], writes=[("p", bank)])
            d = dst[:, tt * 512:(tt + 1) * 512]
            S.op("act", lambda e, o=d, i=ps: e.activation(out=o, in_=i, func=AF.Ln, bias=self.eps_ap(), scale=1.0),
                 reads=[("p", bank), "EPS"], writes=[(dst_res, tt)])
            S.op("act", lambda e, o=d: e.activation(out=o, in_=o, func=AF.Exp, scale=-0.5), reads=[(dst_res, tt)], writes=[(dst_res, tt)])

        stA(0)
        stA(1)
        for tt in range(4):
            stB(tt)
            if tt + 2 < 4:
                stA(tt + 2)

    def eps_ap(self):
        return self.EPS_T[:, 0:1]

    def make_xn(self, gkey, rstd, rstd_res):
        S = self.S
        for tt in range(4):
            for fc in range(NFC):
                S.op("dve", lambda e, fc=fc, tt=tt: e.scalar_tensor_tensor(
                    out=self.BIGA[:, fc * L + tt * 512:fc * L + (tt + 1) * 512], in0=self.HT[:, fc * L + tt * 512:fc * L + (tt + 1) * 512],
                    scalar=self.vcol(gkey, fc), in1=rstd[:, tt * 512:(tt + 1) * 512], op0=ALU.mult, op1=ALU.mult),
                    reads=[("H", fc), "VEC", (rstd_res, tt)], writes=[("A", fc)])

    def run_dense(self, jobs):
        S = self.S
        n = len(jobs)
        slots = {}
        ahead = self.NSLOT - 2

        def load(i):
            if i >= n:
                return
            jb = jobs[i]
            slots[i] = self.load_w(jb["w2d"], jb["k0"], jb["KC"], jb["col0"])

        for i in range(min(ahead, n)):
            load(i)
        for i in range(n):
            load(i + ahead)
            jb = jobs[i]
            sl = slots[i]
            KC = jb["KC"]
            half = self.pshalf
            self.pshalf ^= 1
            rb = self.wres(sl)
            for kc in range(KC):
                for tt in range(4):
                    bank = half * 4 + tt
                    o = self.PS[:, bank * 512:(bank + 1) * 512]
                    S.op("pe", lambda e, o=o, w=self.WBF(sl)[:, kc * 128:(kc + 1) * 128], r=jb["rhs"](kc, tt), a=(kc == 0), z=(kc == KC - 1):
                         e.matmul(o, w, r, start=a, stop=z),
                         reads=rb + jb["rhs_res"](kc), writes=[("p", bank)])
            jb["evac"](self.PS[:, half * 2048:(half + 1) * 2048], [("p", half * 4 + t) for t in range(4)])

    def ffn(self, l):
        S = self.S
        self.rms_rstd(self.FA(), "FA")
        self.make_xn(("ffn_norm", l), self.FA(), "FA")
        if getattr(self, "ffn_hook", None):
            self.ffn_hook()
            self.ffn_hook = None
        groups = [(0, 6), (6, 12), (12, 17), (17, 22)]
        cw = VC[("conv_w", l)]
        cb = VC[("conv_b", l)]
        SG = self.BIGB[:, 12288:12288 + L]
        FBR = [("FB", t) for t in range(4)]
        FCR = [("FC", t) for t in range(4)]

        pending = []

        def conv_to(acc, accres, ps, psres, c):
            w = lambda k: self.VEC[:, cw + k * NUPC + c:cw + k * NUPC + c + 1]
            S.op("act", lambda e: e.activation(out=acc, in_=ps, func=AF.Identity, scale=w(2), bias=self.VEC[:, cb + c:cb + c + 1]),
                 reads=psres + ["VEC"], writes=accres)
            while pending:
                pending.pop(0)()
            S.op("dve", lambda e: e.scalar_tensor_tensor(out=acc[:, 1:L], in0=ps[:, 0:L - 1], scalar=w(1), in1=acc[:, 1:L],
                                                         op0=ALU.mult, op1=ALU.add),
                 reads=psres + ["VEC"] + accres, writes=accres)
            S.op("dve", lambda e: e.scalar_tensor_tensor(out=acc[:, 2:L], in0=ps[:, 0:L - 2], scalar=w(0), in1=acc[:, 2:L],
                                                         op0=ALU.mult, op1=ALU.add),
                 reads=psres + ["VEC"] + accres, writes=accres)

        ups, downs = [], []
        for (g0, g1) in groups:
            jobs = []
            for i in range(g0, g1):
                il = i - g0

                def evac_gate(ps, psres, i=i):
                    conv_to(self.FB(), FBR, ps, psres, i)
                    pending.append(lambda: S.op("act", lambda e: e.activation(out=SG, in_=self.FB(), func=AF.Silu), reads=FBR, writes=self.bres(12288, 14336)))

                def evac_val(ps, psres, i=i, il=il):
                    conv_to(self.FC(), FCR, ps, psres, NACT + i)
                    S.op("dve", lambda e: e.tensor_tensor(self.BIGB[:, il * L:(il + 1) * L], SG, self.FC(), ALU.mult),
                         reads=self.bres(12288, 14336) + FCR, writes=self.bres(il * L, (il + 1) * L))

                for col0, ev in ((i * 128, evac_gate), (DFF + i * 128, evac_val)):
                    jobs.append(dict(w2d=self.w_up_d[l], k0=0, KC=8, col0=col0,
                                     rhs=lambda kc, tt: self.BIGA[:, kc * L + tt * 512:kc * L + (tt + 1) * 512],
                                     rhs_res=lambda kc: [("A", kc)], evac=ev))
            ng = g1 - g0
            ups.append(jobs)
            jobs = []
            for oc in range(NFC):
                def evac_down(ps, psres, oc=oc):
                    S.op("dve", lambda e: e.tensor_tensor(self.HT[:, oc * L:(oc + 1) * L], self.HT[:, oc * L:(oc + 1) * L], ps, ALU.add),
                         reads=psres + [("H", oc)], writes=[("H", oc)])
                jobs.append(dict(w2d=self.w_down_d[l], k0=g0 * 128, KC=ng, col0=oc * 128,
                                 rhs=lambda kc, tt: self.BIGB[:, kc * L + tt * 512:kc * L + (tt + 1) * 512],
                                 rhs_res=lambda kc: self.bres(kc * L, (kc + 1) * L), evac=evac_down))
            downs.append(jobs)
        alljobs = list(ups[0])
        for g in range(len(groups)):
            if g + 1 < len(groups):
                alljobs.append(ups[g + 1][0])
            alljobs.extend(downs[g])
            if g + 1 < len(groups):
                alljobs.extend(ups[g + 1][1:])
        self.run_dense(alljobs)
        while pending:
            pending.pop(0)()

    def s5_layer(self, l, part="all"):
        S = self.S
        P = self.S5S
        skip = {"on": part == "main"}
        sl = lambda a, b: P[:, a:b]
        Are, Aim, ldt = sl(0, 32), sl(32, 64), sl(64, 96)
        dt, ar, ai = sl(96, 128), sl(128, 160), sl(160, 192)
        MAG, ANG, NT, C9, PWR, PWI = sl(192, 480), sl(480, 768), sl(768, 1056), sl(1056, 1344), sl(1344, 1632), sl(1632, 1920)
        cr, ci = sl(1920, 1952), sl(1952, 1984)
        sm = lambda i: self.SMALL[:, 640 + 32 * i:640 + 32 * (i + 1)]
        k3 = lambda ap: ap.rearrange("p (k g) -> p k g", g=32)
        kk = self.cst("kk", 9)
        kkb = kk.unsqueeze(2).broadcast_to([128, 9, 32])

        def dv(name, fn, reads, writes, eng="dve"):
            if skip["on"]:
                return
            S.op(eng, fn, reads=reads, writes=writes)

        if part != "main":
            S.dma("sp", P[:, 0:96], self.s5p_d[l][:, 0:96], writes=["s5raw"])
        FBR = [("FB", t) for t in range(4)]
        FCR = [("FC", t) for t in range(4)]
        if part != "params":
            S.dma("sp", self.FW[:, L:3 * L], self.s5p_d[l][:, 96:96 + 4096], writes=FBR + FCR)
        BMR, BMI = self.FW[:, L:L + 1024], self.FW[:, L + 1024:2 * L]
        CMR, CMI = self.FW[:, 2 * L:2 * L + 1024], self.FW[:, 2 * L + 1024:3 * L]
        g3 = lambda ap: ap.rearrange("p (g c) -> p g c", c=32)
        dv("dt", lambda e: e.activation(out=dt, in_=ldt, func=AF.Exp), ["s5raw"], ["s5dt"], "act")
        dv("ar", lambda e: e.tensor_tensor(ar, Are, dt, ALU.mult), ["s5raw", "s5dt"], ["s5ar"])
        dv("ai", lambda e: e.tensor_tensor(ai, Aim, dt, ALU.mult), ["s5raw", "s5dt"], ["s5ai"])
        dv("mag", lambda e: e.tensor_tensor(k3(MAG), ar.unsqueeze(1).broadcast_to([128, 9, 32]), kkb, ALU.mult), ["s5ar", "CST"], ["s5mag"])
        dv("mage", lambda e: e.activation(out=MAG, in_=MAG, func=AF.Exp), ["s5mag"], ["s5mag"], "act")
        dv("ang", lambda e: e.tensor_tensor(k3(ANG), ai.unsqueeze(1).broadcast_to([128, 9, 32]), kkb, ALU.mult), ["s5ai", "CST"], ["s5ang"])

        def sin_of(src, srcres, tmp, tmpres):
            dv("n1", lambda e: e.tensor_scalar(tmp, src, 1.0 / TWO_PI, MAGIC, ALU.mult, ALU.add), [srcres], [tmpres])
            dv("n2", lambda e: e.tensor_scalar(tmp, tmp, MAGIC, None, ALU.subtract), [tmpres], [tmpres])
            dv("n3", lambda e: e.scalar_tensor_tensor(out=tmp, in0=tmp, scalar=-TWO_PI, in1=src, op0=ALU.mult, op1=ALU.add), [tmpres, srcres], [tmpres])
            dv("n4", lambda e: e.activation(out=tmp, in_=tmp, func=AF.Sin), [tmpres], [tmpres], "act")

        sin_of(ANG, "s5ang", NT, "s5nt")
        dv("pwi", lambda e: e.tensor_tensor(PWI, MAG, NT, ALU.mult), ["s5mag", "s5nt"], ["s5pwi"])
        dv("angc", lambda e: e.tensor_scalar(C9, ANG, math.pi / 2, None, ALU.add), ["s5ang"], ["s5c9"])
        sin_of(C9, "s5c9", NT, "s5nt")
        dv("pwr", lambda e: e.tensor_tensor(PWR, MAG, NT, ALU.mult), ["s5mag", "s5nt"], ["s5pwr"])
        nr, den, x1, x2, rden, y1, y2 = sm(0), sm(1), sm(2), sm(3), sm(4), sm(5), sm(6)
        pw1r, pw1i = PWR[:, 32:64], PWI[:, 32:64]
        dv("nr", lambda e: e.tensor_scalar(nr, pw1r, -1.0, None, ALU.add), ["s5pwr"], ["sm0"])
        dv("den", lambda e: e.tensor_tensor(den, Are, Are, ALU.mult), ["s5raw"], ["sm1"])
        dv("den2", lambda e: e.tensor_tensor(x1, Aim, Aim, ALU.mult), ["s5raw"], ["sm2"])
        dv("den3", lambda e: e.tensor_tensor(den, den, x1, ALU.add), ["sm1", "sm2"], ["sm1"])
        dv("rden", lambda e: e.reciprocal(rden, den), ["sm1"], ["sm4"])
        dv("x1", lambda e: e.tensor_tensor(x1, nr, Are, ALU.mult), ["sm0", "s5raw"], ["sm2"])
        dv("x2", lambda e: e.tensor_tensor(x2, pw1i, Aim, ALU.mult), ["s5pwi", "s5raw"], ["sm3"])
        dv("x3", lambda e: e.tensor_tensor(x1, x1, x2, ALU.add), ["sm2", "sm3"], ["sm2"])
        dv("cr", lambda e: e.tensor_tensor(cr, x1, rden, ALU.mult), ["sm2", "sm4"], ["s5cr"])
        dv("y1", lambda e: e.tensor_tensor(y1, pw1i, Are, ALU.mult), ["s5pwi", "s5raw"], ["sm5"])
        dv("y2", lambda e: e.tensor_tensor(y2, nr, Aim, ALU.mult), ["sm0", "s5raw"], ["sm6"])
        dv("y3", lambda e: e.tensor_tensor(y1, y1, y2, ALU.subtract), ["sm5", "sm6"], ["sm5"])
        dv("ci", lambda e: e.tensor_tensor(ci, y1, rden, ALU.mult), ["sm5", "sm4"], ["s5ci"])
        skip["on"] = False
        if part == "params":
            return
        crb = cr.unsqueeze(2).broadcast_to([128, 32, 32])
        cib = ci.unsqueeze(2).broadcast_to([128, 32, 32])
        T1, T2 = self.TMP[:], self.TMP2[:]
        R_BMR, R_BMI = [("FB", 0), ("FB", 1)], [("FB", 2), ("FB", 3)]
        R_CMR, R_CMI = [("FC", 0), ("FC", 1)], [("FC", 2), ("FC", 3)]
        dv("b1", lambda e: e.tensor_tensor(g3(T1), cib, g3(BMR), ALU.mult), ["s5ci"] + R_BMR, ["TMP"])
        dv("b2", lambda e: e.tensor_tensor(g3(T2), cib, g3(BMI), ALU.mult), ["s5ci"] + R_BMI, ["TMP2"])
        dv("b3", lambda e: e.tensor_tensor(g3(BMR), crb, g3(BMR), ALU.mult), ["s5cr"] + R_BMR, R_BMR)
        dv("b4", lambda e: e.tensor_tensor(BMR, BMR, T2, ALU.subtract), R_BMR + ["TMP2"], R_BMR)
        dv("b5", lambda e: e.tensor_tensor(g3(BMI), crb, g3(BMI), ALU.mult), ["s5cr"] + R_BMI, R_BMI)
        dv("b6", lambda e: e.tensor_tensor(BMI, BMI, T1, ALU.add), R_BMI + ["TMP"], R_BMI)

        FAR = [("FA", t) for t in range(4)]
        self.rms_rstd(self.FA(), "FA")
        for tt in range(4):
            for fc in range(NFC):
                S.op("dve", lambda e, fc=fc, tt=tt: e.scalar_tensor_tensor(
                    out=self.BIGA[:, fc * L:(fc + 1) * L].rearrange("p (j c) -> p j c", c=256)[:, :, tt * 64:(tt + 1) * 64],
                    in0=self.HT[:, fc * L + tt * 512:fc * L + (tt + 1) * 512].rearrange("p (c j) -> p j c", j=8),
                    scalar=self.vcol(("a_norm", l), fc),
                    in1=self.FA()[:, tt * 512:(tt + 1) * 512].rearrange("p (c j) -> p j c", j=8), op0=ALU.mult, op1=ALU.mult),
                    reads=[("H", fc), "VEC", ("FA", tt)], writes=[("A", fc)])
        B2S4 = self.BIGB[:, 0:16448].rearrange("p (r g c) -> p r g c", r=2, g=32)
        ALLC = [("b2S", c) for c in range(257)]
        BALL = self.bres(0, 16512)
        S.op("pool", lambda e: e.memset(B2S4[:, :, :, 0:1], 0.0), reads=[], writes=BALL + [("b2S", 0)])
        Zre, ZimN = self.FW[:, 0:1024], self.FW[:, 1024:2048]
        z4 = lambda ap: ap.rearrange("p (t q c) -> p t q c", t=8, q=4)
        R_ZR, R_ZI = [("FA", 0), ("FA", 1)], [("FA", 2), ("FA", 3)]
        PWR3, PWI3 = k3(PWR), k3(PWI)
        ZB = {0: (Zre, ZimN, R_ZR, R_ZI),
              1: (self.UNI[:, 2048:3072], self.UNI[:, 3072:4096], ["h4", "h5"], ["h6", "h7"])}

        def zcompute(fc, zb=0):
            Zre, ZimN, R_ZR, R_ZI = ZB[zb]
            pwr_b = PWR3[:, 0:8, 4 * fc:4 * fc + 4].unsqueeze(3).broadcast_to([128, 8, 4, 32])
            pwi_b = PWI3[:, 0:8, 4 * fc:4 * fc + 4].unsqueeze(3).broadcast_to([128, 8, 4, 32])
            bmr_b = g3(BMR)[:, 4 * fc:4 * fc + 4, :].unsqueeze(1).broadcast_to([128, 8, 4, 32])
            bmi_b = g3(BMI)[:, 4 * fc:4 * fc + 4, :].unsqueeze(1).broadcast_to([128, 8, 4, 32])
            E = "pool"
            dv("z1", lambda e: e.tensor_tensor(z4(Zre), pwr_b, bmr_b, ALU.mult), ["s5pwr"] + R_BMR, R_ZR, E)
            dv("z2", lambda e: e.tensor_tensor(z4(T1), pwi_b, bmi_b, ALU.mult), ["s5pwi"] + R_BMI, ["TMP"], E)
            dv("z3", lambda e: e.tensor_tensor(Zre, Zre, T1, ALU.subtract), R_ZR + ["TMP"], R_ZR, E)
            E = "pool"
            dv("z4", lambda e: e.tensor_tensor(z4(ZimN), pwr_b, bmi_b, ALU.mult), ["s5pwr"] + R_BMI, R_ZI, E)
            dv("z5", lambda e: e.tensor_tensor(z4(T2), pwi_b, bmr_b, ALU.mult), ["s5pwi"] + R_BMR, ["TMP2"], E)
            dv("z6", lambda e: e.tensor_tensor(ZimN, ZimN, T2, ALU.add), R_ZI + ["TMP2"], R_ZI, E)
            dv("z7", lambda e: e.activation(out=ZimN, in_=ZimN, func=AF.Copy, scale=-1.0), R_ZI, R_ZI, "act")

        def zcompute_dve(fc):
            Zre, ZimN, R_ZR, R_ZI = ZB[0]
            pwr_b = PWR3[:, 0:8, 4 * fc:4 * fc + 4].unsqueeze(3).broadcast_to([128, 8, 4, 32])
            pwi_b = PWI3[:, 0:8, 4 * fc:4 * fc + 4].unsqueeze(3).broadcast_to([128, 8, 4, 32])
            bmr_b = g3(BMR)[:, 4 * fc:4 * fc + 4, :].unsqueeze(1).broadcast_to([128, 8, 4, 32])
            bmi_b = g3(BMI)[:, 4 * fc:4 * fc + 4, :].unsqueeze(1).broadcast_to([128, 8, 4, 32])
            PT = self.PS[:, 3072:4096]
            RPT = [("p", 6), ("p", 7)]
            dv("z1", lambda e: e.tensor_tensor(z4(Zre), pwr_b, bmr_b, ALU.mult), ["s5pwr"] + R_BMR, R_ZR, "pool")
            dv("z2", lambda e: e.tensor_tensor(z4(T1), pwi_b, bmi_b, ALU.mult), ["s5pwi"] + R_BMI, ["TMP"], "pool")
            dv("z3", lambda e: e.tensor_tensor(Zre, Zre, T1, ALU.subtract), R_ZR + ["TMP"], R_ZR, "pool")
            dv("z4", lambda e: e.tensor_tensor(z4(ZimN), pwr_b, bmi_b, ALU.mult), ["s5pwr"] + R_BMI, R_ZI)
            dv("z5", lambda e: e.tensor_tensor(z4(PT), pwi_b, bmr_b, ALU.mult), ["s5pwi"] + R_BMR, RPT)
            dv("z6", lambda e: e.tensor_tensor(ZimN, ZimN, PT, ALU.add), R_ZI + RPT, R_ZI)
            dv("z7", lambda e: e.activation(out=ZimN, in_=ZimN, func=AF.Copy, scale=-1.0), R_ZI, R_ZI, "act")

        W2B = {0: (self.UNI[:, 0:2048].bitcast(BF16).rearrange("p (v t r c) -> p v t r c", v=2, t=8, r=2), ["h0", "h1", "h2", "h3"]),
               1: (self.UNI[:, 4096:6144].bitcast(BF16).rearrange("p (v t r c) -> p v t r c", v=2, t=8, r=2), ["h8", "h9", "h10", "h11"])}
        ident = self.cst("ident")
        rowmask = self.cst("rowmask", 4)

        def stageT(fc):
            zb = fc % 2
            zcompute(fc, zb)
            Zre_, ZimN_, RZR_, RZI_ = ZB[zb]
            W2v, R_W2 = W2B[zb]
            for ri, Zs, ZR in ((0, Zre_, RZR_), (1, ZimN_, RZI_)):
                for tq in range(2):
                    bank = ri * 2 + tq
                    for i in range(4):
                        t = tq * 4 + i
                        S.op("pe", lambda e, o=self.PS[:, bank * 512 + i * 128:bank * 512 + (i + 1) * 128], a=Zs[:, t * 128:(t + 1) * 128]:
                             e.transpose(o, a, ident), reads=ZR + ["CST"], writes=[("p", bank)])
                    for v in range(2):
                        S.op("dve", lambda e, v=v, tq=tq, ri=ri, bank=bank, W2v=W2v: e.tensor_scalar(
                            W2v[:, v, tq * 4:(tq + 1) * 4, ri, :],
                            self.PS[:, bank * 512:(bank + 1) * 512].rearrange("p (t c) -> p t c", c=128),
                            rowmask[:, 2 * ri + v:2 * ri + v + 1], None, ALU.mult),
                            reads=[("p", bank), "CST"], writes=R_W2)

        def stageM(fc):
            W2v, R_W2 = W2B[fc % 2]
            for q in range(4):
                hq, v = q // 2, q % 2
                for ri in range(2):
                    off = 2048 + (q * 2 + ri) * 256
                    bank = off // 512
                    for j in range(8):
                        S.op("pe", lambda e, o=self.PS[:, off:off + 256], w=W2v[64 * hq:64 * hq + 64, v, 7 - j, ri, :],
                             r=self.BIGA[64 * hq:64 * hq + 64, fc * L + j * 256:fc * L + (j + 1) * 256], a=(j == 0), z=(j == 7):
                             e.matmul(o, w, r, start=a, stop=z), reads=R_W2 + [("A", fc)], writes=[("p", bank)])
            S.op("act", lambda e, fc=fc: e.activation(
                out=B2S4[:, :, 4 * fc:4 * fc + 4, 1:257],
                in_=self.PS[:, 2048:4096].rearrange("p (q r c) -> p r q c", q=4, r=2), func=AF.Copy),
                reads=[("p", b) for b in range(4, 8)], writes=BALL + ALLC[1:])

        stageT(0)
        for fc in range(NFC):
            if fc + 1 < NFC:
                stageT(fc + 1)
            stageM(fc)

        SMT = self.SMALL[:]
        A12 = self.SMALL[:, 0:128]
        A1, A2 = self.SMALL[:, 0:64], self.SMALL[:, 64:128]
        SR = lambda k: self.SMALL[:, 128 + 128 * k:256 + 128 * k]
        TT = self.SMALL[:, 512:576]
        PP = self.SMALL[:, 896:1024]
        l8r, l8i = PWR[:, 256:288], PWI[:, 256:288]
        dv("a1a", lambda e: e.tensor_copy(A1[:, 0:32], l8r), ["s5pwr"], ["A1"])
        dv("a1b", lambda e: e.tensor_copy(A1[:, 32:64], l8r), ["s5pwr"], ["A1"])
        dv("a2a", lambda e: e.tensor_scalar(A2[:, 0:32], l8i, -1.0, None, ALU.mult), ["s5pwi"], ["A2"])
        dv("a2b", lambda e: e.tensor_copy(A2[:, 32:64], l8i), ["s5pwi"], ["A2"])
        dv("sr0", lambda e: e.memset(SR(0), 0.0), [], [("SR", 0)])
        h4 = lambda ap: ap.rearrange("p (h j g) -> p h j g", h=2, j=2)
        j3 = lambda ap: ap.rearrange("p (j g) -> p j g", j=2)
        for c in range(256):
            k0, k1 = c % 3, (c + 1) % 3
            nw = SR(k1)
            wv = bass.AP(SMT.tensor, SMT.offset + 128 + 128 * k0, [list(SMT.ap[0]), [32, 2], [32, 2], [1, 32]])
            rp, rn = ("SR", k0), ("SR", k1)
            dv("s12", lambda e, wv=wv: e.tensor_tensor(h4(PP), h4(A12), wv, ALU.mult), ["A1", "A2", rp], ["PP"])
            dv("s3", lambda e: e.tensor_tensor(j3(TT), h4(PP)[:, 0], h4(PP)[:, 1], ALU.add), ["PP"], ["TT1"])
            dv("s4", lambda e, nw=nw, c=c: e.tensor_tensor(h4(nw), j3(TT).unsqueeze(1).broadcast_to([128, 2, 2, 32]),
                                                         B2S4[:, :, :, c + 1].unsqueeze(1).broadcast_to([128, 2, 2, 32]), ALU.add),
               ["TT1", ("b2S", c + 1)], [rn])
            dv("s5", lambda e, nw=nw, c=c: e.tensor_copy(B2S4[:, :, :, c + 1], j3(nw[:, 0:64])), [rn], [("b2S", c + 1)], "pool")

        W3H = {b: self.UNI[:, b * 2048:(b + 1) * 2048].bitcast(BF16).rearrange("p (j q r c) -> p j q r c", j=4, q=4, r=2) for b in range(3)}
        R_W3H = {b: ["h%d" % i_ for i_ in range(4 * b, 4 * b + 4)] for b in range(3)}
        KL = self.UNI[:, 6144:6656].bitcast(BF16).rearrange("p (t c) -> p t c", c=128)
        maskQ = self.cst("maskQ")
        S.op("pool", lambda e: e.memset(self.UNI[:, 0:6144].bitcast(BF16), 0.0), writes=["h%d" % i_ for i_ in range(12)])
        t4 = lambda ap: ap.rearrange("p (j q c) -> p j q c", j=8, q=4)
        wcount = 0
        zcompute_dve(0)
        for fc in range(NFC):
            ypar = 0
            yb = 0
            for tq in range(2):
                bank = 4 + tq
                for i in range(4):
                    t = tq * 4 + i
                    o = self.PS[:, bank * 512 + i * 128:bank * 512 + (i + 1) * 128]
                    S.op("pe", lambda e, o=o, t=t, fc=fc: e.matmul(o, Zre[:, t * 128:(t + 1) * 128], CMR[:, 128 * fc:128 * (fc + 1)], start=True, stop=False),
                         reads=R_ZR + R_CMR, writes=[("p", bank)])
                    S.op("pe", lambda e, o=o, t=t, fc=fc: e.matmul(o, ZimN[:, t * 128:(t + 1) * 128], CMI[:, 128 * fc:128 * (fc + 1)], start=False, stop=True),
                         reads=R_ZI + R_CMI, writes=[("p", bank)])
                S.op("dve", lambda e, tq=tq, bank=bank: e.tensor_tensor(
                    KL[:, tq * 4:(tq + 1) * 4, :], self.PS[:, bank * 512:(bank + 1) * 512].rearrange("p (t c) -> p t c", c=128),
                    maskQ.unsqueeze(1).broadcast_to([128, 4, 128]), ALU.mult), reads=[("p", bank), "CST"], writes=["h12"])
            cmr_b = g3(CMR)[:, 4 * fc:4 * fc + 4, :].unsqueeze(1).broadcast_to([128, 8, 4, 32])
            cmi_b = g3(CMI)[:, 4 * fc:4 * fc + 4, :].unsqueeze(1).broadcast_to([128, 8, 4, 32])
            pr_b = PWR3[:, 1:9, 4 * fc:4 * fc + 4].unsqueeze(3).broadcast_to([128, 8, 4, 32])
            pi_b = PWI3[:, 1:9, 4 * fc:4 * fc + 4].unsqueeze(3).broadcast_to([128, 8, 4, 32])
            wb = [(wcount) % 3, (wcount + 1) % 3]
            wcount += 2
            E = "pool"
            dv("w1", lambda e, a=cmr_b, b=pr_b: e.tensor_tensor(t4(T1), a, b, ALU.mult), R_CMR + ["s5pwr"], ["TMP"], E)
            dv("w2", lambda e, a=cmi_b, b=pi_b: e.tensor_tensor(t4(T2), a, b, ALU.mult), R_CMI + ["s5pwi"], ["TMP2"], E)
            for hb in range(2):
                for q in range(4):
                    dv("w3", lambda e, q=q, hb=hb, wv=W3H[wb[hb]]: e.tensor_tensor(wv[:, :, q, 0, 32 * q:32 * q + 32], t4(T1)[:, 4 * hb:4 * hb + 4, q, :],
                                                                               t4(T2)[:, 4 * hb:4 * hb + 4, q, :], ALU.subtract),
                       ["TMP", "TMP2"], R_W3H[wb[hb]], E)
            dv("w4", lambda e, a=cmi_b, b=pr_b: e.tensor_tensor(t4(T1), a, b, ALU.mult), R_CMI + ["s5pwr"], ["TMP"], E)
            dv("w5", lambda e, a=cmr_b, b=pi_b: e.tensor_tensor(t4(T2), a, b, ALU.mult), R_CMR + ["s5pwi"], ["TMP2"], E)
            dv("w5b", lambda e: e.tensor_tensor(T1, T1, T2, ALU.add), ["TMP", "TMP2"], ["TMP"], E)
            for hb in range(2):
                for q in range(4):
                    dv("w6", lambda e, q=q, hb=hb, wv=W3H[wb[hb]]: e.activation(out=wv[:, :, q, 1, 32 * q:32 * q + 32], in_=t4(T1)[:, 4 * hb:4 * hb + 4, q, :],
                                                                            func=AF.Copy, scale=-1.0),
                       ["TMP"], R_W3H[wb[hb]], "act")
            if fc + 1 < NFC:
                zcompute_dve(fc + 1)
            for t in range(8):
                for b in range(4):
                    jlo, jhi = max(2 * b, t), 2 * b + 2
                    if jlo >= jhi:
                        continue
                    S.op("pe", lambda e, t=t, jlo=jlo, jhi=jhi, fc=fc, yb=yb: e.matmul(
                        self.PS[:, yb + jlo * 256:yb + jhi * 256], KL[:, t, :],
                        self.BIGA[:, fc * L + (jlo - t) * 256:fc * L + (jhi - t) * 256], start=(t == 0), stop=False),
                        reads=["h12", ("A", fc)], writes=[("p", 4 * ypar + b)])
            for j in range(8):
                hb = j // 4
                for q in range(4):
                    for ri in range(2):
                        S.op("pe", lambda e, j=j, q=q, ri=ri, fc=fc, yb=yb, wv=W3H[wb[hb]]: e.matmul(
                            self.PS[:, yb + j * 256:yb + (j + 1) * 256], wv[:, j % 4, q, ri, :], B2S4[:, ri, 4 * fc + q, 0:256],
                            start=False, stop=(q == 3 and ri == 1)),
                            reads=R_W3H[wb[hb]] + ALLC, writes=[("p", 4 * ypar + j // 2)])
            for hv in range(2):
                yv = self.PS[:, yb + hv * 1024:yb + (hv + 1) * 1024]
                sc = self.SMALL[:, 0:1024]
                ry = [("p", 4 * ypar + 2 * hv), ("p", 4 * ypar + 2 * hv + 1)]
                rs = self.S5_SMALL
                ug = self.BIGA[:, fc * L + hv * 1024:fc * L + (hv + 1) * 1024]
                S.op("dve", lambda e, yv=yv, ug=ug, fc=fc: e.scalar_tensor_tensor(out=yv, in0=ug, scalar=self.vcol(("s5_D", l), fc), in1=yv,
                                                                          op0=ALU.mult, op1=ALU.add), reads=ry + [("A", fc), "VEC"], writes=ry)
                S.op("act", lambda e, yv=yv: e.activation(out=sc, in_=yv, func=AF.Square, scale=math.sqrt(0.044715)), reads=ry, writes=rs)
                S.op("dve", lambda e, yv=yv: e.scalar_tensor_tensor(out=sc, in0=sc, scalar=1.0, in1=yv, op0=ALU.add, op1=ALU.mult),
                     reads=rs + ry, writes=rs)
                S.op("act", lambda e: e.activation(out=sc, in_=sc, func=AF.Sigmoid, scale=2.0 * math.sqrt(2.0 / math.pi)), reads=rs, writes=rs)
                S.op("dve", lambda e, yv=yv, ug=ug: e.tensor_tensor(ug, yv, sc, ALU.mult), reads=rs + ry, writes=[("A", fc)])

        jobs = []
        for oc in range(NFC):
            def evac_zb(ps, psres, oc=oc):
                S.op("act", lambda e: e.activation(out=self.FA(), in_=ps, func=AF.Sigmoid, bias=self.vcol(("b_glu", l), 8 + oc), scale=1.0),
                     reads=psres + ["VEC"], writes=FAR)

            def evac_za(ps, psres, oc=oc):
                S.op("dve", lambda e: e.scalar_tensor_tensor(out=self.FB(), in0=ps, scalar=self.vcol(("b_glu", l), oc), in1=self.FA(),
                                                             op0=ALU.add, op1=ALU.mult), reads=psres + ["VEC"] + FAR, writes=FBR)
                hv = self.HT[:, oc * L:(oc + 1) * L].rearrange("p (c j) -> p j c", j=8)
                S.op("dve", lambda e: e.tensor_tensor(hv, hv, self.FB().rearrange("p (j c) -> p j c", c=256), ALU.add),
                     reads=FBR + [("H", oc)], writes=[("H", oc)])
            for col0, ev in ((D + oc * 128, evac_zb), (oc * 128, evac_za)):
                jobs.append(dict(w2d=self.w_glu_d[l], k0=0, KC=8, col0=col0,
                                 rhs=lambda kc, tt: self.BIGA[:, kc * L + tt * 512:kc * L + (tt + 1) * 512],
                                 rhs_res=lambda kc: [("A", kc)], evac=ev))
        self.run_dense(jobs)

    def load_w(self, w2d, k0, KC, col0):
        S = self.S
        sl = self.wslot
        self.wslot = (self.wslot + 1) % self.NSLOT
        src = w2d[k0:k0 + KC * 128, col0:col0 + 128].rearrange("(kc p) c -> p kc c", p=128)
        S.dma("pool", self.WBF(sl)[:, 0:KC * 128].rearrange("p (kc c) -> p kc c", c=128), src, writes=self.wres(sl))
        return sl

    def load_rope(self):
        S = self.S
        S.dma("sp", self.FB(), self.rope_d[:, 0:L], writes=[("FB", t) for t in range(4)])
        S.dma("sp", self.FC(), self.rope_d[:, L:2 * L], writes=[("FC", t) for t in range(4)])

    def qk_unit(self, sl, hf, gcol):
        S = self.S
        u = hf
        c0, c1 = hf * 1024, (hf + 1) * 1024
        FAR = [("FA", 2 * hf), ("FA", 2 * hf + 1)]
        FBR = [("FB", 2 * hf), ("FB", 2 * hf + 1)]
        FCR = [("FC", 2 * hf), ("FC", 2 * hf + 1)]
        rb = self.wres(sl)
        pb = 4 * u
        P = self.PS[:, pb * 512:(pb + 2) * 512]
        P2 = self.PS[:, (pb + 2) * 512:(pb + 4) * 512]
        RP = [("p", pb), ("p", pb + 1)]
        RP2 = [("p", pb + 2), ("p", pb + 3)]
        XG = self.FW[:, c0:c1]
        SQ = self.BIGB[:, u * 1024:(u + 1) * 1024]
        SQR = self.bres(u * 1024, (u + 1) * 1024)

        def st0():
            for kc in range(8):
                for t in range(2):
                    S.op("pe", lambda e, t=t, kc=kc: e.matmul(self.PS[:, (pb + t) * 512:(pb + t + 1) * 512], self.WBF(sl)[:, kc * 128:(kc + 1) * 128],
                                                            self.BIGA[:, kc * L + c0 + t * 512:kc * L + c0 + (t + 1) * 512], start=(kc == 0), stop=(kc == 7)),
                         reads=rb + [("A", kc)], writes=[("p", pb + t)])
            S.op("act", lambda e: e.activation(out=XG, in_=P, func=AF.Copy, scale=gcol), reads=RP + ["VEC"], writes=FAR)
            S.op("act", lambda e: e.activation(out=SQ, in_=P, func=AF.Square), reads=RP, writes=SQR)

        def st1():
            for t in range(2):
                S.op("pe", lambda e, t=t: e.matmul(self.PS[:, (pb + 2 + t) * 512:(pb + 3 + t) * 512], self.onesH(), SQ[:, t * 512:(t + 1) * 512],
                                                 start=True, stop=True), reads=SQR + ["CSTB"], writes=[("p", pb + 2 + t)])
            S.op("act", lambda e: e.activation(out=P2, in_=P2, func=AF.Ln, bias=self.eps_ap(), scale=1.0), reads=RP2 + ["EPS"], writes=RP2)
            S.op("act", lambda e: e.activation(out=P2, in_=P2, func=AF.Exp, scale=-0.5), reads=RP2, writes=RP2)
            pswap = self.cst("pswap")
            for t in range(2):
                S.op("pe", lambda e, t=t: e.matmul(self.PS[:, (pb + t) * 512:(pb + t + 1) * 512], pswap, XG[:, t * 512:(t + 1) * 512], start=True, stop=True),
                     reads=FAR + ["CST"], writes=[("p", pb + t)])
            S.op("dve", lambda e: e.tensor_tensor(XG, XG, self.FW[:, L + c0:L + c1], ALU.mult), reads=FAR + FBR, writes=FAR)
            S.op("dve", lambda e: e.tensor_tensor(P, P, self.FW[:, 2 * L + c0:2 * L + c1], ALU.mult), reads=RP + FCR, writes=RP)
            S.op("dve", lambda e: e.tensor_tensor(XG, XG, P, ALU.add), reads=FAR + RP, writes=FAR)
            S.op("dve", lambda e: e.tensor_tensor(XG, XG, P2, ALU.mult), reads=FAR + RP2, writes=FAR)
        return st0, st1, (XG, FAR, P2, RP2)

    @staticmethod
    def run_staged(units):
        nst = max(len(u) for u in units)
        for it in range(len(units) + nst - 1):
            for k in range(nst - 1, -1, -1):
                i = it - k
                if 0 <= i < len(units) and k < len(units[i]):
                    units[i][k]()

    ATT_SMALL = ["KMEAN", "G0", "G1", "RINV", "NSB"] + [("ACC", i) for i in range(4)] + [("M8", i) for i in range(8)]
    S5_SMALL = ["A1", "A2", "TT1", "PP"] + [("SR", k_) for k_ in range(3)] + ["sm%d" % i_ for i_ in range(7)]
    S5S_ALL = ["s5raw", "s5dt", "s5ar", "s5ai", "s5mag", "s5ang", "s5nt", "s5c9", "s5pwr", "s5pwi", "s5cr", "s5ci"]

    def kv_phase(self):
        S = self.S
        S.op("dve", lambda e: e.memset(self.SMALL[:], 0.0), writes=self.ATT_SMALL + self.S5_SMALL)
        self.rms_rstd(self.FA(), "FA")
        self.make_xn("kv_norm", self.FA(), "FA")
        self.load_rope()
        KMEAN = self.SMALL[:, 0:64]
        KB = lambda k: self.BIGB[:, 2048 + k * 2048:2048 + (k + 1) * 2048]
        KBR = lambda k: self.bres(2048 + k * 2048, 2048 + (k + 1) * 2048)
        VHo = lambda k: 6144 + k * 3072
        VH = lambda k: self.BIGB[:, VHo(k):VHo(k) + 2064]
        VHR = lambda k: self.bres(VHo(k), VHo(k) + 2064)
        for k in range(2):
            S.op("pool", lambda e, k=k: e.memset(VH(k).rearrange("p (t c) -> p t c", c=129)[:, :, 128:129], 1.0), writes=VHR(k))
        slk = {}
        slv = {}
        kunits = []
        for hd in range(NH):
            kp = hd % 2
            for hf in range(2):
                hold = {}

                def st0(hd=hd, hf=hf, hold=hold):
                    if hf == 0:
                        if hd == 0:
                            slk[0] = self.load_w(self.w_kv_d, 0, 8, 0)
                            slv[0] = self.load_w(self.w_kv_d, 0, 8, D)
                        if hd + 1 < NH:
                            slk[hd + 1] = self.load_w(self.w_kv_d, 0, 8, (hd + 1) * 128)
                            slv[hd + 1] = self.load_w(self.w_kv_d, 0, 8, D + (hd + 1) * 128)
                    a, b, h = self.qk_unit(slk[hd], hf, self.vcol("k_norm"))
                    hold["st1"] = b
                    hold["h"] = h
                    a()

                def st1(hold=hold):
                    hold["st1"]()

                def st2(hd=hd, hf=hf, kp=kp, hold=hold):
                    XG, FAR, P2, RP2 = hold["h"]
                    S.op("dve", lambda e: e.reduce_sum(out=KMEAN[:, hd * 8 + hf * 4:hd * 8 + hf * 4 + 4],
                                                      in_=XG.rearrange("p (n t) -> p n t", t=256), axis=AX.X),
                         reads=FAR, writes=["KMEAN"])
                    S.op("act", lambda e: e.activation(out=KB(kp)[:, hf * 1024:(hf + 1) * 1024], in_=XG, func=AF.Copy),
                         reads=FAR, writes=self.bres(2048 + kp * 2048 + hf * 1024, 2048 + kp * 2048 + (hf + 1) * 1024))
                    if hf == 1:
                        S.dma("sp", self.kT_s[hd], KB(kp), reads=KBR(kp), writes=[("kT_s", hd)])
                kunits.append([st0, st1, st2])
            for hf in range(2):
                def vunit(hd=hd, hf=hf, kp=kp):
                    sl = slv[hd]
                    rb = self.wres(sl)
                    VH3 = VH(kp).rearrange("p (t c) -> p t c", c=129)
                    pb = 4 * hf
                    for t8 in range(8):
                        t16 = hf * 8 + t8
                        for kc in range(8):
                            S.op("pe", lambda e, t16=t16, t8=t8, kc=kc: e.matmul(
                                self.PS[:, pb * 512 + t8 * 128:pb * 512 + (t8 + 1) * 128], self.BIGA[:, kc * L + t16 * 128:kc * L + (t16 + 1) * 128],
                                self.WBF(sl)[:, kc * 128:(kc + 1) * 128], start=(kc == 0), stop=(kc == 7)),
                                reads=rb + [("A", kc)], writes=[("p", pb + t8 // 4)])
                    S.op("act", lambda e: e.activation(out=VH3[:, hf * 8:(hf + 1) * 8, 0:128],
                                                       in_=self.PS[:, pb * 512:(pb + 2) * 512].rearrange("p (t c) -> p t c", c=128), func=AF.Copy),
                         reads=[("p", pb), ("p", pb + 1)], writes=VHR(kp))
                    if hf == 1:
                        S.dma("sp", self.V_s[hd], VH(kp), reads=VHR(kp), writes=[("V_s", hd)])
                kunits.append([vunit])
        self.run_staged(kunits)
        S.op("dve", lambda e: e.tensor_scalar(KMEAN, KMEAN, 1.0 / 256, None, ALU.mult), reads=["KMEAN"], writes=["KMEAN"])

    def moba_layer(self, j):
        S = self.S
        self.rms_rstd(self.FA(), "FA")
        self.make_xn(("b_norm", j), self.FA(), "FA")
        self.load_rope()
        KMEAN = self.SMALL[:, 0:64]
        GT = lambda hf: self.SMALL[:, 64 + 64 * hf:128 + 64 * hf]
        M8 = lambda i: self.SMALL[:, 320 + 8 * i:328 + 8 * i]
        RINV = self.SMALL[:, 192:196]
        ACC = lambda i: self.SMALL[:, 384 + 129 * i:384 + 129 * (i + 1)]
        SELALL = self.S5S[:, 0:1024]
        futmask = self.cst("futmask")
        scale = 1.0 / math.sqrt(128.0)
        S.op("dve", lambda e: e.memset(SELALL, 0.0), writes=self.S5S_ALL + ["SELALL"])
        QB = lambda k: self.BIGB[:, 2048 + k * 2048:2048 + (k + 1) * 2048]
        QBR = lambda k: self.bres(2048 + k * 2048, 2048 + (k + 1) * 2048)
        slq = {}
        qunits = []
        ownm = self.cst("ownmask")
        NSB = self.SMALL[0:8, 384:896].bitcast(BF16)
        for hd in range(NH):
            kp = hd % 2
            for hf in range(2):
                def st_load(hd=hd, hf=hf):
                    if hf == 0:
                        if hd == 0:
                            slq[0] = self.load_w(self.w_q_d[j], 0, 8, 0)
                        if hd + 1 < NH:
                            slq[hd + 1] = self.load_w(self.w_q_d[j], 0, 8, (hd + 1) * 128)
                hold = {}

                def st0(hd=hd, hf=hf, hold=hold, st_load=st_load):
                    st_load()
                    a, b, h = self.qk_unit(slq[hd], hf, self.vcol(("q_norm", j)))
                    hold["st1"] = b
                    hold["h"] = h
                    a()

                def st1(hold=hold):
                    hold["st1"]()

                def st2(hd=hd, hf=hf, kp=kp, hold=hold):
                    XG, FAR, P2, RP2 = hold["h"]
                    S.op("act", lambda e: e.activation(out=QB(kp)[:, hf * 1024:(hf + 1) * 1024], in_=XG, func=AF.Copy),
                         reads=FAR, writes=self.bres(2048 + kp * 2048 + hf * 1024, 2048 + kp * 2048 + (hf + 1) * 1024))
                    for q8 in range(8):
                        S.op("pe", lambda e, q8=q8: e.matmul(P2[:, q8 * 8:(q8 + 1) * 8], XG[:, q8 * 128:(q8 + 1) * 128],
                                                           KMEAN[:, hd * 8:(hd + 1) * 8], start=True, stop=True),
                             reads=FAR + ["KMEAN"], writes=[RP2[0]])
                    G = GT(hf)
                    gres = "G%d" % hf
                    S.op("dve", lambda e: e.tensor_tensor(G, P2[:, 0:64], futmask[:, hf * 64:(hf + 1) * 64], ALU.add),
                         reads=[RP2[0], "CST"], writes=[gres])
                    selh = SELALL[:, hd * 128 + hf * 64:hd * 128 + (hf + 1) * 64]
                    if hf == 0:
                        S.op("dve", lambda e: e.tensor_scalar(selh[:, 16:64], G[:, 16:64], -1e29, None, ALU.is_gt),
                             reads=[gres], writes=["SELALL"])
                    else:
                        for q8 in range(8):
                            S.op("dve", lambda e, q8=q8: e.max(out=M8(q8), in_=G[:, q8 * 8:(q8 + 1) * 8]), reads=[gres], writes=[("M8", q8)])
                            S.op("dve", lambda e, q8=q8: e.tensor_scalar(selh[:, q8 * 8:(q8 + 1) * 8], G[:, q8 * 8:(q8 + 1) * 8],
                                                                        M8(q8)[:, 2:3], None, ALU.is_ge),
                                 reads=[gres, ("M8", q8)], writes=["SELALL"])
                    S.op("dve", lambda e: e.tensor_tensor(selh, selh, ownm[:, hf * 64:(hf + 1) * 64], ALU.add),
                         reads=["SELALL", "CST"], writes=["SELALL"])

                def st3(hd=hd, hf=hf, kp=kp, hold=hold):
                    XG, FAR, P2, RP2 = hold["h"]
                    selh = SELALL[:, hd * 128 + hf * 64:hd * 128 + (hf + 1) * 64]
                    for q8 in range(8):
                        S.op("pe", lambda e, q8=q8: e.transpose(P2[0:8, q8 * 128:(q8 + 1) * 128], selh[:, q8 * 8:(q8 + 1) * 8], self.cst("ident")),
                             reads=["SELALL", "CST"], writes=[RP2[q8 // 4]])
                    S.op("dve", lambda e: e.tensor_scalar(NSB, P2[0:8, 0:1024], -1.0, 30000.0, ALU.add, ALU.mult), reads=RP2, writes=["NSB"])
                    S.dma("sp", self.NS_s[hd][:, hf * 1024:(hf + 1) * 1024], NSB, reads=["NSB"], writes=[("NS_s", hd, hf)])
                    if hf == 1:
                        S.dma("sp", self.Q_s[hd], QB(kp), reads=QBR(kp), writes=[("Q_s", hd)])
                qunits.append([st0, st1, st2, st3])
        self.run_staged(qunits)

        QBF = lambda k: self.BIGB[:, k * 2048:(k + 1) * 2048]
        KTH = lambda k: self.BIGB[:, 4096 + k * 2048:4096 + (k + 1) * 2048]
        VHo = lambda k: 8192 + k * 2064
        VH = lambda k: self.BIGB[:, VHo(k):VHo(k) + 2064]
        ET = lambda i: self.BIGB[:, 12320 + 512 * i:12320 + 512 * (i + 1)]
        ETR = lambda i: self.bres(12320 + 512 * i, 12320 + 512 * (i + 1))
        OTH = self.BIGB[:, 14368:14368 + L]
        OTHR = self.bres(14368, 14368 + L)
        RQ = lambda k: [("cQ", k)]
        RK = lambda k: [("cK", k)]
        RV = lambda k: [("cV", k)]
        OPR = lambda par, i: self.PS[:, (2 + 2 * par + i // 2) * 512 + (i % 2) * 256:(2 + 2 * par + i // 2) * 512 + (i % 2) * 256 + 129]
        opres = lambda par, i: ("p", 2 + 2 * par + i // 2)

        def load_head(hd):
            k = hd % 2
            S.dma("sp", QBF(k), self.Q_s[hd], reads=[("Q_s", hd)], writes=RQ(k))
            S.dma("sp", KTH(k), self.kT_s[hd], reads=[("kT_s", hd)], writes=RK(k))
            S.dma("sp", VH(k), self.V_s[hd], reads=[("V_s", hd)], writes=RV(k))

        DEPTH = 2
        NET = 6
        ET = lambda i: self.BIGB[:, 12320 + 512 * i:12320 + 512 * (i + 1)]
        ETR = lambda i: [("cE", i)]
        OTH = self.TMP2[:].bitcast(BF16)
        OTHR = ["TMP2"]
        NEGSEL = lambda k: self.FW[0:8, k * 1024:(k + 1) * 1024].bitcast(BF16)
        NSR = lambda k: [("FA", 2 * k), ("FA", 2 * k + 1)]
        RB = lambda qp: self.FW[:, L + qp * 512:L + (qp + 1) * 512]
        RBR = lambda qp: [("FB", qp)]
        EONE = self.S5S[0:8, 1024:1536].bitcast(BF16)
        identf = self.cst("ident")
        CORE_NAMES = [("cQ", 0), ("cQ", 1), ("cK", 0), ("cK", 1), ("cV", 0), ("cV", 1)] + [("cE", i) for i in range(NET)]
        S.op("dve", lambda e: e.tensor_copy(EONE.rearrange("p (n c) -> p n c", c=128), identf[0:8, 0:8].unsqueeze(2).broadcast_to([8, 8, 128])),
             reads=["CST"], writes=["EONE", "TMP"] + self.bres(0, 16512) + CORE_NAMES)
        state = dict(st=0, et=0)
        tasks = []

        def load_head(hd):
            k = hd % 2
            S.dma("sp", QBF(k), self.Q_s[hd], reads=[("Q_s", hd)], writes=RQ(k))
            S.dma("sp", KTH(k), self.kT_s[hd], reads=[("kT_s", hd)], writes=RK(k))
            S.dma("sp", VH(k), self.V_s[hd], reads=[("V_s", hd)], writes=RV(k))
            S.dma("sp", NEGSEL(k), self.NS_s[hd], reads=[("NS_s", hd, 0), ("NS_s", hd, 1)], writes=NSR(k))

        def mk_block(hd, Q, n, qp):
            hk = hd % 2
            VH3 = VH(hk).rearrange("p (t c) -> p t c", c=129)
            info = {}
            lastkt = 4 * Q + 3

            def s1():
                ets = {}
                for kt in (2 * n, 2 * n + 1):
                    c0 = max(4 * Q, kt) - 4 * Q
                    if c0 > 3:
                        continue
                    sb_ = state["st"] % 4
                    state["st"] += 1
                    eb_ = state["et"] % NET
                    state["et"] += 1
                    ST = self.PS[:, sb_ * 512:(sb_ + 1) * 512]
                    diag = kt >= 4 * Q
                    selm = n < 2 * Q + 1
                    S.op("pe", lambda e, ST=ST, c0=c0, fin=(not diag and not selm), kk_=KTH(hk)[:, kt * 128:(kt + 1) * 128],
                         qq_=QBF(hk)[:, Q * 512 + c0 * 128:(Q + 1) * 512]: e.matmul(
                        ST[:, c0 * 128:512], kk_, qq_, start=True, stop=fin), reads=RQ(hk) + RK(hk), writes=[("p", sb_)])
                    if selm:
                        S.op("pe", lambda e, ST=ST, c0=c0, fin=(not diag), en=EONE[:, n * 128:(n + 1) * 128],
                             ns=NEGSEL(hk)[:, Q * 512 + c0 * 128:(Q + 1) * 512]: e.matmul(ST[:, c0 * 128:512], en, ns, start=False, stop=fin),
                             reads=["EONE"] + NSR(hk), writes=[("p", sb_)])
                    if diag:
                        S.op("pe", lambda e, ST=ST, c0=c0: e.matmul(ST[:, c0 * 128:(c0 + 1) * 128], self.identB(), self.cmaskB(),
                                                                   start=False, stop=True), reads=["CSTB"], writes=[("p", sb_)])
                    et = ET(eb_)
                    S.op("act", lambda e, ST=ST, et=et, c0=c0: e.activation(out=et[:, c0 * 128:512], in_=ST[:, c0 * 128:512], func=AF.Exp, scale=scale),
                         reads=[("p", sb_)], writes=ETR(eb_))
                    ets[kt] = (et, eb_, c0)
                info["ets"] = ets

            def s2():
                if Q == 0 and n == 0 and hd + 1 < NH:
                    load_head(hd + 1)
                for kt in sorted(info["ets"]):
                    et, eb_, c0 = info["ets"][kt]
                    ob, rbk = 4 + qp, 6 + qp
                    S.op("pe", lambda e, et=et, c0=c0, vv=VH3[:, kt, 0:128], a=(kt == 0), z=(kt == lastkt), ob=ob: e.matmul(
                        self.PS[:, ob * 512 + c0 * 128:(ob + 1) * 512], vv, et[:, c0 * 128:512], start=a, stop=z),
                        reads=ETR(eb_) + RV(hk), writes=[("p", ob)])
                    S.op("pe", lambda e, et=et, c0=c0, a=(kt == 0), z=(kt == lastkt), rbk=rbk: e.matmul(
                        self.PS[:, rbk * 512 + c0 * 128:(rbk + 1) * 512], self.onesH(), et[:, c0 * 128:512], start=a, stop=z),
                        reads=ETR(eb_) + ["CSTB"], writes=[("p", rbk)])
            return s1, s2

        def mk_qend(hd, Q, qp):
            def s1():
                pass

            def s2():
                ob, rbk = 4 + qp, 6 + qp
                R = RB(qp)
                S.op("act", lambda e: e.activation(out=R, in_=self.PS[:, rbk * 512:(rbk + 1) * 512], func=AF.Ln, scale=128.0),
                     reads=[("p", rbk)], writes=RBR(qp))
                S.op("act", lambda e: e.activation(out=R, in_=R, func=AF.Exp, scale=-1.0), reads=RBR(qp), writes=RBR(qp))
                S.op("dve", lambda e: e.tensor_tensor(OTH[:, Q * 512:(Q + 1) * 512], self.PS[:, ob * 512:(ob + 1) * 512], R, ALU.mult),
                     reads=[("p", ob)] + RBR(qp), writes=OTHR)
                if Q == 3:
                    S.dma("sp", self.OT_s[hd], OTH, reads=OTHR, writes=[("OT_s", hd)])
            return s1, s2

        load_head(0)
        qcount = 0
        for hd in range(NH):
            for Q in range(4):
                qp = qcount % 2
                qcount += 1
                for n in range(2 * Q + 2):
                    tasks.append(mk_block(hd, Q, n, qp))
                tasks.append(mk_qend(hd, Q, qp))
        for idx in range(len(tasks) + DEPTH):
            if idx < len(tasks):
                tasks[idx][0]()
            if idx - DEPTH >= 0:
                tasks[idx - DEPTH][1]()
        S.op("dve", lambda e: e.memset(self.SMALL[:, 192:200], 1.0), reads=["EONE"], writes=["TMP"] + self.bres(0, 16512) + CORE_NAMES)
        for hd in range(NH):
            S.dma("sp", self.BIGA[:, hd * L:(hd + 1) * L], self.OT_s[hd], reads=[("OT_s", hd)], writes=[("A", hd)])
        jobs = []
        for oc in range(NFC):
            def evac_o(ps, psres, oc=oc):
                S.op("dve", lambda e: e.tensor_tensor(self.HT[:, oc * L:(oc + 1) * L], self.HT[:, oc * L:(oc + 1) * L], ps, ALU.add),
                     reads=psres + [("H", oc)], writes=[("H", oc)])
            jobs.append(dict(w2d=self.w_o_d[j], k0=0, KC=8, col0=oc * 128,
                             rhs=lambda kc, tt: self.BIGA[:, kc * L + tt * 512:kc * L + (tt + 1) * 512],
                             rhs_res=lambda kc: [("A", kc)], evac=evac_o))
        self.run_dense(jobs)

    def build(self):
        with ExitStack() as es:
            self.declare(es)
            self.EPS_T = es.enter_context(self.nc.sbuf_tensor("EPS_T", [128, 2], F32))
            self.S.op("pool", lambda e: e.memset(self.EPS_T[:], EPS), writes=["EPS"])
            self.prologue()
            self.body()
            self.epilogue()
            self.S.emit(self.sems, self.dsems)
        return self.nc

    def body(self):
        st = self.stop
        if st == "ffn_only":
            self.ffn(0)
            return
        self.s5_layer(0)
        if st == "mix0":
            return
        self.ffn_hook = lambda: self.s5_layer(1, "params")
        self.ffn(0)
        if st == "ffn0":
            return
        self.s5_layer(1, "main")
        if st == "mix1":
            return
        self.ffn(1)
        if st == "ffn1":
            return
        self.kv_phase()
        self.moba_layer(0)
        if st == "mix2":
            return
        self.ffn(2)
        if st == "ffn2":
            return
        self.moba_layer(1)
        if st == "mix3":
            return
        self.ffn(3)


def _consts():
    c = np.zeros((128, NCST), np.float32)
    p = np.arange(128)
    c[:, CC["ident"]:CC["ident"] + 128] = np.eye(128, dtype=np.float32)
    sw = np.zeros((128, 128), np.float32)
    sw[(p + 64) % 128, p] = 1.0
    c[:, CC["pswap"]:CC["pswap"] + 128] = sw
    c[:, CC["maskQ"]:CC["maskQ"] + 128] = (p[:, None] // 32 == p[None, :] // 32).astype(np.float32)
    c[:, CC["onesD"]:CC["onesD"] + 128] = 1.0 / D
    c[:, CC["onesH"]:CC["onesH"] + 128] = 1.0 / 128
    c[:, CC["cmask"]:CC["cmask"] + 128] = np.where(p[:, None] > p[None, :], NEG, 0.0)
    fm = np.zeros((16, 8), np.float32)
    for qt in range(16):
        fm[qt, (qt // 2):] = -1e30
    c[:, CC["futmask"]:CC["futmask"] + 128] = fm.reshape(1, 128)
    om = np.zeros((16, 8), np.float32)
    for qt in range(16):
        om[qt, qt // 2] = 1.0
    c[:, CC["ownmask"]:CC["ownmask"] + 128] = om.reshape(1, 128)
    rm = np.zeros((128, 4), np.float32)
    for v in range(2):
        rm[:, v] = ((p // 32) % 2 == v)
        rm[:, 2 + v] = -rm[:, v]
    c[:, CC["rowmask"]:CC["rowmask"] + 4] = rm
    c[:, CC["kk"]:CC["kk"] + 9] = np.arange(9, dtype=np.float32)[None, :]
    return c


def _rope():
    half = 64
    inv = (10000.0 ** (-np.arange(half, dtype=np.float32) * 2.0 / 128)).astype(np.float32)
    ang = np.arange(L, dtype=np.float32)[:, None] * inv[None, :]
    cos, sin = np.cos(ang).T.astype(np.float32), np.sin(ang).T.astype(np.float32)
    r = np.zeros((128, 2 * L), np.float32)
    r[0:64, 0:L] = cos; r[64:128, 0:L] = cos
    r[0:64, L:] = -sin; r[64:128, L:] = sin
    return r


def _fm(v):
    return np.ascontiguousarray(v.reshape(-1, 128).T)


def _host_prep(inp):
    vec = np.zeros((128, NV), np.float32)
    for l in range(2):
        vec[:, VC[("a_norm", l)]:VC[("a_norm", l)] + 8] = _fm(inp["a_norm"][l])
        vec[:, VC[("s5_D", l)]:VC[("s5_D", l)] + 8] = _fm(inp["s5_D"][l])
        vec[:, VC[("b_glu", l)]:VC[("b_glu", l)] + 16] = _fm(inp["b_glu"][l])
    vec[:, VC["kv_norm"]:VC["kv_norm"] + 8] = _fm(inp["kv_norm"])
    vec[:, VC["k_norm"]] = inp["k_norm"]
    for j in range(2):
        vec[:, VC[("b_norm", j)]:VC[("b_norm", j)] + 8] = _fm(inp["b_norm"][j])
        vec[:, VC[("q_norm", j)]] = inp["q_norm"][j]
    for l in range(4):
        vec[:, VC[("ffn_norm", l)]:VC[("ffn_norm", l)] + 8] = _fm(inp["ffn_norm"][l])
        cw = inp["conv_w"][l].reshape(3, NUPC, 128).transpose(2, 0, 1).reshape(128, 3 * NUPC)
        vec[:, VC[("conv_w", l)]:VC[("conv_w", l)] + 3 * NUPC] = cw
        vec[:, VC[("conv_b", l)]:VC[("conv_b", l)] + NUPC] = _fm(inp["conv_b"][l])
    s5p = np.zeros((2, 128, 96 + 4096), np.float32)
    for l in range(2):
        s5p[l, :, 0:32] = inp["s5_A_re"][l].reshape(32, 128).T
        s5p[l, :, 32:64] = inp["s5_A_im"][l].reshape(32, 128).T
        s5p[l, :, 64:96] = np.repeat(inp["s5_log_dt"][l].reshape(32, 2), 64, axis=1).T
        for nm, off in (("s5_B_re", 0), ("s5_B_im", 1024)):
            Bm = np.zeros((2, 64, 32, 2, 16), np.float32)
            Bg = inp[nm][l].reshape(32, 2, 64, 16)
            for m in range(2):
                Bm[m, :, :, m, :] = Bg[:, m].transpose(1, 0, 2)
            s5p[l, :, 96 + off:96 + off + 1024] = Bm.reshape(128, 1024)
        for nm, off in (("s5_C_re", 2048), ("s5_C_im", 3072)):
            Cm = np.zeros((2, 64, 32, 2, 16), np.float32)
            Cg = inp[nm][l].reshape(32, 2, 16, 64)
            for m in range(2):
                Cm[m, :, :, m, :] = Cg[:, m].transpose(2, 0, 1)
            s5p[l, :, 96 + off:96 + off + 1024] = Cm.reshape(128, 1024)
    common = dict(vec=vec, cst=_consts(), rope=_rope(), s5p=s5p,
                  w_glu=np.ascontiguousarray(inp["w_glu"]), w_kv=np.ascontiguousarray(inp["w_kv"]),
                  w_q=np.ascontiguousarray(inp["w_q"]), w_o=np.ascontiguousarray(inp["w_o"]),
                  w_up=np.ascontiguousarray(inp["w_up"]), w_down=np.ascontiguousarray(inp["w_down"]))
    return common


def _x_to_dev(xb):
    return np.ascontiguousarray(xb.T.reshape(NFC, 128, L).transpose(1, 0, 2).reshape(128, NFC * L))


def _dev_to_x(o):
    return np.ascontiguousarray(o.reshape(128, NFC, L).transpose(1, 0, 2).reshape(D, L).T)


def run(inp, stop=None, ncores=8):
    inp = {k: np.asarray(v) for k, v in inp.items()}
    common = _host_prep(inp)
    b = Builder(stop=stop)
    nc = b.build()
    in_maps = []
    for c in range(ncores):
        m = dict(common)
        m["xT"] = _x_to_dev(inp["x"][c])
        in_maps.append(m)
    res = run_bass_kernel_spmd(nc, in_maps, core_ids=list(range(ncores)))
    return np.stack([_dev_to_x(res.results[c]["outT"]) for c in range(ncores)], axis=0)


def kernel(**inputs):
    return run(inputs, stop=None, ncores=8).astype(np.float32)
```

```python
import math
from contextlib import ExitStack

import numpy as np
import concourse.bass as bass
import concourse.mybir as mybir
from concourse.bass_utils import run_bass_kernel_spmd

F32 = mybir.dt.float32
BF16 = mybir.dt.bfloat16
ALU = mybir.AluOpType
AF = mybir.ActivationFunctionType
AX = mybir.AxisListType

L = 2048
D = 1024
NFC = 8
DFF = 2816
NUPC = 44
NACT = 22
NH = 8
EPS = 1e-6
MAGIC = 12582912.0
TWO_PI = 2.0 * math.pi
NEG = -30000.0
ENGS = ("pe", "act", "dve", "pool", "sp")

VC = {}
_c = 0
for _l in range(2):
    VC[("a_norm", _l)] = _c; _c += 8
    VC[("s5_D", _l)] = _c; _c += 8
    VC[("b_glu", _l)] = _c; _c += 16
VC["kv_norm"] = _c; _c += 8
VC["k_norm"] = _c; _c += 1
for _j in range(2):
    VC[("b_norm", _j)] = _c; _c += 8
    VC[("q_norm", _j)] = _c; _c += 1
for _l in range(4):
    VC[("ffn_norm", _l)] = _c; _c += 8
    VC[("conv_w", _l)] = _c; _c += 3 * NUPC
    VC[("conv_b", _l)] = _c; _c += NUPC
NV = _c
CC = {}
_c = 0
for _n, _w in (("ident", 128), ("pswap", 128), ("maskQ", 128), ("onesD", 128), ("onesH", 128),
               ("cmask", 128), ("futmask", 128), ("rowmask", 4), ("kk", 9), ("ownmask", 128)):
    CC[_n] = _c; _c += _w
NCST = _c


class _Op:
    __slots__ = ("eng", "fn", "deps", "dma", "sig", "cnt", "sem", "semval", "semprev", "idx")


class Sched:
    def __init__(self, nc):
        self.nc = nc
        self.ops = []
        self.res = {}

    def op(self, eng, fn, reads=(), writes=(), dma=False):
        o = _Op()
        o.eng = eng; o.fn = fn; o.dma = dma; o.sig = False; o.idx = len(self.ops)
        deps = set()
        for r in reads:
            st = self.res.get(r)
            if st is not None and st[0] is not None:
                deps.add(st[0])
        for w in writes:
            st = self.res.get(w)
            if st is not None:
                if st[0] is not None:
                    deps.add(st[0])
                deps.update(st[1])
        for r in reads:
            st = self.res.setdefault(r, [None, []])
            st[1].append(o.idx)
        for w in writes:
            self.res[w] = [o.idx, []]
        deps.discard(o.idx)
        o.deps = deps
        self.ops.append(o)
        return o.idx

    def dma(self, eng, out, in_, reads=(), writes=(), **kw):
        return self.op(eng, lambda e: e.dma_start(out=out, in_=in_, **kw), reads, writes, dma=True)

    def emit(self, sems, dma_sems):
        ops = self.ops
        for o in ops:
            latest = {}
            ddeps = []
            for d in o.deps:
                od = ops[d]
                if od.dma:
                    ddeps.append(d)
                elif od.eng != o.eng or o.dma or o.eng != "pe":
                    if od.eng not in latest or latest[od.eng] < d:
                        latest[od.eng] = d
            o.deps = (latest, ddeps)
            for d in latest.values():
                ops[d].sig = True
        cnt = {e: 0 for e in ENGS}
        for o in ops:
            if o.dma:
                continue
            if o.sig:
                cnt[o.eng] += 1
            o.cnt = cnt[o.eng]
        semcount = [0] * len(dma_sems)
        k = 0
        for o in ops:
            if o.dma:
                o.sem = k % len(dma_sems)
                o.semprev = semcount[o.sem]
                semcount[o.sem] += 16
                o.semval = semcount[o.sem]
                k += 1
        per_eng = {e: [o for o in ops if o.eng == e] for e in ENGS}

        def run_engine(ename, eng):
            waited = {e: 0 for e in ENGS}
            dwaited = [0] * len(dma_sems)
            for o in per_eng[ename]:
                latest, ddeps = o.deps
                for d in sorted(ddeps):
                    od = ops[d]
                    if dwaited[od.sem] < od.semval:
                        eng.wait_ge(dma_sems[od.sem], od.semval)
                        dwaited[od.sem] = od.semval
                for en_, d in latest.items():
                    od = ops[d]
                    if waited[en_] < od.cnt:
                        eng.wait_ge(sems[en_], od.cnt)
                        waited[en_] = od.cnt
                if o.dma:
                    if o.semprev > 0 and dwaited[o.sem] < o.semprev:
                        eng.wait_ge(dma_sems[o.sem], o.semprev)
                        dwaited[o.sem] = o.semprev
                    o.fn(eng).then_inc(dma_sems[o.sem], 16)
                else:
                    ins = o.fn(eng)
                    if o.sig:
                        ins.then_inc(sems[ename], 1)
            last = {}
            for o in per_eng[ename]:
                if o.dma:
                    last[o.sem] = max(last.get(o.sem, 0), o.semval)
            for s_, v in last.items():
                if dwaited[s_] < v:
                    eng.wait_ge(dma_sems[s_], v)

        with self.nc.Block() as block:
            @block.tensor
            def _(e):
                run_engine("pe", e)

            @block.scalar
            def _(e):
                run_engine("act", e)

            @block.vector
            def _(e):
                run_engine("dve", e)

            @block.gpsimd
            def _(e):
                run_engine("pool", e)

            @block.sync
            def _(e):
                run_engine("sp", e)


class Builder:
    def __init__(self, stop=None):
        self.stop = stop
        self.nc = bass.Bass("TRN2", target_bir_lowering=False)
        self.S = Sched(self.nc)
        self.wslot = 0
        self.pshalf = 0

    def declare(self, es):
        nc = self.nc
        di = lambda n, s, dt=F32: nc.dram_tensor(n, s, dt, kind="ExternalInput").ap()
        self.xT_d = di("xT", [128, NFC * L])
        self.vec_d = di("vec", [128, NV])
        self.cst_d = di("cst", [128, NCST])
        self.rope_d = di("rope", [128, 2 * L])
        self.s5p_d = di("s5p", [2, 128, 96 + 4096])
        self.w_glu_d = di("w_glu", [2, D, 2 * D])
        self.w_kv_d = di("w_kv", [D, 2 * D])
        self.w_q_d = di("w_q", [2, D, D])
        self.w_o_d = di("w_o", [2, D, D])
        self.w_up_d = di("w_up", [4, D, 2 * DFF])
        self.w_down_d = di("w_down", [4, DFF, D])
        self.out_d = nc.dram_tensor("outT", [128, NFC * L], F32, kind="ExternalOutput").ap()
        self.kT_s = nc.dram_tensor("kT_s", [NH, 128, L], BF16, kind="Internal").ap()
        self.V_s = nc.dram_tensor("V_s", [NH, 128, 16 * 129], BF16, kind="Internal").ap()
        self.OT_s = nc.dram_tensor("OT_s", [NH, 128, L], BF16, kind="Internal").ap()
        self.Q_s = nc.dram_tensor("Q_s", [NH, 128, L], BF16, kind="Internal").ap()
        self.NS_s = nc.dram_tensor("NS_s", [NH, 8, L], BF16, kind="Internal").ap()

        sb = lambda n, s, dt=F32: es.enter_context(nc.sbuf_tensor(n, s, dt))
        self.HT = sb("HT", [128, NFC * L])
        self.BIGA = sb("BIGA", [128, NFC * L], BF16)
        self.BIGB = sb("BIGB", [128, 16512], BF16)
        self.FW = sb("FW", [128, 3 * L])
        self.TMP = sb("TMP", [128, 1024])
        self.TMP2 = sb("TMP2", [128, 1024])
        self.UNI = sb("UNI", [128, 6656])
        self.VEC = sb("VEC", [128, NV])
        self.CST = sb("CST", [128, NCST])
        self.CSTB = sb("CSTB", [128, 4 * 128], BF16)
        self.S5S = sb("S5S", [128, 2048])
        self.SMALL = sb("SMALL", [128, 1024])
        self.PS = es.enter_context(nc.psum_tensor("PS", [128, 4096], F32))
        self.sems = {e: es.enter_context(nc.semaphore("s_" + e)) for e in ENGS}
        self.dsems = [es.enter_context(nc.semaphore("d%d" % i)) for i in range(40)]

    def FA(self):
        return self.FW[:, 0:L]

    def FB(self):
        return self.FW[:, L:2 * L]

    def FC(self):
        return self.FW[:, 2 * L:3 * L]

    def cst(self, name, w=128):
        c = CC[name]
        return self.CST[:, c:c + w]

    def vcol(self, key, i=0):
        c = VC[key] + i
        return self.VEC[:, c:c + 1]

    NSLOT = 8

    def WBF(self, i):
        return self.UNI[:, i * 512:(i + 1) * 512].bitcast(BF16)

    @staticmethod
    def wres(i):
        return ["h%d" % i]

    @staticmethod
    def bres(lo, hi):
        return [("B", k) for k in range(lo // 1024, (hi - 1) // 1024 + 1)]

    def prologue(self):
        S = self.S
        S.dma("sp", self.CST[:], self.cst_d, writes=["CST"])
        S.dma("sp", self.VEC[:], self.vec_d, writes=["VEC"])
        for fc in range(NFC):
            S.dma("act" if fc % 2 else "sp", self.HT[:, fc * L:(fc + 1) * L], self.xT_d[:, fc * L:(fc + 1) * L],
                  writes=[("H", fc)])
        for i, n in enumerate(("ident", "onesD", "onesH", "cmask")):
            src = self.cst(n)
            dst = self.CSTB[:, i * 128:(i + 1) * 128]
            S.op("dve", lambda e, d=dst, s=src: e.tensor_copy(d, s), reads=["CST"], writes=["CSTB"])

    def identB(self):
        return self.CSTB[:, 0:128]

    def onesD(self):
        return self.CSTB[:, 128:256]

    def onesH(self):
        return self.CSTB[:, 256:384]

    def cmaskB(self):
        return self.CSTB[:, 384:512]

    def epilogue(self):
        S = self.S
        for fc in range(NFC):
            S.dma("act" if fc % 2 else "sp", self.out_d[:, fc * L:(fc + 1) * L], self.HT[:, fc * L:(fc + 1) * L],
                  reads=[("H", fc)], writes=[("OUT", fc)])

    def rms_rstd(self, dst, dst_res):
        S = self.S

        def stA(tt):
            sq = self.BIGB[:, (tt % 2) * 4096:(tt % 2 + 1) * 4096]
            sqres = self.bres((tt % 2) * 4096, (tt % 2 + 1) * 4096)
            hin = self.HT[:].rearrange("p (f t) -> p f t", t=L)[:, :, tt * 512:(tt + 1) * 512]
            S.op("act", lambda e, o=sq, i=hin: e.activation(out=o.rearrange("p (f t) -> p f t", t=512), in_=i, func=AF.Square),
                 reads=[("H", fc) for fc in range(NFC)], writes=sqres)

        def stB(tt):
            sq = self.BIGB[:, (tt % 2) * 4096:(tt % 2 + 1) * 4096]
            sqres = self.bres((tt % 2) * 4096, (tt % 2 + 1) * 4096)
            bank = 4 + (tt % 2)
            ps = self.PS[:, bank * 512:(bank + 1) * 512]
            for fc in range(NFC):
                S.op("pe", lambda e, o=ps, r=sq[:, fc * 512:(fc + 1) * 512], a=(fc == 0), z=(fc == NFC - 1):
                     e.matmul(o, self.onesD(), r, start=a, stop=z),
                     reads=sqres + ["CSTB"], writes=[("p", bank)])
            d = dst[:, tt * 512:(tt + 1) * 512]
            S.op("act", lambda e, o=d, i=ps: e.activation(out=o, in_=i, func=AF.Ln, bias=self.eps_ap(), scale=1.0),
                 reads=[("p", bank), "EPS"], writes=[(dst_res, tt)])
            S.op("act", lambda e, o=d: e.activation(out=o, in_=o, func=AF.Exp, scale=-0.5), reads=[(dst_res, tt)], writes=[(dst_res, tt)])

        stA(0)
        stA(1)
        for tt in range(4):
            stB(tt)
            if tt + 2 < 4:
                stA(tt + 2)

    def eps_ap(self):
        return self.EPS_T[:, 0:1]

    def make_xn(self, gkey, rstd, rstd_res):
        S = self.S
        for tt in range(4):
            for fc in range(NFC):
                S.op("dve", lambda e, fc=fc, tt=tt: e.scalar_tensor_tensor(
                    out=self.BIGA[:, fc * L + tt * 512:fc * L + (tt + 1) * 512], in0=self.HT[:, fc * L + tt * 512:fc * L + (tt + 1) * 512],
                    scalar=self.vcol(gkey, fc), in1=rstd[:, tt * 512:(tt + 1) * 512], op0=ALU.mult, op1=ALU.mult),
                    reads=[("H", fc), "VEC", (rstd_res, tt)], writes=[("A", fc)])

    def run_dense(self, jobs):
        S = self.S
        n = len(jobs)
        slots = {}
        ahead = self.NSLOT - 2

        def load(i):
            if i >= n:
                return
            jb = jobs[i]
            slots[i] = self.load_w(jb["w2d"], jb["k0"], jb["KC"], jb["col0"])

        for i in range(min(ahead, n)):
            load(i)
        for i in range(n):
            load(i + ahead)
            jb = jobs[i]
            sl = slots[i]
            KC = jb["KC"]
            half = self.pshalf
            self.pshalf ^= 1
            rb = self.wres(sl)
            for kc in range(KC):
                for tt in range(4):
                    bank = half * 4 + tt
                    o = self.PS[:, bank * 512:(bank + 1) * 512]
                    S.op("pe", lambda e, o=o, w=self.WBF(sl)[:, kc * 128:(kc + 1) * 128], r=jb["rhs"](kc, tt), a=(kc == 0), z=(kc == KC - 1):
                         e.matmul(o, w, r, start=a, stop=z),
                         reads=rb + jb["rhs_res"](kc), writes=[("p", bank)])
            jb["evac"](self.PS[:, half * 2048:(half + 1) * 2048], [("p", half * 4 + t) for t in range(4)])

    def ffn(self, l):
        S = self.S
        self.rms_rstd(self.FA(), "FA")
        self.make_xn(("ffn_norm", l), self.FA(), "FA")
        if getattr(self, "ffn_hook", None):
            self.ffn_hook()
            self.ffn_hook = None
        groups = [(0, 6), (6, 12), (12, 17), (17, 22)]
        cw = VC[("conv_w", l)]
        cb = VC[("conv_b", l)]
        SG = self.BIGB[:, 12288:12288 + L]
        FBR = [("FB", t) for t in range(4)]
        FCR = [("FC", t) for t in range(4)]

        pending = []

        def conv_to(acc, accres, ps, psres, c):
            w = lambda k: self.VEC[:, cw + k * NUPC + c:cw + k * NUPC + c + 1]
            S.op("act", lambda e: e.activation(out=acc, in_=ps, func=AF.Identity, scale=w(2), bias=self.VEC[:, cb + c:cb + c + 1]),
                 reads=psres + ["VEC"], writes=accres)
            while pending:
                pending.pop(0)()
            S.op("dve", lambda e: e.scalar_tensor_tensor(out=acc[:, 1:L], in0=ps[:, 0:L - 1], scalar=w(1), in1=acc[:, 1:L],
                                                         op0=ALU.mult, op1=ALU.add),
                 reads=psres + ["VEC"] + accres, writes=accres)
            S.op("dve", lambda e: e.scalar_tensor_tensor(out=acc[:, 2:L], in0=ps[:, 0:L - 2], scalar=w(0), in1=acc[:, 2:L],
                                                         op0=ALU.mult, op1=ALU.add),
                 reads=psres + ["VEC"] + accres, writes=accres)

        ups, downs = [], []
        for (g0, g1) in groups:
            jobs = []
            for i in range(g0, g1):
                il = i - g0

                def evac_gate(ps, psres, i=i):
                    conv_to(self.FB(), FBR, ps, psres, i)
                    pending.append(lambda: S.op("act", lambda e: e.activation(out=SG, in_=self.FB(), func=AF.Silu), reads=FBR, writes=self.bres(12288, 14336)))

                def evac_val(ps, psres, i=i, il=il):
                    conv_to(self.FC(), FCR, ps, psres, NACT + i)
                    S.op("dve", lambda e: e.tensor_tensor(self.BIGB[:, il * L:(il + 1) * L], SG, self.FC(), ALU.mult),
                         reads=self.bres(12288, 14336) + FCR, writes=self.bres(il * L, (il + 1) * L))

                for col0, ev in ((i * 128, evac_gate), (DFF + i * 128, evac_val)):
                    jobs.append(dict(w2d=self.w_up_d[l], k0=0, KC=8, col0=col0,
                                     rhs=lambda kc, tt: self.BIGA[:, kc * L + tt * 512:kc * L + (tt + 1) * 512],
                                     rhs_res=lambda kc: [("A", kc)], evac=ev))
            ng = g1 - g0
            ups.append(jobs)
            jobs = []
            for oc in range(NFC):
                def evac_down(ps, psres, oc=oc):
                    S.op("dve", lambda e: e.tensor_tensor(self.HT[:, oc * L:(oc + 1) * L], self.HT[:, oc * L:(oc + 1) * L], ps, ALU.add),
                         reads=psres + [("H", oc)], writes=[("H", oc)])
                jobs.append(dict(w2d=self.w_down_d[l], k0=g0 * 128, KC=ng, col0=oc * 128,
                                 rhs=lambda kc, tt: self.BIGB[:, kc * L + tt * 512:kc * L + (tt + 1) * 512],
                                 rhs_res=lambda kc: self.bres(kc * L, (kc + 1) * L), evac=evac_down))
            downs.append(jobs)
        alljobs = list(ups[0])
        for g in range(len(groups)):
            if g + 1 < len(groups):
                alljobs.append(ups[g + 1][0])
            alljobs.extend(downs[g])
            if g + 1 < len(groups):
                alljobs.extend(ups[g + 1][1:])
        self.run_dense(alljobs)
        while pending:
            pending.pop(0)()

    def s5_layer(self, l, part="all"):
        S = self.S
        P = self.S5S
        skip = {"on": part == "main"}
        sl = lambda a, b: P[:, a:b]
        Are, Aim, ldt = sl(0, 32), sl(32, 64), sl(64, 96)
        dt, ar, ai = sl(96, 128), sl(128, 160), sl(160, 192)
        MAG, ANG, NT, C9, PWR, PWI = sl(192, 480), sl(480, 768), sl(768, 1056), sl(1056, 1344), sl(1344, 1632), sl(1632, 1920)
        cr, ci = sl(1920, 1952), sl(1952, 1984)
        sm = lambda i: self.SMALL[:, 640 + 32 * i:640 + 32 * (i + 1)]
        k3 = lambda ap: ap.rearrange("p (k g) -> p k g", g=32)
        kk = self.cst("kk", 9)
        kkb = kk.unsqueeze(2).broadcast_to([128, 9, 32])

        def dv(name, fn, reads, writes, eng="dve"):
            if skip["on"]:
                return
            S.op(eng, fn, reads=reads, writes=writes)

        if part != "main":
            S.dma("sp", P[:, 0:96], self.s5p_d[l][:, 0:96], writes=["s5raw"])
        FBR = [("FB", t) for t in range(4)]
        FCR = [("FC", t) for t in range(4)]
        if part != "params":
            S.dma("sp", self.FW[:, L:3 * L], self.s5p_d[l][:, 96:96 + 4096], writes=FBR + FCR)
        BMR, BMI = self.FW[:, L:L + 1024], self.FW[:, L + 1024:2 * L]
        CMR, CMI = self.FW[:, 2 * L:2 * L + 1024], self.FW[:, 2 * L + 1024:3 * L]
        g3 = lambda ap: ap.rearrange("p (g c) -> p g c", c=32)
        dv("dt", lambda e: e.activation(out=dt, in_=ldt, func=AF.Exp), ["s5raw"], ["s5dt"], "act")
        dv("ar", lambda e: e.tensor_tensor(ar, Are, dt, ALU.mult), ["s5raw", "s5dt"], ["s5ar"])
        dv("ai", lambda e: e.tensor_tensor(ai, Aim, dt, ALU.mult), ["s5raw", "s5dt"], ["s5ai"])
        dv("mag", lambda e: e.tensor_tensor(k3(MAG), ar.unsqueeze(1).broadcast_to([128, 9, 32]), kkb, ALU.mult), ["s5ar", "CST"], ["s5mag"])
        dv("mage", lambda e: e.activation(out=MAG, in_=MAG, func=AF.Exp), ["s5mag"], ["s5mag"], "act")
        dv("ang", lambda e: e.tensor_tensor(k3(ANG), ai.unsqueeze(1).broadcast_to([128, 9, 32]), kkb, ALU.mult), ["s5ai", "CST"], ["s5ang"])

        def sin_of(src, srcres, tmp, tmpres):
            dv("n1", lambda e: e.tensor_scalar(tmp, src, 1.0 / TWO_PI, MAGIC, ALU.mult, ALU.add), [srcres], [tmpres])
            dv("n2", lambda e: e.tensor_scalar(tmp, tmp, MAGIC, None, ALU.subtract), [tmpres], [tmpres])
            dv("n3", lambda e: e.scalar_tensor_tensor(out=tmp, in0=tmp, scalar=-TWO_PI, in1=src, op0=ALU.mult, op1=ALU.add), [tmpres, srcres], [tmpres])
            dv("n4", lambda e: e.activation(out=tmp, in_=tmp, func=AF.Sin), [tmpres], [tmpres], "act")

        sin_of(ANG, "s5ang", NT, "s5nt")
        dv("pwi", lambda e: e.tensor_tensor(PWI, MAG, NT, ALU.mult), ["s5mag", "s5nt"], ["s5pwi"])
        dv("angc", lambda e: e.tensor_scalar(C9, ANG, math.pi / 2, None, ALU.add), ["s5ang"], ["s5c9"])
        sin_of(C9, "s5c9", NT, "s5nt")
        dv("pwr", lambda e: e.tensor_tensor(PWR, MAG, NT, ALU.mult), ["s5mag", "s5nt"], ["s5pwr"])
        nr, den, x1, x2, rden, y1, y2 = sm(0), sm(1), sm(2), sm(3), sm(4), sm(5), sm(6)
        pw1r, pw1i = PWR[:, 32:64], PWI[:, 32:64]
        dv("nr", lambda e: e.tensor_scalar(nr, pw1r, -1.0, None, ALU.add), ["s5pwr"], ["sm0"])
        dv("den", lambda e: e.tensor_tensor(den, Are, Are, ALU.mult), ["s5raw"], ["sm1"])
        dv("den2", lambda e: e.tensor_tensor(x1, Aim, Aim, ALU.mult), ["s5raw"], ["sm2"])
        dv("den3", lambda e: e.tensor_tensor(den, den, x1, ALU.add), ["sm1", "sm2"], ["sm1"])
        dv("rden", lambda e: e.reciprocal(rden, den), ["sm1"], ["sm4"])
        dv("x1", lambda e: e.tensor_tensor(x1, nr, Are, ALU.mult), ["sm0", "s5raw"], ["sm2"])
        dv("x2", lambda e: e.tensor_tensor(x2, pw1i, Aim, ALU.mult), ["s5pwi", "s5raw"], ["sm3"])
        dv("x3", lambda e: e.tensor_tensor(x1, x1, x2, ALU.add), ["sm2", "sm3"], ["sm2"])
        dv("cr", lambda e: e.tensor_tensor(cr, x1, rden, ALU.mult), ["sm2", "sm4"], ["s5cr"])
        dv("y1", lambda e: e.tensor_tensor(y1, pw1i, Are, ALU.mult), ["s5pwi", "s5raw"], ["sm5"])
        dv("y2", lambda e: e.tensor_tensor(y2, nr, Aim, ALU.mult), ["sm0", "s5raw"], ["sm6"])
        dv("y3", lambda e: e.tensor_tensor(y1, y1, y2, ALU.subtract), ["sm5", "sm6"], ["sm5"])
        dv("ci", lambda e: e.tensor_tensor(ci, y1, rden, ALU.mult), ["sm5", "sm4"], ["s5ci"])
        skip["on"] = False
        if part == "params":
            return
        crb = cr.unsqueeze(2).broadcast_to([128, 32, 32])
        cib = ci.unsqueeze(2).broadcast_to([128, 32, 32])
        T1, T2 = self.TMP[:], self.TMP2[:]
        R_BMR, R_BMI = [("FB", 0), ("FB", 1)], [("FB", 2), ("FB", 3)]
        R_CMR, R_CMI = [("FC", 0), ("FC", 1)], [("FC", 2), ("FC", 3)]
        dv("b1", lambda e: e.tensor_tensor(g3(T1), cib, g3(BMR), ALU.mult), ["s5ci"] + R_BMR, ["TMP"])
        dv("b2", lambda e: e.tensor_tensor(g3(T2), cib, g3(BMI), ALU.mult), ["s5ci"] + R_BMI, ["TMP2"])
        dv("b3", lambda e: e.tensor_tensor(g3(BMR), crb, g3(BMR), ALU.mult), ["s5cr"] + R_BMR, R_BMR)
        dv("b4", lambda e: e.tensor_tensor(BMR, BMR, T2, ALU.subtract), R_BMR + ["TMP2"], R_BMR)
        dv("b5", lambda e: e.tensor_tensor(g3(BMI), crb, g3(BMI), ALU.mult), ["s5cr"] + R_BMI, R_BMI)
        dv("b6", lambda e: e.tensor_tensor(BMI, BMI, T1, ALU.add), R_BMI + ["TMP"], R_BMI)

        FAR = [("FA", t) for t in range(4)]
        self.rms_rstd(self.FA(), "FA")
        for tt in range(4):
            for fc in range(NFC):
                S.op("dve", lambda e, fc=fc, tt=tt: e.scalar_tensor_tensor(
                    out=self.BIGA[:, fc * L:(fc + 1) * L].rearrange("p (j c) -> p j c", c=256)[:, :, tt * 64:(tt + 1) * 64],
                    in0=self.HT[:, fc * L + tt * 512:fc * L + (tt + 1) * 512].rearrange("p (c j) -> p j c", j=8),
                    scalar=self.vcol(("a_norm", l), fc),
                    in1=self.FA()[:, tt * 512:(tt + 1) * 512].rearrange("p (c j) -> p j c", j=8), op0=ALU.mult, op1=ALU.mult),
                    reads=[("H", fc), "VEC", ("FA", tt)], writes=[("A", fc)])
        B2S4 = self.BIGB[:, 0:16448].rearrange("p (r g c) -> p r g c", r=2, g=32)
        ALLC = [("b2S", c) for c in range(257)]
        BALL = self.bres(0, 16512)
        S.op("pool", lambda e: e.memset(B2S4[:, :, :, 0:1], 0.0), reads=[], writes=BALL + [("b2S", 0)])
        Zre, ZimN = self.FW[:, 0:1024], self.FW[:, 1024:2048]
        z4 = lambda ap: ap.rearrange("p (t q c) -> p t q c", t=8, q=4)
        R_ZR, R_ZI = [("FA", 0), ("FA", 1)], [("FA", 2), ("FA", 3)]
        PWR3, PWI3 = k3(PWR), k3(PWI)
        ZB = {0: (Zre, ZimN, R_ZR, R_ZI),
              1: (self.UNI[:, 2048:3072], self.UNI[:, 3072:4096], ["h4", "h5"], ["h6", "h7"])}

        def zcompute(fc, zb=0):
            Zre, ZimN, R_ZR, R_ZI = ZB[zb]
            pwr_b = PWR3[:, 0:8, 4 * fc:4 * fc + 4].unsqueeze(3).broadcast_to([128, 8, 4, 32])
            pwi_b = PWI3[:, 0:8, 4 * fc:4 * fc + 4].unsqueeze(3).broadcast_to([128, 8, 4, 32])
            bmr_b = g3(BMR)[:, 4 * fc:4 * fc + 4, :].unsqueeze(1).broadcast_to([128, 8, 4, 32])
            bmi_b = g3(BMI)[:, 4 * fc:4 * fc + 4, :].unsqueeze(1).broadcast_to([128, 8, 4, 32])
            E = "pool"
            dv("z1", lambda e: e.tensor_tensor(z4(Zre), pwr_b, bmr_b, ALU.mult), ["s5pwr"] + R_BMR, R_ZR, E)
            dv("z2", lambda e: e.tensor_tensor(z4(T1), pwi_b, bmi_b, ALU.mult), ["s5pwi"] + R_BMI, ["TMP"], E)
            dv("z3", lambda e: e.tensor_tensor(Zre, Zre, T1, ALU.subtract), R_ZR + ["TMP"], R_ZR, E)
            E = "pool"
            dv("z4", lambda e: e.tensor_tensor(z4(ZimN), pwr_b, bmi_b, ALU.mult), ["s5pwr"] + R_BMI, R_ZI, E)
            dv("z5", lambda e: e.tensor_tensor(z4(T2), pwi_b, bmr_b, ALU.mult), ["s5pwi"] + R_BMR, ["TMP2"], E)
            dv("z6", lambda e: e.tensor_tensor(ZimN, ZimN, T2, ALU.add), R_ZI + ["TMP2"], R_ZI, E)
            dv("z7", lambda e: e.activation(out=ZimN, in_=ZimN, func=AF.Copy, scale=-1.0), R_ZI, R_ZI, "act")

        def zcompute_dve(fc):
            Zre, ZimN, R_ZR, R_ZI = ZB[0]
            pwr_b = PWR3[:, 0:8, 4 * fc:4 * fc + 4].unsqueeze(3).broadcast_to([128, 8, 4, 32])
            pwi_b = PWI3[:, 0:8, 4 * fc:4 * fc + 4].unsqueeze(3).broadcast_to([128, 8, 4, 32])
            bmr_b = g3(BMR)[:, 4 * fc:4 * fc + 4, :].unsqueeze(1).broadcast_to([128, 8, 4, 32])
            bmi_b = g3(BMI)[:, 4 * fc:4 * fc + 4, :].unsqueeze(1).broadcast_to([128, 8, 4, 32])
            PT = self.PS[:, 3072:4096]
            RPT = [("p", 6), ("p", 7)]
            dv("z1", lambda e: e.tensor_tensor(z4(Zre), pwr_b, bmr_b, ALU.mult), ["s5pwr"] + R_BMR, R_ZR, "pool")
            dv("z2", lambda e: e.tensor_tensor(z4(T1), pwi_b, bmi_b, ALU.mult), ["s5pwi"] + R_BMI, ["TMP"], "pool")
            dv("z3", lambda e: e.tensor_tensor(Zre, Zre, T1, ALU.subtract), R_ZR + ["TMP"], R_ZR, "pool")
            dv("z4", lambda e: e.tensor_tensor(z4(ZimN), pwr_b, bmi_b, ALU.mult), ["s5pwr"] + R_BMI, R_ZI)
            dv("z5", lambda e: e.tensor_tensor(z4(PT), pwi_b, bmr_b, ALU.mult), ["s5pwi"] + R_BMR, RPT)
            dv("z6", lambda e: e.tensor_tensor(ZimN, ZimN, PT, ALU.add), R_ZI + RPT, R_ZI)
            dv("z7", lambda e: e.activation(out=ZimN, in_=ZimN, func=AF.Copy, scale=-1.0), R_ZI, R_ZI, "act")

        W2B = {0: (self.UNI[:, 0:2048].bitcast(BF16).rearrange("p (v t r c) -> p v t r c", v=2, t=8, r=2), ["h0", "h1", "h2", "h3"]),
               1: (self.UNI[:, 4096:6144].bitcast(BF16).rearrange("p (v t r c) -> p v t r c", v=2, t=8, r=2), ["h8", "h9", "h10", "h11"])}
        ident = self.cst("ident")
        rowmask = self.cst("rowmask", 4)

        def stageT(fc):
            zb = fc % 2
            zcompute(fc, zb)
            Zre_, ZimN_, RZR_, RZI_ = ZB[zb]
            W2v, R_W2 = W2B[zb]
            for ri, Zs, ZR in ((0, Zre_, RZR_), (1, ZimN_, RZI_)):
                for tq in range(2):
                    bank = ri * 2 + tq
                    for i in range(4):
                        t = tq * 4 + i
                        S.op("pe", lambda e, o=self.PS[:, bank * 512 + i * 128:bank * 512 + (i + 1) * 128], a=Zs[:, t * 128:(t + 1) * 128]:
                             e.transpose(o, a, ident), reads=ZR + ["CST"], writes=[("p", bank)])
                    for v in range(2):
                        S.op("dve", lambda e, v=v, tq=tq, ri=ri, bank=bank, W2v=W2v: e.tensor_scalar(
                            W2v[:, v, tq * 4:(tq + 1) * 4, ri, :],
                            self.PS[:, bank * 512:(bank + 1) * 512].rearrange("p (t c) -> p t c", c=128),
                            rowmask[:, 2 * ri + v:2 * ri + v + 1], None, ALU.mult),
                            reads=[("p", bank), "CST"], writes=R_W2)

        def stageM(fc):
            W2v, R_W2 = W2B[fc % 2]
            for q in range(4):
                hq, v = q // 2, q % 2
                for ri in range(2):
                    off = 2048 + (q * 2 + ri) * 256
                    bank = off // 512
                    for j in range(8):
                        S.op("pe", lambda e, o=self.PS[:, off:off + 256], w=W2v[64 * hq:64 * hq + 64, v, 7 - j, ri, :],
                             r=self.BIGA[64 * hq:64 * hq + 64, fc * L + j * 256:fc * L + (j + 1) * 256], a=(j == 0), z=(j == 7):
                             e.matmul(o, w, r, start=a, stop=z), reads=R_W2 + [("A", fc)], writes=[("p", bank)])
            S.op("act", lambda e, fc=fc: e.activation(
                out=B2S4[:, :, 4 * fc:4 * fc + 4, 1:257],
                in_=self.PS[:, 2048:4096].rearrange("p (q r c) -> p r q c", q=4, r=2), func=AF.Copy),
                reads=[("p", b) for b in range(4, 8)], writes=BALL + ALLC[1:])

        stageT(0)
        for fc in range(NFC):
            if fc + 1 < NFC:
                stageT(fc + 1)
            stageM(fc)

        SMT = self.SMALL[:]
        A12 = self.SMALL[:, 0:128]
        A1, A2 = self.SMALL[:, 0:64], self.SMALL[:, 64:128]
        SR = lambda k: self.SMALL[:, 128 + 128 * k:256 + 128 * k]
        TT = self.SMALL[:, 512:576]
        PP = self.SMALL[:, 896:1024]
        l8r, l8i = PWR[:, 256:288], PWI[:, 256:288]
        dv("a1a", lambda e: e.tensor_copy(A1[:, 0:32], l8r), ["s5pwr"], ["A1"])
        dv("a1b", lambda e: e.tensor_copy(A1[:, 32:64], l8r), ["s5pwr"], ["A1"])
        dv("a2a", lambda e: e.tensor_scalar(A2[:, 0:32], l8i, -1.0, None, ALU.mult), ["s5pwi"], ["A2"])
        dv("a2b", lambda e: e.tensor_copy(A2[:, 32:64], l8i), ["s5pwi"], ["A2"])
        dv("sr0", lambda e: e.memset(SR(0), 0.0), [], [("SR", 0)])
        h4 = lambda ap: ap.rearrange("p (h j g) -> p h j g", h=2, j=2)
        j3 = lambda ap: ap.rearrange("p (j g) -> p j g", j=2)
        for c in range(256):
            k0, k1 = c % 3, (c + 1) % 3
            nw = SR(k1)
            wv = bass.AP(SMT.tensor, SMT.offset + 128 + 128 * k0, [list(SMT.ap[0]), [32, 2], [32, 2], [1, 32]])
            rp, rn = ("SR", k0), ("SR", k1)
            dv("s12", lambda e, wv=wv: e.tensor_tensor(h4(PP), h4(A12), wv, ALU.mult), ["A1", "A2", rp], ["PP"])
            dv("s3", lambda e: e.tensor_tensor(j3(TT), h4(PP)[:, 0], h4(PP)[:, 1], ALU.add), ["PP"], ["TT1"])
            dv("s4", lambda e, nw=nw, c=c: e.tensor_tensor(h4(nw), j3(TT).unsqueeze(1).broadcast_to([128, 2, 2, 32]),
                                                         B2S4[:, :, :, c + 1].unsqueeze(1).broadcast_to([128, 2, 2, 32]), ALU.add),
               ["TT1", ("b2S", c + 1)], [rn])
            dv("s5", lambda e, nw=nw, c=c: e.tensor_copy(B2S4[:, :, :, c + 1], j3(nw[:, 0:64])), [rn], [("b2S", c + 1)], "pool")

        W3H = {b: self.UNI[:, b * 2048:(b + 1) * 2048].bitcast(BF16).rearrange("p (j q r c) -> p j q r c", j=4, q=4, r=2) for b in range(3)}
        R_W3H = {b: ["h%d" % i_ for i_ in range(4 * b, 4 * b + 4)] for b in range(3)}
        KL = self.UNI[:, 6144:6656].bitcast(BF16).rearrange("p (t c) -> p t c", c=128)
        maskQ = self.cst("maskQ")
        S.op("pool", lambda e: e.memset(self.UNI[:, 0:6144].bitcast(BF16), 0.0), writes=["h%d" % i_ for i_ in range(12)])
        t4 = lambda ap: ap.rearrange("p (j q c) -> p j q c", j=8, q=4)
        wcount = 0
        zcompute_dve(0)
        for fc in range(NFC):
            ypar = 0
            yb = 0
            for tq in range(2):
                bank = 4 + tq
                for i in range(4):
                    t = tq * 4 + i
                    o = self.PS[:, bank * 512 + i * 128:bank * 512 + (i + 1) * 128]
                    S.op("pe", lambda e, o=o, t=t, fc=fc: e.matmul(o, Zre[:, t * 128:(t + 1) * 128], CMR[:, 128 * fc:128 * (fc + 1)], start=True, stop=False),
                         reads=R_ZR + R_CMR, writes=[("p", bank)])
                    S.op("pe", lambda e, o=o, t=t, fc=fc: e.matmul(o, ZimN[:, t * 128:(t + 1) * 128], CMI[:, 128 * fc:128 * (fc + 1)], start=False, stop=True),
                         reads=R_ZI + R_CMI, writes=[("p", bank)])
                S.op("dve", lambda e, tq=tq, bank=bank: e.tensor_tensor(
                    KL[:, tq * 4:(tq + 1) * 4, :], self.PS[:, bank * 512:(bank + 1) * 512].rearrange("p (t c) -> p t c", c=128),
                    maskQ.unsqueeze(1).broadcast_to([128, 4, 128]), ALU.mult), reads=[("p", bank), "CST"], writes=["h12"])
            cmr_b = g3(CMR)[:, 4 * fc:4 * fc + 4, :].unsqueeze(1).broadcast_to([128, 8, 4, 32])
            cmi_b = g3(CMI)[:, 4 * fc:4 * fc + 4, :].unsqueeze(1).broadcast_to([128, 8, 4, 32])
            pr_b = PWR3[:, 1:9, 4 * fc:4 * fc + 4].unsqueeze(3).broadcast_to([128, 8, 4, 32])
            pi_b = PWI3[:, 1:9, 4 * fc:4 * fc + 4].unsqueeze(3).broadcast_to([128, 8, 4, 32])
            wb = [(wcount) % 3, (wcount + 1) % 3]
            wcount += 2
            E = "pool"
            dv("w1", lambda e, a=cmr_b, b=pr_b: e.tensor_tensor(t4(T1), a, b, ALU.mult), R_CMR + ["s5pwr"], ["TMP"], E)
            dv("w2", lambda e, a=cmi_b, b=pi_b: e.tensor_tensor(t4(T2), a, b, ALU.mult), R_CMI + ["s5pwi"], ["TMP2"], E)
            for hb in range(2):
                for q in range(4):
                    dv("w3", lambda e, q=q, hb=hb, wv=W3H[wb[hb]]: e.tensor_tensor(wv[:, :, q, 0, 32 * q:32 * q + 32], t4(T1)[:, 4 * hb:4 * hb + 4, q, :],
                                                                               t4(T2)[:, 4 * hb:4 * hb + 4, q, :], ALU.subtract),
                       ["TMP", "TMP2"], R_W3H[wb[hb]], E)
            PT = self.PS[:, 3072:4096]
            RPT = [("p", 6), ("p", 7)]
            ST2 = self.SMALL[:, 0:1024]
            dv("w4", lambda e, a=cmi_b, b=pr_b: e.tensor_tensor(t4(PT), a, b, ALU.mult), R_CMI + ["s5pwr"], RPT, "dve")
            dv("w5", lambda e, a=cmr_b, b=pi_b: e.tensor_tensor(t4(ST2), a, b, ALU.mult), R_CMR + ["s5pwi"], self.S5_SMALL, "dve")
            dv("w5b", lambda e: e.tensor_tensor(PT, PT, ST2, ALU.add), RPT + self.S5_SMALL, RPT, "dve")
            for hb in range(2):
                for q in range(4):
                    dv("w6", lambda e, q=q, hb=hb, wv=W3H[wb[hb]]: e.activation(out=wv[:, :, q, 1, 32 * q:32 * q + 32], in_=t4(PT)[:, 4 * hb:4 * hb + 4, q, :],
                                                                            func=AF.Copy, scale=-1.0),
                       RPT, R_W3H[wb[hb]], "act")
            if fc + 1 < NFC:
                zcompute_dve(fc + 1)
            for t in range(8):
                for b in range(4):
                    jlo, jhi = max(2 * b, t), 2 * b + 2
                    if jlo >= jhi:
                        continue
                    S.op("pe", lambda e, t=t, jlo=jlo, jhi=jhi, fc=fc, yb=yb: e.matmul(
                        self.PS[:, yb + jlo * 256:yb + jhi * 256], KL[:, t, :],
                        self.BIGA[:, fc * L + (jlo - t) * 256:fc * L + (jhi - t) * 256], start=(t == 0), stop=False),
                        reads=["h12", ("A", fc)], writes=[("p", 4 * ypar + b)])
            for j in range(8):
                hb = j // 4
                for q in range(4):
                    for ri in range(2):
                        S.op("pe", lambda e, j=j, q=q, ri=ri, fc=fc, yb=yb, wv=W3H[wb[hb]]: e.matmul(
                            self.PS[:, yb + j * 256:yb + (j + 1) * 256], wv[:, j % 4, q, ri, :], B2S4[:, ri, 4 * fc + q, 0:256],
                            start=False, stop=(q == 3 and ri == 1)),
                            reads=R_W3H[wb[hb]] + ALLC, writes=[("p", 4 * ypar + j // 2)])
            for hv in range(2):
                yv = self.PS[:, yb + hv * 1024:yb + (hv + 1) * 1024]
                sc = self.SMALL[:, 0:1024]
                ry = [("p", 4 * ypar + 2 * hv), ("p", 4 * ypar + 2 * hv + 1)]
                rs = self.S5_SMALL
                ug = self.BIGA[:, fc * L + hv * 1024:fc * L + (hv + 1) * 1024]
                S.op("dve", lambda e, yv=yv, ug=ug, fc=fc: e.scalar_tensor_tensor(out=yv, in0=ug, scalar=self.vcol(("s5_D", l), fc), in1=yv,
                                                                          op0=ALU.mult, op1=ALU.add), reads=ry + [("A", fc), "VEC"], writes=ry)
                S.op("act", lambda e, yv=yv, ug=ug: e.activation(out=ug, in_=yv, func=AF.Gelu_apprx_tanh), reads=ry, writes=[("A", fc)])

        jobs = []
        for oc in range(NFC):
            def evac_zb(ps, psres, oc=oc):
                S.op("act", lambda e: e.activation(out=self.FA(), in_=ps, func=AF.Sigmoid, bias=self.vcol(("b_glu", l), 8 + oc), scale=1.0),
                     reads=psres + ["VEC"], writes=FAR)

            def evac_za(ps, psres, oc=oc):
                S.op("dve", lambda e: e.scalar_tensor_tensor(out=self.FB(), in0=ps, scalar=self.vcol(("b_glu", l), oc), in1=self.FA(),
                                                             op0=ALU.add, op1=ALU.mult), reads=psres + ["VEC"] + FAR, writes=FBR)
                hv = self.HT[:, oc * L:(oc + 1) * L].rearrange("p (c j) -> p j c", j=8)
                S.op("dve", lambda e: e.tensor_tensor(hv, hv, self.FB().rearrange("p (j c) -> p j c", c=256), ALU.add),
                     reads=FBR + [("H", oc)], writes=[("H", oc)])
            for col0, ev in ((D + oc * 128, evac_zb), (oc * 128, evac_za)):
                jobs.append(dict(w2d=self.w_glu_d[l], k0=0, KC=8, col0=col0,
                                 rhs=lambda kc, tt: self.BIGA[:, kc * L + tt * 512:kc * L + (tt + 1) * 512],
                                 rhs_res=lambda kc: [("A", kc)], evac=ev))
        self.run_dense(jobs)

    def load_w(self, w2d, k0, KC, col0):
        S = self.S
        sl = self.wslot
        self.wslot = (self.wslot + 1) % self.NSLOT
        src = w2d[k0:k0 + KC * 128, col0:col0 + 128].rearrange("(kc p) c -> p kc c", p=128)
        S.dma("pool", self.WBF(sl)[:, 0:KC * 128].rearrange("p (kc c) -> p kc c", c=128), src, writes=self.wres(sl))
        return sl

    def load_rope(self):
        S = self.S
        S.dma("sp", self.FB(), self.rope_d[:, 0:L], writes=[("FB", t) for t in range(4)])
        S.dma("sp", self.FC(), self.rope_d[:, L:2 * L], writes=[("FC", t) for t in range(4)])

    def qk_unit(self, sl, hf, gcol):
        S = self.S
        u = hf
        c0, c1 = hf * 1024, (hf + 1) * 1024
        FAR = [("FA", 2 * hf), ("FA", 2 * hf + 1)]
        FBR = [("FB", 2 * hf), ("FB", 2 * hf + 1)]
        FCR = [("FC", 2 * hf), ("FC", 2 * hf + 1)]
        rb = self.wres(sl)
        pb = 4 * u
        P = self.PS[:, pb * 512:(pb + 2) * 512]
        P2 = self.PS[:, (pb + 2) * 512:(pb + 4) * 512]
        RP = [("p", pb), ("p", pb + 1)]
        RP2 = [("p", pb + 2), ("p", pb + 3)]
        XG = self.FW[:, c0:c1]
        SQ = self.BIGB[:, u * 1024:(u + 1) * 1024]
        SQR = self.bres(u * 1024, (u + 1) * 1024)

        def st0():
            for kc in range(8):
                for t in range(2):
                    S.op("pe", lambda e, t=t, kc=kc: e.matmul(self.PS[:, (pb + t) * 512:(pb + t + 1) * 512], self.WBF(sl)[:, kc * 128:(kc + 1) * 128],
                                                            self.BIGA[:, kc * L + c0 + t * 512:kc * L + c0 + (t + 1) * 512], start=(kc == 0), stop=(kc == 7)),
                         reads=rb + [("A", kc)], writes=[("p", pb + t)])
            S.op("act", lambda e: e.activation(out=XG, in_=P, func=AF.Copy, scale=gcol), reads=RP + ["VEC"], writes=FAR)
            S.op("act", lambda e: e.activation(out=SQ, in_=P, func=AF.Square), reads=RP, writes=SQR)

        def st1():
            for t in range(2):
                S.op("pe", lambda e, t=t: e.matmul(self.PS[:, (pb + 2 + t) * 512:(pb + 3 + t) * 512], self.onesH(), SQ[:, t * 512:(t + 1) * 512],
                                                 start=True, stop=True), reads=SQR + ["CSTB"], writes=[("p", pb + 2 + t)])
            S.op("act", lambda e: e.activation(out=P2, in_=P2, func=AF.Ln, bias=self.eps_ap(), scale=1.0), reads=RP2 + ["EPS"], writes=RP2)
            S.op("act", lambda e: e.activation(out=P2, in_=P2, func=AF.Exp, scale=-0.5), reads=RP2, writes=RP2)
            pswap = self.cst("pswap")
            for t in range(2):
                S.op("pe", lambda e, t=t: e.matmul(self.PS[:, (pb + t) * 512:(pb + t + 1) * 512], pswap, XG[:, t * 512:(t + 1) * 512], start=True, stop=True),
                     reads=FAR + ["CST"], writes=[("p", pb + t)])
            S.op("dve", lambda e: e.tensor_tensor(XG, XG, self.FW[:, L + c0:L + c1], ALU.mult), reads=FAR + FBR, writes=FAR)
            S.op("dve", lambda e: e.tensor_tensor(P, P, self.FW[:, 2 * L + c0:2 * L + c1], ALU.mult), reads=RP + FCR, writes=RP)
            S.op("dve", lambda e: e.tensor_tensor(XG, XG, P, ALU.add), reads=FAR + RP, writes=FAR)
            S.op("dve", lambda e: e.tensor_tensor(XG, XG, P2, ALU.mult), reads=FAR + RP2, writes=FAR)
        return st0, st1, (XG, FAR, P2, RP2)

    @staticmethod
    def run_staged(units):
        nst = max(len(u) for u in units)
        for it in range(len(units) + nst - 1):
            for k in range(nst - 1, -1, -1):
                i = it - k
                if 0 <= i < len(units) and k < len(units[i]):
                    units[i][k]()

    ATT_SMALL = ["KMEAN", "G0", "G1", "RINV", "NSB"] + [("ACC", i) for i in range(4)] + [("M8", i) for i in range(8)]
    S5_SMALL = ["A1", "A2", "TT1", "PP"] + [("SR", k_) for k_ in range(3)] + ["sm%d" % i_ for i_ in range(7)]
    S5S_ALL = ["s5raw", "s5dt", "s5ar", "s5ai", "s5mag", "s5ang", "s5nt", "s5c9", "s5pwr", "s5pwi", "s5cr", "s5ci"]

    def kv_phase(self):
        S = self.S
        S.op("dve", lambda e: e.memset(self.SMALL[:], 0.0), writes=self.ATT_SMALL + self.S5_SMALL)
        self.rms_rstd(self.FA(), "FA")
        self.make_xn("kv_norm", self.FA(), "FA")
        self.load_rope()
        KMEAN = self.SMALL[:, 0:64]
        KB = lambda k: self.BIGB[:, 2048 + k * 2048:2048 + (k + 1) * 2048]
        KBR = lambda k: self.bres(2048 + k * 2048, 2048 + (k + 1) * 2048)
        VHo = lambda k: 6144 + k * 3072
        VH = lambda k: self.BIGB[:, VHo(k):VHo(k) + 2064]
        VHR = lambda k: self.bres(VHo(k), VHo(k) + 2064)
        for k in range(2):
            S.op("pool", lambda e, k=k: e.memset(VH(k).rearrange("p (t c) -> p t c", c=129)[:, :, 128:129], 1.0), writes=VHR(k))
        slk = {}
        slv = {}
        kunits = []
        for hd in range(NH):
            kp = hd % 2
            for hf in range(2):
                hold = {}

                def st0(hd=hd, hf=hf, hold=hold):
                    if hf == 0:
                        if hd == 0:
                            slk[0] = self.load_w(self.w_kv_d, 0, 8, 0)
                            slv[0] = self.load_w(self.w_kv_d, 0, 8, D)
                        if hd + 1 < NH:
                            slk[hd + 1] = self.load_w(self.w_kv_d, 0, 8, (hd + 1) * 128)
                            slv[hd + 1] = self.load_w(self.w_kv_d, 0, 8, D + (hd + 1) * 128)
                    a, b, h = self.qk_unit(slk[hd], hf, self.vcol("k_norm"))
                    hold["st1"] = b
                    hold["h"] = h
                    a()

                def st1(hold=hold):
                    hold["st1"]()

                def st2(hd=hd, hf=hf, kp=kp, hold=hold):
                    XG, FAR, P2, RP2 = hold["h"]
                    S.op("dve", lambda e: e.reduce_sum(out=KMEAN[:, hd * 8 + hf * 4:hd * 8 + hf * 4 + 4],
                                                      in_=XG.rearrange("p (n t) -> p n t", t=256), axis=AX.X),
                         reads=FAR, writes=["KMEAN"])
                    S.op("act", lambda e: e.activation(out=KB(kp)[:, hf * 1024:(hf + 1) * 1024], in_=XG, func=AF.Copy),
                         reads=FAR, writes=self.bres(2048 + kp * 2048 + hf * 1024, 2048 + kp * 2048 + (hf + 1) * 1024))
                    if hf == 1:
                        S.dma("sp", self.kT_s[hd], KB(kp), reads=KBR(kp), writes=[("kT_s", hd)])
                kunits.append([st0, st1, st2])
            for hf in range(2):
                def vunit(hd=hd, hf=hf, kp=kp):
                    sl = slv[hd]
                    rb = self.wres(sl)
                    VH3 = VH(kp).rearrange("p (t c) -> p t c", c=129)
                    pb = 4 * hf
                    for t8 in range(8):
                        t16 = hf * 8 + t8
                        for kc in range(8):
                            S.op("pe", lambda e, t16=t16, t8=t8, kc=kc: e.matmul(
                                self.PS[:, pb * 512 + t8 * 128:pb * 512 + (t8 + 1) * 128], self.BIGA[:, kc * L + t16 * 128:kc * L + (t16 + 1) * 128],
                                self.WBF(sl)[:, kc * 128:(kc + 1) * 128], start=(kc == 0), stop=(kc == 7)),
                                reads=rb + [("A", kc)], writes=[("p", pb + t8 // 4)])
                    S.op("act", lambda e: e.activation(out=VH3[:, hf * 8:(hf + 1) * 8, 0:128],
                                                       in_=self.PS[:, pb * 512:(pb + 2) * 512].rearrange("p (t c) -> p t c", c=128), func=AF.Copy),
                         reads=[("p", pb), ("p", pb + 1)], writes=VHR(kp))
                    if hf == 1:
                        S.dma("sp", self.V_s[hd], VH(kp), reads=VHR(kp), writes=[("V_s", hd)])
                kunits.append([vunit])
        self.run_staged(kunits)
        S.op("dve", lambda e: e.tensor_scalar(KMEAN, KMEAN, 1.0 / 256, None, ALU.mult), reads=["KMEAN"], writes=["KMEAN"])

    def moba_layer(self, j):
        S = self.S
        self.rms_rstd(self.FA(), "FA")
        self.make_xn(("b_norm", j), self.FA(), "FA")
        self.load_rope()
        KMEAN = self.SMALL[:, 0:64]
        GT = lambda hf: self.SMALL[:, 64 + 64 * hf:128 + 64 * hf]
        M8 = lambda i: self.SMALL[:, 320 + 8 * i:328 + 8 * i]
        RINV = self.SMALL[:, 192:196]
        ACC = lambda i: self.SMALL[:, 384 + 129 * i:384 + 129 * (i + 1)]
        SELALL = self.S5S[:, 0:1024]
        futmask = self.cst("futmask")
        scale = 1.0 / math.sqrt(128.0)
        S.op("dve", lambda e: e.memset(SELALL, 0.0), writes=self.S5S_ALL + ["SELALL"])
        QB = lambda k: self.BIGB[:, 2048 + k * 2048:2048 + (k + 1) * 2048]
        QBR = lambda k: self.bres(2048 + k * 2048, 2048 + (k + 1) * 2048)
        slq = {}
        qunits = []
        ownm = self.cst("ownmask")
        NSB = self.SMALL[0:8, 384:896].bitcast(BF16)
        for hd in range(NH):
            kp = hd % 2
            for hf in range(2):
                def st_load(hd=hd, hf=hf):
                    if hf == 0:
                        if hd == 0:
                            slq[0] = self.load_w(self.w_q_d[j], 0, 8, 0)
                        if hd + 1 < NH:
                            slq[hd + 1] = self.load_w(self.w_q_d[j], 0, 8, (hd + 1) * 128)
                hold = {}

                def st0(hd=hd, hf=hf, hold=hold, st_load=st_load):
                    st_load()
                    a, b, h = self.qk_unit(slq[hd], hf, self.vcol(("q_norm", j)))
                    hold["st1"] = b
                    hold["h"] = h
                    a()

                def st1(hold=hold):
                    hold["st1"]()

                def st2(hd=hd, hf=hf, kp=kp, hold=hold):
                    XG, FAR, P2, RP2 = hold["h"]
                    S.op("act", lambda e: e.activation(out=QB(kp)[:, hf * 1024:(hf + 1) * 1024], in_=XG, func=AF.Copy),
                         reads=FAR, writes=self.bres(2048 + kp * 2048 + hf * 1024, 2048 + kp * 2048 + (hf + 1) * 1024))
                    for q8 in range(8):
                        S.op("pe", lambda e, q8=q8: e.matmul(P2[:, q8 * 8:(q8 + 1) * 8], XG[:, q8 * 128:(q8 + 1) * 128],
                                                           KMEAN[:, hd * 8:(hd + 1) * 8], start=True, stop=True),
                             reads=FAR + ["KMEAN"], writes=[RP2[0]])
                    G = GT(hf)
                    gres = "G%d" % hf
                    S.op("dve", lambda e: e.tensor_tensor(G, P2[:, 0:64], futmask[:, hf * 64:(hf + 1) * 64], ALU.add),
                         reads=[RP2[0], "CST"], writes=[gres])
                    selh = SELALL[:, hd * 128 + hf * 64:hd * 128 + (hf + 1) * 64]
                    if hf == 0:
                        S.op("dve", lambda e: e.tensor_scalar(selh[:, 16:64], G[:, 16:64], -1e29, None, ALU.is_gt),
                             reads=[gres], writes=["SELALL"])
                    else:
                        for q8 in range(8):
                            S.op("dve", lambda e, q8=q8: e.max(out=M8(q8), in_=G[:, q8 * 8:(q8 + 1) * 8]), reads=[gres], writes=[("M8", q8)])
                            S.op("dve", lambda e, q8=q8: e.tensor_scalar(selh[:, q8 * 8:(q8 + 1) * 8], G[:, q8 * 8:(q8 + 1) * 8],
                                                                        M8(q8)[:, 2:3], None, ALU.is_ge),
                                 reads=[gres, ("M8", q8)], writes=["SELALL"])
                    S.op("dve", lambda e: e.tensor_tensor(selh, selh, ownm[:, hf * 64:(hf + 1) * 64], ALU.add),
                         reads=["SELALL", "CST"], writes=["SELALL"])

                def st3(hd=hd, hf=hf, kp=kp, hold=hold):
                    XG, FAR, P2, RP2 = hold["h"]
                    selh = SELALL[:, hd * 128 + hf * 64:hd * 128 + (hf + 1) * 64]
                    for q8 in range(8):
                        S.op("pe", lambda e, q8=q8: e.transpose(P2[0:8, q8 * 128:(q8 + 1) * 128], selh[:, q8 * 8:(q8 + 1) * 8], self.cst("ident")),
                             reads=["SELALL", "CST"], writes=[RP2[q8 // 4]])
                    S.op("dve", lambda e: e.tensor_scalar(NSB, P2[0:8, 0:1024], -1.0, 30000.0, ALU.add, ALU.mult), reads=RP2, writes=["NSB"])
                    S.dma("sp", self.NS_s[hd][:, hf * 1024:(hf + 1) * 1024], NSB, reads=["NSB"], writes=[("NS_s", hd, hf)])
                    if hf == 1:
                        S.dma("sp", self.Q_s[hd], QB(kp), reads=QBR(kp), writes=[("Q_s", hd)])
                qunits.append([st0, st1, st2, st3])
        self.run_staged(qunits)

        QBF = lambda k: self.BIGB[:, k * 2048:(k + 1) * 2048]
        KTH = lambda k: self.BIGB[:, 4096 + k * 2048:4096 + (k + 1) * 2048]
        VHo = lambda k: 8192 + k * 2064
        VH = lambda k: self.BIGB[:, VHo(k):VHo(k) + 2064]
        ET = lambda i: self.BIGB[:, 12320 + 512 * i:12320 + 512 * (i + 1)]
        ETR = lambda i: self.bres(12320 + 512 * i, 12320 + 512 * (i + 1))
        OTH = self.BIGB[:, 14368:14368 + L]
        OTHR = self.bres(14368, 14368 + L)
        RQ = lambda k: [("cQ", k)]
        RK = lambda k: [("cK", k)]
        RV = lambda k: [("cV", k)]
        OPR = lambda par, i: self.PS[:, (2 + 2 * par + i // 2) * 512 + (i % 2) * 256:(2 + 2 * par + i // 2) * 512 + (i % 2) * 256 + 129]
        opres = lambda par, i: ("p", 2 + 2 * par + i // 2)

        def load_head(hd):
            k = hd % 2
            S.dma("sp", QBF(k), self.Q_s[hd], reads=[("Q_s", hd)], writes=RQ(k))
            S.dma("sp", KTH(k), self.kT_s[hd], reads=[("kT_s", hd)], writes=RK(k))
            S.dma("sp", VH(k), self.V_s[hd], reads=[("V_s", hd)], writes=RV(k))

        DEPTH = 2
        NET = 6
        ET = lambda i: self.BIGB[:, 12320 + 512 * i:12320 + 512 * (i + 1)]
        ETR = lambda i: [("cE", i)]
        OTH = self.TMP2[:].bitcast(BF16)
        OTHR = ["TMP2"]
        NEGSEL = lambda k: self.FW[0:8, k * 1024:(k + 1) * 1024].bitcast(BF16)
        NSR = lambda k: [("FA", 2 * k), ("FA", 2 * k + 1)]
        RB = lambda qp: self.FW[:, L + qp * 512:L + (qp + 1) * 512]
        RBR = lambda qp: [("FB", qp)]
        EONE = self.S5S[0:8, 1024:1536].bitcast(BF16)
        identf = self.cst("ident")
        CORE_NAMES = [("cQ", 0), ("cQ", 1), ("cK", 0), ("cK", 1), ("cV", 0), ("cV", 1)] + [("cE", i) for i in range(NET)]
        S.op("dve", lambda e: e.tensor_copy(EONE.rearrange("p (n c) -> p n c", c=128), identf[0:8, 0:8].unsqueeze(2).broadcast_to([8, 8, 128])),
             reads=["CST"], writes=["EONE", "TMP"] + self.bres(0, 16512) + CORE_NAMES)
        state = dict(st=0, et=0)
        tasks = []

        def load_head(hd):
            k = hd % 2
            S.dma("sp", QBF(k), self.Q_s[hd], reads=[("Q_s", hd)], writes=RQ(k))
            S.dma("sp", KTH(k), self.kT_s[hd], reads=[("kT_s", hd)], writes=RK(k))
            S.dma("sp", VH(k), self.V_s[hd], reads=[("V_s", hd)], writes=RV(k))
            S.dma("sp", NEGSEL(k), self.NS_s[hd], reads=[("NS_s", hd, 0), ("NS_s", hd, 1)], writes=NSR(k))

        def mk_block(hd, Q, n, qp):
            hk = hd % 2
            VH3 = VH(hk).rearrange("p (t c) -> p t c", c=129)
            info = {}
            lastkt = 4 * Q + 3

            def s1():
                ets = {}
                for kt in (2 * n, 2 * n + 1):
                    c0 = max(4 * Q, kt) - 4 * Q
                    if c0 > 3:
                        continue
                    sb_ = state["st"] % 4
                    state["st"] += 1
                    eb_ = state["et"] % NET
                    state["et"] += 1
                    ST = self.PS[:, sb_ * 512:(sb_ + 1) * 512]
                    diag = kt >= 4 * Q
                    selm = n < 2 * Q + 1
                    S.op("pe", lambda e, ST=ST, c0=c0, fin=(not diag and not selm), kk_=KTH(hk)[:, kt * 128:(kt + 1) * 128],
                         qq_=QBF(hk)[:, Q * 512 + c0 * 128:(Q + 1) * 512]: e.matmul(
                        ST[:, c0 * 128:512], kk_, qq_, start=True, stop=fin), reads=RQ(hk) + RK(hk), writes=[("p", sb_)])
                    if selm:
                        S.op("pe", lambda e, ST=ST, c0=c0, fin=(not diag), en=EONE[:, n * 128:(n + 1) * 128],
                             ns=NEGSEL(hk)[:, Q * 512 + c0 * 128:(Q + 1) * 512]: e.matmul(ST[:, c0 * 128:512], en, ns, start=False, stop=fin),
                             reads=["EONE"] + NSR(hk), writes=[("p", sb_)])
                    if diag:
                        S.op("pe", lambda e, ST=ST, c0=c0: e.matmul(ST[:, c0 * 128:(c0 + 1) * 128], self.identB(), self.cmaskB(),
                                                                   start=False, stop=True), reads=["CSTB"], writes=[("p", sb_)])
                    et = ET(eb_)
                    S.op("act", lambda e, ST=ST, et=et, c0=c0: e.activation(out=et[:, c0 * 128:512], in_=ST[:, c0 * 128:512], func=AF.Exp, scale=scale),
                         reads=[("p", sb_)], writes=ETR(eb_))
                    ets[kt] = (et, eb_, c0)
                info["ets"] = ets

            def s2():
                if Q == 0 and n == 0 and hd + 1 < NH:
                    load_head(hd + 1)
                for kt in sorted(info["ets"]):
                    et, eb_, c0 = info["ets"][kt]
                    ob, rbk = 4 + qp, 6 + qp
                    S.op("pe", lambda e, et=et, c0=c0, vv=VH3[:, kt, 0:128], a=(kt == 0), z=(kt == lastkt), ob=ob: e.matmul(
                        self.PS[:, ob * 512 + c0 * 128:(ob + 1) * 512], vv, et[:, c0 * 128:512], start=a, stop=z),
                        reads=ETR(eb_) + RV(hk), writes=[("p", ob)])
                    S.op("pe", lambda e, et=et, c0=c0, a=(kt == 0), z=(kt == lastkt), rbk=rbk: e.matmul(
                        self.PS[:, rbk * 512 + c0 * 128:(rbk + 1) * 512], self.onesH(), et[:, c0 * 128:512], start=a, stop=z),
                        reads=ETR(eb_) + ["CSTB"], writes=[("p", rbk)])
            return s1, s2

        def mk_qend(hd, Q, qp):
            def s1():
                pass

            def s2():
                ob, rbk = 4 + qp, 6 + qp
                R = RB(qp)
                S.op("act", lambda e: e.activation(out=R, in_=self.PS[:, rbk * 512:(rbk + 1) * 512], func=AF.Ln, scale=128.0),
                     reads=[("p", rbk)], writes=RBR(qp))
                S.op("act", lambda e: e.activation(out=R, in_=R, func=AF.Exp, scale=-1.0), reads=RBR(qp), writes=RBR(qp))
                S.op("dve", lambda e: e.tensor_tensor(OTH[:, Q * 512:(Q + 1) * 512], self.PS[:, ob * 512:(ob + 1) * 512], R, ALU.mult),
                     reads=[("p", ob)] + RBR(qp), writes=OTHR)
                if Q == 3:
                    S.dma("sp", self.OT_s[hd], OTH, reads=OTHR, writes=[("OT_s", hd)])
            return s1, s2

        load_head(0)
        qcount = 0
        for hd in range(NH):
            for Q in range(4):
                qp = qcount % 2
                qcount += 1
                for n in range(2 * Q + 2):
                    tasks.append(mk_block(hd, Q, n, qp))
                tasks.append(mk_qend(hd, Q, qp))
        for idx in range(len(tasks) + DEPTH):
            if idx < len(tasks):
                tasks[idx][0]()
            if idx - DEPTH >= 0:
                tasks[idx - DEPTH][1]()
        S.op("dve", lambda e: e.memset(self.SMALL[:, 192:200], 1.0), reads=["EONE"], writes=["TMP"] + self.bres(0, 16512) + CORE_NAMES)
        for hd in range(NH):
            S.dma("sp", self.BIGA[:, hd * L:(hd + 1) * L], self.OT_s[hd], reads=[("OT_s", hd)], writes=[("A", hd)])
        jobs = []
        for oc in range(NFC):
            def evac_o(ps, psres, oc=oc):
                S.op("dve", lambda e: e.tensor_tensor(self.HT[:, oc * L:(oc + 1) * L], self.HT[:, oc * L:(oc + 1) * L], ps, ALU.add),
                     reads=psres + [("H", oc)], writes=[("H", oc)])
            jobs.append(dict(w2d=self.w_o_d[j], k0=0, KC=8, col0=oc * 128,
                             rhs=lambda kc, tt: self.BIGA[:, kc * L + tt * 512:kc * L + (tt + 1) * 512],
                             rhs_res=lambda kc: [("A", kc)], evac=evac_o))
        self.run_dense(jobs)

    def build(self):
        with ExitStack() as es:
            self.declare(es)
            self.EPS_T = es.enter_context(self.nc.sbuf_tensor("EPS_T", [128, 2], F32))
            self.S.op("pool", lambda e: e.memset(self.EPS_T[:], EPS), writes=["EPS"])
            self.prologue()
            self.body()
            self.epilogue()
            self.S.emit(self.sems, self.dsems)
        return self.nc

    def body(self):
        st = self.stop
        if st == "ffn_only":
            self.ffn(0)
            return
        self.s5_layer(0)
        if st == "mix0":
            return
        self.ffn_hook = lambda: self.s5_layer(1, "params")
        self.ffn(0)
        if st == "ffn0":
            return
        self.s5_layer(1, "main")
        if st == "mix1":
            return
        self.ffn(1)
        if st == "ffn1":
            return
        self.kv_phase()
        self.moba_layer(0)
        if st == "mix2":
            return
        self.ffn(2)
        if st == "ffn2":
            return
        self.moba_layer(1)
        if st == "mix3":
            return
        self.ffn(3)


def _consts():
    c = np.zeros((128, NCST), np.float32)
    p = np.arange(128)
    c[:, CC["ident"]:CC["ident"] + 128] = np.eye(128, dtype=np.float32)
    sw = np.zeros((128, 128), np.float32)
    sw[(p + 64) % 128, p] = 1.0
    c[:, CC["pswap"]:CC["pswap"] + 128] = sw
    c[:, CC["maskQ"]:CC["maskQ"] + 128] = (p[:, None] // 32 == p[None, :] // 32).astype(np.float32)
    c[:, CC["onesD"]:CC["onesD"] + 128] = 1.0 / D
    c[:, CC["onesH"]:CC["onesH"] + 128] = 1.0 / 128
    c[:, CC["cmask"]:CC["cmask"] + 128] = np.where(p[:, None] > p[None, :], NEG, 0.0)
    fm = np.zeros((16, 8), np.float32)
    for qt in range(16):
        fm[qt, (qt // 2):] = -1e30
    c[:, CC["futmask"]:CC["futmask"] + 128] = fm.reshape(1, 128)
    om = np.zeros((16, 8), np.float32)
    for qt in range(16):
        om[qt, qt // 2] = 1.0
    c[:, CC["ownmask"]:CC["ownmask"] + 128] = om.reshape(1, 128)
    rm = np.zeros((128, 4), np.float32)
    for v in range(2):
        rm[:, v] = ((p // 32) % 2 == v)
        rm[:, 2 + v] = -rm[:, v]
    c[:, CC["rowmask"]:CC["rowmask"] + 4] = rm
    c[:, CC["kk"]:CC["kk"] + 9] = np.arange(9, dtype=np.float32)[None, :]
    return c


def _rope():
    half = 64
    inv = (10000.0 ** (-np.arange(half, dtype=np.float32) * 2.0 / 128)).astype(np.float32)
    ang = np.arange(L, dtype=np.float32)[:, None] * inv[None, :]
    cos, sin = np.cos(ang).T.astype(np.float32), np.sin(ang).T.astype(np.float32)
    r = np.zeros((128, 2 * L), np.float32)
    r[0:64, 0:L] = cos; r[64:128, 0:L] = cos
    r[0:64, L:] = -sin; r[64:128, L:] = sin
    return r


def _fm(v):
    return np.ascontiguousarray(v.reshape(-1, 128).T)


def _host_prep(inp):
    vec = np.zeros((128, NV), np.float32)
    for l in range(2):
        vec[:, VC[("a_norm", l)]:VC[("a_norm", l)] + 8] = _fm(inp["a_norm"][l])
        vec[:, VC[("s5_D", l)]:VC[("s5_D", l)] + 8] = _fm(inp["s5_D"][l])
        vec[:, VC[("b_glu", l)]:VC[("b_glu", l)] + 16] = _fm(inp["b_glu"][l])
    vec[:, VC["kv_norm"]:VC["kv_norm"] + 8] = _fm(inp["kv_norm"])
    vec[:, VC["k_norm"]] = inp["k_norm"]
    for j in range(2):
        vec[:, VC[("b_norm", j)]:VC[("b_norm", j)] + 8] = _fm(inp["b_norm"][j])
        vec[:, VC[("q_norm", j)]] = inp["q_norm"][j]
    for l in range(4):
        vec[:, VC[("ffn_norm", l)]:VC[("ffn_norm", l)] + 8] = _fm(inp["ffn_norm"][l])
        cw = inp["conv_w"][l].reshape(3, NUPC, 128).transpose(2, 0, 1).reshape(128, 3 * NUPC)
        vec[:, VC[("conv_w", l)]:VC[("conv_w", l)] + 3 * NUPC] = cw
        vec[:, VC[("conv_b", l)]:VC[("conv_b", l)] + NUPC] = _fm(inp["conv_b"][l])
    s5p = np.zeros((2, 128, 96 + 4096), np.float32)
    for l in range(2):
        s5p[l, :, 0:32] = inp["s5_A_re"][l].reshape(32, 128).T
        s5p[l, :, 32:64] = inp["s5_A_im"][l].reshape(32, 128).T
        s5p[l, :, 64:96] = np.repeat(inp["s5_log_dt"][l].reshape(32, 2), 64, axis=1).T
        for nm, off in (("s5_B_re", 0), ("s5_B_im", 1024)):
            Bm = np.zeros((2, 64, 32, 2, 16), np.float32)
            Bg = inp[nm][l].reshape(32, 2, 64, 16)
            for m in range(2):
                Bm[m, :, :, m, :] = Bg[:, m].transpose(1, 0, 2)
            s5p[l, :, 96 + off:96 + off + 1024] = Bm.reshape(128, 1024)
        for nm, off in (("s5_C_re", 2048), ("s5_C_im", 3072)):
            Cm = np.zeros((2, 64, 32, 2, 16), np.float32)
            Cg = inp[nm][l].reshape(32, 2, 16, 64)
            for m in range(2):
                Cm[m, :, :, m, :] = Cg[:, m].transpose(2, 0, 1)
            s5p[l, :, 96 + off:96 + off + 1024] = Cm.reshape(128, 1024)
    common = dict(vec=vec, cst=_consts(), rope=_rope(), s5p=s5p,
                  w_glu=np.ascontiguousarray(inp["w_glu"]), w_kv=np.ascontiguousarray(inp["w_kv"]),
                  w_q=np.ascontiguousarray(inp["w_q"]), w_o=np.ascontiguousarray(inp["w_o"]),
                  w_up=np.ascontiguousarray(inp["w_up"]), w_down=np.ascontiguousarray(inp["w_down"]))
    return common


def _x_to_dev(xb):
    return np.ascontiguousarray(xb.T.reshape(NFC, 128, L).transpose(1, 0, 2).reshape(128, NFC * L))


def _dev_to_x(o):
    return np.ascontiguousarray(o.reshape(128, NFC, L).transpose(1, 0, 2).reshape(D, L).T)


def run(inp, stop=None, ncores=8):
    inp = {k: np.asarray(v) for k, v in inp.items()}
    common = _host_prep(inp)
    b = Builder(stop=stop)
    nc = b.build()
    in_maps = []
    for c in range(ncores):
        m = dict(common)
        m["xT"] = _x_to_dev(inp["x"][c])
        in_maps.append(m)
    res = run_bass_kernel_spmd(nc, in_maps, core_ids=list(range(ncores)))
    return np.stack([_dev_to_x(res.results[c]["outT"]) for c in range(ncores)], axis=0)


def kernel(**inputs):
    return run(inputs, stop=None, ncores=8).astype(np.float32)
```

```python
import math
from contextlib import ExitStack

import numpy as np
import concourse.bass as bass
import concourse.mybir as mybir
from concourse.bass_utils import run_bass_kernel_spmd

F32 = mybir.dt.float32
BF16 = mybir.dt.bfloat16
ALU = mybir.AluOpType
AF = mybir.ActivationFunctionType
AX = mybir.AxisListType

L = 2048
D = 1024
NFC = 8
DFF = 2816
NUPC = 44
NACT = 22
NH = 8
EPS = 1e-6
MAGIC = 12582912.0
TWO_PI = 2.0 * math.pi
NEG = -30000.0
ENGS = ("pe", "act", "dve", "pool", "sp")

VC = {}
_c = 0
for _l in range(2):
    VC[("a_norm", _l)] = _c; _c += 8
    VC[("s5_D", _l)] = _c; _c += 8
    VC[("b_glu", _l)] = _c; _c += 16
VC["kv_norm"] = _c; _c += 8
VC["k_norm"] = _c; _c += 1
for _j in range(2):
    VC[("b_norm", _j)] = _c; _c += 8
    VC[("q_norm", _j)] = _c; _c += 1
for _l in range(4):
    VC[("ffn_norm", _l)] = _c; _c += 8
    VC[("conv_w", _l)] = _c; _c += 3 * NUPC
    VC[("conv_b", _l)] = _c; _c += NUPC
NV = _c
CC = {}
_c = 0
for _n, _w in (("ident", 128), ("pswap", 128), ("maskQ", 128), ("onesD", 128), ("onesH", 128),
               ("cmask", 128), ("futmask", 128), ("rowmask", 4), ("kk", 9), ("ownmask", 128)):
    CC[_n] = _c; _c += _w
NCST = _c


class _Op:
    __slots__ = ("eng", "fn", "deps", "dma", "sig", "cnt", "sem", "semval", "semprev", "idx")


class Sched:
    def __init__(self, nc):
        self.nc = nc
        self.ops = []
        self.res = {}

    def op(self, eng, fn, reads=(), writes=(), dma=False):
        o = _Op()
        o.eng = eng; o.fn = fn; o.dma = dma; o.sig = False; o.idx = len(self.ops)
        deps = set()
        for r in reads:
            st = self.res.get(r)
            if st is not None and st[0] is not None:
                deps.add(st[0])
        for w in writes:
            st = self.res.get(w)
            if st is not None:
                if st[0] is not None:
                    deps.add(st[0])
                deps.update(st[1])
        for r in reads:
            st = self.res.setdefault(r, [None, []])
            st[1].append(o.idx)
        for w in writes:
            self.res[w] = [o.idx, []]
        deps.discard(o.idx)
        o.deps = deps
        self.ops.append(o)
        return o.idx

    def dma(self, eng, out, in_, reads=(), writes=(), **kw):
        return self.op(eng, lambda e: e.dma_start(out=out, in_=in_, **kw), reads, writes, dma=True)

    def emit(self, sems, dma_sems):
        ops = self.ops
        for o in ops:
            latest = {}
            ddeps = []
            for d in o.deps:
                od = ops[d]
                if od.dma:
                    ddeps.append(d)
                elif od.eng != o.eng or o.dma or o.eng != "pe":
                    if od.eng not in latest or latest[od.eng] < d:
                        latest[od.eng] = d
            o.deps = (latest, ddeps)
            for d in latest.values():
                ops[d].sig = True
        cnt = {e: 0 for e in ENGS}
        for o in ops:
            if o.dma:
                continue
            if o.sig:
                cnt[o.eng] += 1
            o.cnt = cnt[o.eng]
        semcount = [0] * len(dma_sems)
        k = 0
        for o in ops:
            if o.dma:
                o.sem = k % len(dma_sems)
                o.semprev = semcount[o.sem]
                semcount[o.sem] += 16
                o.semval = semcount[o.sem]
                k += 1
        per_eng = {e: [o for o in ops if o.eng == e] for e in ENGS}

        def run_engine(ename, eng):
            waited = {e: 0 for e in ENGS}
            dwaited = [0] * len(dma_sems)
            for o in per_eng[ename]:
                latest, ddeps = o.deps
                for d in sorted(ddeps):
                    od = ops[d]
                    if dwaited[od.sem] < od.semval:
                        eng.wait_ge(dma_sems[od.sem], od.semval)
                        dwaited[od.sem] = od.semval
                for en_, d in latest.items():
                    od = ops[d]
                    if waited[en_] < od.cnt:
                        eng.wait_ge(sems[en_], od.cnt)
                        waited[en_] = od.cnt
                if o.dma:
                    if o.semprev > 0 and dwaited[o.sem] < o.semprev:
                        eng.wait_ge(dma_sems[o.sem], o.semprev)
                        dwaited[o.sem] = o.semprev
                    o.fn(eng).then_inc(dma_sems[o.sem], 16)
                else:
                    ins = o.fn(eng)
                    if o.sig:
                        ins.then_inc(sems[ename], 1)
            last = {}
            for o in per_eng[ename]:
                if o.dma:
                    last[o.sem] = max(last.get(o.sem, 0), o.semval)
            for s_, v in last.items():
                if dwaited[s_] < v:
                    eng.wait_ge(dma_sems[s_], v)

        with self.nc.Block() as block:
            @block.tensor
            def _(e):
                run_engine("pe", e)

            @block.scalar
            def _(e):
                run_engine("act", e)

            @block.vector
            def _(e):
                run_engine("dve", e)

            @block.gpsimd
            def _(e):
                run_engine("pool", e)

            @block.sync
            def _(e):
                run_engine("sp", e)


class Builder:
    def __init__(self, stop=None):
        self.stop = stop
        self.nc = bass.Bass("TRN2", target_bir_lowering=False)
        self.S = Sched(self.nc)
        self.wslot = 0
        self.pshalf = 0

    def declare(self, es):
        nc = self.nc
        di = lambda n, s, dt=F32: nc.dram_tensor(n, s, dt, kind="ExternalInput").ap()
        self.xT_d = di("xT", [128, NFC * L])
        self.vec_d = di("vec", [128, NV])
        self.cst_d = di("cst", [128, NCST])
        self.rope_d = di("rope", [128, 2 * L])
        self.s5p_d = di("s5p", [2, 128, 96 + 4096])
        self.w_glu_d = di("w_glu", [2, D, 2 * D])
        self.w_kv_d = di("w_kv", [D, 2 * D])
        self.w_q_d = di("w_q", [2, D, D])
        self.w_o_d = di("w_o", [2, D, D])
        self.w_up_d = di("w_up", [4, D, 2 * DFF])
        self.w_down_d = di("w_down", [4, DFF, D])
        self.out_d = nc.dram_tensor("outT", [128, NFC * L], F32, kind="ExternalOutput").ap()
        self.kT_s = nc.dram_tensor("kT_s", [NH, 128, L], BF16, kind="Internal").ap()
        self.V_s = nc.dram_tensor("V_s", [NH, 128, 16 * 129], BF16, kind="Internal").ap()
        self.OT_s = nc.dram_tensor("OT_s", [NH, 128, L], BF16, kind="Internal").ap()
        self.Q_s = nc.dram_tensor("Q_s", [NH, 128, L], BF16, kind="Internal").ap()
        self.NS_s = nc.dram_tensor("NS_s", [NH, 8, L], BF16, kind="Internal").ap()

        sb = lambda n, s, dt=F32: es.enter_context(nc.sbuf_tensor(n, s, dt))
        self.HT = sb("HT", [128, NFC * L])
        self.BIGA = sb("BIGA", [128, NFC * L], BF16)
        self.BIGB = sb("BIGB", [128, 16512], BF16)
        self.FW = sb("FW", [128, 3 * L])
        self.TMP = sb("TMP", [128, 1024])
        self.TMP2 = sb("TMP2", [128, 1024])
        self.UNI = sb("UNI", [128, 6656])
        self.VEC = sb("VEC", [128, NV])
        self.CST = sb("CST", [128, NCST])
        self.CSTB = sb("CSTB", [128, 4 * 128], BF16)
        self.S5S = sb("S5S", [128, 2048])
        self.SMALL = sb("SMALL", [128, 1024])
        self.PS = es.enter_context(nc.psum_tensor("PS", [128, 4096], F32))
        self.sems = {e: es.enter_context(nc.semaphore("s_" + e)) for e in ENGS}
        self.dsems = [es.enter_context(nc.semaphore("d%d" % i)) for i in range(40)]

    def FA(self):
        return self.FW[:, 0:L]

    def FB(self):
        return self.FW[:, L:2 * L]

    def FC(self):
        return self.FW[:, 2 * L:3 * L]

    def cst(self, name, w=128):
        c = CC[name]
        return self.CST[:, c:c + w]

    def vcol(self, key, i=0):
        c = VC[key] + i
        return self.VEC[:, c:c + 1]

    NSLOT = 8

    def WBF(self, i):
        return self.UNI[:, i * 512:(i + 1) * 512].bitcast(BF16)

    @staticmethod
    def wres(i):
        return ["h%d" % i]

    @staticmethod
    def bres(lo, hi):
        return [("B", k) for k in range(lo // 1024, (hi - 1) // 1024 + 1)]

    def prologue(self):
        S = self.S
        S.dma("sp", self.CST[:], self.cst_d, writes=["CST"])
        S.dma("sp", self.VEC[:], self.vec_d, writes=["VEC"])
        for fc in range(NFC):
            S.dma("act" if fc % 2 else "sp", self.HT[:, fc * L:(fc + 1) * L], self.xT_d[:, fc * L:(fc + 1) * L],
                  writes=[("H", fc)])
        for i, n in enumerate(("ident", "onesD", "onesH", "cmask")):
            src = self.cst(n)
            dst = self.CSTB[:, i * 128:(i + 1) * 128]
            S.op("dve", lambda e, d=dst, s=src: e.tensor_copy(d, s), reads=["CST"], writes=["CSTB"])

    def identB(self):
        return self.CSTB[:, 0:128]

    def onesD(self):
        return self.CSTB[:, 128:256]

    def onesH(self):
        return self.CSTB[:, 256:384]

    def cmaskB(self):
        return self.CSTB[:, 384:512]

    def epilogue(self):
        S = self.S
        for fc in range(NFC):
            S.dma("act" if fc % 2 else "sp", self.out_d[:, fc * L:(fc + 1) * L], self.HT[:, fc * L:(fc + 1) * L],
                  reads=[("H", fc)], writes=[("OUT", fc)])

    def rms_rstd(self, dst, dst_res):
        S = self.S

        def stA(tt):
            sq = self.BIGB[:, (tt % 2) * 4096:(tt % 2 + 1) * 4096]
            sqres = self.bres((tt % 2) * 4096, (tt % 2 + 1) * 4096)
            hin = self.HT[:].rearrange("p (f t) -> p f t", t=L)[:, :, tt * 512:(tt + 1) * 512]
            S.op("act", lambda e, o=sq, i=hin: e.activation(out=o.rearrange("p (f t) -> p f t", t=512), in_=i, func=AF.Square),
                 reads=[("H", fc) for fc in range(NFC)], writes=sqres)

        def stB(tt):
            sq = self.BIGB[:, (tt % 2) * 4096:(tt % 2 + 1) * 4096]
            sqres = self.bres((tt % 2) * 4096, (tt % 2 + 1) * 4096)
            bank = 4 + (tt % 2)
            ps = self.PS[:, bank * 512:(bank + 1) * 512]
            for fc in range(NFC):
                S.op("pe", lambda e, o=ps, r=sq[:, fc * 512:(fc + 1) * 512], a=(fc == 0), z=(fc == NFC - 1):
                     e.matmul(o, self.onesD(), r, start=a, stop=z),
                     reads=sqres + ["CSTB"], writes=[("p", bank)])
            d = dst[:, tt * 512:(tt + 1) * 512]
            S.op("act", lambda e, o=d, i=ps: e.activation(out=o, in_=i, func=AF.Ln, bias=self.eps_ap(), scale=1.0),
                 reads=[("p", bank), "EPS"], writes=[(dst_res, tt)])
            S.op("act", lambda e, o=d: e.activation(out=o, in_=o, func=AF.Exp, scale=-0.5), reads=[(dst_res, tt)], writes=[(dst_res, tt)])

        stA(0)
        stA(1)
        for tt in range(4):
            stB(tt)
            if tt + 2 < 4:
                stA(tt + 2)

    def eps_ap(self):
        return self.EPS_T[:, 0:1]

    def make_xn(self, gkey, rstd, rstd_res):
        S = self.S
        for tt in range(4):
            for fc in range(NFC):
                S.op("dve", lambda e, fc=fc, tt=tt: e.scalar_tensor_tensor(
                    out=self.BIGA[:, fc * L + tt * 512:fc * L + (tt + 1) * 512], in0=self.HT[:, fc * L + tt * 512:fc * L + (tt + 1) * 512],
                    scalar=self.vcol(gkey, fc), in1=rstd[:, tt * 512:(tt + 1) * 512], op0=ALU.mult, op1=ALU.mult),
                    reads=[("H", fc), "VEC", (rstd_res, tt)], writes=[("A", fc)])

    def run_dense(self, jobs):
        S = self.S
        n = len(jobs)
        slots = {}
        ahead = self.NSLOT - 2

        def load(i):
            if i >= n:
                return
            jb = jobs[i]
            slots[i] = self.load_w(jb["w2d"], jb["k0"], jb["KC"], jb["col0"])

        for i in range(min(ahead, n)):
            load(i)
        for i in range(n):
            load(i + ahead)
            jb = jobs[i]
            sl = slots[i]
            KC = jb["KC"]
            half = self.pshalf
            self.pshalf ^= 1
            rb = self.wres(sl)
            for kc in range(KC):
                for tt in range(4):
                    bank = half * 4 + tt
                    o = self.PS[:, bank * 512:(bank + 1) * 512]
                    S.op("pe", lambda e, o=o, w=self.WBF(sl)[:, kc * 128:(kc + 1) * 128], r=jb["rhs"](kc, tt), a=(kc == 0), z=(kc == KC - 1):
                         e.matmul(o, w, r, start=a, stop=z),
                         reads=rb + jb["rhs_res"](kc), writes=[("p", bank)])
            jb["evac"](self.PS[:, half * 2048:(half + 1) * 2048], [("p", half * 4 + t) for t in range(4)])

    def ffn(self, l):
        S = self.S
        self.rms_rstd(self.FA(), "FA")
        self.make_xn(("ffn_norm", l), self.FA(), "FA")
        if getattr(self, "ffn_hook", None):
            self.ffn_hook()
            self.ffn_hook = None
        groups = [(0, 6), (6, 12), (12, 17), (17, 22)]
        cw = VC[("conv_w", l)]
        cb = VC[("conv_b", l)]
        SG = self.BIGB[:, 12288:12288 + L]
        FBR = [("FB", t) for t in range(4)]
        FCR = [("FC", t) for t in range(4)]

        pending = []

        def conv_to(acc, accres, ps, psres, c):
            w = lambda k: self.VEC[:, cw + k * NUPC + c:cw + k * NUPC + c + 1]
            S.op("act", lambda e: e.activation(out=acc, in_=ps, func=AF.Identity, scale=w(2), bias=self.VEC[:, cb + c:cb + c + 1]),
                 reads=psres + ["VEC"], writes=accres)
            while pending:
                pending.pop(0)()
            S.op("dve", lambda e: e.scalar_tensor_tensor(out=acc[:, 1:L], in0=ps[:, 0:L - 1], scalar=w(1), in1=acc[:, 1:L],
                                                         op0=ALU.mult, op1=ALU.add),
                 reads=psres + ["VEC"] + accres, writes=accres)
            S.op("dve", lambda e: e.scalar_tensor_tensor(out=acc[:, 2:L], in0=ps[:, 0:L - 2], scalar=w(0), in1=acc[:, 2:L],
                                                         op0=ALU.mult, op1=ALU.add),
                 reads=psres + ["VEC"] + accres, writes=accres)

        ups, downs = [], []
        for (g0, g1) in groups:
            jobs = []
            for i in range(g0, g1):
                il = i - g0

                def evac_gate(ps, psres, i=i):
                    conv_to(self.FB(), FBR, ps, psres, i)
                    pending.append(lambda: S.op("act", lambda e: e.activation(out=SG, in_=self.FB(), func=AF.Silu), reads=FBR, writes=self.bres(12288, 14336)))

                def evac_val(ps, psres, i=i, il=il):
                    conv_to(self.FC(), FCR, ps, psres, NACT + i)
                    S.op("dve", lambda e: e.tensor_tensor(self.BIGB[:, il * L:(il + 1) * L], SG, self.FC(), ALU.mult),
                         reads=self.bres(12288, 14336) + FCR, writes=self.bres(il * L, (il + 1) * L))

                for col0, ev in ((i * 128, evac_gate), (DFF + i * 128, evac_val)):
                    jobs.append(dict(w2d=self.w_up_d[l], k0=0, KC=8, col0=col0,
                                     rhs=lambda kc, tt: self.BIGA[:, kc * L + tt * 512:kc * L + (tt + 1) * 512],
                                     rhs_res=lambda kc: [("A", kc)], evac=ev))
            ng = g1 - g0
            ups.append(jobs)
            jobs = []
            for oc in range(NFC):
                def evac_down(ps, psres, oc=oc):
                    S.op("dve", lambda e: e.tensor_tensor(self.HT[:, oc * L:(oc + 1) * L], self.HT[:, oc * L:(oc + 1) * L], ps, ALU.add),
                         reads=psres + [("H", oc)], writes=[("H", oc)])
                jobs.append(dict(w2d=self.w_down_d[l], k0=g0 * 128, KC=ng, col0=oc * 128,
                                 rhs=lambda kc, tt: self.BIGB[:, kc * L + tt * 512:kc * L + (tt + 1) * 512],
                                 rhs_res=lambda kc: self.bres(kc * L, (kc + 1) * L), evac=evac_down))
            downs.append(jobs)
        alljobs = list(ups[0])
        for g in range(len(groups)):
            if g + 1 < len(groups):
                alljobs.append(ups[g + 1][0])
            alljobs.extend(downs[g])
            if g + 1 < len(groups):
                alljobs.extend(ups[g + 1][1:])
        self.run_dense(alljobs)
        while pending:
            pending.pop(0)()

    def s5_layer(self, l, part="all"):
        S = self.S
        P = self.S5S
        skip = {"on": part == "main"}
        sl = lambda a, b: P[:, a:b]
        Are, Aim, ldt = sl(0, 32), sl(32, 64), sl(64, 96)
        dt, ar, ai = sl(96, 128), sl(128, 160), sl(160, 192)
        MAG, ANG, NT, C9, PWR, PWI = sl(192, 480), sl(480, 768), sl(768, 1056), sl(1056, 1344), sl(1344, 1632), sl(1632, 1920)
        cr, ci = sl(1920, 1952), sl(1952, 1984)
        sm = lambda i: self.SMALL[:, 640 + 32 * i:640 + 32 * (i + 1)]
        k3 = lambda ap: ap.rearrange("p (k g) -> p k g", g=32)
        kk = self.cst("kk", 9)
        kkb = kk.unsqueeze(2).broadcast_to([128, 9, 32])

        def dv(name, fn, reads, writes, eng="dve"):
            if skip["on"]:
                return
            S.op(eng, fn, reads=reads, writes=writes)

        if part != "main":
            S.dma("sp", P[:, 0:96], self.s5p_d[l][:, 0:96], writes=["s5raw"])
        FBR = [("FB", t) for t in range(4)]
        FCR = [("FC", t) for t in range(4)]
        if part != "params":
            S.dma("sp", self.FW[:, L:3 * L], self.s5p_d[l][:, 96:96 + 4096], writes=FBR + FCR)
        BMR, BMI = self.FW[:, L:L + 1024], self.FW[:, L + 1024:2 * L]
        CMR, CMI = self.FW[:, 2 * L:2 * L + 1024], self.FW[:, 2 * L + 1024:3 * L]
        g3 = lambda ap: ap.rearrange("p (g c) -> p g c", c=32)
        dv("dt", lambda e: e.activation(out=dt, in_=ldt, func=AF.Exp), ["s5raw"], ["s5dt"], "act")
        dv("ar", lambda e: e.tensor_tensor(ar, Are, dt, ALU.mult), ["s5raw", "s5dt"], ["s5ar"])
        dv("ai", lambda e: e.tensor_tensor(ai, Aim, dt, ALU.mult), ["s5raw", "s5dt"], ["s5ai"])
        dv("mag", lambda e: e.tensor_tensor(k3(MAG), ar.unsqueeze(1).broadcast_to([128, 9, 32]), kkb, ALU.mult), ["s5ar", "CST"], ["s5mag"])
        dv("mage", lambda e: e.activation(out=MAG, in_=MAG, func=AF.Exp), ["s5mag"], ["s5mag"], "act")
        dv("ang", lambda e: e.tensor_tensor(k3(ANG), ai.unsqueeze(1).broadcast_to([128, 9, 32]), kkb, ALU.mult), ["s5ai", "CST"], ["s5ang"])

        def sin_of(src, srcres, tmp, tmpres):
            dv("n1", lambda e: e.tensor_scalar(tmp, src, 1.0 / TWO_PI, MAGIC, ALU.mult, ALU.add), [srcres], [tmpres])
            dv("n2", lambda e: e.tensor_scalar(tmp, tmp, MAGIC, None, ALU.subtract), [tmpres], [tmpres])
            dv("n3", lambda e: e.scalar_tensor_tensor(out=tmp, in0=tmp, scalar=-TWO_PI, in1=src, op0=ALU.mult, op1=ALU.add), [tmpres, srcres], [tmpres])
            dv("n4", lambda e: e.activation(out=tmp, in_=tmp, func=AF.Sin), [tmpres], [tmpres], "act")

        sin_of(ANG, "s5ang", NT, "s5nt")
        dv("pwi", lambda e: e.tensor_tensor(PWI, MAG, NT, ALU.mult), ["s5mag", "s5nt"], ["s5pwi"])
        dv("angc", lambda e: e.tensor_scalar(C9, ANG, math.pi / 2, None, ALU.add), ["s5ang"], ["s5c9"])
        sin_of(C9, "s5c9", NT, "s5nt")
        dv("pwr", lambda e: e.tensor_tensor(PWR, MAG, NT, ALU.mult), ["s5mag", "s5nt"], ["s5pwr"])
        nr, den, x1, x2, rden, y1, y2 = sm(0), sm(1), sm(2), sm(3), sm(4), sm(5), sm(6)
        pw1r, pw1i = PWR[:, 32:64], PWI[:, 32:64]
        dv("nr", lambda e: e.tensor_scalar(nr, pw1r, -1.0, None, ALU.add), ["s5pwr"], ["sm0"])
        dv("den", lambda e: e.tensor_tensor(den, Are, Are, ALU.mult), ["s5raw"], ["sm1"])
        dv("den2", lambda e: e.tensor_tensor(x1, Aim, Aim, ALU.mult), ["s5raw"], ["sm2"])
        dv("den3", lambda e: e.tensor_tensor(den, den, x1, ALU.add), ["sm1", "sm2"], ["sm1"])
        dv("rden", lambda e: e.reciprocal(rden, den), ["sm1"], ["sm4"])
        dv("x1", lambda e: e.tensor_tensor(x1, nr, Are, ALU.mult), ["sm0", "s5raw"], ["sm2"])
        dv("x2", lambda e: e.tensor_tensor(x2, pw1i, Aim, ALU.mult), ["s5pwi", "s5raw"], ["sm3"])
        dv("x3", lambda e: e.tensor_tensor(x1, x1, x2, ALU.add), ["sm2", "sm3"], ["sm2"])
        dv("cr", lambda e: e.tensor_tensor(cr, x1, rden, ALU.mult), ["sm2", "sm4"], ["s5cr"])
        dv("y1", lambda e: e.tensor_tensor(y1, pw1i, Are, ALU.mult), ["s5pwi", "s5raw"], ["sm5"])
        dv("y2", lambda e: e.tensor_tensor(y2, nr, Aim, ALU.mult), ["sm0", "s5raw"], ["sm6"])
        dv("y3", lambda e: e.tensor_tensor(y1, y1, y2, ALU.subtract), ["sm5", "sm6"], ["sm5"])
        dv("ci", lambda e: e.tensor_tensor(ci, y1, rden, ALU.mult), ["sm5", "sm4"], ["s5ci"])
        skip["on"] = False
        if part == "params":
            return
        crb = cr.unsqueeze(2).broadcast_to([128, 32, 32])
        cib = ci.unsqueeze(2).broadcast_to([128, 32, 32])
        T1, T2 = self.TMP[:], self.TMP2[:]
        R_BMR, R_BMI = [("FB", 0), ("FB", 1)], [("FB", 2), ("FB", 3)]
        R_CMR, R_CMI = [("FC", 0), ("FC", 1)], [("FC", 2), ("FC", 3)]
        dv("b1", lambda e: e.tensor_tensor(g3(T1), cib, g3(BMR), ALU.mult), ["s5ci"] + R_BMR, ["TMP"])
        dv("b2", lambda e: e.tensor_tensor(g3(T2), cib, g3(BMI), ALU.mult), ["s5ci"] + R_BMI, ["TMP2"])
        dv("b3", lambda e: e.tensor_tensor(g3(BMR), crb, g3(BMR), ALU.mult), ["s5cr"] + R_BMR, R_BMR)
        dv("b4", lambda e: e.tensor_tensor(BMR, BMR, T2, ALU.subtract), R_BMR + ["TMP2"], R_BMR)
        dv("b5", lambda e: e.tensor_tensor(g3(BMI), crb, g3(BMI), ALU.mult), ["s5cr"] + R_BMI, R_BMI)
        dv("b6", lambda e: e.tensor_tensor(BMI, BMI, T1, ALU.add), R_BMI + ["TMP"], R_BMI)

        FAR = [("FA", t) for t in range(4)]
        self.rms_rstd(self.FA(), "FA")
        for tt in range(4):
            for fc in range(NFC):
                S.op("dve", lambda e, fc=fc, tt=tt: e.scalar_tensor_tensor(
                    out=self.BIGA[:, fc * L:(fc + 1) * L].rearrange("p (j c) -> p j c", c=256)[:, :, tt * 64:(tt + 1) * 64],
                    in0=self.HT[:, fc * L + tt * 512:fc * L + (tt + 1) * 512].rearrange("p (c j) -> p j c", j=8),
                    scalar=self.vcol(("a_norm", l), fc),
                    in1=self.FA()[:, tt * 512:(tt + 1) * 512].rearrange("p (c j) -> p j c", j=8), op0=ALU.mult, op1=ALU.mult),
                    reads=[("H", fc), "VEC", ("FA", tt)], writes=[("A", fc)])
        B2S4 = self.BIGB[:, 0:16448].rearrange("p (r g c) -> p r g c", r=2, g=32)
        ALLC = [("b2S", c) for c in range(257)]
        BALL = self.bres(0, 16512)
        S.op("pool", lambda e: e.memset(B2S4[:, :, :, 0:1], 0.0), reads=[], writes=BALL + [("b2S", 0)])
        Zre, ZimN = self.FW[:, 0:1024], self.FW[:, 1024:2048]
        z4 = lambda ap: ap.rearrange("p (t q c) -> p t q c", t=8, q=4)
        R_ZR, R_ZI = [("FA", 0), ("FA", 1)], [("FA", 2), ("FA", 3)]
        PWR3, PWI3 = k3(PWR), k3(PWI)
        ZB = {0: (Zre, ZimN, R_ZR, R_ZI),
              1: (self.UNI[:, 2048:3072], self.UNI[:, 3072:4096], ["h4", "h5"], ["h6", "h7"])}

        PWRN = self.SMALL[:, 576:864]
        dv("pwrn", lambda e: e.tensor_scalar(PWRN, PWR, -1.0, None, ALU.mult), ["s5pwr"], self.S5_SMALL)
        PWRN3 = k3(PWRN)

        def zcompute(fc, zb=0):
            Zre, ZimN, R_ZR, R_ZI = ZB[zb]
            pwr_b = PWR3[:, 0:8, 4 * fc:4 * fc + 4].unsqueeze(3).broadcast_to([128, 8, 4, 32])
            pwrn_b = PWRN3[:, 0:8, 4 * fc:4 * fc + 4].unsqueeze(3).broadcast_to([128, 8, 4, 32])
            pwi_b = PWI3[:, 0:8, 4 * fc:4 * fc + 4].unsqueeze(3).broadcast_to([128, 8, 4, 32])
            bmr_b = g3(BMR)[:, 4 * fc:4 * fc + 4, :].unsqueeze(1).broadcast_to([128, 8, 4, 32])
            bmi_b = g3(BMI)[:, 4 * fc:4 * fc + 4, :].unsqueeze(1).broadcast_to([128, 8, 4, 32])
            E = "pool"
            dv("z1", lambda e: e.tensor_tensor(z4(Zre), pwr_b, bmr_b, ALU.mult), ["s5pwr"] + R_BMR, R_ZR, E)
            dv("z2", lambda e: e.tensor_tensor(z4(T1), pwi_b, bmi_b, ALU.mult), ["s5pwi"] + R_BMI, ["TMP"], E)
            dv("z3", lambda e: e.tensor_tensor(Zre, Zre, T1, ALU.subtract), R_ZR + ["TMP"], R_ZR, E)
            E = "pool"
            dv("z4", lambda e: e.tensor_tensor(z4(ZimN), pwrn_b, bmi_b, ALU.mult), ["PWRN"] + R_BMI, R_ZI, E)
            dv("z5", lambda e: e.tensor_tensor(z4(T2), pwi_b, bmr_b, ALU.mult), ["s5pwi"] + R_BMR, ["TMP2"], E)
            dv("z6", lambda e: e.tensor_tensor(ZimN, ZimN, T2, ALU.subtract), R_ZI + ["TMP2"], R_ZI, E)

        def zcompute_dve(fc):
            Zre, ZimN, R_ZR, R_ZI = ZB[0]
            pwr_b = PWR3[:, 0:8, 4 * fc:4 * fc + 4].unsqueeze(3).broadcast_to([128, 8, 4, 32])
            pwi_b = PWI3[:, 0:8, 4 * fc:4 * fc + 4].unsqueeze(3).broadcast_to([128, 8, 4, 32])
            bmr_b = g3(BMR)[:, 4 * fc:4 * fc + 4, :].unsqueeze(1).broadcast_to([128, 8, 4, 32])
            bmi_b = g3(BMI)[:, 4 * fc:4 * fc + 4, :].unsqueeze(1).broadcast_to([128, 8, 4, 32])
            PT = self.PS[:, 3072:4096]
            RPT = [("p", 6), ("p", 7)]
            dv("z1", lambda e: e.tensor_tensor(z4(Zre), pwr_b, bmr_b, ALU.mult), ["s5pwr"] + R_BMR, R_ZR, "pool")
            dv("z2", lambda e: e.tensor_tensor(z4(T1), pwi_b, bmi_b, ALU.mult), ["s5pwi"] + R_BMI, ["TMP"], "pool")
            dv("z3", lambda e: e.tensor_tensor(Zre, Zre, T1, ALU.subtract), R_ZR + ["TMP"], R_ZR, "pool")
            dv("z4", lambda e: e.tensor_tensor(z4(ZimN), pwr_b, bmi_b, ALU.mult), ["s5pwr"] + R_BMI, R_ZI)
            dv("z5", lambda e: e.tensor_tensor(z4(PT), pwi_b, bmr_b, ALU.mult), ["s5pwi"] + R_BMR, RPT)
            dv("z6", lambda e: e.tensor_tensor(ZimN, ZimN, PT, ALU.add), R_ZI + RPT, R_ZI)
            dv("z7", lambda e: e.activation(out=ZimN, in_=ZimN, func=AF.Copy, scale=-1.0), R_ZI, R_ZI, "act")

        W2B = {0: (self.UNI[:, 0:2048].bitcast(BF16).rearrange("p (v t r c) -> p v t r c", v=2, t=8, r=2), ["h0", "h1", "h2", "h3"]),
               1: (self.UNI[:, 4096:6144].bitcast(BF16).rearrange("p (v t r c) -> p v t r c", v=2, t=8, r=2), ["h8", "h9", "h10", "h11"])}
        ident = self.cst("ident")
        rowmask = self.cst("rowmask", 4)

        def stageT(fc):
            zb = fc % 2
            zcompute(fc, zb)
            Zre_, ZimN_, RZR_, RZI_ = ZB[zb]
            W2v, R_W2 = W2B[zb]
            for ri, Zs, ZR in ((0, Zre_, RZR_), (1, ZimN_, RZI_)):
                for tq in range(2):
                    bank = ri * 2 + tq
                    for i in range(4):
                        t = tq * 4 + i
                        S.op("pe", lambda e, o=self.PS[:, bank * 512 + i * 128:bank * 512 + (i + 1) * 128], a=Zs[:, t * 128:(t + 1) * 128]:
                             e.transpose(o, a, ident), reads=ZR + ["CST"], writes=[("p", bank)])
                    for v in range(2):
                        S.op("dve", lambda e, v=v, tq=tq, ri=ri, bank=bank, W2v=W2v: e.tensor_scalar(
                            W2v[:, v, tq * 4:(tq + 1) * 4, ri, :],
                            self.PS[:, bank * 512:(bank + 1) * 512].rearrange("p (t c) -> p t c", c=128),
                            rowmask[:, 2 * ri + v:2 * ri + v + 1], None, ALU.mult),
                            reads=[("p", bank), "CST"], writes=R_W2)

        def stageM(fc):
            W2v, R_W2 = W2B[fc % 2]
            for q in range(4):
                hq, v = q // 2, q % 2
                for ri in range(2):
                    off = 2048 + (q * 2 + ri) * 256
                    bank = off // 512
                    for j in range(8):
                        S.op("pe", lambda e, o=self.PS[:, off:off + 256], w=W2v[64 * hq:64 * hq + 64, v, 7 - j, ri, :],
                             r=self.BIGA[64 * hq:64 * hq + 64, fc * L + j * 256:fc * L + (j + 1) * 256], a=(j == 0), z=(j == 7):
                             e.matmul(o, w, r, start=a, stop=z), reads=R_W2 + [("A", fc)], writes=[("p", bank)])
            S.op("act", lambda e, fc=fc: e.activation(
                out=B2S4[:, :, 4 * fc:4 * fc + 4, 1:257],
                in_=self.PS[:, 2048:4096].rearrange("p (q r c) -> p r q c", q=4, r=2), func=AF.Copy),
                reads=[("p", b) for b in range(4, 8)], writes=BALL + ALLC[1:])

        stageT(0)
        for fc in range(NFC):
            if fc + 1 < NFC:
                stageT(fc + 1)
            stageM(fc)

        SMT = self.SMALL[:]
        A12 = self.SMALL[:, 0:128]
        A1, A2 = self.SMALL[:, 0:64], self.SMALL[:, 64:128]
        SR = lambda k: self.SMALL[:, 128 + 128 * k:256 + 128 * k]
        TT = self.SMALL[:, 512:576]
        PP = self.SMALL[:, 896:1024]
        l8r, l8i = PWR[:, 256:288], PWI[:, 256:288]
        dv("a1a", lambda e: e.tensor_copy(A1[:, 0:32], l8r), ["s5pwr"], ["A1"])
        dv("a1b", lambda e: e.tensor_copy(A1[:, 32:64], l8r), ["s5pwr"], ["A1"])
        dv("a2a", lambda e: e.tensor_scalar(A2[:, 0:32], l8i, -1.0, None, ALU.mult), ["s5pwi"], ["A2"])
        dv("a2b", lambda e: e.tensor_copy(A2[:, 32:64], l8i), ["s5pwi"], ["A2"])
        dv("sr0", lambda e: e.memset(SR(0), 0.0), [], [("SR", 0)])
        h4 = lambda ap: ap.rearrange("p (h j g) -> p h j g", h=2, j=2)
        j3 = lambda ap: ap.rearrange("p (j g) -> p j g", j=2)
        for c in range(256):
            k0, k1 = c % 3, (c + 1) % 3
            nw = SR(k1)
            wv = bass.AP(SMT.tensor, SMT.offset + 128 + 128 * k0, [list(SMT.ap[0]), [32, 2], [32, 2], [1, 32]])
            rp, rn = ("SR", k0), ("SR", k1)
            dv("s12", lambda e, wv=wv: e.tensor_tensor(h4(PP), h4(A12), wv, ALU.mult), ["A1", "A2", rp], ["PP"])
            dv("s3", lambda e: e.tensor_tensor(j3(TT), h4(PP)[:, 0], h4(PP)[:, 1], ALU.add), ["PP"], ["TT1"])
            dv("s4", lambda e, nw=nw, c=c: e.tensor_tensor(h4(nw), j3(TT).unsqueeze(1).broadcast_to([128, 2, 2, 32]),
                                                         B2S4[:, :, :, c + 1].unsqueeze(1).broadcast_to([128, 2, 2, 32]), ALU.add),
               ["TT1", ("b2S", c + 1)], [rn])
            dv("s5", lambda e, nw=nw, c=c: e.tensor_copy(B2S4[:, :, :, c + 1], j3(nw[:, 0:64])), [rn], [("b2S", c + 1)], "pool")

        W3H = {b: self.UNI[:, b * 2048:(b + 1) * 2048].bitcast(BF16).rearrange("p (j q r c) -> p j q r c", j=4, q=4, r=2) for b in range(3)}
        R_W3H = {b: ["h%d" % i_ for i_ in range(4 * b, 4 * b + 4)] for b in range(3)}
        KL = self.UNI[:, 6144:6656].bitcast(BF16).rearrange("p (t c) -> p t c", c=128)
        maskQ = self.cst("maskQ")
        S.op("pool", lambda e: e.memset(self.UNI[:, 0:6144].bitcast(BF16), 0.0), writes=["h%d" % i_ for i_ in range(12)])
        t4 = lambda ap: ap.rearrange("p (j q c) -> p j q c", j=8, q=4)
        wcount = 0
        zcompute_dve(0)
        for fc in range(NFC):
            ypar = 0
            yb = 0
            for tq in range(2):
                bank = 4 + tq
                for i in range(4):
                    t = tq * 4 + i
                    o = self.PS[:, bank * 512 + i * 128:bank * 512 + (i + 1) * 128]
                    S.op("pe", lambda e, o=o, t=t, fc=fc: e.matmul(o, Zre[:, t * 128:(t + 1) * 128], CMR[:, 128 * fc:128 * (fc + 1)], start=True, stop=False),
                         reads=R_ZR + R_CMR, writes=[("p", bank)])
                    S.op("pe", lambda e, o=o, t=t, fc=fc: e.matmul(o, ZimN[:, t * 128:(t + 1) * 128], CMI[:, 128 * fc:128 * (fc + 1)], start=False, stop=True),
                         reads=R_ZI + R_CMI, writes=[("p", bank)])
                S.op("dve", lambda e, tq=tq, bank=bank: e.tensor_tensor(
                    KL[:, tq * 4:(tq + 1) * 4, :], self.PS[:, bank * 512:(bank + 1) * 512].rearrange("p (t c) -> p t c", c=128),
                    maskQ.unsqueeze(1).broadcast_to([128, 4, 128]), ALU.mult), reads=[("p", bank), "CST"], writes=["h12"])
            cmr_b = g3(CMR)[:, 4 * fc:4 * fc + 4, :].unsqueeze(1).broadcast_to([128, 8, 4, 32])
            cmi_b = g3(CMI)[:, 4 * fc:4 * fc + 4, :].unsqueeze(1).broadcast_to([128, 8, 4, 32])
            pr_b = PWR3[:, 1:9, 4 * fc:4 * fc + 4].unsqueeze(3).broadcast_to([128, 8, 4, 32])
            pi_b = PWI3[:, 1:9, 4 * fc:4 * fc + 4].unsqueeze(3).broadcast_to([128, 8, 4, 32])
            wb = [(wcount) % 3, (wcount + 1) % 3]
            wcount += 2
            E = "pool"
            dv("w1", lambda e, a=cmr_b, b=pr_b: e.tensor_tensor(t4(T1), a, b, ALU.mult), R_CMR + ["s5pwr"], ["TMP"], E)
            dv("w2", lambda e, a=cmi_b, b=pi_b: e.tensor_tensor(t4(T2), a, b, ALU.mult), R_CMI + ["s5pwi"], ["TMP2"], E)
            for hb in range(2):
                for q in range(4):
                    dv("w3", lambda e, q=q, hb=hb, wv=W3H[wb[hb]]: e.tensor_tensor(wv[:, :, q, 0, 32 * q:32 * q + 32], t4(T1)[:, 4 * hb:4 * hb + 4, q, :],
                                                                               t4(T2)[:, 4 * hb:4 * hb + 4, q, :], ALU.subtract),
                       ["TMP", "TMP2"], R_W3H[wb[hb]], E)
            PT = self.PS[:, 3072:4096]
            RPT = [("p", 6), ("p", 7)]
            ST2 = self.SMALL[:, 0:1024]
            dv("w4", lambda e, a=cmi_b, b=pr_b: e.tensor_tensor(t4(PT), a, b, ALU.mult), R_CMI + ["s5pwr"], RPT, "dve")
            dv("w5", lambda e, a=cmr_b, b=pi_b: e.tensor_tensor(t4(ST2), a, b, ALU.mult), R_CMR + ["s5pwi"], self.S5_SMALL, "dve")
            dv("w5b", lambda e: e.tensor_tensor(PT, PT, ST2, ALU.add), RPT + self.S5_SMALL, RPT, "dve")
            for hb in range(2):
                for q in range(4):
                    dv("w6", lambda e, q=q, hb=hb, wv=W3H[wb[hb]]: e.activation(out=wv[:, :, q, 1, 32 * q:32 * q + 32], in_=t4(PT)[:, 4 * hb:4 * hb + 4, q, :],
                                                                            func=AF.Copy, scale=-1.0),
                       RPT, R_W3H[wb[hb]], "act")
            if fc + 1 < NFC:
                zcompute_dve(fc + 1)
            for t in range(8):
                for b in range(4):
                    jlo, jhi = max(2 * b, t), 2 * b + 2
                    if jlo >= jhi:
                        continue
                    S.op("pe", lambda e, t=t, jlo=jlo, jhi=jhi, fc=fc, yb=yb: e.matmul(
                        self.PS[:, yb + jlo * 256:yb + jhi * 256], KL[:, t, :],
                        self.BIGA[:, fc * L + (jlo - t) * 256:fc * L + (jhi - t) * 256], start=(t == 0), stop=False),
                        reads=["h12", ("A", fc)], writes=[("p", 4 * ypar + b)])
            for j in range(8):
                hb = j // 4
                for q in range(4):
                    for ri in range(2):
                        S.op("pe", lambda e, j=j, q=q, ri=ri, fc=fc, yb=yb, wv=W3H[wb[hb]]: e.matmul(
                            self.PS[:, yb + j * 256:yb + (j + 1) * 256], wv[:, j % 4, q, ri, :], B2S4[:, ri, 4 * fc + q, 0:256],
                            start=False, stop=(q == 3 and ri == 1)),
                            reads=R_W3H[wb[hb]] + ALLC, writes=[("p", 4 * ypar + j // 2)])
            for hv in range(2):
                yv = self.PS[:, yb + hv * 1024:yb + (hv + 1) * 1024]
                sc = self.SMALL[:, 0:1024]
                ry = [("p", 4 * ypar + 2 * hv), ("p", 4 * ypar + 2 * hv + 1)]
                rs = self.S5_SMALL
                ug = self.BIGA[:, fc * L + hv * 1024:fc * L + (hv + 1) * 1024]
                S.op("dve", lambda e, yv=yv, ug=ug, fc=fc: e.scalar_tensor_tensor(out=yv, in0=ug, scalar=self.vcol(("s5_D", l), fc), in1=yv,
                                                                          op0=ALU.mult, op1=ALU.add), reads=ry + [("A", fc), "VEC"], writes=ry)
                S.op("act", lambda e, yv=yv, ug=ug: e.activation(out=ug, in_=yv, func=AF.Gelu_apprx_tanh), reads=ry, writes=[("A", fc)])

        jobs = []
        for oc in range(NFC):
            def evac_zb(ps, psres, oc=oc):
                S.op("act", lambda e: e.activation(out=self.FA(), in_=ps, func=AF.Sigmoid, bias=self.vcol(("b_glu", l), 8 + oc), scale=1.0),
                     reads=psres + ["VEC"], writes=FAR)

            def evac_za(ps, psres, oc=oc):
                S.op("dve", lambda e: e.scalar_tensor_tensor(out=self.FB(), in0=ps, scalar=self.vcol(("b_glu", l), oc), in1=self.FA(),
                                                             op0=ALU.add, op1=ALU.mult), reads=psres + ["VEC"] + FAR, writes=FBR)
                hv = self.HT[:, oc * L:(oc + 1) * L].rearrange("p (c j) -> p j c", j=8)
                S.op("dve", lambda e: e.tensor_tensor(hv, hv, self.FB().rearrange("p (j c) -> p j c", c=256), ALU.add),
                     reads=FBR + [("H", oc)], writes=[("H", oc)])
            for col0, ev in ((D + oc * 128, evac_zb), (oc * 128, evac_za)):
                jobs.append(dict(w2d=self.w_glu_d[l], k0=0, KC=8, col0=col0,
                                 rhs=lambda kc, tt: self.BIGA[:, kc * L + tt * 512:kc * L + (tt + 1) * 512],
                                 rhs_res=lambda kc: [("A", kc)], evac=ev))
        self.run_dense(jobs)

    def load_w(self, w2d, k0, KC, col0):
        S = self.S
        sl = self.wslot
        self.wslot = (self.wslot + 1) % self.NSLOT
        src = w2d[k0:k0 + KC * 128, col0:col0 + 128].rearrange("(kc p) c -> p kc c", p=128)
        S.dma("pool", self.WBF(sl)[:, 0:KC * 128].rearrange("p (kc c) -> p kc c", c=128), src, writes=self.wres(sl))
        return sl

    def load_rope(self):
        S = self.S
        S.dma("sp", self.FB(), self.rope_d[:, 0:L], writes=[("FB", t) for t in range(4)])
        S.dma("sp", self.FC(), self.rope_d[:, L:2 * L], writes=[("FC", t) for t in range(4)])

    def qk_unit(self, sl, hf, gcol):
        S = self.S
        u = hf
        c0, c1 = hf * 1024, (hf + 1) * 1024
        FAR = [("FA", 2 * hf), ("FA", 2 * hf + 1)]
        FBR = [("FB", 2 * hf), ("FB", 2 * hf + 1)]
        FCR = [("FC", 2 * hf), ("FC", 2 * hf + 1)]
        rb = self.wres(sl)
        pb = 4 * u
        P = self.PS[:, pb * 512:(pb + 2) * 512]
        P2 = self.PS[:, (pb + 2) * 512:(pb + 4) * 512]
        RP = [("p", pb), ("p", pb + 1)]
        RP2 = [("p", pb + 2), ("p", pb + 3)]
        XG = self.FW[:, c0:c1]
        SQ = self.BIGB[:, u * 1024:(u + 1) * 1024]
        SQR = self.bres(u * 1024, (u + 1) * 1024)

        def st0():
            for kc in range(8):
                for t in range(2):
                    S.op("pe", lambda e, t=t, kc=kc: e.matmul(self.PS[:, (pb + t) * 512:(pb + t + 1) * 512], self.WBF(sl)[:, kc * 128:(kc + 1) * 128],
                                                            self.BIGA[:, kc * L + c0 + t * 512:kc * L + c0 + (t + 1) * 512], start=(kc == 0), stop=(kc == 7)),
                         reads=rb + [("A", kc)], writes=[("p", pb + t)])
            S.op("act", lambda e: e.activation(out=XG, in_=P, func=AF.Copy, scale=gcol), reads=RP + ["VEC"], writes=FAR)
            S.op("act", lambda e: e.activation(out=SQ, in_=P, func=AF.Square), reads=RP, writes=SQR)

        def st1():
            for t in range(2):
                S.op("pe", lambda e, t=t: e.matmul(self.PS[:, (pb + 2 + t) * 512:(pb + 3 + t) * 512], self.onesH(), SQ[:, t * 512:(t + 1) * 512],
                                                 start=True, stop=True), reads=SQR + ["CSTB"], writes=[("p", pb + 2 + t)])
            S.op("act", lambda e: e.activation(out=P2, in_=P2, func=AF.Ln, bias=self.eps_ap(), scale=1.0), reads=RP2 + ["EPS"], writes=RP2)
            S.op("act", lambda e: e.activation(out=P2, in_=P2, func=AF.Exp, scale=-0.5), reads=RP2, writes=RP2)
            pswap = self.cst("pswap")
            for t in range(2):
                S.op("pe", lambda e, t=t: e.matmul(self.PS[:, (pb + t) * 512:(pb + t + 1) * 512], pswap, XG[:, t * 512:(t + 1) * 512], start=True, stop=True),
                     reads=FAR + ["CST"], writes=[("p", pb + t)])
            S.op("dve", lambda e: e.tensor_tensor(XG, XG, self.FW[:, L + c0:L + c1], ALU.mult), reads=FAR + FBR, writes=FAR)
            S.op("dve", lambda e: e.tensor_tensor(P, P, self.FW[:, 2 * L + c0:2 * L + c1], ALU.mult), reads=RP + FCR, writes=RP)
            S.op("dve", lambda e: e.tensor_tensor(XG, XG, P, ALU.add), reads=FAR + RP, writes=FAR)
            S.op("dve", lambda e: e.tensor_tensor(XG, XG, P2, ALU.mult), reads=FAR + RP2, writes=FAR)
        return st0, st1, (XG, FAR, P2, RP2)

    @staticmethod
    def run_staged(units):
        nst = max(len(u) for u in units)
        for it in range(len(units) + nst - 1):
            for k in range(nst - 1, -1, -1):
                i = it - k
                if 0 <= i < len(units) and k < len(units[i]):
                    units[i][k]()

    ATT_SMALL = ["KMEAN", "G0", "G1", "RINV", "NSB"] + [("ACC", i) for i in range(4)] + [("M8", i) for i in range(8)]
    S5_SMALL = ["A1", "A2", "TT1", "PP", "PWRN"] + [("SR", k_) for k_ in range(3)] + ["sm%d" % i_ for i_ in range(7)]
    S5S_ALL = ["s5raw", "s5dt", "s5ar", "s5ai", "s5mag", "s5ang", "s5nt", "s5c9", "s5pwr", "s5pwi", "s5cr", "s5ci"]

    def kv_phase(self):
        S = self.S
        S.op("dve", lambda e: e.memset(self.SMALL[:], 0.0), writes=self.ATT_SMALL + self.S5_SMALL)
        self.rms_rstd(self.FA(), "FA")
        self.make_xn("kv_norm", self.FA(), "FA")
        self.load_rope()
        KMEAN = self.SMALL[:, 0:64]
        KB = lambda k: self.BIGB[:, 2048 + k * 2048:2048 + (k + 1) * 2048]
        KBR = lambda k: self.bres(2048 + k * 2048, 2048 + (k + 1) * 2048)
        VHo = lambda k: 6144 + k * 3072
        VH = lambda k: self.BIGB[:, VHo(k):VHo(k) + 2064]
        VHR = lambda k: self.bres(VHo(k), VHo(k) + 2064)
        for k in range(2):
            S.op("pool", lambda e, k=k: e.memset(VH(k).rearrange("p (t c) -> p t c", c=129)[:, :, 128:129], 1.0), writes=VHR(k))
        slk = {}
        slv = {}
        kunits = []
        for hd in range(NH):
            kp = hd % 2
            for hf in range(2):
                hold = {}

                def st0(hd=hd, hf=hf, hold=hold):
                    if hf == 0:
                        if hd == 0:
                            slk[0] = self.load_w(self.w_kv_d, 0, 8, 0)
                            slv[0] = self.load_w(self.w_kv_d, 0, 8, D)
                        if hd + 1 < NH:
                            slk[hd + 1] = self.load_w(self.w_kv_d, 0, 8, (hd + 1) * 128)
                            slv[hd + 1] = self.load_w(self.w_kv_d, 0, 8, D + (hd + 1) * 128)
                    a, b, h = self.qk_unit(slk[hd], hf, self.vcol("k_norm"))
                    hold["st1"] = b
                    hold["h"] = h
                    a()

                def st1(hold=hold):
                    hold["st1"]()

                def st2(hd=hd, hf=hf, kp=kp, hold=hold):
                    XG, FAR, P2, RP2 = hold["h"]
                    S.op("dve", lambda e: e.reduce_sum(out=KMEAN[:, hd * 8 + hf * 4:hd * 8 + hf * 4 + 4],
                                                      in_=XG.rearrange("p (n t) -> p n t", t=256), axis=AX.X),
                         reads=FAR, writes=["KMEAN"])
                    S.op("act", lambda e: e.activation(out=KB(kp)[:, hf * 1024:(hf + 1) * 1024], in_=XG, func=AF.Copy),
                         reads=FAR, writes=self.bres(2048 + kp * 2048 + hf * 1024, 2048 + kp * 2048 + (hf + 1) * 1024))
                    if hf == 1:
                        S.dma("sp", self.kT_s[hd], KB(kp), reads=KBR(kp), writes=[("kT_s", hd)])
                kunits.append([st0, st1, st2])
            for hf in range(2):
                def vunit(hd=hd, hf=hf, kp=kp):
                    sl = slv[hd]
                    rb = self.wres(sl)
                    VH3 = VH(kp).rearrange("p (t c) -> p t c", c=129)
                    pb = 4 * hf
                    for t8 in range(8):
                        t16 = hf * 8 + t8
                        for kc in range(8):
                            S.op("pe", lambda e, t16=t16, t8=t8, kc=kc: e.matmul(
                                self.PS[:, pb * 512 + t8 * 128:pb * 512 + (t8 + 1) * 128], self.BIGA[:, kc * L + t16 * 128:kc * L + (t16 + 1) * 128],
                                self.WBF(sl)[:, kc * 128:(kc + 1) * 128], start=(kc == 0), stop=(kc == 7)),
                                reads=rb + [("A", kc)], writes=[("p", pb + t8 // 4)])
                    S.op("act", lambda e: e.activation(out=VH3[:, hf * 8:(hf + 1) * 8, 0:128],
                                                       in_=self.PS[:, pb * 512:(pb + 2) * 512].rearrange("p (t c) -> p t c", c=128), func=AF.Copy),
                         reads=[("p", pb), ("p", pb + 1)], writes=VHR(kp))
                    if hf == 1:
                        S.dma("sp", self.V_s[hd], VH(kp), reads=VHR(kp), writes=[("V_s", hd)])
                kunits.append([vunit])
        self.run_staged(kunits)
        S.op("dve", lambda e: e.tensor_scalar(KMEAN, KMEAN, 1.0 / 256, None, ALU.mult), reads=["KMEAN"], writes=["KMEAN"])

    def moba_layer(self, j):
        S = self.S
        self.rms_rstd(self.FA(), "FA")
        self.make_xn(("b_norm", j), self.FA(), "FA")
        self.load_rope()
        KMEAN = self.SMALL[:, 0:64]
        GT = lambda hf: self.SMALL[:, 64 + 64 * hf:128 + 64 * hf]
        M8 = lambda i: self.SMALL[:, 320 + 8 * i:328 + 8 * i]
        RINV = self.SMALL[:, 192:196]
        ACC = lambda i: self.SMALL[:, 384 + 129 * i:384 + 129 * (i + 1)]
        SELALL = self.S5S[:, 0:1024]
        futmask = self.cst("futmask")
        scale = 1.0 / math.sqrt(128.0)
        S.op("dve", lambda e: e.memset(SELALL, 0.0), writes=self.S5S_ALL + ["SELALL"])
        QB = lambda k: self.BIGB[:, 2048 + k * 2048:2048 + (k + 1) * 2048]
        QBR = lambda k: self.bres(2048 + k * 2048, 2048 + (k + 1) * 2048)
        slq = {}
        qunits = []
        ownm = self.cst("ownmask")
        NSB = self.SMALL[0:8, 384:896].bitcast(BF16)
        for hd in range(NH):
            kp = hd % 2
            for hf in range(2):
                def st_load(hd=hd, hf=hf):
                    if hf == 0:
                        if hd == 0:
                            slq[0] = self.load_w(self.w_q_d[j], 0, 8, 0)
                        if hd + 1 < NH:
                            slq[hd + 1] = self.load_w(self.w_q_d[j], 0, 8, (hd + 1) * 128)
                hold = {}

                def st0(hd=hd, hf=hf, hold=hold, st_load=st_load):
                    st_load()
                    a, b, h = self.qk_unit(slq[hd], hf, self.vcol(("q_norm", j)))
                    hold["st1"] = b
                    hold["h"] = h
                    a()

                def st1(hold=hold):
                    hold["st1"]()

                def st2(hd=hd, hf=hf, kp=kp, hold=hold):
                    XG, FAR, P2, RP2 = hold["h"]
                    S.op("act", lambda e: e.activation(out=QB(kp)[:, hf * 1024:(hf + 1) * 1024], in_=XG, func=AF.Copy),
                         reads=FAR, writes=self.bres(2048 + kp * 2048 + hf * 1024, 2048 + kp * 2048 + (hf + 1) * 1024))
                    for q8 in range(8):
                        S.op("pe", lambda e, q8=q8: e.matmul(P2[:, q8 * 8:(q8 + 1) * 8], XG[:, q8 * 128:(q8 + 1) * 128],
                                                           KMEAN[:, hd * 8:(hd + 1) * 8], start=True, stop=True),
                             reads=FAR + ["KMEAN"], writes=[RP2[0]])
                    G = GT(hf)
                    gres = "G%d" % hf
                    S.op("dve", lambda e: e.tensor_tensor(G, P2[:, 0:64], futmask[:, hf * 64:(hf + 1) * 64], ALU.add),
                         reads=[RP2[0], "CST"], writes=[gres])
                    selh = SELALL[:, hd * 128 + hf * 64:hd * 128 + (hf + 1) * 64]
                    if hf == 0:
                        S.op("dve", lambda e: e.tensor_scalar(selh[:, 16:64], G[:, 16:64], -1e29, None, ALU.is_gt),
                             reads=[gres], writes=["SELALL"])
                    else:
                        for q8 in range(8):
                            S.op("dve", lambda e, q8=q8: e.max(out=M8(q8), in_=G[:, q8 * 8:(q8 + 1) * 8]), reads=[gres], writes=[("M8", q8)])
                            S.op("dve", lambda e, q8=q8: e.tensor_scalar(selh[:, q8 * 8:(q8 + 1) * 8], G[:, q8 * 8:(q8 + 1) * 8],
                                                                        M8(q8)[:, 2:3], None, ALU.is_ge),
                                 reads=[gres, ("M8", q8)], writes=["SELALL"])
                    S.op("dve", lambda e: e.tensor_tensor(selh, selh, ownm[:, hf * 64:(hf + 1) * 64], ALU.add),
                         reads=["SELALL", "CST"], writes=["SELALL"])

                def st3(hd=hd, hf=hf, kp=kp, hold=hold):
                    XG, FAR, P2, RP2 = hold["h"]
                    selh = SELALL[:, hd * 128 + hf * 64:hd * 128 + (hf + 1) * 64]
                    for q8 in range(8):
                        S.op("pe", lambda e, q8=q8: e.transpose(P2[0:8, q8 * 128:(q8 + 1) * 128], selh[:, q8 * 8:(q8 + 1) * 8], self.cst("ident")),
                             reads=["SELALL", "CST"], writes=[RP2[q8 // 4]])
                    S.op("dve", lambda e: e.tensor_scalar(NSB, P2[0:8, 0:1024], -1.0, 30000.0, ALU.add, ALU.mult), reads=RP2, writes=["NSB"])
                    S.dma("sp", self.NS_s[hd][:, hf * 1024:(hf + 1) * 1024], NSB, reads=["NSB"], writes=[("NS_s", hd, hf)])
                    if hf == 1:
                        S.dma("sp", self.Q_s[hd], QB(kp), reads=QBR(kp), writes=[("Q_s", hd)])
                qunits.append([st0, st1, st2, st3])
        self.run_staged(qunits)

        QBF = lambda k: self.BIGB[:, k * 2048:(k + 1) * 2048]
        KTH = lambda k: self.BIGB[:, 4096 + k * 2048:4096 + (k + 1) * 2048]
        VHo = lambda k: 8192 + k * 2064
        VH = lambda k: self.BIGB[:, VHo(k):VHo(k) + 2064]
        ET = lambda i: self.BIGB[:, 12320 + 512 * i:12320 + 512 * (i + 1)]
        ETR = lambda i: self.bres(12320 + 512 * i, 12320 + 512 * (i + 1))
        OTH = self.BIGB[:, 14368:14368 + L]
        OTHR = self.bres(14368, 14368 + L)
        RQ = lambda k: [("cQ", k)]
        RK = lambda k: [("cK", k)]
        RV = lambda k: [("cV", k)]
        OPR = lambda par, i: self.PS[:, (2 + 2 * par + i // 2) * 512 + (i % 2) * 256:(2 + 2 * par + i // 2) * 512 + (i % 2) * 256 + 129]
        opres = lambda par, i: ("p", 2 + 2 * par + i // 2)

        def load_head(hd):
            k = hd % 2
            S.dma("sp", QBF(k), self.Q_s[hd], reads=[("Q_s", hd)], writes=RQ(k))
            S.dma("sp", KTH(k), self.kT_s[hd], reads=[("kT_s", hd)], writes=RK(k))
            S.dma("sp", VH(k), self.V_s[hd], reads=[("V_s", hd)], writes=RV(k))

        DEPTH = 2
        NET = 6
        ET = lambda i: self.BIGB[:, 12320 + 512 * i:12320 + 512 * (i + 1)]
        ETR = lambda i: [("cE", i)]
        OTH = self.TMP2[:].bitcast(BF16)
        OTHR = ["TMP2"]
        NEGSEL = lambda k: self.FW[0:8, k * 1024:(k + 1) * 1024].bitcast(BF16)
        NSR = lambda k: [("FA", 2 * k), ("FA", 2 * k + 1)]
        RB = lambda qp: self.FW[:, L + qp * 512:L + (qp + 1) * 512]
        RBR = lambda qp: [("FB", qp)]
        EONE = self.S5S[0:8, 1024:1536].bitcast(BF16)
        identf = self.cst("ident")
        CORE_NAMES = [("cQ", 0), ("cQ", 1), ("cK", 0), ("cK", 1), ("cV", 0), ("cV", 1)] + [("cE", i) for i in range(NET)]
        S.op("dve", lambda e: e.tensor_copy(EONE.rearrange("p (n c) -> p n c", c=128), identf[0:8, 0:8].unsqueeze(2).broadcast_to([8, 8, 128])),
             reads=["CST"], writes=["EONE", "TMP"] + self.bres(0, 16512) + CORE_NAMES)
        state = dict(st=0, et=0)
        tasks = []

        def load_head(hd):
            k = hd % 2
            S.dma("sp", QBF(k), self.Q_s[hd], reads=[("Q_s", hd)], writes=RQ(k))
            S.dma("sp", KTH(k), self.kT_s[hd], reads=[("kT_s", hd)], writes=RK(k))
            S.dma("sp", VH(k), self.V_s[hd], reads=[("V_s", hd)], writes=RV(k))
            S.dma("sp", NEGSEL(k), self.NS_s[hd], reads=[("NS_s", hd, 0), ("NS_s", hd, 1)], writes=NSR(k))

        def mk_block(hd, Q, n, qp):
            hk = hd % 2
            VH3 = VH(hk).rearrange("p (t c) -> p t c", c=129)
            info = {}
            lastkt = 4 * Q + 3

            def s1():
                ets = {}
                for kt in (2 * n, 2 * n + 1):
                    c0 = max(4 * Q, kt) - 4 * Q
                    if c0 > 3:
                        continue
                    sb_ = state["st"] % 4
                    state["st"] += 1
                    eb_ = state["et"] % NET
                    state["et"] += 1
                    ST = self.PS[:, sb_ * 512:(sb_ + 1) * 512]
                    diag = kt >= 4 * Q
                    selm = n < 2 * Q + 1
                    S.op("pe", lambda e, ST=ST, c0=c0, fin=(not diag and not selm), kk_=KTH(hk)[:, kt * 128:(kt + 1) * 128],
                         qq_=QBF(hk)[:, Q * 512 + c0 * 128:(Q + 1) * 512]: e.matmul(
                        ST[:, c0 * 128:512], kk_, qq_, start=True, stop=fin), reads=RQ(hk) + RK(hk), writes=[("p", sb_)])
                    if selm:
                        S.op("pe", lambda e, ST=ST, c0=c0, fin=(not diag), en=EONE[:, n * 128:(n + 1) * 128],
                             ns=NEGSEL(hk)[:, Q * 512 + c0 * 128:(Q + 1) * 512]: e.matmul(ST[:, c0 * 128:512], en, ns, start=False, stop=fin),
                             reads=["EONE"] + NSR(hk), writes=[("p", sb_)])
                    if diag:
                        S.op("pe", lambda e, ST=ST, c0=c0: e.matmul(ST[:, c0 * 128:(c0 + 1) * 128], self.identB(), self.cmaskB(),
                                                                   start=False, stop=True), reads=["CSTB"], writes=[("p", sb_)])
                    et = ET(eb_)
                    S.op("act", lambda e, ST=ST, et=et, c0=c0: e.activation(out=et[:, c0 * 128:512], in_=ST[:, c0 * 128:512], func=AF.Exp, scale=scale),
                         reads=[("p", sb_)], writes=ETR(eb_))
                    ets[kt] = (et, eb_, c0)
                info["ets"] = ets

            def s2():
                if Q == 0 and n == 0 and hd + 1 < NH:
                    load_head(hd + 1)
                for kt in sorted(info["ets"]):
                    et, eb_, c0 = info["ets"][kt]
                    ob, rbk = 4 + qp, 6 + qp
                    S.op("pe", lambda e, et=et, c0=c0, vv=VH3[:, kt, 0:128], a=(kt == 0), z=(kt == lastkt), ob=ob: e.matmul(
                        self.PS[:, ob * 512 + c0 * 128:(ob + 1) * 512], vv, et[:, c0 * 128:512], start=a, stop=z),
                        reads=ETR(eb_) + RV(hk), writes=[("p", ob)])
                    S.op("pe", lambda e, et=et, c0=c0, a=(kt == 0), z=(kt == lastkt), rbk=rbk: e.matmul(
                        self.PS[:, rbk * 512 + c0 * 128:(rbk + 1) * 512], self.onesH(), et[:, c0 * 128:512], start=a, stop=z),
                        reads=ETR(eb_) + ["CSTB"], writes=[("p", rbk)])
            return s1, s2

        def mk_qend(hd, Q, qp):
            def s1():
                pass

            def s2():
                ob, rbk = 4 + qp, 6 + qp
                R = RB(qp)
                S.op("act", lambda e: e.activation(out=R, in_=self.PS[:, rbk * 512:(rbk + 1) * 512], func=AF.Ln, scale=128.0),
                     reads=[("p", rbk)], writes=RBR(qp))
                S.op("act", lambda e: e.activation(out=R, in_=R, func=AF.Exp, scale=-1.0), reads=RBR(qp), writes=RBR(qp))
                S.op("dve", lambda e: e.tensor_tensor(OTH[:, Q * 512:(Q + 1) * 512], self.PS[:, ob * 512:(ob + 1) * 512], R, ALU.mult),
                     reads=[("p", ob)] + RBR(qp), writes=OTHR)
                if Q == 3:
                    S.dma("sp", self.OT_s[hd], OTH, reads=OTHR, writes=[("OT_s", hd)])
            return s1, s2

        load_head(0)
        qcount = 0
        for hd in range(NH):
            for Q in range(4):
                qp = qcount % 2
                qcount += 1
                for n in range(2 * Q + 2):
                    tasks.append(mk_block(hd, Q, n, qp))
                tasks.append(mk_qend(hd, Q, qp))
        for idx in range(len(tasks) + DEPTH):
            if idx < len(tasks):
                tasks[idx][0]()
            if idx - DEPTH >= 0:
                tasks[idx - DEPTH][1]()
        S.op("dve", lambda e: e.memset(self.SMALL[:, 192:200], 1.0), reads=["EONE"], writes=["TMP"] + self.bres(0, 16512) + CORE_NAMES)
        for hd in range(NH):
            S.dma("sp", self.BIGA[:, hd * L:(hd + 1) * L], self.OT_s[hd], reads=[("OT_s", hd)], writes=[("A", hd)])
        jobs = []
        for oc in range(NFC):
            def evac_o(ps, psres, oc=oc):
                S.op("dve", lambda e: e.tensor_tensor(self.HT[:, oc * L:(oc + 1) * L], self.HT[:, oc * L:(oc + 1) * L], ps, ALU.add),
                     reads=psres + [("H", oc)], writes=[("H", oc)])
            jobs.append(dict(w2d=self.w_o_d[j], k0=0, KC=8, col0=oc * 128,
                             rhs=lambda kc, tt: self.BIGA[:, kc * L + tt * 512:kc * L + (tt + 1) * 512],
                             rhs_res=lambda kc: [("A", kc)], evac=evac_o))
        self.run_dense(jobs)

    def build(self):
        with ExitStack() as es:
            self.declare(es)
            self.EPS_T = es.enter_context(self.nc.sbuf_tensor("EPS_T", [128, 2], F32))
            self.S.op("pool", lambda e: e.memset(self.EPS_T[:], EPS), writes=["EPS"])
            self.prologue()
            self.body()
            self.epilogue()
            self.S.emit(self.sems, self.dsems)
        return self.nc

    def body(self):
        st = self.stop
        if st == "ffn_only":
            self.ffn(0)
            return
        self.s5_layer(0)
        if st == "mix0":
            return
        self.ffn_hook = lambda: self.s5_layer(1, "params")
        self.ffn(0)
        if st == "ffn0":
            return
        self.s5_layer(1, "main")
        if st == "mix1":
            return
        self.ffn(1)
        if st == "ffn1":
            return
        self.kv_phase()
        self.moba_layer(0)
        if st == "mix2":
            return
        self.ffn(2)
        if st == "ffn2":
            return
        self.moba_layer(1)
        if st == "mix3":
            return
        self.ffn(3)


def _consts():
    c = np.zeros((128, NCST), np.float32)
    p = np.arange(128)
    c[:, CC["ident"]:CC["ident"] + 128] = np.eye(128, dtype=np.float32)
    sw = np.zeros((128, 128), np.float32)
    sw[(p + 64) % 128, p] = 1.0
    c[:, CC["pswap"]:CC["pswap"] + 128] = sw
    c[:, CC["maskQ"]:CC["maskQ"] + 128] = (p[:, None] // 32 == p[None, :] // 32).astype(np.float32)
    c[:, CC["onesD"]:CC["onesD"] + 128] = 1.0 / D
    c[:, CC["onesH"]:CC["onesH"] + 128] = 1.0 / 128
    c[:, CC["cmask"]:CC["cmask"] + 128] = np.where(p[:, None] > p[None, :], NEG, 0.0)
    fm = np.zeros((16, 8), np.float32)
    for qt in range(16):
        fm[qt, (qt // 2):] = -1e30
    c[:, CC["futmask"]:CC["futmask"] + 128] = fm.reshape(1, 128)
    om = np.zeros((16, 8), np.float32)
    for qt in range(16):
        om[qt, qt // 2] = 1.0
    c[:, CC["ownmask"]:CC["ownmask"] + 128] = om.reshape(1, 128)
    rm = np.zeros((128, 4), np.float32)
    for v in range(2):
        rm[:, v] = ((p // 32) % 2 == v)
        rm[:, 2 + v] = -rm[:, v]
    c[:, CC["rowmask"]:CC["rowmask"] + 4] = rm
    c[:, CC["kk"]:CC["kk"] + 9] = np.arange(9, dtype=np.float32)[None, :]
    return c


def _rope():
    half = 64
    inv = (10000.0 ** (-np.arange(half, dtype=np.float32) * 2.0 / 128)).astype(np.float32)
    ang = np.arange(L, dtype=np.float32)[:, None] * inv[None, :]
    cos, sin = np.cos(ang).T.astype(np.float32), np.sin(ang).T.astype(np.float32)
    r = np.zeros((128, 2 * L), np.float32)
    r[0:64, 0:L] = cos; r[64:128, 0:L] = cos
    r[0:64, L:] = -sin; r[64:128, L:] = sin
    return r


def _fm(v):
    return np.ascontiguousarray(v.reshape(-1, 128).T)


def _host_prep(inp):
    vec = np.zeros((128, NV), np.float32)
    for l in range(2):
        vec[:, VC[("a_norm", l)]:VC[("a_norm", l)] + 8] = _fm(inp["a_norm"][l])
        vec[:, VC[("s5_D", l)]:VC[("s5_D", l)] + 8] = _fm(inp["s5_D"][l])
        vec[:, VC[("b_glu", l)]:VC[("b_glu", l)] + 16] = _fm(inp["b_glu"][l])
    vec[:, VC["kv_norm"]:VC["kv_norm"] + 8] = _fm(inp["kv_norm"])
    vec[:, VC["k_norm"]] = inp["k_norm"]
    for j in range(2):
        vec[:, VC[("b_norm", j)]:VC[("b_norm", j)] + 8] = _fm(inp["b_norm"][j])
        vec[:, VC[("q_norm", j)]] = inp["q_norm"][j]
    for l in range(4):
        vec[:, VC[("ffn_norm", l)]:VC[("ffn_norm", l)] + 8] = _fm(inp["ffn_norm"][l])
        cw = inp["conv_w"][l].reshape(3, NUPC, 128).transpose(2, 0, 1).reshape(128, 3 * NUPC)
        vec[:, VC[("conv_w", l)]:VC[("conv_w", l)] + 3 * NUPC] = cw
        vec[:, VC[("conv_b", l)]:VC[("conv_b", l)] + NUPC] = _fm(inp["conv_b"][l])
    s5p = np.zeros((2, 128, 96 + 4096), np.float32)
    for l in range(2):
        s5p[l, :, 0:32] = inp["s5_A_re"][l].reshape(32, 128).T
        s5p[l, :, 32:64] = inp["s5_A_im"][l].reshape(32, 128).T
        s5p[l, :, 64:96] = np.repeat(inp["s5_log_dt"][l].reshape(32, 2), 64, axis=1).T
        for nm, off in (("s5_B_re", 0), ("s5_B_im", 1024)):
            Bm = np.zeros((2, 64, 32, 2, 16), np.float32)
            Bg = inp[nm][l].reshape(32, 2, 64, 16)
            for m in range(2):
                Bm[m, :, :, m, :] = Bg[:, m].transpose(1, 0, 2)
            s5p[l, :, 96 + off:96 + off + 1024] = Bm.reshape(128, 1024)
        for nm, off in (("s5_C_re", 2048), ("s5_C_im", 3072)):
            Cm = np.zeros((2, 64, 32, 2, 16), np.float32)
            Cg = inp[nm][l].reshape(32, 2, 16, 64)
            for m in range(2):
                Cm[m, :, :, m, :] = Cg[:, m].transpose(2, 0, 1)
            s5p[l, :, 96 + off:96 + off + 1024] = Cm.reshape(128, 1024)
    common = dict(vec=vec, cst=_consts(), rope=_rope(), s5p=s5p,
                  w_glu=np.ascontiguousarray(inp["w_glu"]), w_kv=np.ascontiguousarray(inp["w_kv"]),
                  w_q=np.ascontiguousarray(inp["w_q"]), w_o=np.ascontiguousarray(inp["w_o"]),
                  w_up=np.ascontiguousarray(inp["w_up"]), w_down=np.ascontiguousarray(inp["w_down"]))
    return common


def _x_to_dev(xb):
    return np.ascontiguousarray(xb.T.reshape(NFC, 128, L).transpose(1, 0, 2).reshape(128, NFC * L))


def _dev_to_x(o):
    return np.ascontiguousarray(o.reshape(128, NFC, L).transpose(1, 0, 2).reshape(D, L).T)


def run(inp, stop=None, ncores=8):
    inp = {k: np.asarray(v) for k, v in inp.items()}
    common = _host_prep(inp)
    b = Builder(stop=stop)
    nc = b.build()
    in_maps = []
    for c in range(ncores):
        m = dict(common)
        m["xT"] = _x_to_dev(inp["x"][c])
        in_maps.append(m)
    res = run_bass_kernel_spmd(nc, in_maps, core_ids=list(range(ncores)))
    return np.stack([_dev_to_x(res.results[c]["outT"]) for c in range(ncores)], axis=0)


def kernel(**inputs):
    return run(inputs, stop=None, ncores=8).astype(np.float32)
```

```python
import math
from contextlib import ExitStack

import numpy as np
import concourse.bass as bass
import concourse.mybir as mybir
from concourse.bass_utils import run_bass_kernel_spmd

F32 = mybir.dt.float32
BF16 = mybir.dt.bfloat16
ALU = mybir.AluOpType
AF = mybir.ActivationFunctionType
AX = mybir.AxisListType

L = 2048
D = 1024
NFC = 8
DFF = 2816
NUPC = 44
NACT = 22
NH = 8
EPS = 1e-6
MAGIC = 12582912.0
TWO_PI = 2.0 * math.pi
NEG = -30000.0
ENGS = ("pe", "act", "dve", "pool", "sp")

VC = {}
_c = 0
for _l in range(2):
    VC[("a_norm", _l)] = _c; _c += 8
    VC[("s5_D", _l)] = _c; _c += 8
    VC[("b_glu", _l)] = _c; _c += 16
VC["kv_norm"] = _c; _c += 8
VC["k_norm"] = _c; _c += 1
for _j in range(2):
    VC[("b_norm", _j)] = _c; _c += 8
    VC[("q_norm", _j)] = _c; _c += 1
for _l in range(4):
    VC[("ffn_norm", _l)] = _c; _c += 8
    VC[("conv_w", _l)] = _c; _c += 3 * NUPC
    VC[("conv_b", _l)] = _c; _c += NUPC
NV = _c
CC = {}
_c = 0
for _n, _w in (("ident", 128), ("pswap", 128), ("maskQ", 128), ("onesD", 128), ("onesH", 128),
               ("cmask", 128), ("futmask", 128), ("rowmask", 4), ("kk", 9), ("ownmask", 128), ("kk64", 64)):
    CC[_n] = _c; _c += _w
NCST = _c


class _Op:
    __slots__ = ("eng", "fn", "deps", "dma", "sig", "cnt", "sem", "semval", "semprev", "idx")


class Sched:
    def __init__(self, nc):
        self.nc = nc
        self.ops = []
        self.res = {}

    def op(self, eng, fn, reads=(), writes=(), dma=False):
        o = _Op()
        o.eng = eng; o.fn = fn; o.dma = dma; o.sig = False; o.idx = len(self.ops)
        deps = set()
        for r in reads:
            st = self.res.get(r)
            if st is not None and st[0] is not None:
                deps.add(st[0])
        for w in writes:
            st = self.res.get(w)
            if st is not None:
                if st[0] is not None:
                    deps.add(st[0])
                deps.update(st[1])
        for r in reads:
            st = self.res.setdefault(r, [None, []])
            st[1].append(o.idx)
        for w in writes:
            self.res[w] = [o.idx, []]
        deps.discard(o.idx)
        o.deps = deps
        self.ops.append(o)
        return o.idx

    def dma(self, eng, out, in_, reads=(), writes=(), **kw):
        return self.op(eng, lambda e: e.dma_start(out=out, in_=in_, **kw), reads, writes, dma=True)

    def emit(self, sems, dma_sems):
        ops = self.ops
        for o in ops:
            latest = {}
            ddeps = []
            for d in o.deps:
                od = ops[d]
                if od.dma:
                    ddeps.append(d)
                elif od.eng != o.eng or o.dma or o.eng != "pe":
                    if od.eng not in latest or latest[od.eng] < d:
                        latest[od.eng] = d
            o.deps = (latest, ddeps)
            for d in latest.values():
                ops[d].sig = True
        cnt = {e: 0 for e in ENGS}
        for o in ops:
            if o.dma:
                continue
            if o.sig:
                cnt[o.eng] += 1
            o.cnt = cnt[o.eng]
        semcount = [0] * len(dma_sems)
        k = 0
        for o in ops:
            if o.dma:
                o.sem = k % len(dma_sems)
                o.semprev = semcount[o.sem]
                semcount[o.sem] += 16
                o.semval = semcount[o.sem]
                k += 1
        per_eng = {e: [o for o in ops if o.eng == e] for e in ENGS}

        def run_engine(ename, eng):
            waited = {e: 0 for e in ENGS}
            dwaited = [0] * len(dma_sems)
            for o in per_eng[ename]:
                latest, ddeps = o.deps
                for d in sorted(ddeps):
                    od = ops[d]
                    if dwaited[od.sem] < od.semval:
                        eng.wait_ge(dma_sems[od.sem], od.semval)
                        dwaited[od.sem] = od.semval
                for en_, d in latest.items():
                    od = ops[d]
                    if waited[en_] < od.cnt:
                        eng.wait_ge(sems[en_], od.cnt)
                        waited[en_] = od.cnt
                if o.dma:
                    if o.semprev > 0 and dwaited[o.sem] < o.semprev:
                        eng.wait_ge(dma_sems[o.sem], o.semprev)
                        dwaited[o.sem] = o.semprev
                    o.fn(eng).then_inc(dma_sems[o.sem], 16)
                else:
                    ins = o.fn(eng)
                    if o.sig:
                        ins.then_inc(sems[ename], 1)
            last = {}
            for o in per_eng[ename]:
                if o.dma:
                    last[o.sem] = max(last.get(o.sem, 0), o.semval)
            for s_, v in last.items():
                if dwaited[s_] < v:
                    eng.wait_ge(dma_sems[s_], v)

        with self.nc.Block() as block:
            @block.tensor
            def _(e):
                run_engine("pe", e)

            @block.scalar
            def _(e):
                run_engine("act", e)

            @block.vector
            def _(e):
                run_engine("dve", e)

            @block.gpsimd
            def _(e):
                run_engine("pool", e)

            @block.sync
            def _(e):
                run_engine("sp", e)


class Builder:
    def __init__(self, stop=None):
        self.stop = stop
        self.nc = bass.Bass("TRN2", target_bir_lowering=False)
        self.S = Sched(self.nc)
        self.wslot = 0
        self.pshalf = 0

    def declare(self, es):
        nc = self.nc
        di = lambda n, s, dt=F32: nc.dram_tensor(n, s, dt, kind="ExternalInput").ap()
        self.xT_d = di("xT", [128, NFC * L])
        self.vec_d = di("vec", [128, NV])
        self.cst_d = di("cst", [128, NCST])
        self.rope_d = di("rope", [128, 2 * L])
        self.s5p_d = di("s5p", [2, 128, 96 + 4096])
        self.w_glu_d = di("w_glu", [2, D, 2 * D])
        self.w_kv_d = di("w_kv", [D, 2 * D])
        self.w_q_d = di("w_q", [2, D, D])
        self.w_o_d = di("w_o", [2, D, D])
        self.w_up_d = di("w_up", [4, D, 2 * DFF])
        self.w_down_d = di("w_down", [4, DFF, D])
        self.out_d = nc.dram_tensor("outT", [128, NFC * L], F32, kind="ExternalOutput").ap()
        self.kT_s = nc.dram_tensor("kT_s", [NH, 128, L], BF16, kind="Internal").ap()
        self.V_s = nc.dram_tensor("V_s", [NH, 128, 16 * 129], BF16, kind="Internal").ap()
        self.OT_s = nc.dram_tensor("OT_s", [NH, 128, L], BF16, kind="Internal").ap()
        self.Q_s = nc.dram_tensor("Q_s", [NH, 128, L], BF16, kind="Internal").ap()
        self.NS_s = nc.dram_tensor("NS_s", [NH, 8, L], BF16, kind="Internal").ap()

        sb = lambda n, s, dt=F32: es.enter_context(nc.sbuf_tensor(n, s, dt))
        self.HT = sb("HT", [128, NFC * L])
        self.BIGA = sb("BIGA", [128, NFC * L], BF16)
        self.BIGB = sb("BIGB", [128, 16512], BF16)
        self.FW = sb("FW", [128, 3 * L])
        self.TMP = sb("TMP", [128, 1024])
        self.TMP2 = sb("TMP2", [128, 1024])
        self.UNI = sb("UNI", [128, 6656])
        self.VEC = sb("VEC", [128, NV])
        self.CST = sb("CST", [128, NCST])
        self.CSTB = sb("CSTB", [128, 4 * 128], BF16)
        self.S5S = sb("S5S", [128, 2048])
        self.SMALL = sb("SMALL", [128, 1024])
        self.PS = es.enter_context(nc.psum_tensor("PS", [128, 4096], F32))
        self.sems = {e: es.enter_context(nc.semaphore("s_" + e)) for e in ENGS}
        self.dsems = [es.enter_context(nc.semaphore("d%d" % i)) for i in range(40)]

    def FA(self):
        return self.FW[:, 0:L]

    def FB(self):
        return self.FW[:, L:2 * L]

    def FC(self):
        return self.FW[:, 2 * L:3 * L]

    def cst(self, name, w=128):
        c = CC[name]
        return self.CST[:, c:c + w]

    def vcol(self, key, i=0):
        c = VC[key] + i
        return self.VEC[:, c:c + 1]

    NSLOT = 8

    def WBF(self, i):
        return self.UNI[:, i * 512:(i + 1) * 512].bitcast(BF16)

    @staticmethod
    def wres(i):
        return ["h%d" % i]

    @staticmethod
    def bres(lo, hi):
        return [("B", k) for k in range(lo // 1024, (hi - 1) // 1024 + 1)]

    def prologue(self):
        S = self.S
        S.dma("sp", self.CST[:], self.cst_d, writes=["CST"])
        S.dma("sp", self.VEC[:], self.vec_d, writes=["VEC"])
        for fc in range(NFC):
            S.dma("act" if fc % 2 else "sp", self.HT[:, fc * L:(fc + 1) * L], self.xT_d[:, fc * L:(fc + 1) * L],
                  writes=[("H", fc)])
        for i, n in enumerate(("ident", "onesD", "onesH", "cmask")):
            src = self.cst(n)
            dst = self.CSTB[:, i * 128:(i + 1) * 128]
            S.op("dve", lambda e, d=dst, s=src: e.tensor_copy(d, s), reads=["CST"], writes=["CSTB"])

    def identB(self):
        return self.CSTB[:, 0:128]

    def onesD(self):
        return self.CSTB[:, 128:256]

    def onesH(self):
        return self.CSTB[:, 256:384]

    def cmaskB(self):
        return self.CSTB[:, 384:512]

    def epilogue(self):
        S = self.S
        for fc in range(NFC):
            S.dma("act" if fc % 2 else "sp", self.out_d[:, fc * L:(fc + 1) * L], self.HT[:, fc * L:(fc + 1) * L],
                  reads=[("H", fc)], writes=[("OUT", fc)])

    def rms_rstd(self, dst, dst_res):
        S = self.S

        def stA(tt):
            sq = self.BIGB[:, (tt % 2) * 4096:(tt % 2 + 1) * 4096]
            sqres = self.bres((tt % 2) * 4096, (tt % 2 + 1) * 4096)
            hin = self.HT[:].rearrange("p (f t) -> p f t", t=L)[:, :, tt * 512:(tt + 1) * 512]
            S.op("act", lambda e, o=sq, i=hin: e.activation(out=o.rearrange("p (f t) -> p f t", t=512), in_=i, func=AF.Square),
                 reads=[("H", fc) for fc in range(NFC)], writes=sqres)

        def stB(tt):
            sq = self.BIGB[:, (tt % 2) * 4096:(tt % 2 + 1) * 4096]
            sqres = self.bres((tt % 2) * 4096, (tt % 2 + 1) * 4096)
            bank = 4 + (tt % 2)
            ps = self.PS[:, bank * 512:(bank + 1) * 512]
            for fc in range(NFC):
                S.op("pe", lambda e, o=ps, r=sq[:, fc * 512:(fc + 1) * 512], a=(fc == 0), z=(fc == NFC - 1):
                     e.matmul(o, self.onesD(), r, start=a, stop=z),
                     reads=sqres + ["CSTB"], writes=[("p", bank)])
            d = dst[:, tt * 512:(tt + 1) * 512]
            S.op("act", lambda e, o=d, i=ps: e.activation(out=o, in_=i, func=AF.Ln, bias=self.eps_ap(), scale=1.0),
                 reads=[("p", bank), "EPS"], writes=[(dst_res, tt)])
            S.op("act", lambda e, o=d: e.activation(out=o, in_=o, func=AF.Exp, scale=-0.5), reads=[(dst_res, tt)], writes=[(dst_res, tt)])

        stA(0)
        stA(1)
        for tt in range(4):
            stB(tt)
            if tt + 2 < 4:
                stA(tt + 2)

    def eps_ap(self):
        return self.EPS_T[:, 0:1]

    def make_xn(self, gkey, rstd, rstd_res):
        S = self.S
        for tt in range(4):
            for fc in range(NFC):
                S.op("dve", lambda e, fc=fc, tt=tt: e.scalar_tensor_tensor(
                    out=self.BIGA[:, fc * L + tt * 512:fc * L + (tt + 1) * 512], in0=self.HT[:, fc * L + tt * 512:fc * L + (tt + 1) * 512],
                    scalar=self.vcol(gkey, fc), in1=rstd[:, tt * 512:(tt + 1) * 512], op0=ALU.mult, op1=ALU.mult),
                    reads=[("H", fc), "VEC", (rstd_res, tt)], writes=[("A", fc)])

    def run_dense(self, jobs):
        S = self.S
        n = len(jobs)
        slots = {}
        ahead = self.NSLOT - 2

        def load(i):
            if i >= n:
                return
            jb = jobs[i]
            slots[i] = self.load_w(jb["w2d"], jb["k0"], jb["KC"], jb["col0"])

        for i in range(min(ahead, n)):
            load(i)
        for i in range(n):
            load(i + ahead)
            jb = jobs[i]
            sl = slots[i]
            KC = jb["KC"]
            half = self.pshalf
            self.pshalf ^= 1
            rb = self.wres(sl)
            for kc in range(KC):
                for tt in range(4):
                    bank = half * 4 + tt
                    o = self.PS[:, bank * 512:(bank + 1) * 512]
                    S.op("pe", lambda e, o=o, w=self.WBF(sl)[:, kc * 128:(kc + 1) * 128], r=jb["rhs"](kc, tt), a=(kc == 0), z=(kc == KC - 1):
                         e.matmul(o, w, r, start=a, stop=z),
                         reads=rb + jb["rhs_res"](kc), writes=[("p", bank)])
            jb["evac"](self.PS[:, half * 2048:(half + 1) * 2048], [("p", half * 4 + t) for t in range(4)])

    def ffn(self, l):
        S = self.S
        self.rms_rstd(self.FA(), "FA")
        self.make_xn(("ffn_norm", l), self.FA(), "FA")
        if getattr(self, "ffn_hook", None):
            self.ffn_hook()
            self.ffn_hook = None
        groups = [(0, 6), (6, 12), (12, 17), (17, 22)]
        cw = VC[("conv_w", l)]
        cb = VC[("conv_b", l)]
        SG = self.BIGB[:, 12288:12288 + L]
        FBR = [("FB", t) for t in range(4)]
        FCR = [("FC", t) for t in range(4)]

        pending = []

        def conv_to(acc, accres, ps, psres, c):
            w = lambda k: self.VEC[:, cw + k * NUPC + c:cw + k * NUPC + c + 1]
            S.op("act", lambda e: e.activation(out=acc, in_=ps, func=AF.Identity, scale=w(2), bias=self.VEC[:, cb + c:cb + c + 1]),
                 reads=psres + ["VEC"], writes=accres)
            while pending:
                pending.pop(0)()
            S.op("dve", lambda e: e.scalar_tensor_tensor(out=acc[:, 1:L], in0=ps[:, 0:L - 1], scalar=w(1), in1=acc[:, 1:L],
                                                         op0=ALU.mult, op1=ALU.add),
                 reads=psres + ["VEC"] + accres, writes=accres)
            S.op("dve", lambda e: e.scalar_tensor_tensor(out=acc[:, 2:L], in0=ps[:, 0:L - 2], scalar=w(0), in1=acc[:, 2:L],
                                                         op0=ALU.mult, op1=ALU.add),
                 reads=psres + ["VEC"] + accres, writes=accres)

        ups, downs = [], []
        for (g0, g1) in groups:
            jobs = []
            for i in range(g0, g1):
                il = i - g0

                def evac_gate(ps, psres, i=i):
                    conv_to(self.FB(), FBR, ps, psres, i)
                    pending.append(lambda: S.op("act", lambda e: e.activation(out=SG, in_=self.FB(), func=AF.Silu), reads=FBR, writes=self.bres(12288, 14336)))

                def evac_val(ps, psres, i=i, il=il):
                    conv_to(self.FC(), FCR, ps, psres, NACT + i)
                    S.op("dve", lambda e: e.tensor_tensor(self.BIGB[:, il * L:(il + 1) * L], SG, self.FC(), ALU.mult),
                         reads=self.bres(12288, 14336) + FCR, writes=self.bres(il * L, (il + 1) * L))

                for col0, ev in ((i * 128, evac_gate), (DFF + i * 128, evac_val)):
                    jobs.append(dict(w2d=self.w_up_d[l], k0=0, KC=8, col0=col0,
                                     rhs=lambda kc, tt: self.BIGA[:, kc * L + tt * 512:kc * L + (tt + 1) * 512],
                                     rhs_res=lambda kc: [("A", kc)], evac=ev))
            ng = g1 - g0
            ups.append(jobs)
            jobs = []
            for oc in range(NFC):
                def evac_down(ps, psres, oc=oc):
                    S.op("dve", lambda e: e.tensor_tensor(self.HT[:, oc * L:(oc + 1) * L], self.HT[:, oc * L:(oc + 1) * L], ps, ALU.add),
                         reads=psres + [("H", oc)], writes=[("H", oc)])
                jobs.append(dict(w2d=self.w_down_d[l], k0=g0 * 128, KC=ng, col0=oc * 128,
                                 rhs=lambda kc, tt: self.BIGB[:, kc * L + tt * 512:kc * L + (tt + 1) * 512],
                                 rhs_res=lambda kc: self.bres(kc * L, (kc + 1) * L), evac=evac_down))
            downs.append(jobs)
        alljobs = list(ups[0])
        for g in range(len(groups)):
            if g + 1 < len(groups):
                alljobs.append(ups[g + 1][0])
            alljobs.extend(downs[g])
            if g + 1 < len(groups):
                alljobs.extend(ups[g + 1][1:])
        self.run_dense(alljobs)
        while pending:
            pending.pop(0)()

    def s5_layer(self, l, part="all"):
        S = self.S
        P = self.S5S
        skip = {"on": part == "main"}
        sl = lambda a, b: P[:, a:b]
        Are, Aim, ldt = sl(0, 32), sl(32, 64), sl(64, 96)
        dt, ar, ai = sl(96, 128), sl(128, 160), sl(160, 192)
        MAG, ANG, NT, C9, PWR, PWI = sl(192, 480), sl(480, 768), sl(768, 1056), sl(1056, 1344), sl(1344, 1632), sl(1632, 1920)
        cr, ci = sl(1920, 1952), sl(1952, 1984)
        sm = lambda i: self.SMALL[:, 640 + 32 * i:640 + 32 * (i + 1)]
        k3 = lambda ap: ap.rearrange("p (k g) -> p k g", g=32)
        kk = self.cst("kk", 9)
        kkb = kk.unsqueeze(2).broadcast_to([128, 9, 32])

        def dv(name, fn, reads, writes, eng="dve"):
            if skip["on"]:
                return
            S.op(eng, fn, reads=reads, writes=writes)

        if part != "main":
            S.dma("sp", P[:, 0:96], self.s5p_d[l][:, 0:96], writes=["s5raw"])
        FBR = [("FB", t) for t in range(4)]
        FCR = [("FC", t) for t in range(4)]
        if part != "params":
            S.dma("sp", self.FW[:, L:3 * L], self.s5p_d[l][:, 96:96 + 4096], writes=FBR + FCR)
        BMR, BMI = self.FW[:, L:L + 1024], self.FW[:, L + 1024:2 * L]
        CMR, CMI = self.FW[:, 2 * L:2 * L + 1024], self.FW[:, 2 * L + 1024:3 * L]
        g3 = lambda ap: ap.rearrange("p (g c) -> p g c", c=32)
        dv("dt", lambda e: e.activation(out=dt, in_=ldt, func=AF.Exp), ["s5raw"], ["s5dt"], "act")
        dv("ar", lambda e: e.tensor_tensor(ar, Are, dt, ALU.mult), ["s5raw", "s5dt"], ["s5ar"])
        dv("ai", lambda e: e.tensor_tensor(ai, Aim, dt, ALU.mult), ["s5raw", "s5dt"], ["s5ai"])
        dv("mag", lambda e: e.tensor_tensor(k3(MAG), ar.unsqueeze(1).broadcast_to([128, 9, 32]), kkb, ALU.mult), ["s5ar", "CST"], ["s5mag"])
        dv("mage", lambda e: e.activation(out=MAG, in_=MAG, func=AF.Exp), ["s5mag"], ["s5mag"], "act")
        dv("ang", lambda e: e.tensor_tensor(k3(ANG), ai.unsqueeze(1).broadcast_to([128, 9, 32]), kkb, ALU.mult), ["s5ai", "CST"], ["s5ang"])

        def sin_of(src, srcres, tmp, tmpres):
            dv("n1", lambda e: e.tensor_scalar(tmp, src, 1.0 / TWO_PI, MAGIC, ALU.mult, ALU.add), [srcres], [tmpres])
            dv("n2", lambda e: e.tensor_scalar(tmp, tmp, MAGIC, None, ALU.subtract), [tmpres], [tmpres])
            dv("n3", lambda e: e.scalar_tensor_tensor(out=tmp, in0=tmp, scalar=-TWO_PI, in1=src, op0=ALU.mult, op1=ALU.add), [tmpres, srcres], [tmpres])
            dv("n4", lambda e: e.activation(out=tmp, in_=tmp, func=AF.Sin), [tmpres], [tmpres], "act")

        sin_of(ANG, "s5ang", NT, "s5nt")
        dv("pwi", lambda e: e.tensor_tensor(PWI, MAG, NT, ALU.mult), ["s5mag", "s5nt"], ["s5pwi"])
        dv("angc", lambda e: e.tensor_scalar(C9, ANG, math.pi / 2, None, ALU.add), ["s5ang"], ["s5c9"])
        sin_of(C9, "s5c9", NT, "s5nt")
        dv("pwr", lambda e: e.tensor_tensor(PWR, MAG, NT, ALU.mult), ["s5mag", "s5nt"], ["s5pwr"])
        nr, den, x1, x2, rden, y1, y2 = sm(0), sm(1), sm(2), sm(3), sm(4), sm(5), sm(6)
        pw1r, pw1i = PWR[:, 32:64], PWI[:, 32:64]
        dv("nr", lambda e: e.tensor_scalar(nr, pw1r, -1.0, None, ALU.add), ["s5pwr"], ["sm0"])
        dv("den", lambda e: e.tensor_tensor(den, Are, Are, ALU.mult), ["s5raw"], ["sm1"])
        dv("den2", lambda e: e.tensor_tensor(x1, Aim, Aim, ALU.mult), ["s5raw"], ["sm2"])
        dv("den3", lambda e: e.tensor_tensor(den, den, x1, ALU.add), ["sm1", "sm2"], ["sm1"])
        dv("rden", lambda e: e.reciprocal(rden, den), ["sm1"], ["sm4"])
        dv("x1", lambda e: e.tensor_tensor(x1, nr, Are, ALU.mult), ["sm0", "s5raw"], ["sm2"])
        dv("x2", lambda e: e.tensor_tensor(x2, pw1i, Aim, ALU.mult), ["s5pwi", "s5raw"], ["sm3"])
        dv("x3", lambda e: e.tensor_tensor(x1, x1, x2, ALU.add), ["sm2", "sm3"], ["sm2"])
        dv("cr", lambda e: e.tensor_tensor(cr, x1, rden, ALU.mult), ["sm2", "sm4"], ["s5cr"])
        dv("y1", lambda e: e.tensor_tensor(y1, pw1i, Are, ALU.mult), ["s5pwi", "s5raw"], ["sm5"])
        dv("y2", lambda e: e.tensor_tensor(y2, nr, Aim, ALU.mult), ["sm0", "s5raw"], ["sm6"])
        dv("y3", lambda e: e.tensor_tensor(y1, y1, y2, ALU.subtract), ["sm5", "sm6"], ["sm5"])
        dv("ci", lambda e: e.tensor_tensor(ci, y1, rden, ALU.mult), ["sm5", "sm4"], ["s5ci"])
        skip["on"] = False
        if part == "params":
            return
        crb = cr.unsqueeze(2).broadcast_to([128, 32, 32])
        cib = ci.unsqueeze(2).broadcast_to([128, 32, 32])
        T1, T2 = self.TMP[:], self.TMP2[:]
        R_BMR, R_BMI = [("FB", 0), ("FB", 1)], [("FB", 2), ("FB", 3)]
        R_CMR, R_CMI = [("FC", 0), ("FC", 1)], [("FC", 2), ("FC", 3)]
        dv("b1", lambda e: e.tensor_tensor(g3(T1), cib, g3(BMR), ALU.mult), ["s5ci"] + R_BMR, ["TMP"])
        dv("b2", lambda e: e.tensor_tensor(g3(T2), cib, g3(BMI), ALU.mult), ["s5ci"] + R_BMI, ["TMP2"])
        dv("b3", lambda e: e.tensor_tensor(g3(BMR), crb, g3(BMR), ALU.mult), ["s5cr"] + R_BMR, R_BMR)
        dv("b4", lambda e: e.tensor_tensor(BMR, BMR, T2, ALU.subtract), R_BMR + ["TMP2"], R_BMR)
        dv("b5", lambda e: e.tensor_tensor(g3(BMI), crb, g3(BMI), ALU.mult), ["s5cr"] + R_BMI, R_BMI)
        dv("b6", lambda e: e.tensor_tensor(BMI, BMI, T1, ALU.add), R_BMI + ["TMP"], R_BMI)

        FAR = [("FA", t) for t in range(4)]
        self.rms_rstd(self.FA(), "FA")
        for tt in range(4):
            for fc in range(NFC):
                S.op("dve", lambda e, fc=fc, tt=tt: e.scalar_tensor_tensor(
                    out=self.BIGA[:, fc * L:(fc + 1) * L].rearrange("p (j c) -> p j c", c=256)[:, :, tt * 64:(tt + 1) * 64],
                    in0=self.HT[:, fc * L + tt * 512:fc * L + (tt + 1) * 512].rearrange("p (c j) -> p j c", j=8),
                    scalar=self.vcol(("a_norm", l), fc),
                    in1=self.FA()[:, tt * 512:(tt + 1) * 512].rearrange("p (c j) -> p j c", j=8), op0=ALU.mult, op1=ALU.mult),
                    reads=[("H", fc), "VEC", ("FA", tt)], writes=[("A", fc)])
        B2S4 = self.BIGB[:, 0:16448].rearrange("p (r g c) -> p r g c", r=2, g=32)
        ALLC = [("b2S", c) for c in range(257)]
        BALL = self.bres(0, 16512)
        S.op("pool", lambda e: e.memset(B2S4[:, :, :, 0:1], 0.0), reads=[], writes=BALL + [("b2S", 0)])
        Zre, ZimN = self.FW[:, 0:1024], self.FW[:, 1024:2048]
        z4 = lambda ap: ap.rearrange("p (t q c) -> p t q c", t=8, q=4)
        R_ZR, R_ZI = [("FA", 0), ("FA", 1)], [("FA", 2), ("FA", 3)]
        PWR3, PWI3 = k3(PWR), k3(PWI)
        ZB = {0: (Zre, ZimN, R_ZR, R_ZI),
              1: (self.UNI[:, 2048:3072], self.UNI[:, 3072:4096], ["h4", "h5"], ["h6", "h7"])}

        PWRN = self.SMALL[:, 576:864]
        dv("pwrn", lambda e: e.tensor_scalar(PWRN, PWR, -1.0, None, ALU.mult), ["s5pwr"], self.S5_SMALL)
        PWRN3 = k3(PWRN)

        def zcompute(fc, zb=0):
            Zre, ZimN, R_ZR, R_ZI = ZB[zb]
            pwr_b = PWR3[:, 0:8, 4 * fc:4 * fc + 4].unsqueeze(3).broadcast_to([128, 8, 4, 32])
            pwrn_b = PWRN3[:, 0:8, 4 * fc:4 * fc + 4].unsqueeze(3).broadcast_to([128, 8, 4, 32])
            pwi_b = PWI3[:, 0:8, 4 * fc:4 * fc + 4].unsqueeze(3).broadcast_to([128, 8, 4, 32])
            bmr_b = g3(BMR)[:, 4 * fc:4 * fc + 4, :].unsqueeze(1).broadcast_to([128, 8, 4, 32])
            bmi_b = g3(BMI)[:, 4 * fc:4 * fc + 4, :].unsqueeze(1).broadcast_to([128, 8, 4, 32])
            E = "pool"
            dv("z1", lambda e: e.tensor_tensor(z4(Zre), pwr_b, bmr_b, ALU.mult), ["s5pwr"] + R_BMR, R_ZR, E)
            dv("z2", lambda e: e.tensor_tensor(z4(T1), pwi_b, bmi_b, ALU.mult), ["s5pwi"] + R_BMI, ["TMP"], E)
            dv("z3", lambda e: e.tensor_tensor(Zre, Zre, T1, ALU.subtract), R_ZR + ["TMP"], R_ZR, E)
            E = "pool"
            dv("z4", lambda e: e.tensor_tensor(z4(ZimN), pwrn_b, bmi_b, ALU.mult), ["PWRN"] + R_BMI, R_ZI, E)
            dv("z5", lambda e: e.tensor_tensor(z4(T2), pwi_b, bmr_b, ALU.mult), ["s5pwi"] + R_BMR, ["TMP2"], E)
            dv("z6", lambda e: e.tensor_tensor(ZimN, ZimN, T2, ALU.subtract), R_ZI + ["TMP2"], R_ZI, E)

        def zcompute_dve(fc):
            Zre, ZimN, R_ZR, R_ZI = ZB[0]
            pwr_b = PWR3[:, 0:8, 4 * fc:4 * fc + 4].unsqueeze(3).broadcast_to([128, 8, 4, 32])
            pwi_b = PWI3[:, 0:8, 4 * fc:4 * fc + 4].unsqueeze(3).broadcast_to([128, 8, 4, 32])
            bmr_b = g3(BMR)[:, 4 * fc:4 * fc + 4, :].unsqueeze(1).broadcast_to([128, 8, 4, 32])
            bmi_b = g3(BMI)[:, 4 * fc:4 * fc + 4, :].unsqueeze(1).broadcast_to([128, 8, 4, 32])
            PT = self.PS[:, 3072:4096]
            RPT = [("p", 6), ("p", 7)]
            dv("z1", lambda e: e.tensor_tensor(z4(Zre), pwr_b, bmr_b, ALU.mult), ["s5pwr"] + R_BMR, R_ZR, "pool")
            dv("z2", lambda e: e.tensor_tensor(z4(T1), pwi_b, bmi_b, ALU.mult), ["s5pwi"] + R_BMI, ["TMP"], "pool")
            dv("z3", lambda e: e.tensor_tensor(Zre, Zre, T1, ALU.subtract), R_ZR + ["TMP"], R_ZR, "pool")
            dv("z4", lambda e: e.tensor_tensor(z4(ZimN), pwr_b, bmi_b, ALU.mult), ["s5pwr"] + R_BMI, R_ZI)
            dv("z5", lambda e: e.tensor_tensor(z4(PT), pwi_b, bmr_b, ALU.mult), ["s5pwi"] + R_BMR, RPT)
            dv("z6", lambda e: e.scalar_tensor_tensor(out=ZimN, in0=ZimN, scalar=-1.0, in1=PT, op0=ALU.mult, op1=ALU.subtract),
               R_ZI + RPT, R_ZI)

        W2B = {0: (self.UNI[:, 0:2048].bitcast(BF16).rearrange("p (v t r c) -> p v t r c", v=2, t=8, r=2), ["h0", "h1", "h2", "h3"]),
               1: (self.UNI[:, 4096:6144].bitcast(BF16).rearrange("p (v t r c) -> p v t r c", v=2, t=8, r=2), ["h8", "h9", "h10", "h11"])}
        ident = self.cst("ident")
        rowmask = self.cst("rowmask", 4)

        def stageT(fc):
            zb = fc % 2
            zcompute(fc, zb)
            Zre_, ZimN_, RZR_, RZI_ = ZB[zb]
            W2v, R_W2 = W2B[zb]
            for ri, Zs, ZR in ((0, Zre_, RZR_), (1, ZimN_, RZI_)):
                for tq in range(2):
                    bank = ri * 2 + tq
                    for i in range(4):
                        t = tq * 4 + i
                        S.op("pe", lambda e, o=self.PS[:, bank * 512 + i * 128:bank * 512 + (i + 1) * 128], a=Zs[:, t * 128:(t + 1) * 128]:
                             e.transpose(o, a, ident), reads=ZR + ["CST"], writes=[("p", bank)])
                    for v in range(2):
                        S.op("dve", lambda e, v=v, tq=tq, ri=ri, bank=bank, W2v=W2v: e.tensor_scalar(
                            W2v[:, v, tq * 4:(tq + 1) * 4, ri, :],
                            self.PS[:, bank * 512:(bank + 1) * 512].rearrange("p (t c) -> p t c", c=128),
                            rowmask[:, 2 * ri + v:2 * ri + v + 1], None, ALU.mult),
                            reads=[("p", bank), "CST"], writes=R_W2)

        def stageM(fc):
            W2v, R_W2 = W2B[fc % 2]
            for q in range(4):
                hq, v = q // 2, q % 2
                for ri in range(2):
                    off = 2048 + (q * 2 + ri) * 256
                    bank = off // 512
                    for j in range(8):
                        S.op("pe", lambda e, o=self.PS[:, off:off + 256], w=W2v[64 * hq:64 * hq + 64, v, 7 - j, ri, :],
                             r=self.BIGA[64 * hq:64 * hq + 64, fc * L + j * 256:fc * L + (j + 1) * 256], a=(j == 0), z=(j == 7):
                             e.matmul(o, w, r, start=a, stop=z), reads=R_W2 + [("A", fc)], writes=[("p", bank)])
            S.op("act", lambda e, fc=fc: e.activation(
                out=B2S4[:, :, 4 * fc:4 * fc + 4, 1:257],
                in_=self.PS[:, 2048:4096].rearrange("p (q r c) -> p r q c", q=4, r=2), func=AF.Copy),
                reads=[("p", b) for b in range(4, 8)], writes=BALL + ALLC[1:])

        stageT(0)
        for fc in range(NFC):
            if fc + 1 < NFC:
                stageT(fc + 1)
            stageM(fc)

        UA = self.UNI[:]
        NSEG, SEGL = 4, 64
        RING = lambda k, sg: self.UNI[:, k * 512 + sg * 128:k * 512 + (sg + 1) * 128]
        PP4 = lambda sg: self.UNI[:, 1536 + sg * 128:1536 + (sg + 1) * 128]
        TT4 = lambda sg: self.UNI[:, 2048 + sg * 64:2048 + (sg + 1) * 64]
        PW64R, PW64I = self.UNI[:, 2304:4352], self.UNI[:, 4352:6400]
        gi = lambda ap: ap.rearrange("p (g i) -> p g i", i=64)
        UNI_H = ["h%d" % i_ for i_ in range(13)]
        SCN = [("R4", k_, s_) for k_ in range(3) for s_ in range(4)] + [("PP4", s_) for s_ in range(4)] + [("TT4", s_) for s_ in range(4)] + ["PWR64", "PWI64"]
        A12 = self.SMALL[:, 0:128]
        A1, A2 = self.SMALL[:, 0:64], self.SMALL[:, 64:128]
        l8r, l8i = PWR[:, 256:288], PWI[:, 256:288]
        dv("a1a", lambda e: e.tensor_copy(A1[:, 0:32], l8r), ["s5pwr"], ["A1"])
        dv("a1b", lambda e: e.tensor_copy(A1[:, 32:64], l8r), ["s5pwr"], ["A1"])
        dv("a2a", lambda e: e.tensor_scalar(A2[:, 0:32], l8i, -1.0, None, ALU.mult), ["s5pwi"], ["A2"])
        dv("a2b", lambda e: e.tensor_copy(A2[:, 32:64], l8i), ["s5pwi"], ["A2"])
        dv("sr0", lambda e: e.memset(self.UNI[:, 0:512], 0.0), [], UNI_H + SCN)
        h4 = lambda ap: ap.rearrange("p (h j g) -> p h j g", h=2, j=2)
        j3 = lambda ap: ap.rearrange("p (j g) -> p j g", j=2)
        kk64 = self.cst("kk64", 64)
        kkb64 = kk64.unsqueeze(1).broadcast_to([128, 32, 64])
        FAt = self.FA()
        FAR_ = [("FA", t_) for t_ in range(4)]
        PA, PB = self.PS[:, 0:2048], self.PS[:, 2048:4096]
        RPA, RPB = [("p", b_) for b_ in range(4)], [("p", b_) for b_ in range(4, 8)]
        dv("t1", lambda e: e.tensor_tensor(gi(FAt), ar.unsqueeze(2).broadcast_to([128, 32, 64]), kkb64, ALU.mult), ["s5ar", "CST"], FAR_)
        dv("t2", lambda e: e.activation(out=FAt, in_=FAt, func=AF.Exp), FAR_, FAR_, "act")
        dv("t3", lambda e: e.tensor_tensor(gi(PW64I), ai.unsqueeze(2).broadcast_to([128, 32, 64]), kkb64, ALU.mult), ["s5ai", "CST"], ["PWI64"])

        def sin64(dst, dres):
            dv("u1", lambda e: e.tensor_scalar(dst, PW64I, 1.0 / TWO_PI, MAGIC, ALU.mult, ALU.add), ["PWI64"], dres)
            dv("u2", lambda e: e.tensor_scalar(dst, dst, MAGIC, None, ALU.subtract), dres, dres)
            dv("u3", lambda e: e.scalar_tensor_tensor(out=dst, in0=dst, scalar=-TWO_PI, in1=PW64I, op0=ALU.mult, op1=ALU.add), dres + ["PWI64"], dres)
            dv("u4", lambda e: e.activation(out=dst, in_=dst, func=AF.Sin), dres, dres, "act")

        sin64(PA, RPA)
        dv("t4", lambda e: e.tensor_scalar(PW64I, PW64I, math.pi / 2, None, ALU.add), ["PWI64"] + RPA, ["PWI64"])
        sin64(PB, RPB)
        dv("t5", lambda e: e.tensor_tensor(PW64R, FAt, PB, ALU.mult), FAR_ + RPB, ["PWR64"])
        dv("t6", lambda e: e.tensor_tensor(PW64I, FAt, PA, ALU.mult), FAR_ + RPA + RPB, ["PWI64"])

        for t in range(SEGL):
            k0, k1 = t % 3, (t + 1) % 3
            for sg in range(NSEG):
                wv = bass.AP(UA.tensor, UA.offset + k0 * 512 + sg * 128, [list(UA.ap[0]), [32, 2], [32, 2], [1, 32]])
                dv("s12", lambda e, wv=wv, sg=sg: e.tensor_tensor(h4(PP4(sg)), h4(A12), wv, ALU.mult), ["A1", "A2", ("R4", k0, sg)], [("PP4", sg)])
            for sg in range(NSEG):
                dv("s3", lambda e, sg=sg: e.tensor_tensor(j3(TT4(sg)), h4(PP4(sg))[:, 0], h4(PP4(sg))[:, 1], ALU.add), [("PP4", sg)], [("TT4", sg)])
            for sg in range(NSEG):
                col = SEGL * sg + t + 1
                dv("s4", lambda e, sg=sg, col=col, k1=k1: e.tensor_tensor(h4(RING(k1, sg)), j3(TT4(sg)).unsqueeze(1).broadcast_to([128, 2, 2, 32]),
                                                                       B2S4[:, :, :, col].unsqueeze(1).broadcast_to([128, 2, 2, 32]), ALU.add),
                   [("TT4", sg), ("b2S", col)], [("R4", k1, sg)])
            cols = [SEGL * sg + t + 1 for sg in range(NSEG)]
            ring_v = self.UNI[:, k1 * 512:(k1 + 1) * 512].rearrange("p (s b) -> p s b", b=128)[:, :, 0:64].rearrange("p s (r g) -> p r g s", r=2)
            dv("s5", lambda e, t=t, ring_v=ring_v: e.tensor_copy(B2S4[:, :, :, t + 1:t + 1 + SEGL * (NSEG - 1) + 1:SEGL], ring_v),
               [("R4", k1, sg) for sg in range(NSEG)], [("b2S", c_) for c_ in cols], "pool")
        kf = SEGL % 3
        A64 = self.SMALL[:, 576:704]
        PP64, TT64 = self.SMALL[:, 704:832], self.SMALL[:, 832:896]
        p64r, p64i = gi(PW64R)[:, :, 63], gi(PW64I)[:, :, 63]
        dv("b1", lambda e: e.tensor_copy(A64[:, 0:32], p64r), ["PWR64"], ["PWRN"])
        dv("b2", lambda e: e.tensor_copy(A64[:, 32:64], p64r), ["PWR64"], ["PWRN"])
        dv("b3", lambda e: e.tensor_scalar(A64[:, 64:96], p64i, -1.0, None, ALU.mult), ["PWI64"], ["PWRN"])
        dv("b4", lambda e: e.tensor_copy(A64[:, 96:128], p64i), ["PWI64"], ["PWRN"])
        kc_ = (kf + 1) % 3
        dv("c1", lambda e: e.tensor_copy(RING(kc_, 1), RING(kf, 0)), [("R4", kf, 0)], [("R4", kc_, 1)])
        for sg in (2, 3):
            wv = bass.AP(UA.tensor, UA.offset + kc_ * 512 + (sg - 1) * 128, [list(UA.ap[0]), [32, 2], [32, 2], [1, 32]])
            dv("c2", lambda e, wv=wv: e.tensor_tensor(h4(PP64), h4(A64), wv, ALU.mult), ["PWRN", ("R4", kc_, sg - 1)], ["PWRN"])
            dv("c3", lambda e: e.tensor_tensor(j3(TT64), h4(PP64)[:, 0], h4(PP64)[:, 1], ALU.add), ["PWRN"], ["PWRN"])
            dv("c4", lambda e, sg=sg: e.tensor_tensor(h4(RING(kc_, sg)), j3(TT64).unsqueeze(1).broadcast_to([128, 2, 2, 32]), h4(RING(kf, sg - 1)), ALU.add),
               ["PWRN", ("R4", kf, sg - 1)], [("R4", kc_, sg)])
        for sg in (1, 2, 3):
            car = RING(kc_, sg)
            cr_b = car[:, 0:32].unsqueeze(2).broadcast_to([128, 32, 64])
            ci_b = car[:, 32:64].unsqueeze(2).broadcast_to([128, 32, 64])
            c0_ = SEGL * sg + 1
            cn = [("b2S", c_) for c_ in range(c0_, c0_ + SEGL)]
            rc = [("R4", kc_, sg)]
            o_re = B2S4[:, 0, :, c0_:c0_ + SEGL]
            o_im = B2S4[:, 1, :, c0_:c0_ + SEGL]
            dv("f1", lambda e, cr_b=cr_b: e.tensor_tensor(gi(FAt), gi(PW64R), cr_b, ALU.mult), ["PWR64"] + rc, FAR_)
            dv("f2", lambda e, ci_b=ci_b: e.tensor_tensor(gi(PA), gi(PW64I), ci_b, ALU.mult), ["PWI64"] + rc, RPA)
            dv("f3", lambda e: e.tensor_tensor(FAt, FAt, PA, ALU.subtract), FAR_ + RPA, FAR_)
            dv("f4", lambda e, o_re=o_re: e.tensor_tensor(o_re, o_re, gi(FAt), ALU.add), FAR_ + cn, cn)
            dv("f5", lambda e, ci_b=ci_b: e.tensor_tensor(gi(FAt), gi(PW64R), ci_b, ALU.mult), ["PWR64"] + rc, FAR_)
            dv("f6", lambda e, cr_b=cr_b: e.tensor_tensor(gi(PB), gi(PW64I), cr_b, ALU.mult), ["PWI64"] + rc, RPB)
            dv("f7", lambda e: e.tensor_tensor(FAt, FAt, PB, ALU.add), FAR_ + RPB, FAR_)
            dv("f8", lambda e, o_im=o_im: e.tensor_tensor(o_im, o_im, gi(FAt), ALU.add), FAR_ + cn, cn)

        W3H = {b: self.UNI[:, b * 2048:(b + 1) * 2048].bitcast(BF16).rearrange("p (j q r c) -> p j q r c", j=4, q=4, r=2) for b in range(3)}
        R_W3H = {b: ["h%d" % i_ for i_ in range(4 * b, 4 * b + 4)] for b in range(3)}
        KL = self.UNI[:, 6144:6656].bitcast(BF16).rearrange("p (t c) -> p t c", c=128)
        maskQ = self.cst("maskQ")
        S.op("pool", lambda e: e.memset(self.UNI[:, 0:6144].bitcast(BF16), 0.0), writes=UNI_H + SCN)
        t4 = lambda ap: ap.rearrange("p (j q c) -> p j q c", j=8, q=4)
        wcount = 0
        zcompute_dve(0)
        for fc in range(NFC):
            ypar = 0
            yb = 0
            for tq in range(2):
                bank = 4 + tq
                for i in range(4):
                    t = tq * 4 + i
                    o = self.PS[:, bank * 512 + i * 128:bank * 512 + (i + 1) * 128]
                    S.op("pe", lambda e, o=o, t=t, fc=fc: e.matmul(o, Zre[:, t * 128:(t + 1) * 128], CMR[:, 128 * fc:128 * (fc + 1)], start=True, stop=False),
                         reads=R_ZR + R_CMR, writes=[("p", bank)])
                    S.op("pe", lambda e, o=o, t=t, fc=fc: e.matmul(o, ZimN[:, t * 128:(t + 1) * 128], CMI[:, 128 * fc:128 * (fc + 1)], start=False, stop=True),
                         reads=R_ZI + R_CMI, writes=[("p", bank)])
                S.op("dve", lambda e, tq=tq, bank=bank: e.tensor_tensor(
                    KL[:, tq * 4:(tq + 1) * 4, :], self.PS[:, bank * 512:(bank + 1) * 512].rearrange("p (t c) -> p t c", c=128),
                    maskQ.unsqueeze(1).broadcast_to([128, 4, 128]), ALU.mult), reads=[("p", bank), "CST"], writes=["h12"])
            cmr_b = g3(CMR)[:, 4 * fc:4 * fc + 4, :].unsqueeze(1).broadcast_to([128, 8, 4, 32])
            cmi_b = g3(CMI)[:, 4 * fc:4 * fc + 4, :].unsqueeze(1).broadcast_to([128, 8, 4, 32])
            pr_b = PWR3[:, 1:9, 4 * fc:4 * fc + 4].unsqueeze(3).broadcast_to([128, 8, 4, 32])
            pi_b = PWI3[:, 1:9, 4 * fc:4 * fc + 4].unsqueeze(3).broadcast_to([128, 8, 4, 32])
            wb = [(wcount) % 3, (wcount + 1) % 3]
            wcount += 2
            E = "pool"
            dv("w1", lambda e, a=cmr_b, b=pr_b: e.tensor_tensor(t4(T1), a, b, ALU.mult), R_CMR + ["s5pwr"], ["TMP"], E)
            dv("w2", lambda e, a=cmi_b, b=pi_b: e.tensor_tensor(t4(T2), a, b, ALU.mult), R_CMI + ["s5pwi"], ["TMP2"], E)
            for hb in range(2):
                for q in range(4):
                    dv("w3", lambda e, q=q, hb=hb, wv=W3H[wb[hb]]: e.tensor_tensor(wv[:, :, q, 0, 32 * q:32 * q + 32], t4(T1)[:, 4 * hb:4 * hb + 4, q, :],
                                                                               t4(T2)[:, 4 * hb:4 * hb + 4, q, :], ALU.subtract),
                       ["TMP", "TMP2"], R_W3H[wb[hb]], E)
            PT = self.PS[:, 3072:4096]
            RPT = [("p", 6), ("p", 7)]
            ST2 = self.SMALL[:, 0:1024]
            dv("w4", lambda e, a=cmi_b, b=pr_b: e.tensor_tensor(t4(PT), a, b, ALU.mult), R_CMI + ["s5pwr"], RPT, "dve")
            dv("w5", lambda e, a=cmr_b, b=pi_b: e.tensor_tensor(t4(ST2), a, b, ALU.mult), R_CMR + ["s5pwi"], self.S5_SMALL, "dve")
            dv("w5b", lambda e: e.tensor_tensor(PT, PT, ST2, ALU.add), RPT + self.S5_SMALL, RPT, "dve")
            for hb in range(2):
                for q in range(4):
                    dv("w6", lambda e, q=q, hb=hb, wv=W3H[wb[hb]]: e.activation(out=wv[:, :, q, 1, 32 * q:32 * q + 32], in_=t4(PT)[:, 4 * hb:4 * hb + 4, q, :],
                                                                            func=AF.Copy, scale=-1.0),
                       RPT, R_W3H[wb[hb]], "act")
            if fc + 1 < NFC:
                zcompute_dve(fc + 1)
            for t in range(8):
                for b in range(4):
                    jlo, jhi = max(2 * b, t), 2 * b + 2
                    if jlo >= jhi:
                        continue
                    S.op("pe", lambda e, t=t, jlo=jlo, jhi=jhi, fc=fc, yb=yb: e.matmul(
                        self.PS[:, yb + jlo * 256:yb + jhi * 256], KL[:, t, :],
                        self.BIGA[:, fc * L + (jlo - t) * 256:fc * L + (jhi - t) * 256], start=(t == 0), stop=False),
                        reads=["h12", ("A", fc)], writes=[("p", 4 * ypar + b)])
            for j in range(8):
                hb = j // 4
                for q in range(4):
                    for ri in range(2):
                        S.op("pe", lambda e, j=j, q=q, ri=ri, fc=fc, yb=yb, wv=W3H[wb[hb]]: e.matmul(
                            self.PS[:, yb + j * 256:yb + (j + 1) * 256], wv[:, j % 4, q, ri, :], B2S4[:, ri, 4 * fc + q, 0:256],
                            start=False, stop=(q == 3 and ri == 1)),
                            reads=R_W3H[wb[hb]] + ALLC, writes=[("p", 4 * ypar + j // 2)])
            for hv in range(2):
                yv = self.PS[:, yb + hv * 1024:yb + (hv + 1) * 1024]
                sc = self.SMALL[:, 0:1024]
                ry = [("p", 4 * ypar + 2 * hv), ("p", 4 * ypar + 2 * hv + 1)]
                rs = self.S5_SMALL
                ug = self.BIGA[:, fc * L + hv * 1024:fc * L + (hv + 1) * 1024]
                S.op("dve", lambda e, yv=yv, ug=ug, fc=fc: e.scalar_tensor_tensor(out=yv, in0=ug, scalar=self.vcol(("s5_D", l), fc), in1=yv,
                                                                          op0=ALU.mult, op1=ALU.add), reads=ry + [("A", fc), "VEC"], writes=ry)
                S.op("act", lambda e, yv=yv, ug=ug: e.activation(out=ug, in_=yv, func=AF.Gelu_apprx_tanh), reads=ry, writes=[("A", fc)])

        jobs = []
        for oc in range(NFC):
            def evac_zb(ps, psres, oc=oc):
                S.op("act", lambda e: e.activation(out=self.FA(), in_=ps, func=AF.Sigmoid, bias=self.vcol(("b_glu", l), 8 + oc), scale=1.0),
                     reads=psres + ["VEC"], writes=FAR)

            def evac_za(ps, psres, oc=oc):
                S.op("dve", lambda e: e.scalar_tensor_tensor(out=self.FB(), in0=ps, scalar=self.vcol(("b_glu", l), oc), in1=self.FA(),
                                                             op0=ALU.add, op1=ALU.mult), reads=psres + ["VEC"] + FAR, writes=FBR)
                hv = self.HT[:, oc * L:(oc + 1) * L].rearrange("p (c j) -> p j c", j=8)
                S.op("dve", lambda e: e.tensor_tensor(hv, hv, self.FB().rearrange("p (j c) -> p j c", c=256), ALU.add),
                     reads=FBR + [("H", oc)], writes=[("H", oc)])
            for col0, ev in ((D + oc * 128, evac_zb), (oc * 128, evac_za)):
                jobs.append(dict(w2d=self.w_glu_d[l], k0=0, KC=8, col0=col0,
                                 rhs=lambda kc, tt: self.BIGA[:, kc * L + tt * 512:kc * L + (tt + 1) * 512],
                                 rhs_res=lambda kc: [("A", kc)], evac=ev))
        self.run_dense(jobs)

    def load_w(self, w2d, k0, KC, col0):
        S = self.S
        sl = self.wslot
        self.wslot = (self.wslot + 1) % self.NSLOT
        src = w2d[k0:k0 + KC * 128, col0:col0 + 128].rearrange("(kc p) c -> p kc c", p=128)
        S.dma("pool", self.WBF(sl)[:, 0:KC * 128].rearrange("p (kc c) -> p kc c", c=128), src, writes=self.wres(sl))
        return sl

    def load_rope(self):
        S = self.S
        S.dma("sp", self.FB(), self.rope_d[:, 0:L], writes=[("FB", t) for t in range(4)])
        S.dma("sp", self.FC(), self.rope_d[:, L:2 * L], writes=[("FC", t) for t in range(4)])

    def qk_unit(self, sl, hf, gcol):
        S = self.S
        u = hf
        c0, c1 = hf * 1024, (hf + 1) * 1024
        FAR = [("FA", 2 * hf), ("FA", 2 * hf + 1)]
        FBR = [("FB", 2 * hf), ("FB", 2 * hf + 1)]
        FCR = [("FC", 2 * hf), ("FC", 2 * hf + 1)]
        rb = self.wres(sl)
        pb = 4 * u
        P = self.PS[:, pb * 512:(pb + 2) * 512]
        P2 = self.PS[:, (pb + 2) * 512:(pb + 4) * 512]
        RP = [("p", pb), ("p", pb + 1)]
        RP2 = [("p", pb + 2), ("p", pb + 3)]
        XG = self.FW[:, c0:c1]
        SQ = self.BIGB[:, u * 1024:(u + 1) * 1024]
        SQR = self.bres(u * 1024, (u + 1) * 1024)

        def st0():
            for kc in range(8):
                for t in range(2):
                    S.op("pe", lambda e, t=t, kc=kc: e.matmul(self.PS[:, (pb + t) * 512:(pb + t + 1) * 512], self.WBF(sl)[:, kc * 128:(kc + 1) * 128],
                                                            self.BIGA[:, kc * L + c0 + t * 512:kc * L + c0 + (t + 1) * 512], start=(kc == 0), stop=(kc == 7)),
                         reads=rb + [("A", kc)], writes=[("p", pb + t)])
            S.op("act", lambda e: e.activation(out=XG, in_=P, func=AF.Copy, scale=gcol), reads=RP + ["VEC"], writes=FAR)
            S.op("act", lambda e: e.activation(out=SQ, in_=P, func=AF.Square), reads=RP, writes=SQR)

        def st1():
            for t in range(2):
                S.op("pe", lambda e, t=t: e.matmul(self.PS[:, (pb + 2 + t) * 512:(pb + 3 + t) * 512], self.onesH(), SQ[:, t * 512:(t + 1) * 512],
                                                 start=True, stop=True), reads=SQR + ["CSTB"], writes=[("p", pb + 2 + t)])
            S.op("act", lambda e: e.activation(out=P2, in_=P2, func=AF.Ln, bias=self.eps_ap(), scale=1.0), reads=RP2 + ["EPS"], writes=RP2)
            S.op("act", lambda e: e.activation(out=P2, in_=P2, func=AF.Exp, scale=-0.5), reads=RP2, writes=RP2)
            pswap = self.cst("pswap")
            for t in range(2):
                S.op("pe", lambda e, t=t: e.matmul(self.PS[:, (pb + t) * 512:(pb + t + 1) * 512], pswap, XG[:, t * 512:(t + 1) * 512], start=True, stop=True),
                     reads=FAR + ["CST"], writes=[("p", pb + t)])
            S.op("dve", lambda e: e.tensor_tensor(XG, XG, self.FW[:, L + c0:L + c1], ALU.mult), reads=FAR + FBR, writes=FAR)
            S.op("dve", lambda e: e.tensor_tensor(P, P, self.FW[:, 2 * L + c0:2 * L + c1], ALU.mult), reads=RP + FCR, writes=RP)
            S.op("dve", lambda e: e.tensor_tensor(XG, XG, P, ALU.add), reads=FAR + RP, writes=FAR)
            S.op("dve", lambda e: e.tensor_tensor(XG, XG, P2, ALU.mult), reads=FAR + RP2, writes=FAR)
        return st0, st1, (XG, FAR, P2, RP2)

    @staticmethod
    def run_staged(units):
        nst = max(len(u) for u in units)
        for it in range(len(units) + nst - 1):
            for k in range(nst - 1, -1, -1):
                i = it - k
                if 0 <= i < len(units) and k < len(units[i]):
                    units[i][k]()

    ATT_SMALL = ["KMEAN", "G0", "G1", "RINV", "NSB"] + [("ACC", i) for i in range(4)] + [("M8", i) for i in range(8)]
    S5_SMALL = ["A1", "A2", "TT1", "PP", "PWRN"] + [("SR", k_) for k_ in range(3)] + ["sm%d" % i_ for i_ in range(7)]
    S5S_ALL = ["s5raw", "s5dt", "s5ar", "s5ai", "s5mag", "s5ang", "s5nt", "s5c9", "s5pwr", "s5pwi", "s5cr", "s5ci"]

    def kv_phase(self):
        S = self.S
        S.op("dve", lambda e: e.memset(self.SMALL[:], 0.0), writes=self.ATT_SMALL + self.S5_SMALL)
        self.rms_rstd(self.FA(), "FA")
        self.make_xn("kv_norm", self.FA(), "FA")
        self.load_rope()
        KMEAN = self.SMALL[:, 0:64]
        KB = lambda k: self.BIGB[:, 2048 + k * 2048:2048 + (k + 1) * 2048]
        KBR = lambda k: self.bres(2048 + k * 2048, 2048 + (k + 1) * 2048)
        VHo = lambda k: 6144 + k * 3072
        VH = lambda k: self.BIGB[:, VHo(k):VHo(k) + 2064]
        VHR = lambda k: self.bres(VHo(k), VHo(k) + 2064)
        for k in range(2):
            S.op("pool", lambda e, k=k: e.memset(VH(k).rearrange("p (t c) -> p t c", c=129)[:, :, 128:129], 1.0), writes=VHR(k))
        slk = {}
        slv = {}
        kunits = []
        for hd in range(NH):
            kp = hd % 2
            for hf in range(2):
                hold = {}

                def st0(hd=hd, hf=hf, hold=hold):
                    if hf == 0:
                        if hd == 0:
                            slk[0] = self.load_w(self.w_kv_d, 0, 8, 0)
                            slv[0] = self.load_w(self.w_kv_d, 0, 8, D)
                        if hd + 1 < NH:
                            slk[hd + 1] = self.load_w(self.w_kv_d, 0, 8, (hd + 1) * 128)
                            slv[hd + 1] = self.load_w(self.w_kv_d, 0, 8, D + (hd + 1) * 128)
                    a, b, h = self.qk_unit(slk[hd], hf, self.vcol("k_norm"))
                    hold["st1"] = b
                    hold["h"] = h
                    a()

                def st1(hold=hold):
                    hold["st1"]()

                def st2(hd=hd, hf=hf, kp=kp, hold=hold):
                    XG, FAR, P2, RP2 = hold["h"]
                    S.op("dve", lambda e: e.reduce_sum(out=KMEAN[:, hd * 8 + hf * 4:hd * 8 + hf * 4 + 4],
                                                      in_=XG.rearrange("p (n t) -> p n t", t=256), axis=AX.X),
                         reads=FAR, writes=["KMEAN"])
                    S.op("act", lambda e: e.activation(out=KB(kp)[:, hf * 1024:(hf + 1) * 1024], in_=XG, func=AF.Copy),
                         reads=FAR, writes=self.bres(2048 + kp * 2048 + hf * 1024, 2048 + kp * 2048 + (hf + 1) * 1024))
                    if hf == 1:
                        S.dma("sp", self.kT_s[hd], KB(kp), reads=KBR(kp), writes=[("kT_s", hd)])
                kunits.append([st0, st1, st2])
            for hf in range(2):
                def vunit(hd=hd, hf=hf, kp=kp):
                    sl = slv[hd]
                    rb = self.wres(sl)
                    VH3 = VH(kp).rearrange("p (t c) -> p t c", c=129)
                    pb = 4 * hf
                    for t8 in range(8):
                        t16 = hf * 8 + t8
                        for kc in range(8):
                            S.op("pe", lambda e, t16=t16, t8=t8, kc=kc: e.matmul(
                                self.PS[:, pb * 512 + t8 * 128:pb * 512 + (t8 + 1) * 128], self.BIGA[:, kc * L + t16 * 128:kc * L + (t16 + 1) * 128],
                                self.WBF(sl)[:, kc * 128:(kc + 1) * 128], start=(kc == 0), stop=(kc == 7)),
                                reads=rb + [("A", kc)], writes=[("p", pb + t8 // 4)])
                    S.op("act", lambda e: e.activation(out=VH3[:, hf * 8:(hf + 1) * 8, 0:128],
                                                       in_=self.PS[:, pb * 512:(pb + 2) * 512].rearrange("p (t c) -> p t c", c=128), func=AF.Copy),
                         reads=[("p", pb), ("p", pb + 1)], writes=VHR(kp))
                    if hf == 1:
                        S.dma("sp", self.V_s[hd], VH(kp), reads=VHR(kp), writes=[("V_s", hd)])
                kunits.append([vunit])
        self.run_staged(kunits)
        S.op("dve", lambda e: e.tensor_scalar(KMEAN, KMEAN, 1.0 / 256, None, ALU.mult), reads=["KMEAN"], writes=["KMEAN"])

    def moba_layer(self, j):
        S = self.S
        self.rms_rstd(self.FA(), "FA")
        self.make_xn(("b_norm", j), self.FA(), "FA")
        self.load_rope()
        KMEAN = self.SMALL[:, 0:64]
        GT = lambda hf: self.SMALL[:, 64 + 64 * hf:128 + 64 * hf]
        M8 = lambda i: self.SMALL[:, 320 + 8 * i:328 + 8 * i]
        RINV = self.SMALL[:, 192:196]
        ACC = lambda i: self.SMALL[:, 384 + 129 * i:384 + 129 * (i + 1)]
        SELALL = self.S5S[:, 0:1024]
        futmask = self.cst("futmask")
        scale = 1.0 / math.sqrt(128.0)
        S.op("dve", lambda e: e.memset(SELALL, 0.0), writes=self.S5S_ALL + ["SELALL"])
        QB = lambda k: self.BIGB[:, 2048 + k * 2048:2048 + (k + 1) * 2048]
        QBR = lambda k: self.bres(2048 + k * 2048, 2048 + (k + 1) * 2048)
        slq = {}
        qunits = []
        ownm = self.cst("ownmask")
        NSB = self.SMALL[0:8, 384:896].bitcast(BF16)
        for hd in range(NH):
            kp = hd % 2
            for hf in range(2):
                def st_load(hd=hd, hf=hf):
                    if hf == 0:
                        if hd == 0:
                            slq[0] = self.load_w(self.w_q_d[j], 0, 8, 0)
                        if hd + 1 < NH:
                            slq[hd + 1] = self.load_w(self.w_q_d[j], 0, 8, (hd + 1) * 128)
                hold = {}

                def st0(hd=hd, hf=hf, hold=hold, st_load=st_load):
                    st_load()
                    a, b, h = self.qk_unit(slq[hd], hf, self.vcol(("q_norm", j)))
                    hold["st1"] = b
                    hold["h"] = h
                    a()

                def st1(hold=hold):
                    hold["st1"]()

                def st2(hd=hd, hf=hf, kp=kp, hold=hold):
                    XG, FAR, P2, RP2 = hold["h"]
                    S.op("act", lambda e: e.activation(out=QB(kp)[:, hf * 1024:(hf + 1) * 1024], in_=XG, func=AF.Copy),
                         reads=FAR, writes=self.bres(2048 + kp * 2048 + hf * 1024, 2048 + kp * 2048 + (hf + 1) * 1024))
                    for q8 in range(8):
                        S.op("pe", lambda e, q8=q8: e.matmul(P2[:, q8 * 8:(q8 + 1) * 8], XG[:, q8 * 128:(q8 + 1) * 128],
                                                           KMEAN[:, hd * 8:(hd + 1) * 8], start=True, stop=True),
                             reads=FAR + ["KMEAN"], writes=[RP2[0]])
                    G = GT(hf)
                    gres = "G%d" % hf
                    S.op("dve", lambda e: e.tensor_tensor(G, P2[:, 0:64], futmask[:, hf * 64:(hf + 1) * 64], ALU.add),
                         reads=[RP2[0], "CST"], writes=[gres])
                    selh = SELALL[:, hd * 128 + hf * 64:hd * 128 + (hf + 1) * 64]
                    if hf == 0:
                        S.op("dve", lambda e: e.tensor_scalar(selh[:, 16:64], G[:, 16:64], -1e29, None, ALU.is_gt),
                             reads=[gres], writes=["SELALL"])
                    else:
                        for q8 in range(8):
                            S.op("dve", lambda e, q8=q8: e.max(out=M8(q8), in_=G[:, q8 * 8:(q8 + 1) * 8]), reads=[gres], writes=[("M8", q8)])
                            S.op("dve", lambda e, q8=q8: e.tensor_scalar(selh[:, q8 * 8:(q8 + 1) * 8], G[:, q8 * 8:(q8 + 1) * 8],
                                                                        M8(q8)[:, 2:3], None, ALU.is_ge),
                                 reads=[gres, ("M8", q8)], writes=["SELALL"])
                    S.op("dve", lambda e: e.tensor_tensor(selh, selh, ownm[:, hf * 64:(hf + 1) * 64], ALU.add),
                         reads=["SELALL", "CST"], writes=["SELALL"])

                def st3(hd=hd, hf=hf, kp=kp, hold=hold):
                    XG, FAR, P2, RP2 = hold["h"]
                    selh = SELALL[:, hd * 128 + hf * 64:hd * 128 + (hf + 1) * 64]
                    for q8 in range(8):
                        S.op("pe", lambda e, q8=q8: e.transpose(P2[0:8, q8 * 128:(q8 + 1) * 128], selh[:, q8 * 8:(q8 + 1) * 8], self.cst("ident")),
                             reads=["SELALL", "CST"], writes=[RP2[q8 // 4]])
                    S.op("dve", lambda e: e.tensor_scalar(NSB, P2[0:8, 0:1024], -1.0, 30000.0, ALU.add, ALU.mult), reads=RP2, writes=["NSB"])
                    S.dma("sp", self.NS_s[hd][:, hf * 1024:(hf + 1) * 1024], NSB, reads=["NSB"], writes=[("NS_s", hd, hf)])
                    if hf == 1:
                        S.dma("sp", self.Q_s[hd], QB(kp), reads=QBR(kp), writes=[("Q_s", hd)])
                qunits.append([st0, st1, st2, st3])
        self.run_staged(qunits)

        QBF = lambda k: self.BIGB[:, k * 2048:(k + 1) * 2048]
        KTH = lambda k: self.BIGB[:, 4096 + k * 2048:4096 + (k + 1) * 2048]
        VHo = lambda k: 8192 + k * 2064
        VH = lambda k: self.BIGB[:, VHo(k):VHo(k) + 2064]
        ET = lambda i: self.BIGB[:, 12320 + 512 * i:12320 + 512 * (i + 1)]
        ETR = lambda i: self.bres(12320 + 512 * i, 12320 + 512 * (i + 1))
        OTH = self.BIGB[:, 14368:14368 + L]
        OTHR = self.bres(14368, 14368 + L)
        RQ = lambda k: [("cQ", k)]
        RK = lambda k: [("cK", k)]
        RV = lambda k: [("cV", k)]
        OPR = lambda par, i: self.PS[:, (2 + 2 * par + i // 2) * 512 + (i % 2) * 256:(2 + 2 * par + i // 2) * 512 + (i % 2) * 256 + 129]
        opres = lambda par, i: ("p", 2 + 2 * par + i // 2)

        def load_head(hd):
            k = hd % 2
            S.dma("sp", QBF(k), self.Q_s[hd], reads=[("Q_s", hd)], writes=RQ(k))
            S.dma("sp", KTH(k), self.kT_s[hd], reads=[("kT_s", hd)], writes=RK(k))
            S.dma("sp", VH(k), self.V_s[hd], reads=[("V_s", hd)], writes=RV(k))

        DEPTH = 2
        NET = 6
        ET = lambda i: self.BIGB[:, 12320 + 512 * i:12320 + 512 * (i + 1)]
        ETR = lambda i: [("cE", i)]
        OTH = self.TMP2[:].bitcast(BF16)
        OTHR = ["TMP2"]
        NEGSEL = lambda k: self.FW[0:8, k * 1024:(k + 1) * 1024].bitcast(BF16)
        NSR = lambda k: [("FA", 2 * k), ("FA", 2 * k + 1)]
        RB = lambda qp: self.FW[:, L + qp * 512:L + (qp + 1) * 512]
        RBR = lambda qp: [("FB", qp)]
        EONE = self.S5S[0:8, 1024:1536].bitcast(BF16)
        identf = self.cst("ident")
        CORE_NAMES = [("cQ", 0), ("cQ", 1), ("cK", 0), ("cK", 1), ("cV", 0), ("cV", 1)] + [("cE", i) for i in range(NET)]
        S.op("dve", lambda e: e.tensor_copy(EONE.rearrange("p (n c) -> p n c", c=128), identf[0:8, 0:8].unsqueeze(2).broadcast_to([8, 8, 128])),
             reads=["CST"], writes=["EONE", "TMP"] + self.bres(0, 16512) + CORE_NAMES)
        state = dict(st=0, et=0)
        tasks = []

        def load_head(hd):
            k = hd % 2
            S.dma("sp", QBF(k), self.Q_s[hd], reads=[("Q_s", hd)], writes=RQ(k))
            S.dma("sp", KTH(k), self.kT_s[hd], reads=[("kT_s", hd)], writes=RK(k))
            S.dma("sp", VH(k), self.V_s[hd], reads=[("V_s", hd)], writes=RV(k))
            S.dma("sp", NEGSEL(k), self.NS_s[hd], reads=[("NS_s", hd, 0), ("NS_s", hd, 1)], writes=NSR(k))

        def mk_block(hd, Q, n, qp):
            hk = hd % 2
            VH3 = VH(hk).rearrange("p (t c) -> p t c", c=129)
            info = {}
            lastkt = 4 * Q + 3

            def s1():
                ets = {}
                for kt in (2 * n, 2 * n + 1):
                    c0 = max(4 * Q, kt) - 4 * Q
                    if c0 > 3:
                        continue
                    sb_ = state["st"] % 4
                    state["st"] += 1
                    eb_ = state["et"] % NET
                    state["et"] += 1
                    ST = self.PS[:, sb_ * 512:(sb_ + 1) * 512]
                    diag = kt >= 4 * Q
                    selm = n < 2 * Q + 1
                    S.op("pe", lambda e, ST=ST, c0=c0, fin=(not diag and not selm), kk_=KTH(hk)[:, kt * 128:(kt + 1) * 128],
                         qq_=QBF(hk)[:, Q * 512 + c0 * 128:(Q + 1) * 512]: e.matmul(
                        ST[:, c0 * 128:512], kk_, qq_, start=True, stop=fin), reads=RQ(hk) + RK(hk), writes=[("p", sb_)])
                    if selm:
                        S.op("pe", lambda e, ST=ST, c0=c0, fin=(not diag), en=EONE[:, n * 128:(n + 1) * 128],
                             ns=NEGSEL(hk)[:, Q * 512 + c0 * 128:(Q + 1) * 512]: e.matmul(ST[:, c0 * 128:512], en, ns, start=False, stop=fin),
                             reads=["EONE"] + NSR(hk), writes=[("p", sb_)])
                    if diag:
                        S.op("pe", lambda e, ST=ST, c0=c0: e.matmul(ST[:, c0 * 128:(c0 + 1) * 128], self.identB(), self.cmaskB(),
                                                                   start=False, stop=True), reads=["CSTB"], writes=[("p", sb_)])
                    et = ET(eb_)
                    S.op("act", lambda e, ST=ST, et=et, c0=c0: e.activation(out=et[:, c0 * 128:512], in_=ST[:, c0 * 128:512], func=AF.Exp, scale=scale),
                         reads=[("p", sb_)], writes=ETR(eb_))
                    ets[kt] = (et, eb_, c0)
                info["ets"] = ets

            def s2():
                if Q == 0 and n == 0 and hd + 1 < NH:
                    load_head(hd + 1)
                for kt in sorted(info["ets"]):
                    et, eb_, c0 = info["ets"][kt]
                    ob, rbk = 4 + qp, 6 + qp
                    S.op("pe", lambda e, et=et, c0=c0, vv=VH3[:, kt, 0:128], a=(kt == 0), z=(kt == lastkt), ob=ob: e.matmul(
                        self.PS[:, ob * 512 + c0 * 128:(ob + 1) * 512], vv, et[:, c0 * 128:512], start=a, stop=z),
                        reads=ETR(eb_) + RV(hk), writes=[("p", ob)])
                    S.op("pe", lambda e, et=et, c0=c0, a=(kt == 0), z=(kt == lastkt), rbk=rbk: e.matmul(
                        self.PS[:, rbk * 512 + c0 * 128:(rbk + 1) * 512], self.onesH(), et[:, c0 * 128:512], start=a, stop=z),
                        reads=ETR(eb_) + ["CSTB"], writes=[("p", rbk)])
            return s1, s2

        def mk_qend(hd, Q, qp):
            def s1():
                pass

            def s2():
                ob, rbk = 4 + qp, 6 + qp
                R = RB(qp)
                S.op("act", lambda e: e.activation(out=R, in_=self.PS[:, rbk * 512:(rbk + 1) * 512], func=AF.Ln, scale=128.0),
                     reads=[("p", rbk)], writes=RBR(qp))
                S.op("act", lambda e: e.activation(out=R, in_=R, func=AF.Exp, scale=-1.0), reads=RBR(qp), writes=RBR(qp))
                S.op("dve", lambda e: e.tensor_tensor(OTH[:, Q * 512:(Q + 1) * 512], self.PS[:, ob * 512:(ob + 1) * 512], R, ALU.mult),
                     reads=[("p", ob)] + RBR(qp), writes=OTHR)
                if Q == 3:
                    S.dma("sp", self.OT_s[hd], OTH, reads=OTHR, writes=[("OT_s", hd)])
            return s1, s2

        load_head(0)
        qcount = 0
        for hd in range(NH):
            for Q in range(4):
                qp = qcount % 2
                qcount += 1
                for n in range(2 * Q + 2):
                    tasks.append(mk_block(hd, Q, n, qp))
                tasks.append(mk_qend(hd, Q, qp))
        for idx in range(len(tasks) + DEPTH):
            if idx < len(tasks):
                tasks[idx][0]()
            if idx - DEPTH >= 0:
                tasks[idx - DEPTH][1]()
        S.op("dve", lambda e: e.memset(self.SMALL[:, 192:200], 1.0), reads=["EONE"], writes=["TMP"] + self.bres(0, 16512) + CORE_NAMES)
        for hd in range(NH):
            S.dma("sp", self.BIGA[:, hd * L:(hd + 1) * L], self.OT_s[hd], reads=[("OT_s", hd)], writes=[("A", hd)])
        jobs = []
        for oc in range(NFC):
            def evac_o(ps, psres, oc=oc):
                S.op("dve", lambda e: e.tensor_tensor(self.HT[:, oc * L:(oc + 1) * L], self.HT[:, oc * L:(oc + 1) * L], ps, ALU.add),
                     reads=psres + [("H", oc)], writes=[("H", oc)])
            jobs.append(dict(w2d=self.w_o_d[j], k0=0, KC=8, col0=oc * 128,
                             rhs=lambda kc, tt: self.BIGA[:, kc * L + tt * 512:kc * L + (tt + 1) * 512],
                             rhs_res=lambda kc: [("A", kc)], evac=evac_o))
        self.run_dense(jobs)

    def build(self):
        with ExitStack() as es:
            self.declare(es)
            self.EPS_T = es.enter_context(self.nc.sbuf_tensor("EPS_T", [128, 2], F32))
            self.S.op("pool", lambda e: e.memset(self.EPS_T[:], EPS), writes=["EPS"])
            self.prologue()
            self.body()
            self.epilogue()
            self.S.emit(self.sems, self.dsems)
        return self.nc

    def body(self):
        st = self.stop
        if st == "ffn_only":
            self.ffn(0)
            return
        self.s5_layer(0)
        if st == "mix0":
            return
        self.ffn_hook = lambda: self.s5_layer(1, "params")
        self.ffn(0)
        if st == "ffn0":
            return
        self.s5_layer(1, "main")
        if st == "mix1":
            return
        self.ffn(1)
        if st == "ffn1":
            return
        self.kv_phase()
        self.moba_layer(0)
        if st == "mix2":
            return
        self.ffn(2)
        if st == "ffn2":
            return
        self.moba_layer(1)
        if st == "mix3":
            return
        self.ffn(3)


def _consts():
    c = np.zeros((128, NCST), np.float32)
    p = np.arange(128)
    c[:, CC["ident"]:CC["ident"] + 128] = np.eye(128, dtype=np.float32)
    sw = np.zeros((128, 128), np.float32)
    sw[(p + 64) % 128, p] = 1.0
    c[:, CC["pswap"]:CC["pswap"] + 128] = sw
    c[:, CC["maskQ"]:CC["maskQ"] + 128] = (p[:, None] // 32 == p[None, :] // 32).astype(np.float32)
    c[:, CC["onesD"]:CC["onesD"] + 128] = 1.0 / D
    c[:, CC["onesH"]:CC["onesH"] + 128] = 1.0 / 128
    c[:, CC["cmask"]:CC["cmask"] + 128] = np.where(p[:, None] > p[None, :], NEG, 0.0)
    fm = np.zeros((16, 8), np.float32)
    for qt in range(16):
        fm[qt, (qt // 2):] = -1e30
    c[:, CC["futmask"]:CC["futmask"] + 128] = fm.reshape(1, 128)
    om = np.zeros((16, 8), np.float32)
    for qt in range(16):
        om[qt, qt // 2] = 1.0
    c[:, CC["ownmask"]:CC["ownmask"] + 128] = om.reshape(1, 128)
    rm = np.zeros((128, 4), np.float32)
    for v in range(2):
        rm[:, v] = ((p // 32) % 2 == v)
        rm[:, 2 + v] = -rm[:, v]
    c[:, CC["rowmask"]:CC["rowmask"] + 4] = rm
    c[:, CC["kk"]:CC["kk"] + 9] = np.arange(9, dtype=np.float32)[None, :]
    c[:, CC["kk64"]:CC["kk64"] + 64] = (8.0 * np.arange(1, 65, dtype=np.float32))[None, :]
    return c


def _rope():
    half = 64
    inv = (10000.0 ** (-np.arange(half, dtype=np.float32) * 2.0 / 128)).astype(np.float32)
    ang = np.arange(L, dtype=np.float32)[:, None] * inv[None, :]
    cos, sin = np.cos(ang).T.astype(np.float32), np.sin(ang).T.astype(np.float32)
    r = np.zeros((128, 2 * L), np.float32)
    r[0:64, 0:L] = cos; r[64:128, 0:L] = cos
    r[0:64, L:] = -sin; r[64:128, L:] = sin
    return r


def _fm(v):
    return np.ascontiguousarray(v.reshape(-1, 128).T)


def _host_prep(inp):
    vec = np.zeros((128, NV), np.float32)
    for l in range(2):
        vec[:, VC[("a_norm", l)]:VC[("a_norm", l)] + 8] = _fm(inp["a_norm"][l])
        vec[:, VC[("s5_D", l)]:VC[("s5_D", l)] + 8] = _fm(inp["s5_D"][l])
        vec[:, VC[("b_glu", l)]:VC[("b_glu", l)] + 16] = _fm(inp["b_glu"][l])
    vec[:, VC["kv_norm"]:VC["kv_norm"] + 8] = _fm(inp["kv_norm"])
    vec[:, VC["k_norm"]] = inp["k_norm"]
    for j in range(2):
        vec[:, VC[("b_norm", j)]:VC[("b_norm", j)] + 8] = _fm(inp["b_norm"][j])
        vec[:, VC[("q_norm", j)]] = inp["q_norm"][j]
    for l in range(4):
        vec[:, VC[("ffn_norm", l)]:VC[("ffn_norm", l)] + 8] = _fm(inp["ffn_norm"][l])
        cw = inp["conv_w"][l].reshape(3, NUPC, 128).transpose(2, 0, 1).reshape(128, 3 * NUPC)
        vec[:, VC[("conv_w", l)]:VC[("conv_w", l)] + 3 * NUPC] = cw
        vec[:, VC[("conv_b", l)]:VC[("conv_b", l)] + NUPC] = _fm(inp["conv_b"][l])
    s5p = np.zeros((2, 128, 96 + 4096), np.float32)
    for l in range(2):
        s5p[l, :, 0:32] = inp["s5_A_re"][l].reshape(32, 128).T
        s5p[l, :, 32:64] = inp["s5_A_im"][l].reshape(32, 128).T
        s5p[l, :, 64:96] = np.repeat(inp["s5_log_dt"][l].reshape(32, 2), 64, axis=1).T
        for nm, off in (("s5_B_re", 0), ("s5_B_im", 1024)):
            Bm = np.zeros((2, 64, 32, 2, 16), np.float32)
            Bg = inp[nm][l].reshape(32, 2, 64, 16)
            for m in range(2):
                Bm[m, :, :, m, :] = Bg[:, m].transpose(1, 0, 2)
            s5p[l, :, 96 + off:96 + off + 1024] = Bm.reshape(128, 1024)
        for nm, off in (("s5_C_re", 2048), ("s5_C_im", 3072)):
            Cm = np.zeros((2, 64, 32, 2, 16), np.float32)
            Cg = inp[nm][l].reshape(32, 2, 16, 64)
            for m in range(2):
                Cm[m, :, :, m, :] = Cg[:, m].transpose(2, 0, 1)
            s5p[l, :, 96 + off:96 + off + 1024] = Cm.reshape(128, 1024)
    common = dict(vec=vec, cst=_consts(), rope=_rope(), s5p=s5p,
                  w_glu=np.ascontiguousarray(inp["w_glu"]), w_kv=np.ascontiguousarray(inp["w_kv"]),
                  w_q=np.ascontiguousarray(inp["w_q"]), w_o=np.ascontiguousarray(inp["w_o"]),
                  w_up=np.ascontiguousarray(inp["w_up"]), w_down=np.ascontiguousarray(inp["w_down"]))
    return common


def _x_to_dev(xb):
    return np.ascontiguousarray(xb.T.reshape(NFC, 128, L).transpose(1, 0, 2).reshape(128, NFC * L))


def _dev_to_x(o):
    return np.ascontiguousarray(o.reshape(128, NFC, L).transpose(1, 0, 2).reshape(D, L).T)


def run(inp, stop=None, ncores=8):
    inp = {k: np.asarray(v) for k, v in inp.items()}
    common = _host_prep(inp)
    b = Builder(stop=stop)
    nc = b.build()
    in_maps = []
    for c in range(ncores):
        m = dict(common)
        m["xT"] = _x_to_dev(inp["x"][c])
        in_maps.append(m)
    res = run_bass_kernel_spmd(nc, in_maps, core_ids=list(range(ncores)))
    return np.stack([_dev_to_x(res.results[c]["outT"]) for c in range(ncores)], axis=0)


def kernel(**inputs):
    return run(inputs, stop=None, ncores=8).astype(np.float32)
```

```python
import math
from contextlib import ExitStack

import numpy as np
import concourse.bass as bass
import concourse.mybir as mybir
from concourse.bass_utils import run_bass_kernel_spmd

F32 = mybir.dt.float32
BF16 = mybir.dt.bfloat16
ALU = mybir.AluOpType
AF = mybir.ActivationFunctionType
AX = mybir.AxisListType

L = 2048
D = 1024
NFC = 8
DFF = 2816
NUPC = 44
NACT = 22
NH = 8
EPS = 1e-6
MAGIC = 12582912.0
TWO_PI = 2.0 * math.pi
NEG = -30000.0
ENGS = ("pe", "act", "dve", "pool", "sp")

VC = {}
_c = 0
for _l in range(2):
    VC[("a_norm", _l)] = _c; _c += 8
    VC[("s5_D", _l)] = _c; _c += 8
    VC[("b_glu", _l)] = _c; _c += 16
VC["kv_norm"] = _c; _c += 8
VC["k_norm"] = _c; _c += 1
for _j in range(2):
    VC[("b_norm", _j)] = _c; _c += 8
    VC[("q_norm", _j)] = _c; _c += 1
for _l in range(4):
    VC[("ffn_norm", _l)] = _c; _c += 8
    VC[("conv_w", _l)] = _c; _c += 3 * NUPC
    VC[("conv_b", _l)] = _c; _c += NUPC
NV = _c
CC = {}
_c = 0
for _n, _w in (("ident", 128), ("pswap", 128), ("maskQ", 128), ("onesD", 128), ("onesH", 128),
               ("cmask", 128), ("futmask", 128), ("rowmask", 4), ("kk", 9), ("ownmask", 128), ("kk64", 64)):
    CC[_n] = _c; _c += _w
NCST = _c


class _Op:
    __slots__ = ("eng", "fn", "deps", "dma", "sig", "cnt", "sem", "semval", "semprev", "idx")


class Sched:
    def __init__(self, nc):
        self.nc = nc
        self.ops = []
        self.res = {}

    def op(self, eng, fn, reads=(), writes=(), dma=False):
        o = _Op()
        o.eng = eng; o.fn = fn; o.dma = dma; o.sig = False; o.idx = len(self.ops)
        deps = set()
        for r in reads:
            st = self.res.get(r)
            if st is not None and st[0] is not None:
                deps.add(st[0])
        for w in writes:
            st = self.res.get(w)
            if st is not None:
                if st[0] is not None:
                    deps.add(st[0])
                deps.update(st[1])
        for r in reads:
            st = self.res.setdefault(r, [None, []])
            st[1].append(o.idx)
        for w in writes:
            self.res[w] = [o.idx, []]
        deps.discard(o.idx)
        o.deps = deps
        self.ops.append(o)
        return o.idx

    def dma(self, eng, out, in_, reads=(), writes=(), **kw):
        return self.op(eng, lambda e: e.dma_start(out=out, in_=in_, **kw), reads, writes, dma=True)

    def emit(self, sems, dma_sems):
        ops = self.ops
        for o in ops:
            latest = {}
            ddeps = []
            for d in o.deps:
                od = ops[d]
                if od.dma:
                    ddeps.append(d)
                elif od.eng != o.eng or o.dma or o.eng != "pe":
                    if od.eng not in latest or latest[od.eng] < d:
                        latest[od.eng] = d
            o.deps = (latest, ddeps)
            for d in latest.values():
                ops[d].sig = True
        cnt = {e: 0 for e in ENGS}
        for o in ops:
            if o.dma:
                continue
            if o.sig:
                cnt[o.eng] += 1
            o.cnt = cnt[o.eng]
        semcount = [0] * len(dma_sems)
        k = 0
        for o in ops:
            if o.dma:
                o.sem = k % len(dma_sems)
                o.semprev = semcount[o.sem]
                semcount[o.sem] += 16
                o.semval = semcount[o.sem]
                k += 1
        per_eng = {e: [o for o in ops if o.eng == e] for e in ENGS}

        def run_engine(ename, eng):
            waited = {e: 0 for e in ENGS}
            dwaited = [0] * len(dma_sems)
            for o in per_eng[ename]:
                latest, ddeps = o.deps
                for d in sorted(ddeps):
                    od = ops[d]
                    if dwaited[od.sem] < od.semval:
                        eng.wait_ge(dma_sems[od.sem], od.semval)
                        dwaited[od.sem] = od.semval
                for en_, d in latest.items():
                    od = ops[d]
                    if waited[en_] < od.cnt:
                        eng.wait_ge(sems[en_], od.cnt)
                        waited[en_] = od.cnt
                if o.dma:
                    if o.semprev > 0 and dwaited[o.sem] < o.semprev:
                        eng.wait_ge(dma_sems[o.sem], o.semprev)
                        dwaited[o.sem] = o.semprev
                    o.fn(eng).then_inc(dma_sems[o.sem], 16)
                else:
                    ins = o.fn(eng)
                    if o.sig:
                        ins.then_inc(sems[ename], 1)
            last = {}
            for o in per_eng[ename]:
                if o.dma:
                    last[o.sem] = max(last.get(o.sem, 0), o.semval)
            for s_, v in last.items():
                if dwaited[s_] < v:
                    eng.wait_ge(dma_sems[s_], v)

        with self.nc.Block() as block:
            @block.tensor
            def _(e):
                run_engine("pe", e)

            @block.scalar
            def _(e):
                run_engine("act", e)

            @block.vector
            def _(e):
                run_engine("dve", e)

            @block.gpsimd
            def _(e):
                run_engine("pool", e)

            @block.sync
            def _(e):
                run_engine("sp", e)


class Builder:
    def __init__(self, stop=None):
        self.stop = stop
        self.nc = bass.Bass("TRN2", target_bir_lowering=False)
        self.S = Sched(self.nc)
        self.wslot = 0
        self.pshalf = 0

    def declare(self, es):
        nc = self.nc
        di = lambda n, s, dt=F32: nc.dram_tensor(n, s, dt, kind="ExternalInput").ap()
        self.xT_d = di("xT", [128, NFC * L])
        self.vec_d = di("vec", [128, NV])
        self.cst_d = di("cst", [128, NCST])
        self.rope_d = di("rope", [128, 2 * L])
        self.s5p_d = di("s5p", [2, 128, 96 + 4096])
        self.w_glu_d = di("w_glu", [2, D, 2 * D])
        self.w_kv_d = di("w_kv", [D, 2 * D])
        self.w_q_d = di("w_q", [2, D, D])
        self.w_o_d = di("w_o", [2, D, D])
        self.w_up_d = di("w_up", [4, D, 2 * DFF])
        self.w_down_d = di("w_down", [4, DFF, D])
        self.out_d = nc.dram_tensor("outT", [128, NFC * L], F32, kind="ExternalOutput").ap()
        self.kT_s = nc.dram_tensor("kT_s", [NH, 128, L], BF16, kind="Internal").ap()
        self.V_s = nc.dram_tensor("V_s", [NH, 128, 16 * 129], BF16, kind="Internal").ap()
        self.OT_s = nc.dram_tensor("OT_s", [NH, 128, L], BF16, kind="Internal").ap()
        self.Q_s = nc.dram_tensor("Q_s", [NH, 128, L], BF16, kind="Internal").ap()
        self.NS_s = nc.dram_tensor("NS_s", [NH, 8, L], BF16, kind="Internal").ap()

        sb = lambda n, s, dt=F32: es.enter_context(nc.sbuf_tensor(n, s, dt))
        self.HT = sb("HT", [128, NFC * L])
        self.BIGA = sb("BIGA", [128, NFC * L], BF16)
        self.BIGB = sb("BIGB", [128, 16512], BF16)
        self.FW = sb("FW", [128, 3 * L])
        self.TMP = sb("TMP", [128, 1024])
        self.TMP2 = sb("TMP2", [128, 1024])
        self.UNI = sb("UNI", [128, 6656])
        self.VEC = sb("VEC", [128, NV])
        self.CST = sb("CST", [128, NCST])
        self.CSTB = sb("CSTB", [128, 4 * 128], BF16)
        self.S5S = sb("S5S", [128, 2048])
        self.SMALL = sb("SMALL", [128, 1024])
        self.PS = es.enter_context(nc.psum_tensor("PS", [128, 4096], F32))
        self.sems = {e: es.enter_context(nc.semaphore("s_" + e)) for e in ENGS}
        self.dsems = [es.enter_context(nc.semaphore("d%d" % i)) for i in range(40)]

    def FA(self):
        return self.FW[:, 0:L]

    def FB(self):
        return self.FW[:, L:2 * L]

    def FC(self):
        return self.FW[:, 2 * L:3 * L]

    def cst(self, name, w=128):
        c = CC[name]
        return self.CST[:, c:c + w]

    def vcol(self, key, i=0):
        c = VC[key] + i
        return self.VEC[:, c:c + 1]

    NSLOT = 8

    def WBF(self, i):
        return self.UNI[:, i * 512:(i + 1) * 512].bitcast(BF16)

    @staticmethod
    def wres(i):
        return ["h%d" % i]

    @staticmethod
    def bres(lo, hi):
        return [("B", k) for k in range(lo // 1024, (hi - 1) // 1024 + 1)]

    def prologue(self):
        S = self.S
        S.dma("sp", self.CST[:], self.cst_d, writes=["CST"])
        S.dma("sp", self.VEC[:], self.vec_d, writes=["VEC"])
        for fc in range(NFC):
            S.dma("act" if fc % 2 else "sp", self.HT[:, fc * L:(fc + 1) * L], self.xT_d[:, fc * L:(fc + 1) * L],
                  writes=[("H", fc)])
        for i, n in enumerate(("ident", "onesD", "onesH", "cmask")):
            src = self.cst(n)
            dst = self.CSTB[:, i * 128:(i + 1) * 128]
            S.op("dve", lambda e, d=dst, s=src: e.tensor_copy(d, s), reads=["CST"], writes=["CSTB"])

    def identB(self):
        return self.CSTB[:, 0:128]

    def onesD(self):
        return self.CSTB[:, 128:256]

    def onesH(self):
        return self.CSTB[:, 256:384]

    def cmaskB(self):
        return self.CSTB[:, 384:512]

    def epilogue(self):
        S = self.S
        for fc in range(NFC):
            S.dma("act" if fc % 2 else "sp", self.out_d[:, fc * L:(fc + 1) * L], self.HT[:, fc * L:(fc + 1) * L],
                  reads=[("H", fc)], writes=[("OUT", fc)])

    def rms_rstd(self, dst, dst_res):
        S = self.S

        def stA(tt):
            sq = self.BIGB[:, (tt % 2) * 4096:(tt % 2 + 1) * 4096]
            sqres = self.bres((tt % 2) * 4096, (tt % 2 + 1) * 4096)
            hin = self.HT[:].rearrange("p (f t) -> p f t", t=L)[:, :, tt * 512:(tt + 1) * 512]
            S.op("act", lambda e, o=sq, i=hin: e.activation(out=o.rearrange("p (f t) -> p f t", t=512), in_=i, func=AF.Square),
                 reads=[("H", fc) for fc in range(NFC)], writes=sqres)

        def stB(tt):
            sq = self.BIGB[:, (tt % 2) * 4096:(tt % 2 + 1) * 4096]
            sqres = self.bres((tt % 2) * 4096, (tt % 2 + 1) * 4096)
            bank = 4 + (tt % 2)
            ps = self.PS[:, bank * 512:(bank + 1) * 512]
            for fc in range(NFC):
                S.op("pe", lambda e, o=ps, r=sq[:, fc * 512:(fc + 1) * 512], a=(fc == 0), z=(fc == NFC - 1):
                     e.matmul(o, self.onesD(), r, start=a, stop=z),
                     reads=sqres + ["CSTB"], writes=[("p", bank)])
            d = dst[:, tt * 512:(tt + 1) * 512]
            S.op("act", lambda e, o=d, i=ps: e.activation(out=o, in_=i, func=AF.Ln, bias=self.eps_ap(), scale=1.0),
                 reads=[("p", bank), "EPS"], writes=[(dst_res, tt)])
            S.op("act", lambda e, o=d: e.activation(out=o, in_=o, func=AF.Exp, scale=-0.5), reads=[(dst_res, tt)], writes=[(dst_res, tt)])

        stA(0)
        stA(1)
        for tt in range(4):
            stB(tt)
            if tt + 2 < 4:
                stA(tt + 2)

    def eps_ap(self):
        return self.EPS_T[:, 0:1]

    def make_xn(self, gkey, rstd, rstd_res):
        S = self.S
        for tt in range(4):
            for fc in range(NFC):
                S.op("dve", lambda e, fc=fc, tt=tt: e.scalar_tensor_tensor(
                    out=self.BIGA[:, fc * L + tt * 512:fc * L + (tt + 1) * 512], in0=self.HT[:, fc * L + tt * 512:fc * L + (tt + 1) * 512],
                    scalar=self.vcol(gkey, fc), in1=rstd[:, tt * 512:(tt + 1) * 512], op0=ALU.mult, op1=ALU.mult),
                    reads=[("H", fc), "VEC", (rstd_res, tt)], writes=[("A", fc)])

    def run_dense(self, jobs):
        S = self.S
        n = len(jobs)
        slots = {}
        ahead = self.NSLOT - 2

        def load(i):
            if i >= n:
                return
            jb = jobs[i]
            slots[i] = self.load_w(jb["w2d"], jb["k0"], jb["KC"], jb["col0"])

        for i in range(min(ahead, n)):
            load(i)
        for i in range(n):
            load(i + ahead)
            jb = jobs[i]
            sl = slots[i]
            KC = jb["KC"]
            half = self.pshalf
            self.pshalf ^= 1
            rb = self.wres(sl)
            for kc in range(KC):
                for tt in range(4):
                    bank = half * 4 + tt
                    o = self.PS[:, bank * 512:(bank + 1) * 512]
                    S.op("pe", lambda e, o=o, w=self.WBF(sl)[:, kc * 128:(kc + 1) * 128], r=jb["rhs"](kc, tt), a=(kc == 0), z=(kc == KC - 1):
                         e.matmul(o, w, r, start=a, stop=z),
                         reads=rb + jb["rhs_res"](kc), writes=[("p", bank)])
            jb["evac"](self.PS[:, half * 2048:(half + 1) * 2048], [("p", half * 4 + t) for t in range(4)])

    def ffn(self, l):
        S = self.S
        self.rms_rstd(self.FA(), "FA")
        self.make_xn(("ffn_norm", l), self.FA(), "FA")
        if getattr(self, "ffn_hook", None):
            self.ffn_hook()
            self.ffn_hook = None
        groups = [(0, 6), (6, 12), (12, 17), (17, 22)]
        cw = VC[("conv_w", l)]
        cb = VC[("conv_b", l)]
        SG = self.BIGB[:, 12288:12288 + L]
        FBR = [("FB", t) for t in range(4)]
        FCR = [("FC", t) for t in range(4)]

        pending = []

        def conv_to(acc, accres, ps, psres, c):
            w = lambda k: self.VEC[:, cw + k * NUPC + c:cw + k * NUPC + c + 1]
            S.op("act", lambda e: e.activation(out=acc, in_=ps, func=AF.Identity, scale=w(2), bias=self.VEC[:, cb + c:cb + c + 1]),
                 reads=psres + ["VEC"], writes=accres)
            while pending:
                pending.pop(0)()
            S.op("dve", lambda e: e.scalar_tensor_tensor(out=acc[:, 1:L], in0=ps[:, 0:L - 1], scalar=w(1), in1=acc[:, 1:L],
                                                         op0=ALU.mult, op1=ALU.add),
                 reads=psres + ["VEC"] + accres, writes=accres)
            S.op("dve", lambda e: e.scalar_tensor_tensor(out=acc[:, 2:L], in0=ps[:, 0:L - 2], scalar=w(0), in1=acc[:, 2:L],
                                                         op0=ALU.mult, op1=ALU.add),
                 reads=psres + ["VEC"] + accres, writes=accres)

        ups, downs = [], []
        for (g0, g1) in groups:
            jobs = []
            for i in range(g0, g1):
                il = i - g0

                def evac_gate(ps, psres, i=i):
                    conv_to(self.FB(), FBR, ps, psres, i)
                    pending.append(lambda: S.op("act", lambda e: e.activation(out=SG, in_=self.FB(), func=AF.Silu), reads=FBR, writes=self.bres(12288, 14336)))

                def evac_val(ps, psres, i=i, il=il):
                    conv_to(self.FC(), FCR, ps, psres, NACT + i)
                    S.op("dve", lambda e: e.tensor_tensor(self.BIGB[:, il * L:(il + 1) * L], SG, self.FC(), ALU.mult),
                         reads=self.bres(12288, 14336) + FCR, writes=self.bres(il * L, (il + 1) * L))

                for col0, ev in ((i * 128, evac_gate), (DFF + i * 128, evac_val)):
                    jobs.append(dict(w2d=self.w_up_d[l], k0=0, KC=8, col0=col0,
                                     rhs=lambda kc, tt: self.BIGA[:, kc * L + tt * 512:kc * L + (tt + 1) * 512],
                                     rhs_res=lambda kc: [("A", kc)], evac=ev))
            ng = g1 - g0
            ups.append(jobs)
            jobs = []
            for oc in range(NFC):
                def evac_down(ps, psres, oc=oc):
                    S.op("dve", lambda e: e.tensor_tensor(self.HT[:, oc * L:(oc + 1) * L], self.HT[:, oc * L:(oc + 1) * L], ps, ALU.add),
                         reads=psres + [("H", oc)], writes=[("H", oc)])
                jobs.append(dict(w2d=self.w_down_d[l], k0=g0 * 128, KC=ng, col0=oc * 128,
                                 rhs=lambda kc, tt: self.BIGB[:, kc * L + tt * 512:kc * L + (tt + 1) * 512],
                                 rhs_res=lambda kc: self.bres(kc * L, (kc + 1) * L), evac=evac_down))
            downs.append(jobs)
        alljobs = list(ups[0])
        for g in range(len(groups)):
            if g + 1 < len(groups):
                alljobs.append(ups[g + 1][0])
            alljobs.extend(downs[g])
            if g + 1 < len(groups):
                alljobs.extend(ups[g + 1][1:])
        self.run_dense(alljobs)
        while pending:
            pending.pop(0)()

    def s5_layer(self, l, part="all"):
        S = self.S
        P = self.S5S
        skip = {"on": part == "main"}
        sl = lambda a, b: P[:, a:b]
        Are, Aim, ldt = sl(0, 32), sl(32, 64), sl(64, 96)
        dt, ar, ai = sl(96, 128), sl(128, 160), sl(160, 192)
        MAG, ANG, NT, C9, PWR, PWI = sl(192, 480), sl(480, 768), sl(768, 1056), sl(1056, 1344), sl(1344, 1632), sl(1632, 1920)
        cr, ci = sl(1920, 1952), sl(1952, 1984)
        sm = lambda i: self.SMALL[:, 640 + 32 * i:640 + 32 * (i + 1)]
        k3 = lambda ap: ap.rearrange("p (k g) -> p k g", g=32)
        kk = self.cst("kk", 9)
        kkb = kk.unsqueeze(2).broadcast_to([128, 9, 32])

        def dv(name, fn, reads, writes, eng="dve"):
            if skip["on"]:
                return
            S.op(eng, fn, reads=reads, writes=writes)

        if part != "main":
            S.dma("sp", P[:, 0:96], self.s5p_d[l][:, 0:96], writes=["s5raw"])
        FBR = [("FB", t) for t in range(4)]
        FCR = [("FC", t) for t in range(4)]
        if part != "params":
            S.dma("sp", self.FW[:, L:3 * L], self.s5p_d[l][:, 96:96 + 4096], writes=FBR + FCR)
        BMR, BMI = self.FW[:, L:L + 1024], self.FW[:, L + 1024:2 * L]
        CMR, CMI = self.FW[:, 2 * L:2 * L + 1024], self.FW[:, 2 * L + 1024:3 * L]
        g3 = lambda ap: ap.rearrange("p (g c) -> p g c", c=32)
        dv("dt", lambda e: e.activation(out=dt, in_=ldt, func=AF.Exp), ["s5raw"], ["s5dt"], "act")
        dv("ar", lambda e: e.tensor_tensor(ar, Are, dt, ALU.mult), ["s5raw", "s5dt"], ["s5ar"])
        dv("ai", lambda e: e.tensor_tensor(ai, Aim, dt, ALU.mult), ["s5raw", "s5dt"], ["s5ai"])
        dv("mag", lambda e: e.tensor_tensor(k3(MAG), ar.unsqueeze(1).broadcast_to([128, 9, 32]), kkb, ALU.mult), ["s5ar", "CST"], ["s5mag"])
        dv("mage", lambda e: e.activation(out=MAG, in_=MAG, func=AF.Exp), ["s5mag"], ["s5mag"], "act")
        dv("ang", lambda e: e.tensor_tensor(k3(ANG), ai.unsqueeze(1).broadcast_to([128, 9, 32]), kkb, ALU.mult), ["s5ai", "CST"], ["s5ang"])

        def sin_of(src, srcres, tmp, tmpres):
            dv("n1", lambda e: e.tensor_scalar(tmp, src, 1.0 / TWO_PI, MAGIC, ALU.mult, ALU.add), [srcres], [tmpres])
            dv("n2", lambda e: e.tensor_scalar(tmp, tmp, MAGIC, None, ALU.subtract), [tmpres], [tmpres])
            dv("n3", lambda e: e.scalar_tensor_tensor(out=tmp, in0=tmp, scalar=-TWO_PI, in1=src, op0=ALU.mult, op1=ALU.add), [tmpres, srcres], [tmpres])
            dv("n4", lambda e: e.activation(out=tmp, in_=tmp, func=AF.Sin), [tmpres], [tmpres], "act")

        sin_of(ANG, "s5ang", NT, "s5nt")
        dv("pwi", lambda e: e.tensor_tensor(PWI, MAG, NT, ALU.mult), ["s5mag", "s5nt"], ["s5pwi"])
        dv("angc", lambda e: e.tensor_scalar(C9, ANG, math.pi / 2, None, ALU.add), ["s5ang"], ["s5c9"])
        sin_of(C9, "s5c9", NT, "s5nt")
        dv("pwr", lambda e: e.tensor_tensor(PWR, MAG, NT, ALU.mult), ["s5mag", "s5nt"], ["s5pwr"])
        nr, den, x1, x2, rden, y1, y2 = sm(0), sm(1), sm(2), sm(3), sm(4), sm(5), sm(6)
        pw1r, pw1i = PWR[:, 32:64], PWI[:, 32:64]
        dv("nr", lambda e: e.tensor_scalar(nr, pw1r, -1.0, None, ALU.add), ["s5pwr"], ["sm0"])
        dv("den", lambda e: e.tensor_tensor(den, Are, Are, ALU.mult), ["s5raw"], ["sm1"])
        dv("den2", lambda e: e.tensor_tensor(x1, Aim, Aim, ALU.mult), ["s5raw"], ["sm2"])
        dv("den3", lambda e: e.tensor_tensor(den, den, x1, ALU.add), ["sm1", "sm2"], ["sm1"])
        dv("rden", lambda e: e.reciprocal(rden, den), ["sm1"], ["sm4"])
        dv("x1", lambda e: e.tensor_tensor(x1, nr, Are, ALU.mult), ["sm0", "s5raw"], ["sm2"])
        dv("x2", lambda e: e.tensor_tensor(x2, pw1i, Aim, ALU.mult), ["s5pwi", "s5raw"], ["sm3"])
        dv("x3", lambda e: e.tensor_tensor(x1, x1, x2, ALU.add), ["sm2", "sm3"], ["sm2"])
        dv("cr", lambda e: e.tensor_tensor(cr, x1, rden, ALU.mult), ["sm2", "sm4"], ["s5cr"])
        dv("y1", lambda e: e.tensor_tensor(y1, pw1i, Are, ALU.mult), ["s5pwi", "s5raw"], ["sm5"])
        dv("y2", lambda e: e.tensor_tensor(y2, nr, Aim, ALU.mult), ["sm0", "s5raw"], ["sm6"])
        dv("y3", lambda e: e.tensor_tensor(y1, y1, y2, ALU.subtract), ["sm5", "sm6"], ["sm5"])
        dv("ci", lambda e: e.tensor_tensor(ci, y1, rden, ALU.mult), ["sm5", "sm4"], ["s5ci"])
        skip["on"] = False
        if part == "params":
            return
        crb = cr.unsqueeze(2).broadcast_to([128, 32, 32])
        cib = ci.unsqueeze(2).broadcast_to([128, 32, 32])
        T1, T2 = self.TMP[:], self.TMP2[:]
        R_BMR, R_BMI = [("FB", 0), ("FB", 1)], [("FB", 2), ("FB", 3)]
        R_CMR, R_CMI = [("FC", 0), ("FC", 1)], [("FC", 2), ("FC", 3)]
        dv("b1", lambda e: e.tensor_tensor(g3(T1), cib, g3(BMR), ALU.mult), ["s5ci"] + R_BMR, ["TMP"])
        dv("b2", lambda e: e.tensor_tensor(g3(T2), cib, g3(BMI), ALU.mult), ["s5ci"] + R_BMI, ["TMP2"])
        dv("b3", lambda e: e.tensor_tensor(g3(BMR), crb, g3(BMR), ALU.mult), ["s5cr"] + R_BMR, R_BMR)
        dv("b4", lambda e: e.tensor_tensor(BMR, BMR, T2, ALU.subtract), R_BMR + ["TMP2"], R_BMR)
        dv("b5", lambda e: e.tensor_tensor(g3(BMI), crb, g3(BMI), ALU.mult), ["s5cr"] + R_BMI, R_BMI)
        dv("b6", lambda e: e.tensor_tensor(BMI, BMI, T1, ALU.add), R_BMI + ["TMP"], R_BMI)

        FAR = [("FA", t) for t in range(4)]
        self.rms_rstd(self.FA(), "FA")
        for tt in range(4):
            for fc in range(NFC):
                S.op("dve", lambda e, fc=fc, tt=tt: e.scalar_tensor_tensor(
                    out=self.BIGA[:, fc * L:(fc + 1) * L].rearrange("p (j c) -> p j c", c=256)[:, :, tt * 64:(tt + 1) * 64],
                    in0=self.HT[:, fc * L + tt * 512:fc * L + (tt + 1) * 512].rearrange("p (c j) -> p j c", j=8),
                    scalar=self.vcol(("a_norm", l), fc),
                    in1=self.FA()[:, tt * 512:(tt + 1) * 512].rearrange("p (c j) -> p j c", j=8), op0=ALU.mult, op1=ALU.mult),
                    reads=[("H", fc), "VEC", ("FA", tt)], writes=[("A", fc)])
        B2S4 = self.BIGB[:, 0:16448].rearrange("p (r g c) -> p r g c", r=2, g=32)
        ALLC = [("b2S", c) for c in range(257)]
        BALL = self.bres(0, 16512)
        S.op("pool", lambda e: e.memset(B2S4[:, :, :, 0:1], 0.0), reads=[], writes=BALL + [("b2S", 0)])
        Zre, ZimN = self.FW[:, 0:1024], self.FW[:, 1024:2048]
        z4 = lambda ap: ap.rearrange("p (t q c) -> p t q c", t=8, q=4)
        R_ZR, R_ZI = [("FA", 0), ("FA", 1)], [("FA", 2), ("FA", 3)]
        PWR3, PWI3 = k3(PWR), k3(PWI)
        ZB = {0: (Zre, ZimN, R_ZR, R_ZI),
              1: (self.UNI[:, 2048:3072], self.UNI[:, 3072:4096], ["h4", "h5"], ["h6", "h7"])}

        PWRN = self.SMALL[:, 576:864]
        dv("pwrn", lambda e: e.tensor_scalar(PWRN, PWR, -1.0, None, ALU.mult), ["s5pwr"], self.S5_SMALL)
        PWRN3 = k3(PWRN)

        def zcompute(fc, zb=0):
            Zre, ZimN, R_ZR, R_ZI = ZB[zb]
            pwr_b = PWR3[:, 0:8, 4 * fc:4 * fc + 4].unsqueeze(3).broadcast_to([128, 8, 4, 32])
            pwrn_b = PWRN3[:, 0:8, 4 * fc:4 * fc + 4].unsqueeze(3).broadcast_to([128, 8, 4, 32])
            pwi_b = PWI3[:, 0:8, 4 * fc:4 * fc + 4].unsqueeze(3).broadcast_to([128, 8, 4, 32])
            bmr_b = g3(BMR)[:, 4 * fc:4 * fc + 4, :].unsqueeze(1).broadcast_to([128, 8, 4, 32])
            bmi_b = g3(BMI)[:, 4 * fc:4 * fc + 4, :].unsqueeze(1).broadcast_to([128, 8, 4, 32])
            E = "pool"
            dv("z1", lambda e: e.tensor_tensor(z4(Zre), pwr_b, bmr_b, ALU.mult), ["s5pwr"] + R_BMR, R_ZR, E)
            dv("z2", lambda e: e.tensor_tensor(z4(T1), pwi_b, bmi_b, ALU.mult), ["s5pwi"] + R_BMI, ["TMP"], E)
            dv("z3", lambda e: e.tensor_tensor(Zre, Zre, T1, ALU.subtract), R_ZR + ["TMP"], R_ZR, E)
            E = "pool"
            dv("z4", lambda e: e.tensor_tensor(z4(ZimN), pwrn_b, bmi_b, ALU.mult), ["PWRN"] + R_BMI, R_ZI, E)
            dv("z5", lambda e: e.tensor_tensor(z4(T2), pwi_b, bmr_b, ALU.mult), ["s5pwi"] + R_BMR, ["TMP2"], E)
            dv("z6", lambda e: e.tensor_tensor(ZimN, ZimN, T2, ALU.subtract), R_ZI + ["TMP2"], R_ZI, E)

        def zcompute_dve(fc):
            Zre, ZimN, R_ZR, R_ZI = ZB[0]
            pwr_b = PWR3[:, 0:8, 4 * fc:4 * fc + 4].unsqueeze(3).broadcast_to([128, 8, 4, 32])
            pwi_b = PWI3[:, 0:8, 4 * fc:4 * fc + 4].unsqueeze(3).broadcast_to([128, 8, 4, 32])
            bmr_b = g3(BMR)[:, 4 * fc:4 * fc + 4, :].unsqueeze(1).broadcast_to([128, 8, 4, 32])
            bmi_b = g3(BMI)[:, 4 * fc:4 * fc + 4, :].unsqueeze(1).broadcast_to([128, 8, 4, 32])
            PT = self.PS[:, 3072:4096]
            RPT = [("p", 6), ("p", 7)]
            dv("z1", lambda e: e.tensor_tensor(z4(Zre), pwr_b, bmr_b, ALU.mult), ["s5pwr"] + R_BMR, R_ZR, "pool")
            dv("z2", lambda e: e.tensor_tensor(z4(T1), pwi_b, bmi_b, ALU.mult), ["s5pwi"] + R_BMI, ["TMP"], "pool")
            dv("z3", lambda e: e.tensor_tensor(Zre, Zre, T1, ALU.subtract), R_ZR + ["TMP"], R_ZR, "pool")
            dv("z4", lambda e: e.tensor_tensor(z4(ZimN), pwr_b, bmi_b, ALU.mult), ["s5pwr"] + R_BMI, R_ZI)
            dv("z5", lambda e: e.tensor_tensor(z4(PT), pwi_b, bmr_b, ALU.mult), ["s5pwi"] + R_BMR, RPT)
            dv("z6", lambda e: e.scalar_tensor_tensor(out=ZimN, in0=ZimN, scalar=-1.0, in1=PT, op0=ALU.mult, op1=ALU.subtract),
               R_ZI + RPT, R_ZI)

        W2B = {0: (self.UNI[:, 0:2048].bitcast(BF16).rearrange("p (v t r c) -> p v t r c", v=2, t=8, r=2), ["h0", "h1", "h2", "h3"]),
               1: (self.UNI[:, 4096:6144].bitcast(BF16).rearrange("p (v t r c) -> p v t r c", v=2, t=8, r=2), ["h8", "h9", "h10", "h11"])}
        ident = self.cst("ident")
        rowmask = self.cst("rowmask", 4)

        def stageT(fc):
            zb = fc % 2
            zcompute(fc, zb)
            Zre_, ZimN_, RZR_, RZI_ = ZB[zb]
            W2v, R_W2 = W2B[zb]
            for ri, Zs, ZR in ((0, Zre_, RZR_), (1, ZimN_, RZI_)):
                for tq in range(2):
                    bank = ri * 2 + tq
                    for i in range(4):
                        t = tq * 4 + i
                        S.op("pe", lambda e, o=self.PS[:, bank * 512 + i * 128:bank * 512 + (i + 1) * 128], a=Zs[:, t * 128:(t + 1) * 128]:
                             e.transpose(o, a, ident), reads=ZR + ["CST"], writes=[("p", bank)])
                    for v in range(2):
                        S.op("dve", lambda e, v=v, tq=tq, ri=ri, bank=bank, W2v=W2v: e.tensor_scalar(
                            W2v[:, v, tq * 4:(tq + 1) * 4, ri, :],
                            self.PS[:, bank * 512:(bank + 1) * 512].rearrange("p (t c) -> p t c", c=128),
                            rowmask[:, 2 * ri + v:2 * ri + v + 1], None, ALU.mult),
                            reads=[("p", bank), "CST"], writes=R_W2)

        def stageM(fc):
            W2v, R_W2 = W2B[fc % 2]
            for q in range(4):
                hq, v = q // 2, q % 2
                for ri in range(2):
                    off = 2048 + (q * 2 + ri) * 256
                    bank = off // 512
                    for j in range(8):
                        S.op("pe", lambda e, o=self.PS[:, off:off + 256], w=W2v[64 * hq:64 * hq + 64, v, 7 - j, ri, :],
                             r=self.BIGA[64 * hq:64 * hq + 64, fc * L + j * 256:fc * L + (j + 1) * 256], a=(j == 0), z=(j == 7):
                             e.matmul(o, w, r, start=a, stop=z), reads=R_W2 + [("A", fc)], writes=[("p", bank)])
            S.op("act", lambda e, fc=fc: e.activation(
                out=B2S4[:, :, 4 * fc:4 * fc + 4, 1:257],
                in_=self.PS[:, 2048:4096].rearrange("p (q r c) -> p r q c", q=4, r=2), func=AF.Copy),
                reads=[("p", b) for b in range(4, 8)], writes=BALL + ALLC[1:])

        stageT(0)
        for fc in range(NFC):
            if fc + 1 < NFC:
                stageT(fc + 1)
            stageM(fc)

        UA = self.UNI[:]
        NSEG, SEGL = 4, 64
        RING = lambda k, sg: self.UNI[:, k * 512 + sg * 128:k * 512 + (sg + 1) * 128]
        PP4 = lambda sg: self.UNI[:, 1536 + sg * 128:1536 + (sg + 1) * 128]
        TT4 = lambda sg: self.UNI[:, 2048 + sg * 64:2048 + (sg + 1) * 64]
        PW64R, PW64I = self.UNI[:, 2304:4352], self.UNI[:, 4352:6400]
        gi = lambda ap: ap.rearrange("p (g i) -> p g i", i=64)
        UNI_H = ["h%d" % i_ for i_ in range(13)]
        SCN = [("R4", k_, s_) for k_ in range(3) for s_ in range(4)] + [("PP4", s_) for s_ in range(4)] + [("TT4", s_) for s_ in range(4)] + ["PWR64", "PWI64"]
        A12 = self.SMALL[:, 0:128]
        A1, A2 = self.SMALL[:, 0:64], self.SMALL[:, 64:128]
        l8r, l8i = PWR[:, 256:288], PWI[:, 256:288]
        dv("a1a", lambda e: e.tensor_copy(A1[:, 0:32], l8r), ["s5pwr"], ["A1"])
        dv("a1b", lambda e: e.tensor_copy(A1[:, 32:64], l8r), ["s5pwr"], ["A1"])
        dv("a2a", lambda e: e.tensor_scalar(A2[:, 0:32], l8i, -1.0, None, ALU.mult), ["s5pwi"], ["A2"])
        dv("a2b", lambda e: e.tensor_copy(A2[:, 32:64], l8i), ["s5pwi"], ["A2"])
        dv("sr0", lambda e: e.memset(self.UNI[:, 0:512], 0.0), [], UNI_H + SCN)
        h4 = lambda ap: ap.rearrange("p (h j g) -> p h j g", h=2, j=2)
        j3 = lambda ap: ap.rearrange("p (j g) -> p j g", j=2)
        kk64 = self.cst("kk64", 64)
        kkb64 = kk64.unsqueeze(1).broadcast_to([128, 32, 64])
        FAt = self.FA()
        FAR_ = [("FA", t_) for t_ in range(4)]
        PA, PB = self.PS[:, 0:2048], self.PS[:, 2048:4096]
        RPA, RPB = [("p", b_) for b_ in range(4)], [("p", b_) for b_ in range(4, 8)]
        dv("t1", lambda e: e.tensor_tensor(gi(FAt), ar.unsqueeze(2).broadcast_to([128, 32, 64]), kkb64, ALU.mult), ["s5ar", "CST"], FAR_)
        dv("t2", lambda e: e.activation(out=FAt, in_=FAt, func=AF.Exp), FAR_, FAR_, "act")
        dv("t3", lambda e: e.tensor_tensor(gi(PW64I), ai.unsqueeze(2).broadcast_to([128, 32, 64]), kkb64, ALU.mult), ["s5ai", "CST"], ["PWI64"])

        def sin64(dst, dres):
            dv("u1", lambda e: e.tensor_scalar(dst, PW64I, 1.0 / TWO_PI, MAGIC, ALU.mult, ALU.add), ["PWI64"], dres)
            dv("u2", lambda e: e.tensor_scalar(dst, dst, MAGIC, None, ALU.subtract), dres, dres)
            dv("u3", lambda e: e.scalar_tensor_tensor(out=dst, in0=dst, scalar=-TWO_PI, in1=PW64I, op0=ALU.mult, op1=ALU.add), dres + ["PWI64"], dres)
            dv("u4", lambda e: e.activation(out=dst, in_=dst, func=AF.Sin), dres, dres, "act")

        sin64(PA, RPA)
        dv("t4", lambda e: e.tensor_scalar(PW64I, PW64I, math.pi / 2, None, ALU.add), ["PWI64"] + RPA, ["PWI64"])
        sin64(PB, RPB)
        dv("t5", lambda e: e.tensor_tensor(PW64R, FAt, PB, ALU.mult), FAR_ + RPB, ["PWR64"])
        dv("t6", lambda e: e.tensor_tensor(PW64I, FAt, PA, ALU.mult), FAR_ + RPA + RPB, ["PWI64"])

        for t in range(SEGL):
            k0, k1 = t % 3, (t + 1) % 3
            for sg in range(NSEG):
                wv = bass.AP(UA.tensor, UA.offset + k0 * 512 + sg * 128, [list(UA.ap[0]), [32, 2], [32, 2], [1, 32]])
                dv("s12", lambda e, wv=wv, sg=sg: e.tensor_tensor(h4(PP4(sg)), h4(A12), wv, ALU.mult), ["A1", "A2", ("R4", k0, sg)], [("PP4", sg)])
            for sg in range(NSEG):
                dv("s3", lambda e, sg=sg: e.tensor_tensor(j3(TT4(sg)), h4(PP4(sg))[:, 0], h4(PP4(sg))[:, 1], ALU.add), [("PP4", sg)], [("TT4", sg)])
            for sg in range(NSEG):
                col = SEGL * sg + t + 1
                dv("s4", lambda e, sg=sg, col=col, k1=k1: e.tensor_tensor(h4(RING(k1, sg)), j3(TT4(sg)).unsqueeze(1).broadcast_to([128, 2, 2, 32]),
                                                                       B2S4[:, :, :, col].unsqueeze(1).broadcast_to([128, 2, 2, 32]), ALU.add),
                   [("TT4", sg), ("b2S", col)], [("R4", k1, sg)])
            cols = [SEGL * sg + t + 1 for sg in range(NSEG)]
            ring_v = self.UNI[:, k1 * 512:(k1 + 1) * 512].rearrange("p (s b) -> p s b", b=128)[:, :, 0:64].rearrange("p s (r g) -> p r g s", r=2)
            dv("s5", lambda e, t=t, ring_v=ring_v: e.activation(out=B2S4[:, :, :, t + 1:t + 1 + SEGL * (NSEG - 1) + 1:SEGL], in_=ring_v, func=AF.Copy),
               [("R4", k1, sg) for sg in range(NSEG)], [("b2S", c_) for c_ in cols], "act")
        kf = SEGL % 3
        A64 = self.SMALL[:, 576:704]
        PP64, TT64 = self.SMALL[:, 704:832], self.SMALL[:, 832:896]
        p64r, p64i = gi(PW64R)[:, :, 63], gi(PW64I)[:, :, 63]
        dv("b1", lambda e: e.tensor_copy(A64[:, 0:32], p64r), ["PWR64"], ["PWRN"])
        dv("b2", lambda e: e.tensor_copy(A64[:, 32:64], p64r), ["PWR64"], ["PWRN"])
        dv("b3", lambda e: e.tensor_scalar(A64[:, 64:96], p64i, -1.0, None, ALU.mult), ["PWI64"], ["PWRN"])
        dv("b4", lambda e: e.tensor_copy(A64[:, 96:128], p64i), ["PWI64"], ["PWRN"])
        kc_ = (kf + 1) % 3
        dv("c1", lambda e: e.tensor_copy(RING(kc_, 1), RING(kf, 0)), [("R4", kf, 0)], [("R4", kc_, 1)])
        for sg in (2, 3):
            wv = bass.AP(UA.tensor, UA.offset + kc_ * 512 + (sg - 1) * 128, [list(UA.ap[0]), [32, 2], [32, 2], [1, 32]])
            dv("c2", lambda e, wv=wv: e.tensor_tensor(h4(PP64), h4(A64), wv, ALU.mult), ["PWRN", ("R4", kc_, sg - 1)], ["PWRN"])
            dv("c3", lambda e: e.tensor_tensor(j3(TT64), h4(PP64)[:, 0], h4(PP64)[:, 1], ALU.add), ["PWRN"], ["PWRN"])
            dv("c4", lambda e, sg=sg: e.tensor_tensor(h4(RING(kc_, sg)), j3(TT64).unsqueeze(1).broadcast_to([128, 2, 2, 32]), h4(RING(kf, sg - 1)), ALU.add),
               ["PWRN", ("R4", kf, sg - 1)], [("R4", kc_, sg)])
        for sg in (1, 2, 3):
            car = RING(kc_, sg)
            cr_b = car[:, 0:32].unsqueeze(2).broadcast_to([128, 32, 64])
            ci_b = car[:, 32:64].unsqueeze(2).broadcast_to([128, 32, 64])
            c0_ = SEGL * sg + 1
            cn = [("b2S", c_) for c_ in range(c0_, c0_ + SEGL)]
            rc = [("R4", kc_, sg)]
            o_re = B2S4[:, 0, :, c0_:c0_ + SEGL]
            o_im = B2S4[:, 1, :, c0_:c0_ + SEGL]
            dv("f1", lambda e, cr_b=cr_b: e.tensor_tensor(gi(FAt), gi(PW64R), cr_b, ALU.mult), ["PWR64"] + rc, FAR_)
            dv("f2", lambda e, ci_b=ci_b: e.tensor_tensor(gi(PA), gi(PW64I), ci_b, ALU.mult), ["PWI64"] + rc, RPA)
            dv("f3", lambda e: e.tensor_tensor(FAt, FAt, PA, ALU.subtract), FAR_ + RPA, FAR_)
            dv("f4", lambda e, o_re=o_re: e.tensor_tensor(o_re, o_re, gi(FAt), ALU.add), FAR_ + cn, cn)
            dv("f5", lambda e, ci_b=ci_b: e.tensor_tensor(gi(FAt), gi(PW64R), ci_b, ALU.mult), ["PWR64"] + rc, FAR_)
            dv("f6", lambda e, cr_b=cr_b: e.tensor_tensor(gi(PB), gi(PW64I), cr_b, ALU.mult), ["PWI64"] + rc, RPB)
            dv("f7", lambda e: e.tensor_tensor(FAt, FAt, PB, ALU.add), FAR_ + RPB, FAR_)
            dv("f8", lambda e, o_im=o_im: e.tensor_tensor(o_im, o_im, gi(FAt), ALU.add), FAR_ + cn, cn)

        W3H = {b: self.UNI[:, b * 2048:(b + 1) * 2048].bitcast(BF16).rearrange("p (j q r c) -> p j q r c", j=4, q=4, r=2) for b in range(3)}
        R_W3H = {b: ["h%d" % i_ for i_ in range(4 * b, 4 * b + 4)] for b in range(3)}
        KL = self.UNI[:, 6144:6656].bitcast(BF16).rearrange("p (t c) -> p t c", c=128)
        maskQ = self.cst("maskQ")
        S.op("pool", lambda e: e.memset(self.UNI[:, 0:6144].bitcast(BF16), 0.0), writes=UNI_H + SCN)
        t4 = lambda ap: ap.rearrange("p (j q c) -> p j q c", j=8, q=4)
        wcount = 0
        zcompute_dve(0)
        for fc in range(NFC):
            ypar = 0
            yb = 0
            for tq in range(2):
                bank = 4 + tq
                for i in range(4):
                    t = tq * 4 + i
                    o = self.PS[:, bank * 512 + i * 128:bank * 512 + (i + 1) * 128]
                    S.op("pe", lambda e, o=o, t=t, fc=fc: e.matmul(o, Zre[:, t * 128:(t + 1) * 128], CMR[:, 128 * fc:128 * (fc + 1)], start=True, stop=False),
                         reads=R_ZR + R_CMR, writes=[("p", bank)])
                    S.op("pe", lambda e, o=o, t=t, fc=fc: e.matmul(o, ZimN[:, t * 128:(t + 1) * 128], CMI[:, 128 * fc:128 * (fc + 1)], start=False, stop=True),
                         reads=R_ZI + R_CMI, writes=[("p", bank)])
                S.op("dve", lambda e, tq=tq, bank=bank: e.tensor_tensor(
                    KL[:, tq * 4:(tq + 1) * 4, :], self.PS[:, bank * 512:(bank + 1) * 512].rearrange("p (t c) -> p t c", c=128),
                    maskQ.unsqueeze(1).broadcast_to([128, 4, 128]), ALU.mult), reads=[("p", bank), "CST"], writes=["h12"])
            cmr_b = g3(CMR)[:, 4 * fc:4 * fc + 4, :].unsqueeze(1).broadcast_to([128, 8, 4, 32])
            cmi_b = g3(CMI)[:, 4 * fc:4 * fc + 4, :].unsqueeze(1).broadcast_to([128, 8, 4, 32])
            pr_b = PWR3[:, 1:9, 4 * fc:4 * fc + 4].unsqueeze(3).broadcast_to([128, 8, 4, 32])
            pi_b = PWI3[:, 1:9, 4 * fc:4 * fc + 4].unsqueeze(3).broadcast_to([128, 8, 4, 32])
            wb = [(wcount) % 3, (wcount + 1) % 3]
            wcount += 2
            E = "pool"
            dv("w1", lambda e, a=cmr_b, b=pr_b: e.tensor_tensor(t4(T1), a, b, ALU.mult), R_CMR + ["s5pwr"], ["TMP"], E)
            dv("w2", lambda e, a=cmi_b, b=pi_b: e.tensor_tensor(t4(T2), a, b, ALU.mult), R_CMI + ["s5pwi"], ["TMP2"], E)
            for hb in range(2):
                for q in range(4):
                    dv("w3", lambda e, q=q, hb=hb, wv=W3H[wb[hb]]: e.tensor_tensor(wv[:, :, q, 0, 32 * q:32 * q + 32], t4(T1)[:, 4 * hb:4 * hb + 4, q, :],
                                                                               t4(T2)[:, 4 * hb:4 * hb + 4, q, :], ALU.subtract),
                       ["TMP", "TMP2"], R_W3H[wb[hb]], E)
            PT = self.PS[:, 3072:4096]
            RPT = [("p", 6), ("p", 7)]
            ST2 = self.SMALL[:, 0:1024]
            dv("w4", lambda e, a=cmi_b, b=pr_b: e.tensor_tensor(t4(PT), a, b, ALU.mult), R_CMI + ["s5pwr"], RPT, "dve")
            dv("w5", lambda e, a=cmr_b, b=pi_b: e.tensor_tensor(t4(ST2), a, b, ALU.mult), R_CMR + ["s5pwi"], self.S5_SMALL, "dve")
            dv("w5b", lambda e: e.tensor_tensor(PT, PT, ST2, ALU.add), RPT + self.S5_SMALL, RPT, "dve")
            for hb in range(2):
                for q in range(4):
                    dv("w6", lambda e, q=q, hb=hb, wv=W3H[wb[hb]]: e.activation(out=wv[:, :, q, 1, 32 * q:32 * q + 32], in_=t4(PT)[:, 4 * hb:4 * hb + 4, q, :],
                                                                            func=AF.Copy, scale=-1.0),
                       RPT, R_W3H[wb[hb]], "act")
            if fc + 1 < NFC:
                zcompute_dve(fc + 1)
            for t in range(8):
                for b in range(4):
                    jlo, jhi = max(2 * b, t), 2 * b + 2
                    if jlo >= jhi:
                        continue
                    S.op("pe", lambda e, t=t, jlo=jlo, jhi=jhi, fc=fc, yb=yb: e.matmul(
                        self.PS[:, yb + jlo * 256:yb + jhi * 256], KL[:, t, :],
                        self.BIGA[:, fc * L + (jlo - t) * 256:fc * L + (jhi - t) * 256], start=(t == 0), stop=False),
                        reads=["h12", ("A", fc)], writes=[("p", 4 * ypar + b)])
            for j in range(8):
                hb = j // 4
                for q in range(4):
                    for ri in range(2):
                        S.op("pe", lambda e, j=j, q=q, ri=ri, fc=fc, yb=yb, wv=W3H[wb[hb]]: e.matmul(
                            self.PS[:, yb + j * 256:yb + (j + 1) * 256], wv[:, j % 4, q, ri, :], B2S4[:, ri, 4 * fc + q, 0:256],
                            start=False, stop=(q == 3 and ri == 1)),
                            reads=R_W3H[wb[hb]] + ALLC, writes=[("p", 4 * ypar + j // 2)])
            for hv in range(2):
                yv = self.PS[:, yb + hv * 1024:yb + (hv + 1) * 1024]
                sc = self.SMALL[:, 0:1024]
                ry = [("p", 4 * ypar + 2 * hv), ("p", 4 * ypar + 2 * hv + 1)]
                rs = self.S5_SMALL
                ug = self.BIGA[:, fc * L + hv * 1024:fc * L + (hv + 1) * 1024]
                S.op("dve", lambda e, yv=yv, ug=ug, fc=fc: e.scalar_tensor_tensor(out=yv, in0=ug, scalar=self.vcol(("s5_D", l), fc), in1=yv,
                                                                          op0=ALU.mult, op1=ALU.add), reads=ry + [("A", fc), "VEC"], writes=ry)
                S.op("act", lambda e, yv=yv, ug=ug: e.activation(out=ug, in_=yv, func=AF.Gelu_apprx_tanh), reads=ry, writes=[("A", fc)])

        jobs = []
        for oc in range(NFC):
            def evac_zb(ps, psres, oc=oc):
                S.op("act", lambda e: e.activation(out=self.FA(), in_=ps, func=AF.Sigmoid, bias=self.vcol(("b_glu", l), 8 + oc), scale=1.0),
                     reads=psres + ["VEC"], writes=FAR)

            def evac_za(ps, psres, oc=oc):
                S.op("dve", lambda e: e.scalar_tensor_tensor(out=self.FB(), in0=ps, scalar=self.vcol(("b_glu", l), oc), in1=self.FA(),
                                                             op0=ALU.add, op1=ALU.mult), reads=psres + ["VEC"] + FAR, writes=FBR)
                hv = self.HT[:, oc * L:(oc + 1) * L].rearrange("p (c j) -> p j c", j=8)
                S.op("dve", lambda e: e.tensor_tensor(hv, hv, self.FB().rearrange("p (j c) -> p j c", c=256), ALU.add),
                     reads=FBR + [("H", oc)], writes=[("H", oc)])
            for col0, ev in ((D + oc * 128, evac_zb), (oc * 128, evac_za)):
                jobs.append(dict(w2d=self.w_glu_d[l], k0=0, KC=8, col0=col0,
                                 rhs=lambda kc, tt: self.BIGA[:, kc * L + tt * 512:kc * L + (tt + 1) * 512],
                                 rhs_res=lambda kc: [("A", kc)], evac=ev))
        self.run_dense(jobs)

    def load_w(self, w2d, k0, KC, col0):
        S = self.S
        sl = self.wslot
        self.wslot = (self.wslot + 1) % self.NSLOT
        src = w2d[k0:k0 + KC * 128, col0:col0 + 128].rearrange("(kc p) c -> p kc c", p=128)
        S.dma("pool", self.WBF(sl)[:, 0:KC * 128].rearrange("p (kc c) -> p kc c", c=128), src, writes=self.wres(sl))
        return sl

    def load_rope(self):
        S = self.S
        S.dma("sp", self.FB(), self.rope_d[:, 0:L], writes=[("FB", t) for t in range(4)])
        S.dma("sp", self.FC(), self.rope_d[:, L:2 * L], writes=[("FC", t) for t in range(4)])

    def qk_unit(self, sl, hf, gcol):
        S = self.S
        u = hf
        c0, c1 = hf * 1024, (hf + 1) * 1024
        FAR = [("FA", 2 * hf), ("FA", 2 * hf + 1)]
        FBR = [("FB", 2 * hf), ("FB", 2 * hf + 1)]
        FCR = [("FC", 2 * hf), ("FC", 2 * hf + 1)]
        rb = self.wres(sl)
        pb = 4 * u
        P = self.PS[:, pb * 512:(pb + 2) * 512]
        P2 = self.PS[:, (pb + 2) * 512:(pb + 4) * 512]
        RP = [("p", pb), ("p", pb + 1)]
        RP2 = [("p", pb + 2), ("p", pb + 3)]
        XG = self.FW[:, c0:c1]
        SQ = self.BIGB[:, u * 1024:(u + 1) * 1024]
        SQR = self.bres(u * 1024, (u + 1) * 1024)

        def st0():
            for kc in range(8):
                for t in range(2):
                    S.op("pe", lambda e, t=t, kc=kc: e.matmul(self.PS[:, (pb + t) * 512:(pb + t + 1) * 512], self.WBF(sl)[:, kc * 128:(kc + 1) * 128],
                                                            self.BIGA[:, kc * L + c0 + t * 512:kc * L + c0 + (t + 1) * 512], start=(kc == 0), stop=(kc == 7)),
                         reads=rb + [("A", kc)], writes=[("p", pb + t)])
            S.op("act", lambda e: e.activation(out=XG, in_=P, func=AF.Copy, scale=gcol), reads=RP + ["VEC"], writes=FAR)
            S.op("act", lambda e: e.activation(out=SQ, in_=P, func=AF.Square), reads=RP, writes=SQR)

        def st1():
            for t in range(2):
                S.op("pe", lambda e, t=t: e.matmul(self.PS[:, (pb + 2 + t) * 512:(pb + 3 + t) * 512], self.onesH(), SQ[:, t * 512:(t + 1) * 512],
                                                 start=True, stop=True), reads=SQR + ["CSTB"], writes=[("p", pb + 2 + t)])
            S.op("act", lambda e: e.activation(out=P2, in_=P2, func=AF.Ln, bias=self.eps_ap(), scale=1.0), reads=RP2 + ["EPS"], writes=RP2)
            S.op("act", lambda e: e.activation(out=P2, in_=P2, func=AF.Exp, scale=-0.5), reads=RP2, writes=RP2)
            pswap = self.cst("pswap")
            for t in range(2):
                S.op("pe", lambda e, t=t: e.matmul(self.PS[:, (pb + t) * 512:(pb + t + 1) * 512], pswap, XG[:, t * 512:(t + 1) * 512], start=True, stop=True),
                     reads=FAR + ["CST"], writes=[("p", pb + t)])
            S.op("dve", lambda e: e.tensor_tensor(XG, XG, self.FW[:, L + c0:L + c1], ALU.mult), reads=FAR + FBR, writes=FAR)
            S.op("dve", lambda e: e.tensor_tensor(P, P, self.FW[:, 2 * L + c0:2 * L + c1], ALU.mult), reads=RP + FCR, writes=RP)
            S.op("dve", lambda e: e.tensor_tensor(XG, XG, P, ALU.add), reads=FAR + RP, writes=FAR)
            S.op("dve", lambda e: e.tensor_tensor(XG, XG, P2, ALU.mult), reads=FAR + RP2, writes=FAR)
        return st0, st1, (XG, FAR, P2, RP2)

    @staticmethod
    def run_staged(units):
        nst = max(len(u) for u in units)
        for it in range(len(units) + nst - 1):
            for k in range(nst - 1, -1, -1):
                i = it - k
                if 0 <= i < len(units) and k < len(units[i]):
                    units[i][k]()

    ATT_SMALL = ["KMEAN", "G0", "G1", "RINV", "NSB"] + [("ACC", i) for i in range(4)] + [("M8", i) for i in range(8)]
    S5_SMALL = ["A1", "A2", "TT1", "PP", "PWRN"] + [("SR", k_) for k_ in range(3)] + ["sm%d" % i_ for i_ in range(7)]
    S5S_ALL = ["s5raw", "s5dt", "s5ar", "s5ai", "s5mag", "s5ang", "s5nt", "s5c9", "s5pwr", "s5pwi", "s5cr", "s5ci"]

    def kv_phase(self):
        S = self.S
        S.op("dve", lambda e: e.memset(self.SMALL[:], 0.0), writes=self.ATT_SMALL + self.S5_SMALL)
        self.rms_rstd(self.FA(), "FA")
        self.make_xn("kv_norm", self.FA(), "FA")
        self.load_rope()
        KMEAN = self.SMALL[:, 0:64]
        KB = lambda k: self.BIGB[:, 2048 + k * 2048:2048 + (k + 1) * 2048]
        KBR = lambda k: self.bres(2048 + k * 2048, 2048 + (k + 1) * 2048)
        VHo = lambda k: 6144 + k * 3072
        VH = lambda k: self.BIGB[:, VHo(k):VHo(k) + 2064]
        VHR = lambda k: self.bres(VHo(k), VHo(k) + 2064)
        for k in range(2):
            S.op("pool", lambda e, k=k: e.memset(VH(k).rearrange("p (t c) -> p t c", c=129)[:, :, 128:129], 1.0), writes=VHR(k))
        slk = {}
        slv = {}
        kunits = []
        for hd in range(NH):
            kp = hd % 2
            for hf in range(2):
                hold = {}

                def st0(hd=hd, hf=hf, hold=hold):
                    if hf == 0:
                        if hd == 0:
                            slk[0] = self.load_w(self.w_kv_d, 0, 8, 0)
                            slv[0] = self.load_w(self.w_kv_d, 0, 8, D)
                        if hd + 1 < NH:
                            slk[hd + 1] = self.load_w(self.w_kv_d, 0, 8, (hd + 1) * 128)
                            slv[hd + 1] = self.load_w(self.w_kv_d, 0, 8, D + (hd + 1) * 128)
                    a, b, h = self.qk_unit(slk[hd], hf, self.vcol("k_norm"))
                    hold["st1"] = b
                    hold["h"] = h
                    a()

                def st1(hold=hold):
                    hold["st1"]()

                def st2(hd=hd, hf=hf, kp=kp, hold=hold):
                    XG, FAR, P2, RP2 = hold["h"]
                    S.op("dve", lambda e: e.reduce_sum(out=KMEAN[:, hd * 8 + hf * 4:hd * 8 + hf * 4 + 4],
                                                      in_=XG.rearrange("p (n t) -> p n t", t=256), axis=AX.X),
                         reads=FAR, writes=["KMEAN"])
                    S.op("act", lambda e: e.activation(out=KB(kp)[:, hf * 1024:(hf + 1) * 1024], in_=XG, func=AF.Copy),
                         reads=FAR, writes=self.bres(2048 + kp * 2048 + hf * 1024, 2048 + kp * 2048 + (hf + 1) * 1024))
                    if hf == 1:
                        S.dma("sp", self.kT_s[hd], KB(kp), reads=KBR(kp), writes=[("kT_s", hd)])
                kunits.append([st0, st1, st2])
            for hf in range(2):
                def vunit(hd=hd, hf=hf, kp=kp):
                    sl = slv[hd]
                    rb = self.wres(sl)
                    VH3 = VH(kp).rearrange("p (t c) -> p t c", c=129)
                    pb = 4 * hf
                    for t8 in range(8):
                        t16 = hf * 8 + t8
                        for kc in range(8):
                            S.op("pe", lambda e, t16=t16, t8=t8, kc=kc: e.matmul(
                                self.PS[:, pb * 512 + t8 * 128:pb * 512 + (t8 + 1) * 128], self.BIGA[:, kc * L + t16 * 128:kc * L + (t16 + 1) * 128],
                                self.WBF(sl)[:, kc * 128:(kc + 1) * 128], start=(kc == 0), stop=(kc == 7)),
                                reads=rb + [("A", kc)], writes=[("p", pb + t8 // 4)])
                    S.op("act", lambda e: e.activation(out=VH3[:, hf * 8:(hf + 1) * 8, 0:128],
                                                       in_=self.PS[:, pb * 512:(pb + 2) * 512].rearrange("p (t c) -> p t c", c=128), func=AF.Copy),
                         reads=[("p", pb), ("p", pb + 1)], writes=VHR(kp))
                    if hf == 1:
                        S.dma("sp", self.V_s[hd], VH(kp), reads=VHR(kp), writes=[("V_s", hd)])
                kunits.append([vunit])
        self.run_staged(kunits)
        S.op("dve", lambda e: e.tensor_scalar(KMEAN, KMEAN, 1.0 / 256, None, ALU.mult), reads=["KMEAN"], writes=["KMEAN"])

    def moba_layer(self, j):
        S = self.S
        self.rms_rstd(self.FA(), "FA")
        self.make_xn(("b_norm", j), self.FA(), "FA")
        self.load_rope()
        KMEAN = self.SMALL[:, 0:64]
        GT = lambda hf: self.SMALL[:, 64 + 64 * hf:128 + 64 * hf]
        M8 = lambda i: self.SMALL[:, 320 + 8 * i:328 + 8 * i]
        RINV = self.SMALL[:, 192:196]
        ACC = lambda i: self.SMALL[:, 384 + 129 * i:384 + 129 * (i + 1)]
        SELALL = self.S5S[:, 0:1024]
        futmask = self.cst("futmask")
        scale = 1.0 / math.sqrt(128.0)
        S.op("dve", lambda e: e.memset(SELALL, 0.0), writes=self.S5S_ALL + ["SELALL"])
        QB = lambda k: self.BIGB[:, 2048 + k * 2048:2048 + (k + 1) * 2048]
        QBR = lambda k: self.bres(2048 + k * 2048, 2048 + (k + 1) * 2048)
        slq = {}
        qunits = []
        ownm = self.cst("ownmask")
        NSB = self.SMALL[0:8, 384:896].bitcast(BF16)
        for hd in range(NH):
            kp = hd % 2
            for hf in range(2):
                def st_load(hd=hd, hf=hf):
                    if hf == 0:
                        if hd == 0:
                            slq[0] = self.load_w(self.w_q_d[j], 0, 8, 0)
                        if hd + 1 < NH:
                            slq[hd + 1] = self.load_w(self.w_q_d[j], 0, 8, (hd + 1) * 128)
                hold = {}

                def st0(hd=hd, hf=hf, hold=hold, st_load=st_load):
                    st_load()
                    a, b, h = self.qk_unit(slq[hd], hf, self.vcol(("q_norm", j)))
                    hold["st1"] = b
                    hold["h"] = h
                    a()

                def st1(hold=hold):
                    hold["st1"]()

                def st2(hd=hd, hf=hf, kp=kp, hold=hold):
                    XG, FAR, P2, RP2 = hold["h"]
                    S.op("act", lambda e: e.activation(out=QB(kp)[:, hf * 1024:(hf + 1) * 1024], in_=XG, func=AF.Copy),
                         reads=FAR, writes=self.bres(2048 + kp * 2048 + hf * 1024, 2048 + kp * 2048 + (hf + 1) * 1024))
                    for q8 in range(8):
                        S.op("pe", lambda e, q8=q8: e.matmul(P2[:, q8 * 8:(q8 + 1) * 8], XG[:, q8 * 128:(q8 + 1) * 128],
                                                           KMEAN[:, hd * 8:(hd + 1) * 8], start=True, stop=True),
                             reads=FAR + ["KMEAN"], writes=[RP2[0]])
                    G = GT(hf)
                    gres = "G%d" % hf
                    S.op("dve", lambda e: e.tensor_tensor(G, P2[:, 0:64], futmask[:, hf * 64:(hf + 1) * 64], ALU.add),
                         reads=[RP2[0], "CST"], writes=[gres])
                    selh = SELALL[:, hd * 128 + hf * 64:hd * 128 + (hf + 1) * 64]
                    if hf == 0:
                        S.op("dve", lambda e: e.tensor_scalar(selh[:, 16:64], G[:, 16:64], -1e29, None, ALU.is_gt),
                             reads=[gres], writes=["SELALL"])
                    else:
                        for q8 in range(8):
                            S.op("dve", lambda e, q8=q8: e.max(out=M8(q8), in_=G[:, q8 * 8:(q8 + 1) * 8]), reads=[gres], writes=[("M8", q8)])
                            S.op("dve", lambda e, q8=q8: e.tensor_scalar(selh[:, q8 * 8:(q8 + 1) * 8], G[:, q8 * 8:(q8 + 1) * 8],
                                                                        M8(q8)[:, 2:3], None, ALU.is_ge),
                                 reads=[gres, ("M8", q8)], writes=["SELALL"])
                    S.op("dve", lambda e: e.tensor_tensor(selh, selh, ownm[:, hf * 64:(hf + 1) * 64], ALU.add),
                         reads=["SELALL", "CST"], writes=["SELALL"])

                def st3(hd=hd, hf=hf, kp=kp, hold=hold):
                    XG, FAR, P2, RP2 = hold["h"]
                    selh = SELALL[:, hd * 128 + hf * 64:hd * 128 + (hf + 1) * 64]
                    for q8 in range(8):
                        S.op("pe", lambda e, q8=q8: e.transpose(P2[0:8, q8 * 128:(q8 + 1) * 128], selh[:, q8 * 8:(q8 + 1) * 8], self.cst("ident")),
                             reads=["SELALL", "CST"], writes=[RP2[q8 // 4]])
                    S.op("dve", lambda e: e.tensor_scalar(NSB, P2[0:8, 0:1024], -1.0, 30000.0, ALU.add, ALU.mult), reads=RP2, writes=["NSB"])
                    S.dma("sp", self.NS_s[hd][:, hf * 1024:(hf + 1) * 1024], NSB, reads=["NSB"], writes=[("NS_s", hd, hf)])
                    if hf == 1:
                        S.dma("sp", self.Q_s[hd], QB(kp), reads=QBR(kp), writes=[("Q_s", hd)])
                qunits.append([st0, st1, st2, st3])
        self.run_staged(qunits)

        QBF = lambda k: self.BIGB[:, k * 2048:(k + 1) * 2048]
        KTH = lambda k: self.BIGB[:, 4096 + k * 2048:4096 + (k + 1) * 2048]
        VHo = lambda k: 8192 + k * 2064
        VH = lambda k: self.BIGB[:, VHo(k):VHo(k) + 2064]
        ET = lambda i: self.BIGB[:, 12320 + 512 * i:12320 + 512 * (i + 1)]
        ETR = lambda i: self.bres(12320 + 512 * i, 12320 + 512 * (i + 1))
        OTH = self.BIGB[:, 14368:14368 + L]
        OTHR = self.bres(14368, 14368 + L)
        RQ = lambda k: [("cQ", k)]
        RK = lambda k: [("cK", k)]
        RV = lambda k: [("cV", k)]
        OPR = lambda par, i: self.PS[:, (2 + 2 * par + i // 2) * 512 + (i % 2) * 256:(2 + 2 * par + i // 2) * 512 + (i % 2) * 256 + 129]
        opres = lambda par, i: ("p", 2 + 2 * par + i // 2)

        def load_head(hd):
            k = hd % 2
            S.dma("sp", QBF(k), self.Q_s[hd], reads=[("Q_s", hd)], writes=RQ(k))
            S.dma("sp", KTH(k), self.kT_s[hd], reads=[("kT_s", hd)], writes=RK(k))
            S.dma("sp", VH(k), self.V_s[hd], reads=[("V_s", hd)], writes=RV(k))

        DEPTH = 2
        NET = 6
        ET = lambda i: self.BIGB[:, 12320 + 512 * i:12320 + 512 * (i + 1)]
        ETR = lambda i: [("cE", i)]
        OTH = self.TMP2[:].bitcast(BF16)
        OTHR = ["TMP2"]
        NEGSEL = lambda k: self.FW[0:8, k * 1024:(k + 1) * 1024].bitcast(BF16)
        NSR = lambda k: [("FA", 2 * k), ("FA", 2 * k + 1)]
        RB = lambda qp: self.FW[:, L + qp * 512:L + (qp + 1) * 512]
        RBR = lambda qp: [("FB", qp)]
        EONE = self.S5S[0:8, 1024:1536].bitcast(BF16)
        identf = self.cst("ident")
        CORE_NAMES = [("cQ", 0), ("cQ", 1), ("cK", 0), ("cK", 1), ("cV", 0), ("cV", 1)] + [("cE", i) for i in range(NET)]
        S.op("dve", lambda e: e.tensor_copy(EONE.rearrange("p (n c) -> p n c", c=128), identf[0:8, 0:8].unsqueeze(2).broadcast_to([8, 8, 128])),
             reads=["CST"], writes=["EONE", "TMP"] + self.bres(0, 16512) + CORE_NAMES)
        state = dict(st=0, et=0)
        tasks = []

        def load_head(hd):
            k = hd % 2
            S.dma("sp", QBF(k), self.Q_s[hd], reads=[("Q_s", hd)], writes=RQ(k))
            S.dma("sp", KTH(k), self.kT_s[hd], reads=[("kT_s", hd)], writes=RK(k))
            S.dma("sp", VH(k), self.V_s[hd], reads=[("V_s", hd)], writes=RV(k))
            S.dma("sp", NEGSEL(k), self.NS_s[hd], reads=[("NS_s", hd, 0), ("NS_s", hd, 1)], writes=NSR(k))

        def mk_block(hd, Q, n, qp):
            hk = hd % 2
            VH3 = VH(hk).rearrange("p (t c) -> p t c", c=129)
            info = {}
            lastkt = 4 * Q + 3

            def s1():
                ets = {}
                for kt in (2 * n, 2 * n + 1):
                    c0 = max(4 * Q, kt) - 4 * Q
                    if c0 > 3:
                        continue
                    sb_ = state["st"] % 4
                    state["st"] += 1
                    eb_ = state["et"] % NET
                    state["et"] += 1
                    ST = self.PS[:, sb_ * 512:(sb_ + 1) * 512]
                    diag = kt >= 4 * Q
                    selm = n < 2 * Q + 1
                    S.op("pe", lambda e, ST=ST, c0=c0, fin=(not diag and not selm), kk_=KTH(hk)[:, kt * 128:(kt + 1) * 128],
                         qq_=QBF(hk)[:, Q * 512 + c0 * 128:(Q + 1) * 512]: e.matmul(
                        ST[:, c0 * 128:512], kk_, qq_, start=True, stop=fin), reads=RQ(hk) + RK(hk), writes=[("p", sb_)])
                    if selm:
                        S.op("pe", lambda e, ST=ST, c0=c0, fin=(not diag), en=EONE[:, n * 128:(n + 1) * 128],
                             ns=NEGSEL(hk)[:, Q * 512 + c0 * 128:(Q + 1) * 512]: e.matmul(ST[:, c0 * 128:512], en, ns, start=False, stop=fin),
                             reads=["EONE"] + NSR(hk), writes=[("p", sb_)])
                    if diag:
                        S.op("pe", lambda e, ST=ST, c0=c0: e.matmul(ST[:, c0 * 128:(c0 + 1) * 128], self.identB(), self.cmaskB(),
                                                                   start=False, stop=True), reads=["CSTB"], writes=[("p", sb_)])
                    et = ET(eb_)
                    S.op("act", lambda e, ST=ST, et=et, c0=c0: e.activation(out=et[:, c0 * 128:512], in_=ST[:, c0 * 128:512], func=AF.Exp, scale=scale),
                         reads=[("p", sb_)], writes=ETR(eb_))
                    ets[kt] = (et, eb_, c0)
                info["ets"] = ets

            def s2():
                if Q == 0 and n == 0 and hd + 1 < NH:
                    load_head(hd + 1)
                for kt in sorted(info["ets"]):
                    et, eb_, c0 = info["ets"][kt]
                    ob, rbk = 4 + qp, 6 + qp
                    S.op("pe", lambda e, et=et, c0=c0, vv=VH3[:, kt, 0:128], a=(kt == 0), z=(kt == lastkt), ob=ob: e.matmul(
                        self.PS[:, ob * 512 + c0 * 128:(ob + 1) * 512], vv, et[:, c0 * 128:512], start=a, stop=z),
                        reads=ETR(eb_) + RV(hk), writes=[("p", ob)])
                    S.op("pe", lambda e, et=et, c0=c0, a=(kt == 0), z=(kt == lastkt), rbk=rbk: e.matmul(
                        self.PS[:, rbk * 512 + c0 * 128:(rbk + 1) * 512], self.onesH(), et[:, c0 * 128:512], start=a, stop=z),
                        reads=ETR(eb_) + ["CSTB"], writes=[("p", rbk)])
            return s1, s2

        def mk_qend(hd, Q, qp):
            def s1():
                pass

            def s2():
                ob, rbk = 4 + qp, 6 + qp
                R = RB(qp)
                S.op("act", lambda e: e.activation(out=R, in_=self.PS[:, rbk * 512:(rbk + 1) * 512], func=AF.Ln, scale=128.0),
                     reads=[("p", rbk)], writes=RBR(qp))
                S.op("act", lambda e: e.activation(out=R, in_=R, func=AF.Exp, scale=-1.0), reads=RBR(qp), writes=RBR(qp))
                S.op("dve", lambda e: e.tensor_tensor(OTH[:, Q * 512:(Q + 1) * 512], self.PS[:, ob * 512:(ob + 1) * 512], R, ALU.mult),
                     reads=[("p", ob)] + RBR(qp), writes=OTHR)
                if Q == 3:
                    S.dma("sp", self.OT_s[hd], OTH, reads=OTHR, writes=[("OT_s", hd)])
            return s1, s2

        load_head(0)
        qcount = 0
        for hd in range(NH):
            for Q in range(4):
                qp = qcount % 2
                qcount += 1
                for n in range(2 * Q + 2):
                    tasks.append(mk_block(hd, Q, n, qp))
                tasks.append(mk_qend(hd, Q, qp))
        for idx in range(len(tasks) + DEPTH):
            if idx < len(tasks):
                tasks[idx][0]()
            if idx - DEPTH >= 0:
                tasks[idx - DEPTH][1]()
        S.op("dve", lambda e: e.memset(self.SMALL[:, 192:200], 1.0), reads=["EONE"], writes=["TMP"] + self.bres(0, 16512) + CORE_NAMES)
        for hd in range(NH):
            S.dma("sp", self.BIGA[:, hd * L:(hd + 1) * L], self.OT_s[hd], reads=[("OT_s", hd)], writes=[("A", hd)])
        jobs = []
        for oc in range(NFC):
            def evac_o(ps, psres, oc=oc):
                S.op("dve", lambda e: e.tensor_tensor(self.HT[:, oc * L:(oc + 1) * L], self.HT[:, oc * L:(oc + 1) * L], ps, ALU.add),
                     reads=psres + [("H", oc)], writes=[("H", oc)])
            jobs.append(dict(w2d=self.w_o_d[j], k0=0, KC=8, col0=oc * 128,
                             rhs=lambda kc, tt: self.BIGA[:, kc * L + tt * 512:kc * L + (tt + 1) * 512],
                             rhs_res=lambda kc: [("A", kc)], evac=evac_o))
        self.run_dense(jobs)

    def build(self):
        with ExitStack() as es:
            self.declare(es)
            self.EPS_T = es.enter_context(self.nc.sbuf_tensor("EPS_T", [128, 2], F32))
            self.S.op("pool", lambda e: e.memset(self.EPS_T[:], EPS), writes=["EPS"])
            self.prologue()
            self.body()
            self.epilogue()
            self.S.emit(self.sems, self.dsems)
        return self.nc

    def body(self):
        st = self.stop
        if st == "ffn_only":
            self.ffn(0)
            return
        self.s5_layer(0)
        if st == "mix0":
            return
        self.ffn_hook = lambda: self.s5_layer(1, "params")
        self.ffn(0)
        if st == "ffn0":
            return
        self.s5_layer(1, "main")
        if st == "mix1":
            return
        self.ffn(1)
        if st == "ffn1":
            return
        self.kv_phase()
        self.moba_layer(0)
        if st == "mix2":
            return
        self.ffn(2)
        if st == "ffn2":
            return
        self.moba_layer(1)
        if st == "mix3":
            return
        self.ffn(3)


def _consts():
    c = np.zeros((128, NCST), np.float32)
    p = np.arange(128)
    c[:, CC["ident"]:CC["ident"] + 128] = np.eye(128, dtype=np.float32)
    sw = np.zeros((128, 128), np.float32)
    sw[(p + 64) % 128, p] = 1.0
    c[:, CC["pswap"]:CC["pswap"] + 128] = sw
    c[:, CC["maskQ"]:CC["maskQ"] + 128] = (p[:, None] // 32 == p[None, :] // 32).astype(np.float32)
    c[:, CC["onesD"]:CC["onesD"] + 128] = 1.0 / D
    c[:, CC["onesH"]:CC["onesH"] + 128] = 1.0 / 128
    c[:, CC["cmask"]:CC["cmask"] + 128] = np.where(p[:, None] > p[None, :], NEG, 0.0)
    fm = np.zeros((16, 8), np.float32)
    for qt in range(16):
        fm[qt, (qt // 2):] = -1e30
    c[:, CC["futmask"]:CC["futmask"] + 128] = fm.reshape(1, 128)
    om = np.zeros((16, 8), np.float32)
    for qt in range(16):
        om[qt, qt // 2] = 1.0
    c[:, CC["ownmask"]:CC["ownmask"] + 128] = om.reshape(1, 128)
    rm = np.zeros((128, 4), np.float32)
    for v in range(2):
        rm[:, v] = ((p // 32) % 2 == v)
        rm[:, 2 + v] = -rm[:, v]
    c[:, CC["rowmask"]:CC["rowmask"] + 4] = rm
    c[:, CC["kk"]:CC["kk"] + 9] = np.arange(9, dtype=np.float32)[None, :]
    c[:, CC["kk64"]:CC["kk64"] + 64] = (8.0 * np.arange(1, 65, dtype=np.float32))[None, :]
    return c


def _rope():
    half = 64
    inv = (10000.0 ** (-np.arange(half, dtype=np.float32) * 2.0 / 128)).astype(np.float32)
    ang = np.arange(L, dtype=np.float32)[:, None] * inv[None, :]
    cos, sin = np.cos(ang).T.astype(np.float32), np.sin(ang).T.astype(np.float32)
    r = np.zeros((128, 2 * L), np.float32)
    r[0:64, 0:L] = cos; r[64:128, 0:L] = cos
    r[0:64, L:] = -sin; r[64:128, L:] = sin
    return r


def _fm(v):
    return np.ascontiguousarray(v.reshape(-1, 128).T)


def _host_prep(inp):
    vec = np.zeros((128, NV), np.float32)
    for l in range(2):
        vec[:, VC[("a_norm", l)]:VC[("a_norm", l)] + 8] = _fm(inp["a_norm"][l])
        vec[:, VC[("s5_D", l)]:VC[("s5_D", l)] + 8] = _fm(inp["s5_D"][l])
        vec[:, VC[("b_glu", l)]:VC[("b_glu", l)] + 16] = _fm(inp["b_glu"][l])
    vec[:, VC["kv_norm"]:VC["kv_norm"] + 8] = _fm(inp["kv_norm"])
    vec[:, VC["k_norm"]] = inp["k_norm"]
    for j in range(2):
        vec[:, VC[("b_norm", j)]:VC[("b_norm", j)] + 8] = _fm(inp["b_norm"][j])
        vec[:, VC[("q_norm", j)]] = inp["q_norm"][j]
    for l in range(4):
        vec[:, VC[("ffn_norm", l)]:VC[("ffn_norm", l)] + 8] = _fm(inp["ffn_norm"][l])
        cw = inp["conv_w"][l].reshape(3, NUPC, 128).transpose(2, 0, 1).reshape(128, 3 * NUPC)
        vec[:, VC[("conv_w", l)]:VC[("conv_w", l)] + 3 * NUPC] = cw
        vec[:, VC[("conv_b", l)]:VC[("conv_b", l)] + NUPC] = _fm(inp["conv_b"][l])
    s5p = np.zeros((2, 128, 96 + 4096), np.float32)
    for l in range(2):
        s5p[l, :, 0:32] = inp["s5_A_re"][l].reshape(32, 128).T
        s5p[l, :, 32:64] = inp["s5_A_im"][l].reshape(32, 128).T
        s5p[l, :, 64:96] = np.repeat(inp["s5_log_dt"][l].reshape(32, 2), 64, axis=1).T
        for nm, off in (("s5_B_re", 0), ("s5_B_im", 1024)):
            Bm = np.zeros((2, 64, 32, 2, 16), np.float32)
            Bg = inp[nm][l].reshape(32, 2, 64, 16)
            for m in range(2):
                Bm[m, :, :, m, :] = Bg[:, m].transpose(1, 0, 2)
            s5p[l, :, 96 + off:96 + off + 1024] = Bm.reshape(128, 1024)
        for nm, off in (("s5_C_re", 2048), ("s5_C_im", 3072)):
            Cm = np.zeros((2, 64, 32, 2, 16), np.float32)
            Cg = inp[nm][l].reshape(32, 2, 16, 64)
            for m in range(2):
                Cm[m, :, :, m, :] = Cg[:, m].transpose(2, 0, 1)
            s5p[l, :, 96 + off:96 + off + 1024] = Cm.reshape(128, 1024)
    common = dict(vec=vec, cst=_consts(), rope=_rope(), s5p=s5p,
                  w_glu=np.ascontiguousarray(inp["w_glu"]), w_kv=np.ascontiguousarray(inp["w_kv"]),
                  w_q=np.ascontiguousarray(inp["w_q"]), w_o=np.ascontiguousarray(inp["w_o"]),
                  w_up=np.ascontiguousarray(inp["w_up"]), w_down=np.ascontiguousarray(inp["w_down"]))
    return common


def _x_to_dev(xb):
    return np.ascontiguousarray(xb.T.reshape(NFC, 128, L).transpose(1, 0, 2).reshape(128, NFC * L))


def _dev_to_x(o):
    return np.ascontiguousarray(o.reshape(128, NFC, L).transpose(1, 0, 2).reshape(D, L).T)


def run(inp, stop=None, ncores=8):
    inp = {k: np.asarray(v) for k, v in inp.items()}
    common = _host_prep(inp)
    b = Builder(stop=stop)
    nc = b.build()
    in_maps = []
    for c in range(ncores):
        m = dict(common)
        m["xT"] = _x_to_dev(inp["x"][c])
        in_maps.append(m)
    res = run_bass_kernel_spmd(nc, in_maps, core_ids=list(range(ncores)))
    return np.stack([_dev_to_x(res.results[c]["outT"]) for c in range(ncores)], axis=0)


def kernel(**inputs):
    return run(inputs, stop=None, ncores=8).astype(np.float32)
```

```python
import math
from contextlib import ExitStack

import numpy as np
import concourse.bass as bass
import concourse.mybir as mybir
from concourse.bass_utils import run_bass_kernel_spmd

F32 = mybir.dt.float32
BF16 = mybir.dt.bfloat16
ALU = mybir.AluOpType
AF = mybir.ActivationFunctionType
AX = mybir.AxisListType

L = 2048
D = 1024
NFC = 8
DFF = 2816
NUPC = 44
NACT = 22
NH = 8
EPS = 1e-6
MAGIC = 12582912.0
TWO_PI = 2.0 * math.pi
NEG = -30000.0
ENGS = ("pe", "act", "dve", "pool", "sp")

VC = {}
_c = 0
for _l in range(2):
    VC[("a_norm", _l)] = _c; _c += 8
    VC[("s5_D", _l)] = _c; _c += 8
    VC[("b_glu", _l)] = _c; _c += 16
VC["kv_norm"] = _c; _c += 8
VC["k_norm"] = _c; _c += 1
for _j in range(2):
    VC[("b_norm", _j)] = _c; _c += 8
    VC[("q_norm", _j)] = _c; _c += 1
for _l in range(4):
    VC[("ffn_norm", _l)] = _c; _c += 8
    VC[("conv_w", _l)] = _c; _c += 3 * NUPC
    VC[("conv_b", _l)] = _c; _c += NUPC
NV = _c
CC = {}
_c = 0
for _n, _w in (("ident", 128), ("pswap", 128), ("maskQ", 128), ("onesD", 128), ("onesH", 128),
               ("cmask", 128), ("futmask", 128), ("rowmask", 4), ("kk", 9), ("ownmask", 128), ("kk64", 64)):
    CC[_n] = _c; _c += _w
NCST = _c


class _Op:
    __slots__ = ("eng", "fn", "deps", "dma", "sig", "cnt", "sem", "semval", "semprev", "idx")


class Sched:
    def __init__(self, nc):
        self.nc = nc
        self.ops = []
        self.res = {}

    def op(self, eng, fn, reads=(), writes=(), dma=False):
        o = _Op()
        o.eng = eng; o.fn = fn; o.dma = dma; o.sig = False; o.idx = len(self.ops)
        deps = set()
        for r in reads:
            st = self.res.get(r)
            if st is not None and st[0] is not None:
                deps.add(st[0])
        for w in writes:
            st = self.res.get(w)
            if st is not None:
                if st[0] is not None:
                    deps.add(st[0])
                deps.update(st[1])
        for r in reads:
            st = self.res.setdefault(r, [None, []])
            st[1].append(o.idx)
        for w in writes:
            self.res[w] = [o.idx, []]
        deps.discard(o.idx)
        o.deps = deps
        self.ops.append(o)
        return o.idx

    def dma(self, eng, out, in_, reads=(), writes=(), **kw):
        return self.op(eng, lambda e: e.dma_start(out=out, in_=in_, **kw), reads, writes, dma=True)

    def emit(self, sems, dma_sems):
        ops = self.ops
        for o in ops:
            latest = {}
            ddeps = []
            for d in o.deps:
                od = ops[d]
                if od.dma:
                    ddeps.append(d)
                elif od.eng != o.eng or o.dma or o.eng != "pe":
                    if od.eng not in latest or latest[od.eng] < d:
                        latest[od.eng] = d
            o.deps = (latest, ddeps)
            for d in latest.values():
                ops[d].sig = True
        cnt = {e: 0 for e in ENGS}
        for o in ops:
            if o.dma:
                continue
            if o.sig:
                cnt[o.eng] += 1
            o.cnt = cnt[o.eng]
        semcount = [0] * len(dma_sems)
        k = 0
        for o in ops:
            if o.dma:
                o.sem = k % len(dma_sems)
                o.semprev = semcount[o.sem]
                semcount[o.sem] += 16
                o.semval = semcount[o.sem]
                k += 1
        per_eng = {e: [o for o in ops if o.eng == e] for e in ENGS}

        def run_engine(ename, eng):
            waited = {e: 0 for e in ENGS}
            dwaited = [0] * len(dma_sems)
            for o in per_eng[ename]:
                latest, ddeps = o.deps
                for d in sorted(ddeps):
                    od = ops[d]
                    if dwaited[od.sem] < od.semval:
                        eng.wait_ge(dma_sems[od.sem], od.semval)
                        dwaited[od.sem] = od.semval
                for en_, d in latest.items():
                    od = ops[d]
                    if waited[en_] < od.cnt:
                        eng.wait_ge(sems[en_], od.cnt)
                        waited[en_] = od.cnt
                if o.dma:
                    if o.semprev > 0 and dwaited[o.sem] < o.semprev:
                        eng.wait_ge(dma_sems[o.sem], o.semprev)
                        dwaited[o.sem] = o.semprev
                    o.fn(eng).then_inc(dma_sems[o.sem], 16)
                else:
                    ins = o.fn(eng)
                    if o.sig:
                        ins.then_inc(sems[ename], 1)
            last = {}
            for o in per_eng[ename]:
                if o.dma:
                    last[o.sem] = max(last.get(o.sem, 0), o.semval)
            for s_, v in last.items():
                if dwaited[s_] < v:
                    eng.wait_ge(dma_sems[s_], v)

        with self.nc.Block() as block:
            @block.tensor
            def _(e):
                run_engine("pe", e)

            @block.scalar
            def _(e):
                run_engine("act", e)

            @block.vector
            def _(e):
                run_engine("dve", e)

            @block.gpsimd
            def _(e):
                run_engine("pool", e)

            @block.sync
            def _(e):
                run_engine("sp", e)


class Builder:
    def __init__(self, stop=None):
        self.stop = stop
        self.nc = bass.Bass("TRN2", target_bir_lowering=False)
        self.S = Sched(self.nc)
        self.wslot = 0
        self.pshalf = 0

    def declare(self, es):
        nc = self.nc
        di = lambda n, s, dt=F32: nc.dram_tensor(n, s, dt, kind="ExternalInput").ap()
        self.xT_d = di("xT", [128, NFC * L])
        self.vec_d = di("vec", [128, NV])
        self.cst_d = di("cst", [128, NCST])
        self.rope_d = di("rope", [128, 2 * L])
        self.s5p_d = di("s5p", [2, 128, 96 + 4096])
        self.w_glu_d = di("w_glu", [2, D, 2 * D])
        self.w_kv_d = di("w_kv", [D, 2 * D])
        self.w_q_d = di("w_q", [2, D, D])
        self.w_o_d = di("w_o", [2, D, D])
        self.w_up_d = di("w_up", [4, D, 2 * DFF])
        self.w_down_d = di("w_down", [4, DFF, D])
        self.out_d = nc.dram_tensor("outT", [128, NFC * L], F32, kind="ExternalOutput").ap()
        self.kT_s = nc.dram_tensor("kT_s", [NH, 128, L], BF16, kind="Internal").ap()
        self.V_s = nc.dram_tensor("V_s", [NH, 128, 16 * 129], BF16, kind="Internal").ap()
        self.OT_s = nc.dram_tensor("OT_s", [NH, 128, L], BF16, kind="Internal").ap()
        self.Q_s = nc.dram_tensor("Q_s", [NH, 128, L], BF16, kind="Internal").ap()
        self.NS_s = nc.dram_tensor("NS_s", [NH, 8, L], BF16, kind="Internal").ap()

        sb = lambda n, s, dt=F32: es.enter_context(nc.sbuf_tensor(n, s, dt))
        self.HT = sb("HT", [128, NFC * L])
        self.BIGA = sb("BIGA", [128, NFC * L], BF16)
        self.BIGB = sb("BIGB", [128, 16512], BF16)
        self.FW = sb("FW", [128, 3 * L])
        self.TMP = sb("TMP", [128, 1024])
        self.TMP2 = sb("TMP2", [128, 1024])
        self.UNI = sb("UNI", [128, 6656])
        self.VEC = sb("VEC", [128, NV])
        self.CST = sb("CST", [128, NCST])
        self.CSTB = sb("CSTB", [128, 4 * 128], BF16)
        self.S5S = sb("S5S", [128, 2048])
        self.SMALL = sb("SMALL", [128, 1024])
        self.PS = es.enter_context(nc.psum_tensor("PS", [128, 4096], F32))
        self.sems = {e: es.enter_context(nc.semaphore("s_" + e)) for e in ENGS}
        self.dsems = [es.enter_context(nc.semaphore("d%d" % i)) for i in range(40)]

    def FA(self):
        return self.FW[:, 0:L]

    def FB(self):
        return self.FW[:, L:2 * L]

    def FC(self):
        return self.FW[:, 2 * L:3 * L]

    def cst(self, name, w=128):
        c = CC[name]
        return self.CST[:, c:c + w]

    def vcol(self, key, i=0):
        c = VC[key] + i
        return self.VEC[:, c:c + 1]

    NSLOT = 8

    def WBF(self, i):
        return self.UNI[:, i * 512:(i + 1) * 512].bitcast(BF16)

    @staticmethod
    def wres(i):
        return ["h%d" % i]

    @staticmethod
    def bres(lo, hi):
        return [("B", k) for k in range(lo // 1024, (hi - 1) // 1024 + 1)]

    def prologue(self):
        S = self.S
        S.dma("sp", self.CST[:], self.cst_d, writes=["CST"])
        S.dma("sp", self.VEC[:], self.vec_d, writes=["VEC"])
        for fc in range(NFC):
            S.dma("act" if fc % 2 else "sp", self.HT[:, fc * L:(fc + 1) * L], self.xT_d[:, fc * L:(fc + 1) * L],
                  writes=[("H", fc)])
        for i, n in enumerate(("ident", "onesD", "onesH", "cmask")):
            src = self.cst(n)
            dst = self.CSTB[:, i * 128:(i + 1) * 128]
            S.op("dve", lambda e, d=dst, s=src: e.tensor_copy(d, s), reads=["CST"], writes=["CSTB"])

    def identB(self):
        return self.CSTB[:, 0:128]

    def onesD(self):
        return self.CSTB[:, 128:256]

    def onesH(self):
        return self.CSTB[:, 256:384]

    def cmaskB(self):
        return self.CSTB[:, 384:512]

    def epilogue(self):
        S = self.S
        for fc in range(NFC):
            S.dma("act" if fc % 2 else "sp", self.out_d[:, fc * L:(fc + 1) * L], self.HT[:, fc * L:(fc + 1) * L],
                  reads=[("H", fc)], writes=[("OUT", fc)])

    def rms_rstd(self, dst, dst_res):
        S = self.S

        def stA(tt):
            sq = self.BIGB[:, (tt % 2) * 4096:(tt % 2 + 1) * 4096]
            sqres = self.bres((tt % 2) * 4096, (tt % 2 + 1) * 4096)
            hin = self.HT[:].rearrange("p (f t) -> p f t", t=L)[:, :, tt * 512:(tt + 1) * 512]
            S.op("act", lambda e, o=sq, i=hin: e.activation(out=o.rearrange("p (f t) -> p f t", t=512), in_=i, func=AF.Square),
                 reads=[("H", fc) for fc in range(NFC)], writes=sqres)

        def stB(tt):
            sq = self.BIGB[:, (tt % 2) * 4096:(tt % 2 + 1) * 4096]
            sqres = self.bres((tt % 2) * 4096, (tt % 2 + 1) * 4096)
            bank = 4 + (tt % 2)
            ps = self.PS[:, bank * 512:(bank + 1) * 512]
            for fc in range(NFC):
                S.op("pe", lambda e, o=ps, r=sq[:, fc * 512:(fc + 1) * 512], a=(fc == 0), z=(fc == NFC - 1):
                     e.matmul(o, self.onesD(), r, start=a, stop=z),
                     reads=sqres + ["CSTB"], writes=[("p", bank)])
            d = dst[:, tt * 512:(tt + 1) * 512]
            S.op("act", lambda e, o=d, i=ps: e.activation(out=o, in_=i, func=AF.Ln, bias=self.eps_ap(), scale=1.0),
                 reads=[("p", bank), "EPS"], writes=[(dst_res, tt)])
            S.op("act", lambda e, o=d: e.activation(out=o, in_=o, func=AF.Exp, scale=-0.5), reads=[(dst_res, tt)], writes=[(dst_res, tt)])

        stA(0)
        stA(1)
        for tt in range(4):
            stB(tt)
            if tt + 2 < 4:
                stA(tt + 2)

    def eps_ap(self):
        return self.EPS_T[:, 0:1]

    def make_xn(self, gkey, rstd, rstd_res):
        S = self.S
        for tt in range(4):
            for fc in range(NFC):
                S.op("dve", lambda e, fc=fc, tt=tt: e.scalar_tensor_tensor(
                    out=self.BIGA[:, fc * L + tt * 512:fc * L + (tt + 1) * 512], in0=self.HT[:, fc * L + tt * 512:fc * L + (tt + 1) * 512],
                    scalar=self.vcol(gkey, fc), in1=rstd[:, tt * 512:(tt + 1) * 512], op0=ALU.mult, op1=ALU.mult),
                    reads=[("H", fc), "VEC", (rstd_res, tt)], writes=[("A", fc)])

    def run_dense(self, jobs):
        S = self.S
        n = len(jobs)
        slots = {}
        ahead = self.NSLOT - 2

        def load(i):
            if i >= n:
                return
            jb = jobs[i]
            slots[i] = self.load_w(jb["w2d"], jb["k0"], jb["KC"], jb["col0"])

        for i in range(min(ahead, n)):
            load(i)
        for i in range(n):
            load(i + ahead)
            jb = jobs[i]
            sl = slots[i]
            KC = jb["KC"]
            half = self.pshalf
            self.pshalf ^= 1
            rb = self.wres(sl)
            for kc in range(KC):
                for tt in range(4):
                    bank = half * 4 + tt
                    o = self.PS[:, bank * 512:(bank + 1) * 512]
                    S.op("pe", lambda e, o=o, w=self.WBF(sl)[:, kc * 128:(kc + 1) * 128], r=jb["rhs"](kc, tt), a=(kc == 0), z=(kc == KC - 1):
                         e.matmul(o, w, r, start=a, stop=z),
                         reads=rb + jb["rhs_res"](kc), writes=[("p", bank)])
            jb["evac"](self.PS[:, half * 2048:(half + 1) * 2048], [("p", half * 4 + t) for t in range(4)])

    def ffn(self, l):
        S = self.S
        self.rms_rstd(self.FA(), "FA")
        self.make_xn(("ffn_norm", l), self.FA(), "FA")
        if getattr(self, "ffn_hook", None):
            self.ffn_hook()
            self.ffn_hook = None
        groups = [(0, 6), (6, 12), (12, 17), (17, 22)]
        cw = VC[("conv_w", l)]
        cb = VC[("conv_b", l)]
        SG = self.BIGB[:, 12288:12288 + L]
        FBR = [("FB", t) for t in range(4)]
        FCR = [("FC", t) for t in range(4)]

        pending = []

        def conv_to(acc, accres, ps, psres, c):
            w = lambda k: self.VEC[:, cw + k * NUPC + c:cw + k * NUPC + c + 1]
            S.op("act", lambda e: e.activation(out=acc, in_=ps, func=AF.Identity, scale=w(2), bias=self.VEC[:, cb + c:cb + c + 1]),
                 reads=psres + ["VEC"], writes=accres)
            while pending:
                pending.pop(0)()
            S.op("dve", lambda e: e.scalar_tensor_tensor(out=acc[:, 1:L], in0=ps[:, 0:L - 1], scalar=w(1), in1=acc[:, 1:L],
                                                         op0=ALU.mult, op1=ALU.add),
                 reads=psres + ["VEC"] + accres, writes=accres)
            S.op("dve", lambda e: e.scalar_tensor_tensor(out=acc[:, 2:L], in0=ps[:, 0:L - 2], scalar=w(0), in1=acc[:, 2:L],
                                                         op0=ALU.mult, op1=ALU.add),
                 reads=psres + ["VEC"] + accres, writes=accres)

        ups, downs = [], []
        for (g0, g1) in groups:
            jobs = []
            for i in range(g0, g1):
                il = i - g0

                def evac_gate(ps, psres, i=i):
                    conv_to(self.FB(), FBR, ps, psres, i)
                    pending.append(lambda: S.op("act", lambda e: e.activation(out=SG, in_=self.FB(), func=AF.Silu), reads=FBR, writes=self.bres(12288, 14336)))

                def evac_val(ps, psres, i=i, il=il):
                    conv_to(self.FC(), FCR, ps, psres, NACT + i)
                    S.op("dve", lambda e: e.tensor_tensor(self.BIGB[:, il * L:(il + 1) * L], SG, self.FC(), ALU.mult),
                         reads=self.bres(12288, 14336) + FCR, writes=self.bres(il * L, (il + 1) * L))

                for col0, ev in ((i * 128, evac_gate), (DFF + i * 128, evac_val)):
                    jobs.append(dict(w2d=self.w_up_d[l], k0=0, KC=8, col0=col0,
                                     rhs=lambda kc, tt: self.BIGA[:, kc * L + tt * 512:kc * L + (tt + 1) * 512],
                                     rhs_res=lambda kc: [("A", kc)], evac=ev))
            ng = g1 - g0
            ups.append(jobs)
            jobs = []
            for oc in range(NFC):
                def evac_down(ps, psres, oc=oc):
                    S.op("dve", lambda e: e.tensor_tensor(self.HT[:, oc * L:(oc + 1) * L], self.HT[:, oc * L:(oc + 1) * L], ps, ALU.add),
                         reads=psres + [("H", oc)], writes=[("H", oc)])
                jobs.append(dict(w2d=self.w_down_d[l], k0=g0 * 128, KC=ng, col0=oc * 128,
                                 rhs=lambda kc, tt: self.BIGB[:, kc * L + tt * 512:kc * L + (tt + 1) * 512],
                                 rhs_res=lambda kc: self.bres(kc * L, (kc + 1) * L), evac=evac_down))
            downs.append(jobs)
        alljobs = list(ups[0])
        for g in range(len(groups)):
            if g + 1 < len(groups):
                alljobs.append(ups[g + 1][0])
            alljobs.extend(downs[g])
            if g + 1 < len(groups):
                alljobs.extend(ups[g + 1][1:])
        self.run_dense(alljobs)
        while pending:
            pending.pop(0)()

    def s5_layer(self, l, part="all"):
        S = self.S
        P = self.S5S
        skip = {"on": part == "main"}
        sl = lambda a, b: P[:, a:b]
        Are, Aim, ldt = sl(0, 32), sl(32, 64), sl(64, 96)
        dt, ar, ai = sl(96, 128), sl(128, 160), sl(160, 192)
        MAG, ANG, NT, C9, PWR, PWI = sl(192, 480), sl(480, 768), sl(768, 1056), sl(1056, 1344), sl(1344, 1632), sl(1632, 1920)
        cr, ci = sl(1920, 1952), sl(1952, 1984)
        sm = lambda i: self.SMALL[:, 640 + 32 * i:640 + 32 * (i + 1)]
        k3 = lambda ap: ap.rearrange("p (k g) -> p k g", g=32)
        kk = self.cst("kk", 9)
        kkb = kk.unsqueeze(2).broadcast_to([128, 9, 32])

        def dv(name, fn, reads, writes, eng="dve"):
            if skip["on"]:
                return
            S.op(eng, fn, reads=reads, writes=writes)

        if part != "main":
            S.dma("sp", P[:, 0:96], self.s5p_d[l][:, 0:96], writes=["s5raw"])
        FBR = [("FB", t) for t in range(4)]
        FCR = [("FC", t) for t in range(4)]
        if part != "params":
            S.dma("sp", self.FW[:, L:3 * L], self.s5p_d[l][:, 96:96 + 4096], writes=FBR + FCR)
        BMR, BMI = self.FW[:, L:L + 1024], self.FW[:, L + 1024:2 * L]
        CMR, CMI = self.FW[:, 2 * L:2 * L + 1024], self.FW[:, 2 * L + 1024:3 * L]
        g3 = lambda ap: ap.rearrange("p (g c) -> p g c", c=32)
        dv("dt", lambda e: e.activation(out=dt, in_=ldt, func=AF.Exp), ["s5raw"], ["s5dt"], "act")
        dv("ar", lambda e: e.tensor_tensor(ar, Are, dt, ALU.mult), ["s5raw", "s5dt"], ["s5ar"])
        dv("ai", lambda e: e.tensor_tensor(ai, Aim, dt, ALU.mult), ["s5raw", "s5dt"], ["s5ai"])
        dv("mag", lambda e: e.tensor_tensor(k3(MAG), ar.unsqueeze(1).broadcast_to([128, 9, 32]), kkb, ALU.mult), ["s5ar", "CST"], ["s5mag"])
        dv("mage", lambda e: e.activation(out=MAG, in_=MAG, func=AF.Exp), ["s5mag"], ["s5mag"], "act")
        dv("ang", lambda e: e.tensor_tensor(k3(ANG), ai.unsqueeze(1).broadcast_to([128, 9, 32]), kkb, ALU.mult), ["s5ai", "CST"], ["s5ang"])

        def sin_of(src, srcres, tmp, tmpres):
            dv("n1", lambda e: e.tensor_scalar(tmp, src, 1.0 / TWO_PI, MAGIC, ALU.mult, ALU.add), [srcres], [tmpres])
            dv("n2", lambda e: e.tensor_scalar(tmp, tmp, MAGIC, None, ALU.subtract), [tmpres], [tmpres])
            dv("n3", lambda e: e.scalar_tensor_tensor(out=tmp, in0=tmp, scalar=-TWO_PI, in1=src, op0=ALU.mult, op1=ALU.add), [tmpres, srcres], [tmpres])
            dv("n4", lambda e: e.activation(out=tmp, in_=tmp, func=AF.Sin), [tmpres], [tmpres], "act")

        sin_of(ANG, "s5ang", NT, "s5nt")
        dv("pwi", lambda e: e.tensor_tensor(PWI, MAG, NT, ALU.mult), ["s5mag", "s5nt"], ["s5pwi"])
        dv("angc", lambda e: e.tensor_scalar(C9, ANG, math.pi / 2, None, ALU.add), ["s5ang"], ["s5c9"])
        sin_of(C9, "s5c9", NT, "s5nt")
        dv("pwr", lambda e: e.tensor_tensor(PWR, MAG, NT, ALU.mult), ["s5mag", "s5nt"], ["s5pwr"])
        nr, den, x1, x2, rden, y1, y2 = sm(0), sm(1), sm(2), sm(3), sm(4), sm(5), sm(6)
        pw1r, pw1i = PWR[:, 32:64], PWI[:, 32:64]
        dv("nr", lambda e: e.tensor_scalar(nr, pw1r, -1.0, None, ALU.add), ["s5pwr"], ["sm0"])
        dv("den", lambda e: e.tensor_tensor(den, Are, Are, ALU.mult), ["s5raw"], ["sm1"])
        dv("den2", lambda e: e.tensor_tensor(x1, Aim, Aim, ALU.mult), ["s5raw"], ["sm2"])
        dv("den3", lambda e: e.tensor_tensor(den, den, x1, ALU.add), ["sm1", "sm2"], ["sm1"])
        dv("rden", lambda e: e.reciprocal(rden, den), ["sm1"], ["sm4"])
        dv("x1", lambda e: e.tensor_tensor(x1, nr, Are, ALU.mult), ["sm0", "s5raw"], ["sm2"])
        dv("x2", lambda e: e.tensor_tensor(x2, pw1i, Aim, ALU.mult), ["s5pwi", "s5raw"], ["sm3"])
        dv("x3", lambda e: e.tensor_tensor(x1, x1, x2, ALU.add), ["sm2", "sm3"], ["sm2"])
        dv("cr", lambda e: e.tensor_tensor(cr, x1, rden, ALU.mult), ["sm2", "sm4"], ["s5cr"])
        dv("y1", lambda e: e.tensor_tensor(y1, pw1i, Are, ALU.mult), ["s5pwi", "s5raw"], ["sm5"])
        dv("y2", lambda e: e.tensor_tensor(y2, nr, Aim, ALU.mult), ["sm0", "s5raw"], ["sm6"])
        dv("y3", lambda e: e.tensor_tensor(y1, y1, y2, ALU.subtract), ["sm5", "sm6"], ["sm5"])
        dv("ci", lambda e: e.tensor_tensor(ci, y1, rden, ALU.mult), ["sm5", "sm4"], ["s5ci"])
        skip["on"] = False
        if part == "params":
            return
        crb = cr.unsqueeze(2).broadcast_to([128, 32, 32])
        cib = ci.unsqueeze(2).broadcast_to([128, 32, 32])
        T1, T2 = self.TMP[:], self.TMP2[:]
        R_BMR, R_BMI = [("FB", 0), ("FB", 1)], [("FB", 2), ("FB", 3)]
        R_CMR, R_CMI = [("FC", 0), ("FC", 1)], [("FC", 2), ("FC", 3)]
        dv("b1", lambda e: e.tensor_tensor(g3(T1), cib, g3(BMR), ALU.mult), ["s5ci"] + R_BMR, ["TMP"])
        dv("b2", lambda e: e.tensor_tensor(g3(T2), cib, g3(BMI), ALU.mult), ["s5ci"] + R_BMI, ["TMP2"])
        dv("b3", lambda e: e.tensor_tensor(g3(BMR), crb, g3(BMR), ALU.mult), ["s5cr"] + R_BMR, R_BMR)
        dv("b4", lambda e: e.tensor_tensor(BMR, BMR, T2, ALU.subtract), R_BMR + ["TMP2"], R_BMR)
        dv("b5", lambda e: e.tensor_tensor(g3(BMI), crb, g3(BMI), ALU.mult), ["s5cr"] + R_BMI, R_BMI)
        dv("b6", lambda e: e.tensor_tensor(BMI, BMI, T1, ALU.add), R_BMI + ["TMP"], R_BMI)

        FAR = [("FA", t) for t in range(4)]
        self.rms_rstd(self.FA(), "FA")
        for tt in range(4):
            for fc in range(NFC):
                S.op("dve", lambda e, fc=fc, tt=tt: e.scalar_tensor_tensor(
                    out=self.BIGA[:, fc * L:(fc + 1) * L].rearrange("p (j c) -> p j c", c=256)[:, :, tt * 64:(tt + 1) * 64],
                    in0=self.HT[:, fc * L + tt * 512:fc * L + (tt + 1) * 512].rearrange("p (c j) -> p j c", j=8),
                    scalar=self.vcol(("a_norm", l), fc),
                    in1=self.FA()[:, tt * 512:(tt + 1) * 512].rearrange("p (c j) -> p j c", j=8), op0=ALU.mult, op1=ALU.mult),
                    reads=[("H", fc), "VEC", ("FA", tt)], writes=[("A", fc)])
        B2S4 = self.BIGB[:, 0:16448].rearrange("p (r g c) -> p r g c", r=2, g=32)
        ALLC = [("b2S", c) for c in range(257)]
        BALL = self.bres(0, 16512)
        S.op("pool", lambda e: e.memset(B2S4[:, :, :, 0:1], 0.0), reads=[], writes=BALL + [("b2S", 0)])
        Zre, ZimN = self.FW[:, 0:1024], self.FW[:, 1024:2048]
        z4 = lambda ap: ap.rearrange("p (t q c) -> p t q c", t=8, q=4)
        R_ZR, R_ZI = [("FA", 0), ("FA", 1)], [("FA", 2), ("FA", 3)]
        PWR3, PWI3 = k3(PWR), k3(PWI)
        ZB = {0: (Zre, ZimN, R_ZR, R_ZI),
              1: (self.UNI[:, 2048:3072], self.UNI[:, 3072:4096], ["h4", "h5"], ["h6", "h7"])}

        PWRN = self.SMALL[:, 576:864]
        dv("pwrn", lambda e: e.tensor_scalar(PWRN, PWR, -1.0, None, ALU.mult), ["s5pwr"], self.S5_SMALL)
        PWRN3 = k3(PWRN)

        def zcompute(fc, zb=0):
            Zre, ZimN, R_ZR, R_ZI = ZB[zb]
            pwr_b = PWR3[:, 0:8, 4 * fc:4 * fc + 4].unsqueeze(3).broadcast_to([128, 8, 4, 32])
            pwrn_b = PWRN3[:, 0:8, 4 * fc:4 * fc + 4].unsqueeze(3).broadcast_to([128, 8, 4, 32])
            pwi_b = PWI3[:, 0:8, 4 * fc:4 * fc + 4].unsqueeze(3).broadcast_to([128, 8, 4, 32])
            bmr_b = g3(BMR)[:, 4 * fc:4 * fc + 4, :].unsqueeze(1).broadcast_to([128, 8, 4, 32])
            bmi_b = g3(BMI)[:, 4 * fc:4 * fc + 4, :].unsqueeze(1).broadcast_to([128, 8, 4, 32])
            E = "pool"
            dv("z1", lambda e: e.tensor_tensor(z4(Zre), pwr_b, bmr_b, ALU.mult), ["s5pwr"] + R_BMR, R_ZR, E)
            dv("z2", lambda e: e.tensor_tensor(z4(T1), pwi_b, bmi_b, ALU.mult), ["s5pwi"] + R_BMI, ["TMP"], E)
            dv("z3", lambda e: e.tensor_tensor(Zre, Zre, T1, ALU.subtract), R_ZR + ["TMP"], R_ZR, E)
            E = "pool"
            dv("z4", lambda e: e.tensor_tensor(z4(ZimN), pwrn_b, bmi_b, ALU.mult), ["PWRN"] + R_BMI, R_ZI, E)
            dv("z5", lambda e: e.tensor_tensor(z4(T2), pwi_b, bmr_b, ALU.mult), ["s5pwi"] + R_BMR, ["TMP2"], E)
            dv("z6", lambda e: e.tensor_tensor(ZimN, ZimN, T2, ALU.subtract), R_ZI + ["TMP2"], R_ZI, E)

        def zcompute_dve(fc):
            Zre, ZimN, R_ZR, R_ZI = ZB[0]
            pwr_b = PWR3[:, 0:8, 4 * fc:4 * fc + 4].unsqueeze(3).broadcast_to([128, 8, 4, 32])
            pwi_b = PWI3[:, 0:8, 4 * fc:4 * fc + 4].unsqueeze(3).broadcast_to([128, 8, 4, 32])
            bmr_b = g3(BMR)[:, 4 * fc:4 * fc + 4, :].unsqueeze(1).broadcast_to([128, 8, 4, 32])
            bmi_b = g3(BMI)[:, 4 * fc:4 * fc + 4, :].unsqueeze(1).broadcast_to([128, 8, 4, 32])
            PT = self.PS[:, 3072:4096]
            RPT = [("p", 6), ("p", 7)]
            dv("z1", lambda e: e.tensor_tensor(z4(Zre), pwr_b, bmr_b, ALU.mult), ["s5pwr"] + R_BMR, R_ZR, "pool")
            dv("z2", lambda e: e.tensor_tensor(z4(T1), pwi_b, bmi_b, ALU.mult), ["s5pwi"] + R_BMI, ["TMP"], "pool")
            dv("z3", lambda e: e.tensor_tensor(Zre, Zre, T1, ALU.subtract), R_ZR + ["TMP"], R_ZR, "pool")
            dv("z4", lambda e: e.tensor_tensor(z4(ZimN), pwr_b, bmi_b, ALU.mult), ["s5pwr"] + R_BMI, R_ZI)
            dv("z5", lambda e: e.tensor_tensor(z4(PT), pwi_b, bmr_b, ALU.mult), ["s5pwi"] + R_BMR, RPT)
            dv("z6", lambda e: e.scalar_tensor_tensor(out=ZimN, in0=ZimN, scalar=-1.0, in1=PT, op0=ALU.mult, op1=ALU.subtract),
               R_ZI + RPT, R_ZI)

        W2B = {0: (self.UNI[:, 0:2048].bitcast(BF16).rearrange("p (v t r c) -> p v t r c", v=2, t=8, r=2), ["h0", "h1", "h2", "h3"]),
               1: (self.UNI[:, 4096:6144].bitcast(BF16).rearrange("p (v t r c) -> p v t r c", v=2, t=8, r=2), ["h8", "h9", "h10", "h11"])}
        ident = self.cst("ident")
        rowmask = self.cst("rowmask", 4)

        def stageT(fc):
            zb = fc % 2
            zcompute(fc, zb)
            Zre_, ZimN_, RZR_, RZI_ = ZB[zb]
            W2v, R_W2 = W2B[zb]
            for ri, Zs, ZR in ((0, Zre_, RZR_), (1, ZimN_, RZI_)):
                for tq in range(2):
                    bank = ri * 2 + tq
                    for i in range(4):
                        t = tq * 4 + i
                        S.op("pe", lambda e, o=self.PS[:, bank * 512 + i * 128:bank * 512 + (i + 1) * 128], a=Zs[:, t * 128:(t + 1) * 128]:
                             e.transpose(o, a, ident), reads=ZR + ["CST"], writes=[("p", bank)])
                    for v in range(2):
                        S.op("dve", lambda e, v=v, tq=tq, ri=ri, bank=bank, W2v=W2v: e.tensor_scalar(
                            W2v[:, v, tq * 4:(tq + 1) * 4, ri, :],
                            self.PS[:, bank * 512:(bank + 1) * 512].rearrange("p (t c) -> p t c", c=128),
                            rowmask[:, 2 * ri + v:2 * ri + v + 1], None, ALU.mult),
                            reads=[("p", bank), "CST"], writes=R_W2)

        def stageM(fc):
            W2v, R_W2 = W2B[fc % 2]
            for q in range(4):
                hq, v = q // 2, q % 2
                for ri in range(2):
                    off = 2048 + (q * 2 + ri) * 256
                    bank = off // 512
                    for j in range(8):
                        S.op("pe", lambda e, o=self.PS[:, off:off + 256], w=W2v[64 * hq:64 * hq + 64, v, 7 - j, ri, :],
                             r=self.BIGA[64 * hq:64 * hq + 64, fc * L + j * 256:fc * L + (j + 1) * 256], a=(j == 0), z=(j == 7):
                             e.matmul(o, w, r, start=a, stop=z), reads=R_W2 + [("A", fc)], writes=[("p", bank)])
            S.op("act", lambda e, fc=fc: e.activation(
                out=B2S4[:, :, 4 * fc:4 * fc + 4, 1:257],
                in_=self.PS[:, 2048:4096].rearrange("p (q r c) -> p r q c", q=4, r=2), func=AF.Copy),
                reads=[("p", b) for b in range(4, 8)], writes=BALL + ALLC[1:])

        stageT(0)
        for fc in range(NFC):
            if fc + 1 < NFC:
                stageT(fc + 1)
            stageM(fc)

        UA = self.UNI[:]
        NSEG, SEGL = 4, 64
        RING = lambda k, sg: self.UNI[:, k * 512 + sg * 128:k * 512 + (sg + 1) * 128]
        PP4 = lambda sg: self.UNI[:, 1536 + sg * 128:1536 + (sg + 1) * 128]
        TT4 = lambda sg: self.UNI[:, 2048 + sg * 64:2048 + (sg + 1) * 64]
        PW64R, PW64I = self.UNI[:, 2304:4352], self.UNI[:, 4352:6400]
        gi = lambda ap: ap.rearrange("p (g i) -> p g i", i=64)
        UNI_H = ["h%d" % i_ for i_ in range(13)]
        SCN = [("R4", k_, s_) for k_ in range(3) for s_ in range(4)] + [("PP4", s_) for s_ in range(4)] + [("TT4", s_) for s_ in range(4)] + ["PWR64", "PWI64"]
        A12 = self.SMALL[:, 0:128]
        A1, A2 = self.SMALL[:, 0:64], self.SMALL[:, 64:128]
        l8r, l8i = PWR[:, 256:288], PWI[:, 256:288]
        dv("a1a", lambda e: e.tensor_copy(A1[:, 0:32], l8r), ["s5pwr"], ["A1"])
        dv("a1b", lambda e: e.tensor_copy(A1[:, 32:64], l8r), ["s5pwr"], ["A1"])
        dv("a2a", lambda e: e.tensor_scalar(A2[:, 0:32], l8i, -1.0, None, ALU.mult), ["s5pwi"], ["A2"])
        dv("a2b", lambda e: e.tensor_copy(A2[:, 32:64], l8i), ["s5pwi"], ["A2"])
        dv("sr0", lambda e: e.memset(self.UNI[:, 0:512], 0.0), [], UNI_H + SCN)
        h4 = lambda ap: ap.rearrange("p (h j g) -> p h j g", h=2, j=2)
        j3 = lambda ap: ap.rearrange("p (j g) -> p j g", j=2)
        kk64 = self.cst("kk64", 64)
        kkb64 = kk64.unsqueeze(1).broadcast_to([128, 32, 64])
        FAt = self.FA()
        FAR_ = [("FA", t_) for t_ in range(4)]
        PA, PB = self.PS[:, 0:2048], self.PS[:, 2048:4096]
        RPA, RPB = [("p", b_) for b_ in range(4)], [("p", b_) for b_ in range(4, 8)]
        dv("t1", lambda e: e.tensor_tensor(gi(FAt), ar.unsqueeze(2).broadcast_to([128, 32, 64]), kkb64, ALU.mult), ["s5ar", "CST"], FAR_)
        dv("t2", lambda e: e.activation(out=FAt, in_=FAt, func=AF.Exp), FAR_, FAR_, "act")
        dv("t3", lambda e: e.tensor_tensor(gi(PW64I), ai.unsqueeze(2).broadcast_to([128, 32, 64]), kkb64, ALU.mult), ["s5ai", "CST"], ["PWI64"])

        def sin64(dst, dres):
            dv("u1", lambda e: e.tensor_scalar(dst, PW64I, 1.0 / TWO_PI, MAGIC, ALU.mult, ALU.add), ["PWI64"], dres)
            dv("u2", lambda e: e.tensor_scalar(dst, dst, MAGIC, None, ALU.subtract), dres, dres)
            dv("u3", lambda e: e.scalar_tensor_tensor(out=dst, in0=dst, scalar=-TWO_PI, in1=PW64I, op0=ALU.mult, op1=ALU.add), dres + ["PWI64"], dres)
            dv("u4", lambda e: e.activation(out=dst, in_=dst, func=AF.Sin), dres, dres, "act")

        sin64(PA, RPA)
        dv("t4", lambda e: e.tensor_scalar(PW64I, PW64I, math.pi / 2, None, ALU.add), ["PWI64"] + RPA, ["PWI64"])
        sin64(PB, RPB)
        dv("t5", lambda e: e.tensor_tensor(PW64R, FAt, PB, ALU.mult), FAR_ + RPB, ["PWR64"])
        dv("t6", lambda e: e.tensor_tensor(PW64I, FAt, PA, ALU.mult), FAR_ + RPA + RPB, ["PWI64"])

        for t in range(SEGL):
            k0, k1 = t % 3, (t + 1) % 3
            for sg in range(NSEG):
                wv = bass.AP(UA.tensor, UA.offset + k0 * 512 + sg * 128, [list(UA.ap[0]), [32, 2], [32, 2], [1, 32]])
                dv("s12", lambda e, wv=wv, sg=sg: e.tensor_tensor(h4(PP4(sg)), h4(A12), wv, ALU.mult), ["A1", "A2", ("R4", k0, sg)], [("PP4", sg)])
            for sg in range(NSEG):
                dv("s3", lambda e, sg=sg: e.tensor_tensor(j3(TT4(sg)), h4(PP4(sg))[:, 0], h4(PP4(sg))[:, 1], ALU.add), [("PP4", sg)], [("TT4", sg)])
            for sg in range(NSEG):
                col = SEGL * sg + t + 1
                dv("s4", lambda e, sg=sg, col=col, k1=k1: e.tensor_tensor(h4(RING(k1, sg)), j3(TT4(sg)).unsqueeze(1).broadcast_to([128, 2, 2, 32]),
                                                                       B2S4[:, :, :, col].unsqueeze(1).broadcast_to([128, 2, 2, 32]), ALU.add),
                   [("TT4", sg), ("b2S", col)], [("R4", k1, sg)])
            cols = [SEGL * sg + t + 1 for sg in range(NSEG)]
            ring_v = self.UNI[:, k1 * 512:(k1 + 1) * 512].rearrange("p (s b) -> p s b", b=128)[:, :, 0:64].rearrange("p s (r g) -> p r g s", r=2)
            dv("s5", lambda e, t=t, ring_v=ring_v: e.activation(out=B2S4[:, :, :, t + 1:t + 1 + SEGL * (NSEG - 1) + 1:SEGL], in_=ring_v, func=AF.Copy),
               [("R4", k1, sg) for sg in range(NSEG)], [("b2S", c_) for c_ in cols], "act")
        kf = SEGL % 3
        A64 = self.SMALL[:, 576:704]
        PP64, TT64 = self.SMALL[:, 704:832], self.SMALL[:, 832:896]
        p64r, p64i = gi(PW64R)[:, :, 63], gi(PW64I)[:, :, 63]
        dv("b1", lambda e: e.tensor_copy(A64[:, 0:32], p64r), ["PWR64"], ["PWRN"])
        dv("b2", lambda e: e.tensor_copy(A64[:, 32:64], p64r), ["PWR64"], ["PWRN"])
        dv("b3", lambda e: e.tensor_scalar(A64[:, 64:96], p64i, -1.0, None, ALU.mult), ["PWI64"], ["PWRN"])
        dv("b4", lambda e: e.tensor_copy(A64[:, 96:128], p64i), ["PWI64"], ["PWRN"])
        kc_ = (kf + 1) % 3
        dv("c1", lambda e: e.tensor_copy(RING(kc_, 1), RING(kf, 0)), [("R4", kf, 0)], [("R4", kc_, 1)])
        for sg in (2, 3):
            wv = bass.AP(UA.tensor, UA.offset + kc_ * 512 + (sg - 1) * 128, [list(UA.ap[0]), [32, 2], [32, 2], [1, 32]])
            dv("c2", lambda e, wv=wv: e.tensor_tensor(h4(PP64), h4(A64), wv, ALU.mult), ["PWRN", ("R4", kc_, sg - 1)], ["PWRN"])
            dv("c3", lambda e: e.tensor_tensor(j3(TT64), h4(PP64)[:, 0], h4(PP64)[:, 1], ALU.add), ["PWRN"], ["PWRN"])
            dv("c4", lambda e, sg=sg: e.tensor_tensor(h4(RING(kc_, sg)), j3(TT64).unsqueeze(1).broadcast_to([128, 2, 2, 32]), h4(RING(kf, sg - 1)), ALU.add),
               ["PWRN", ("R4", kf, sg - 1)], [("R4", kc_, sg)])
        for sg in (1, 2, 3):
            car = RING(kc_, sg)
            cr_b = car[:, 0:32].unsqueeze(2).broadcast_to([128, 32, 64])
            ci_b = car[:, 32:64].unsqueeze(2).broadcast_to([128, 32, 64])
            c0_ = SEGL * sg + 1
            cn = [("b2S", c_) for c_ in range(c0_, c0_ + SEGL)]
            rc = [("R4", kc_, sg)]
            o_re = B2S4[:, 0, :, c0_:c0_ + SEGL]
            o_im = B2S4[:, 1, :, c0_:c0_ + SEGL]
            dv("f1", lambda e, cr_b=cr_b: e.tensor_tensor(gi(FAt), gi(PW64R), cr_b, ALU.mult), ["PWR64"] + rc, FAR_)
            dv("f2", lambda e, ci_b=ci_b: e.tensor_tensor(gi(PA), gi(PW64I), ci_b, ALU.mult), ["PWI64"] + rc, RPA)
            dv("f3", lambda e: e.tensor_tensor(FAt, FAt, PA, ALU.subtract), FAR_ + RPA, FAR_)
            dv("f4", lambda e, o_re=o_re: e.tensor_tensor(o_re, o_re, gi(FAt), ALU.add), FAR_ + cn, cn)
            dv("f5", lambda e, ci_b=ci_b: e.tensor_tensor(gi(FAt), gi(PW64R), ci_b, ALU.mult), ["PWR64"] + rc, FAR_)
            dv("f6", lambda e, cr_b=cr_b: e.tensor_tensor(gi(PB), gi(PW64I), cr_b, ALU.mult), ["PWI64"] + rc, RPB)
            dv("f7", lambda e: e.tensor_tensor(FAt, FAt, PB, ALU.add), FAR_ + RPB, FAR_)
            dv("f8", lambda e, o_im=o_im: e.tensor_tensor(o_im, o_im, gi(FAt), ALU.add), FAR_ + cn, cn)

        W3H = {b: self.UNI[:, b * 2048:(b + 1) * 2048].bitcast(BF16).rearrange("p (j q r c) -> p j q r c", j=4, q=4, r=2) for b in range(3)}
        R_W3H = {b: ["h%d" % i_ for i_ in range(4 * b, 4 * b + 4)] for b in range(3)}
        KL = self.UNI[:, 6144:6656].bitcast(BF16).rearrange("p (t c) -> p t c", c=128)
        maskQ = self.cst("maskQ")
        S.op("pool", lambda e: e.memset(self.UNI[:, 0:6144].bitcast(BF16), 0.0), writes=UNI_H + SCN)
        t4 = lambda ap: ap.rearrange("p (j q c) -> p j q c", j=8, q=4)
        wcount = 0
        zcompute_dve(0)
        for fc in range(NFC):
            ypar = 0
            yb = 0
            for tq in range(2):
                bank = 4 + tq
                for i in range(4):
                    t = tq * 4 + i
                    o = self.PS[:, bank * 512 + i * 128:bank * 512 + (i + 1) * 128]
                    S.op("pe", lambda e, o=o, t=t, fc=fc: e.matmul(o, Zre[:, t * 128:(t + 1) * 128], CMR[:, 128 * fc:128 * (fc + 1)], start=True, stop=False),
                         reads=R_ZR + R_CMR, writes=[("p", bank)])
                    S.op("pe", lambda e, o=o, t=t, fc=fc: e.matmul(o, ZimN[:, t * 128:(t + 1) * 128], CMI[:, 128 * fc:128 * (fc + 1)], start=False, stop=True),
                         reads=R_ZI + R_CMI, writes=[("p", bank)])
                S.op("dve", lambda e, tq=tq, bank=bank: e.tensor_tensor(
                    KL[:, tq * 4:(tq + 1) * 4, :], self.PS[:, bank * 512:(bank + 1) * 512].rearrange("p (t c) -> p t c", c=128),
                    maskQ.unsqueeze(1).broadcast_to([128, 4, 128]), ALU.mult), reads=[("p", bank), "CST"], writes=["h12"])
            cmr_b = g3(CMR)[:, 4 * fc:4 * fc + 4, :].unsqueeze(1).broadcast_to([128, 8, 4, 32])
            cmi_b = g3(CMI)[:, 4 * fc:4 * fc + 4, :].unsqueeze(1).broadcast_to([128, 8, 4, 32])
            pr_b = PWR3[:, 1:9, 4 * fc:4 * fc + 4].unsqueeze(3).broadcast_to([128, 8, 4, 32])
            pi_b = PWI3[:, 1:9, 4 * fc:4 * fc + 4].unsqueeze(3).broadcast_to([128, 8, 4, 32])
            wb = [(wcount) % 3, (wcount + 1) % 3]
            wcount += 2
            E = "pool"
            dv("w1", lambda e, a=cmr_b, b=pr_b: e.tensor_tensor(t4(T1), a, b, ALU.mult), R_CMR + ["s5pwr"], ["TMP"], E)
            dv("w2", lambda e, a=cmi_b, b=pi_b: e.tensor_tensor(t4(T2), a, b, ALU.mult), R_CMI + ["s5pwi"], ["TMP2"], E)
            for hb in range(2):
                for q in range(4):
                    dv("w3", lambda e, q=q, hb=hb, wv=W3H[wb[hb]]: e.tensor_tensor(wv[:, :, q, 0, 32 * q:32 * q + 32], t4(T1)[:, 4 * hb:4 * hb + 4, q, :],
                                                                               t4(T2)[:, 4 * hb:4 * hb + 4, q, :], ALU.subtract),
                       ["TMP", "TMP2"], R_W3H[wb[hb]], E)
            PT = self.PS[:, 3072:4096]
            RPT = [("p", 6), ("p", 7)]
            ST2 = self.SMALL[:, 0:1024]
            dv("w4", lambda e, a=cmi_b, b=pr_b: e.tensor_tensor(t4(PT), a, b, ALU.mult), R_CMI + ["s5pwr"], RPT, "dve")
            dv("w5", lambda e, a=cmr_b, b=pi_b: e.tensor_tensor(t4(ST2), a, b, ALU.mult), R_CMR + ["s5pwi"], self.S5_SMALL, "dve")
            dv("w5b", lambda e: e.tensor_tensor(PT, PT, ST2, ALU.add), RPT + self.S5_SMALL, RPT, "dve")
            for hb in range(2):
                for q in range(4):
                    dv("w6", lambda e, q=q, hb=hb, wv=W3H[wb[hb]]: e.activation(out=wv[:, :, q, 1, 32 * q:32 * q + 32], in_=t4(PT)[:, 4 * hb:4 * hb + 4, q, :],
                                                                            func=AF.Copy, scale=-1.0),
                       RPT, R_W3H[wb[hb]], "act")
            if fc + 1 < NFC:
                zcompute_dve(fc + 1)
            for t in range(8):
                for b in range(4):
                    jlo, jhi = max(2 * b, t), 2 * b + 2
                    if jlo >= jhi:
                        continue
                    S.op("pe", lambda e, t=t, jlo=jlo, jhi=jhi, fc=fc, yb=yb: e.matmul(
                        self.PS[:, yb + jlo * 256:yb + jhi * 256], KL[:, t, :],
                        self.BIGA[:, fc * L + (jlo - t) * 256:fc * L + (jhi - t) * 256], start=(t == 0), stop=False),
                        reads=["h12", ("A", fc)], writes=[("p", 4 * ypar + b)])
            for j in range(8):
                hb = j // 4
                for q in range(4):
                    for ri in range(2):
                        S.op("pe", lambda e, j=j, q=q, ri=ri, fc=fc, yb=yb, wv=W3H[wb[hb]]: e.matmul(
                            self.PS[:, yb + j * 256:yb + (j + 1) * 256], wv[:, j % 4, q, ri, :], B2S4[:, ri, 4 * fc + q, 0:256],
                            start=False, stop=(q == 3 and ri == 1)),
                            reads=R_W3H[wb[hb]] + ALLC, writes=[("p", 4 * ypar + j // 2)])
            for hv in range(2):
                yv = self.PS[:, yb + hv * 1024:yb + (hv + 1) * 1024]
                sc = self.SMALL[:, 0:1024]
                ry = [("p", 4 * ypar + 2 * hv), ("p", 4 * ypar + 2 * hv + 1)]
                rs = self.S5_SMALL
                ug = self.BIGA[:, fc * L + hv * 1024:fc * L + (hv + 1) * 1024]
                S.op("dve", lambda e, yv=yv, ug=ug, fc=fc: e.scalar_tensor_tensor(out=yv, in0=ug, scalar=self.vcol(("s5_D", l), fc), in1=yv,
                                                                          op0=ALU.mult, op1=ALU.add), reads=ry + [("A", fc), "VEC"], writes=ry)
                S.op("act", lambda e, yv=yv, ug=ug: e.activation(out=ug, in_=yv, func=AF.Gelu_apprx_tanh), reads=ry, writes=[("A", fc)])

        jobs = []
        for oc in range(NFC):
            def evac_zb(ps, psres, oc=oc):
                S.op("act", lambda e: e.activation(out=self.FA(), in_=ps, func=AF.Sigmoid, bias=self.vcol(("b_glu", l), 8 + oc), scale=1.0),
                     reads=psres + ["VEC"], writes=FAR)

            def evac_za(ps, psres, oc=oc):
                S.op("dve", lambda e: e.scalar_tensor_tensor(out=self.FB(), in0=ps, scalar=self.vcol(("b_glu", l), oc), in1=self.FA(),
                                                             op0=ALU.add, op1=ALU.mult), reads=psres + ["VEC"] + FAR, writes=FBR)
                hv = self.HT[:, oc * L:(oc + 1) * L].rearrange("p (c j) -> p j c", j=8)
                S.op("dve", lambda e: e.tensor_tensor(hv, hv, self.FB().rearrange("p (j c) -> p j c", c=256), ALU.add),
                     reads=FBR + [("H", oc)], writes=[("H", oc)])
            for col0, ev in ((D + oc * 128, evac_zb), (oc * 128, evac_za)):
                jobs.append(dict(w2d=self.w_glu_d[l], k0=0, KC=8, col0=col0,
                                 rhs=lambda kc, tt: self.BIGA[:, kc * L + tt * 512:kc * L + (tt + 1) * 512],
                                 rhs_res=lambda kc: [("A", kc)], evac=ev))
        self.run_dense(jobs)

    def load_w(self, w2d, k0, KC, col0):
        S = self.S
        sl = self.wslot
        self.wslot = (self.wslot + 1) % self.NSLOT
        src = w2d[k0:k0 + KC * 128, col0:col0 + 128].rearrange("(kc p) c -> p kc c", p=128)
        S.dma("pool", self.WBF(sl)[:, 0:KC * 128].rearrange("p (kc c) -> p kc c", c=128), src, writes=self.wres(sl))
        return sl

    def load_rope(self):
        S = self.S
        S.dma("sp", self.FB(), self.rope_d[:, 0:L], writes=[("FB", t) for t in range(4)])
        S.dma("sp", self.FC(), self.rope_d[:, L:2 * L], writes=[("FC", t) for t in range(4)])

    def qk_unit(self, sl, hf, gcol):
        S = self.S
        u = hf
        c0, c1 = hf * 1024, (hf + 1) * 1024
        FAR = [("FA", 2 * hf), ("FA", 2 * hf + 1)]
        FBR = [("FB", 2 * hf), ("FB", 2 * hf + 1)]
        FCR = [("FC", 2 * hf), ("FC", 2 * hf + 1)]
        rb = self.wres(sl)
        pb = 4 * u
        P = self.PS[:, pb * 512:(pb + 2) * 512]
        P2 = self.PS[:, (pb + 2) * 512:(pb + 4) * 512]
        RP = [("p", pb), ("p", pb + 1)]
        RP2 = [("p", pb + 2), ("p", pb + 3)]
        XG = self.FW[:, c0:c1]
        SQ = self.BIGB[:, u * 1024:(u + 1) * 1024]
        SQR = self.bres(u * 1024, (u + 1) * 1024)

        def st0():
            for kc in range(8):
                for t in range(2):
                    S.op("pe", lambda e, t=t, kc=kc: e.matmul(self.PS[:, (pb + t) * 512:(pb + t + 1) * 512], self.WBF(sl)[:, kc * 128:(kc + 1) * 128],
                                                            self.BIGA[:, kc * L + c0 + t * 512:kc * L + c0 + (t + 1) * 512], start=(kc == 0), stop=(kc == 7)),
                         reads=rb + [("A", kc)], writes=[("p", pb + t)])
            S.op("act", lambda e: e.activation(out=XG, in_=P, func=AF.Copy, scale=gcol), reads=RP + ["VEC"], writes=FAR)
            S.op("act", lambda e: e.activation(out=SQ, in_=P, func=AF.Square), reads=RP, writes=SQR)

        def st1():
            for t in range(2):
                S.op("pe", lambda e, t=t: e.matmul(self.PS[:, (pb + 2 + t) * 512:(pb + 3 + t) * 512], self.onesH(), SQ[:, t * 512:(t + 1) * 512],
                                                 start=True, stop=True), reads=SQR + ["CSTB"], writes=[("p", pb + 2 + t)])
            S.op("act", lambda e: e.activation(out=P2, in_=P2, func=AF.Ln, bias=self.eps_ap(), scale=1.0), reads=RP2 + ["EPS"], writes=RP2)
            S.op("act", lambda e: e.activation(out=P2, in_=P2, func=AF.Exp, scale=-0.5), reads=RP2, writes=RP2)
            pswap = self.cst("pswap")
            for t in range(2):
                S.op("pe", lambda e, t=t: e.matmul(self.PS[:, (pb + t) * 512:(pb + t + 1) * 512], pswap, XG[:, t * 512:(t + 1) * 512], start=True, stop=True),
                     reads=FAR + ["CST"], writes=[("p", pb + t)])
            S.op("dve", lambda e: e.tensor_tensor(XG, XG, self.FW[:, L + c0:L + c1], ALU.mult), reads=FAR + FBR, writes=FAR)
            S.op("dve", lambda e: e.tensor_tensor(P, P, self.FW[:, 2 * L + c0:2 * L + c1], ALU.mult), reads=RP + FCR, writes=RP)
            S.op("dve", lambda e: e.tensor_tensor(XG, XG, P, ALU.add), reads=FAR + RP, writes=FAR)
            S.op("dve", lambda e: e.tensor_tensor(XG, XG, P2, ALU.mult), reads=FAR + RP2, writes=FAR)
        return st0, st1, (XG, FAR, P2, RP2)

    @staticmethod
    def run_staged(units):
        nst = max(len(u) for u in units)
        for it in range(len(units) + nst - 1):
            for k in range(nst - 1, -1, -1):
                i = it - k
                if 0 <= i < len(units) and k < len(units[i]):
                    units[i][k]()

    ATT_SMALL = ["KMEAN", "G0", "G1", "RINV", "NSB"] + [("ACC", i) for i in range(4)] + [("M8", i) for i in range(8)]
    S5_SMALL = ["A1", "A2", "TT1", "PP", "PWRN"] + [("SR", k_) for k_ in range(3)] + ["sm%d" % i_ for i_ in range(7)]
    S5S_ALL = ["s5raw", "s5dt", "s5ar", "s5ai", "s5mag", "s5ang", "s5nt", "s5c9", "s5pwr", "s5pwi", "s5cr", "s5ci"]

    def kv_phase(self):
        S = self.S
        S.op("dve", lambda e: e.memset(self.SMALL[:], 0.0), writes=self.ATT_SMALL + self.S5_SMALL)
        self.rms_rstd(self.FA(), "FA")
        self.make_xn("kv_norm", self.FA(), "FA")
        self.load_rope()
        KMEAN = self.SMALL[:, 0:64]
        KB = lambda k: self.BIGB[:, 2048 + k * 2048:2048 + (k + 1) * 2048]
        KBR = lambda k: self.bres(2048 + k * 2048, 2048 + (k + 1) * 2048)
        VHo = lambda k: 6144 + k * 3072
        VH = lambda k: self.BIGB[:, VHo(k):VHo(k) + 2064]
        VHR = lambda k: self.bres(VHo(k), VHo(k) + 2064)
        for k in range(2):
            S.op("pool", lambda e, k=k: e.memset(VH(k).rearrange("p (t c) -> p t c", c=129)[:, :, 128:129], 1.0), writes=VHR(k))
        slk = {}
        slv = {}
        kunits = []
        for hd in range(NH):
            kp = hd % 2
            for hf in range(2):
                hold = {}

                def st0(hd=hd, hf=hf, hold=hold):
                    if hf == 0:
                        if hd == 0:
                            slk[0] = self.load_w(self.w_kv_d, 0, 8, 0)
                            slv[0] = self.load_w(self.w_kv_d, 0, 8, D)
                        if hd + 1 < NH:
                            slk[hd + 1] = self.load_w(self.w_kv_d, 0, 8, (hd + 1) * 128)
                            slv[hd + 1] = self.load_w(self.w_kv_d, 0, 8, D + (hd + 1) * 128)
                    a, b, h = self.qk_unit(slk[hd], hf, self.vcol("k_norm"))
                    hold["st1"] = b
                    hold["h"] = h
                    a()

                def st1(hold=hold):
                    hold["st1"]()

                def st2(hd=hd, hf=hf, kp=kp, hold=hold):
                    XG, FAR, P2, RP2 = hold["h"]
                    S.op("dve", lambda e: e.reduce_sum(out=KMEAN[:, hd * 8 + hf * 4:hd * 8 + hf * 4 + 4],
                                                      in_=XG.rearrange("p (n t) -> p n t", t=256), axis=AX.X),
                         reads=FAR, writes=["KMEAN"])
                    S.op("act", lambda e: e.activation(out=KB(kp)[:, hf * 1024:(hf + 1) * 1024], in_=XG, func=AF.Copy),
                         reads=FAR, writes=self.bres(2048 + kp * 2048 + hf * 1024, 2048 + kp * 2048 + (hf + 1) * 1024))
                    if hf == 1:
                        S.dma("sp", self.kT_s[hd], KB(kp), reads=KBR(kp), writes=[("kT_s", hd)])
                kunits.append([st0, st1, st2])
            for hf in range(2):
                def vunit(hd=hd, hf=hf, kp=kp):
                    sl = slv[hd]
                    rb = self.wres(sl)
                    VH3 = VH(kp).rearrange("p (t c) -> p t c", c=129)
                    pb = 4 * hf
                    for t8 in range(8):
                        t16 = hf * 8 + t8
                        for kc in range(8):
                            S.op("pe", lambda e, t16=t16, t8=t8, kc=kc: e.matmul(
                                self.PS[:, pb * 512 + t8 * 128:pb * 512 + (t8 + 1) * 128], self.BIGA[:, kc * L + t16 * 128:kc * L + (t16 + 1) * 128],
                                self.WBF(sl)[:, kc * 128:(kc + 1) * 128], start=(kc == 0), stop=(kc == 7)),
                                reads=rb + [("A", kc)], writes=[("p", pb + t8 // 4)])
                    S.op("act", lambda e: e.activation(out=VH3[:, hf * 8:(hf + 1) * 8, 0:128],
                                                       in_=self.PS[:, pb * 512:(pb + 2) * 512].rearrange("p (t c) -> p t c", c=128), func=AF.Copy),
                         reads=[("p", pb), ("p", pb + 1)], writes=VHR(kp))
                    if hf == 1:
                        S.dma("sp", self.V_s[hd], VH(kp), reads=VHR(kp), writes=[("V_s", hd)])
                kunits.append([vunit])
        self.run_staged(kunits)
        S.op("dve", lambda e: e.tensor_scalar(KMEAN, KMEAN, 1.0 / 256, None, ALU.mult), reads=["KMEAN"], writes=["KMEAN"])

    def moba_layer(self, j):
        S = self.S
        self.rms_rstd(self.FA(), "FA")
        self.make_xn(("b_norm", j), self.FA(), "FA")
        self.load_rope()
        KMEAN = self.SMALL[:, 0:64]
        GT = lambda hf: self.SMALL[:, 64 + 64 * hf:128 + 64 * hf]
        M8 = lambda i: self.SMALL[:, 320 + 8 * i:328 + 8 * i]
        RINV = self.SMALL[:, 192:196]
        ACC = lambda i: self.SMALL[:, 384 + 129 * i:384 + 129 * (i + 1)]
        SELALL = self.S5S[:, 0:1024]
        futmask = self.cst("futmask")
        scale = 1.0 / math.sqrt(128.0)
        S.op("dve", lambda e: e.memset(SELALL, 0.0), writes=self.S5S_ALL + ["SELALL"])
        QB = lambda k: self.BIGB[:, 2048 + k * 2048:2048 + (k + 1) * 2048]
        QBR = lambda k: self.bres(2048 + k * 2048, 2048 + (k + 1) * 2048)
        slq = {}
        qunits = []
        ownm = self.cst("ownmask")
        NSB = self.SMALL[0:8, 384:896].bitcast(BF16)
        for hd in range(NH):
            kp = hd % 2
            for hf in range(2):
                def st_load(hd=hd, hf=hf):
                    if hf == 0:
                        if hd == 0:
                            slq[0] = self.load_w(self.w_q_d[j], 0, 8, 0)
                        if hd + 1 < NH:
                            slq[hd + 1] = self.load_w(self.w_q_d[j], 0, 8, (hd + 1) * 128)
                hold = {}

                def st0(hd=hd, hf=hf, hold=hold, st_load=st_load):
                    st_load()
                    a, b, h = self.qk_unit(slq[hd], hf, self.vcol(("q_norm", j)))
                    hold["st1"] = b
                    hold["h"] = h
                    a()

                def st1(hold=hold):
                    hold["st1"]()

                def st2(hd=hd, hf=hf, kp=kp, hold=hold):
                    XG, FAR, P2, RP2 = hold["h"]
                    S.op("act", lambda e: e.activation(out=QB(kp)[:, hf * 1024:(hf + 1) * 1024], in_=XG, func=AF.Copy),
                         reads=FAR, writes=self.bres(2048 + kp * 2048 + hf * 1024, 2048 + kp * 2048 + (hf + 1) * 1024))
                    for q8 in range(8):
                        S.op("pe", lambda e, q8=q8: e.matmul(P2[:, q8 * 8:(q8 + 1) * 8], XG[:, q8 * 128:(q8 + 1) * 128],
                                                           KMEAN[:, hd * 8:(hd + 1) * 8], start=True, stop=True),
                             reads=FAR + ["KMEAN"], writes=[RP2[0]])
                    G = GT(hf)
                    gres = "G%d" % hf
                    S.op("dve", lambda e: e.tensor_tensor(G, P2[:, 0:64], futmask[:, hf * 64:(hf + 1) * 64], ALU.add),
                         reads=[RP2[0], "CST"], writes=[gres])
                    selh = SELALL[:, hd * 128 + hf * 64:hd * 128 + (hf + 1) * 64]
                    if hf == 0:
                        S.op("dve", lambda e: e.tensor_scalar(selh[:, 16:64], G[:, 16:64], -1e29, None, ALU.is_gt),
                             reads=[gres], writes=["SELALL"])
                    else:
                        for q8 in range(8):
                            S.op("dve", lambda e, q8=q8: e.max(out=M8(q8), in_=G[:, q8 * 8:(q8 + 1) * 8]), reads=[gres], writes=[("M8", q8)])
                            S.op("dve", lambda e, q8=q8: e.tensor_scalar(selh[:, q8 * 8:(q8 + 1) * 8], G[:, q8 * 8:(q8 + 1) * 8],
                                                                        M8(q8)[:, 2:3], None, ALU.is_ge),
                                 reads=[gres, ("M8", q8)], writes=["SELALL"])
                    S.op("dve", lambda e: e.tensor_tensor(selh, selh, ownm[:, hf * 64:(hf + 1) * 64], ALU.add),
                         reads=["SELALL", "CST"], writes=["SELALL"])

                def st3(hd=hd, hf=hf, kp=kp, hold=hold):
                    XG, FAR, P2, RP2 = hold["h"]
                    selh = SELALL[:, hd * 128 + hf * 64:hd * 128 + (hf + 1) * 64]
                    for q8 in range(8):
                        S.op("pe", lambda e, q8=q8: e.transpose(P2[0:8, q8 * 128:(q8 + 1) * 128], selh[:, q8 * 8:(q8 + 1) * 8], self.cst("ident")),
                             reads=["SELALL", "CST"], writes=[RP2[q8 // 4]])
                    S.op("dve", lambda e: e.tensor_scalar(NSB, P2[0:8, 0:1024], -1.0, 30000.0, ALU.add, ALU.mult), reads=RP2, writes=["NSB"])
                    S.dma("sp", self.NS_s[hd][:, hf * 1024:(hf + 1) * 1024], NSB, reads=["NSB"], writes=[("NS_s", hd, hf)])
                    if hf == 1:
                        S.dma("sp", self.Q_s[hd], QB(kp), reads=QBR(kp), writes=[("Q_s", hd)])
                qunits.append([st0, st1, st2, st3])
        self.run_staged(qunits)

        QBF = lambda k: self.BIGB[:, k * 2048:(k + 1) * 2048]
        KTH = lambda k: self.BIGB[:, 4096 + k * 2048:4096 + (k + 1) * 2048]
        VHo = lambda k: 8192 + k * 2064
        VH = lambda k: self.BIGB[:, VHo(k):VHo(k) + 2064]
        ET = lambda i: self.BIGB[:, 12320 + 512 * i:12320 + 512 * (i + 1)]
        ETR = lambda i: self.bres(12320 + 512 * i, 12320 + 512 * (i + 1))
        OTH = self.BIGB[:, 14368:14368 + L]
        OTHR = self.bres(14368, 14368 + L)
        RQ = lambda k: [("cQ", k)]
        RK = lambda k: [("cK", k)]
        RV = lambda k: [("cV", k)]
        OPR = lambda par, i: self.PS[:, (2 + 2 * par + i // 2) * 512 + (i % 2) * 256:(2 + 2 * par + i // 2) * 512 + (i % 2) * 256 + 129]
        opres = lambda par, i: ("p", 2 + 2 * par + i // 2)

        def load_head(hd):
            k = hd % 2
            S.dma("sp", QBF(k), self.Q_s[hd], reads=[("Q_s", hd)], writes=RQ(k))
            S.dma("sp", KTH(k), self.kT_s[hd], reads=[("kT_s", hd)], writes=RK(k))
            S.dma("sp", VH(k), self.V_s[hd], reads=[("V_s", hd)], writes=RV(k))

        DEPTH = 2
        NET = 6
        ET = lambda i: self.BIGB[:, 12320 + 512 * i:12320 + 512 * (i + 1)]
        ETR = lambda i: [("cE", i)]
        OTH = self.TMP2[:].bitcast(BF16)
        OTHR = ["TMP2"]
        NEGSEL = lambda k: self.FW[0:8, k * 1024:(k + 1) * 1024].bitcast(BF16)
        NSR = lambda k: [("FA", 2 * k), ("FA", 2 * k + 1)]
        RB = lambda qp: self.FW[:, L + qp * 512:L + (qp + 1) * 512]
        RBR = lambda qp: [("FB", qp)]
        EONE = self.S5S[0:8, 1024:1536].bitcast(BF16)
        identf = self.cst("ident")
        CORE_NAMES = [("cQ", 0), ("cQ", 1), ("cK", 0), ("cK", 1), ("cV", 0), ("cV", 1)] + [("cE", i) for i in range(NET)]
        S.op("dve", lambda e: e.tensor_copy(EONE.rearrange("p (n c) -> p n c", c=128), identf[0:8, 0:8].unsqueeze(2).broadcast_to([8, 8, 128])),
             reads=["CST"], writes=["EONE", "TMP"] + self.bres(0, 16512) + CORE_NAMES)
        state = dict(st=0, et=0)
        tasks = []

        def load_head(hd):
            k = hd % 2
            S.dma("sp", QBF(k), self.Q_s[hd], reads=[("Q_s", hd)], writes=RQ(k))
            S.dma("sp", KTH(k), self.kT_s[hd], reads=[("kT_s", hd)], writes=RK(k))
            S.dma("sp", VH(k), self.V_s[hd], reads=[("V_s", hd)], writes=RV(k))
            S.dma("sp", NEGSEL(k), self.NS_s[hd], reads=[("NS_s", hd, 0), ("NS_s", hd, 1)], writes=NSR(k))

        def mk_block(hd, Q, n, qp):
            hk = hd % 2
            VH3 = VH(hk).rearrange("p (t c) -> p t c", c=129)
            info = {}
            lastkt = 4 * Q + 3

            def s1():
                ets = {}
                for kt in (2 * n, 2 * n + 1):
                    c0 = max(4 * Q, kt) - 4 * Q
                    if c0 > 3:
                        continue
                    sb_ = state["st"] % 4
                    state["st"] += 1
                    eb_ = state["et"] % NET
                    state["et"] += 1
                    ST = self.PS[:, sb_ * 512:(sb_ + 1) * 512]
                    diag = kt >= 4 * Q
                    selm = (n < 2 * Q + 1) and Q >= 2
                    S.op("pe", lambda e, ST=ST, c0=c0, fin=(not diag and not selm), kk_=KTH(hk)[:, kt * 128:(kt + 1) * 128],
                         qq_=QBF(hk)[:, Q * 512 + c0 * 128:(Q + 1) * 512]: e.matmul(
                        ST[:, c0 * 128:512], kk_, qq_, start=True, stop=fin), reads=RQ(hk) + RK(hk), writes=[("p", sb_)])
                    if selm:
                        S.op("pe", lambda e, ST=ST, c0=c0, fin=(not diag), en=EONE[:, n * 128:(n + 1) * 128],
                             ns=NEGSEL(hk)[:, Q * 512 + c0 * 128:(Q + 1) * 512]: e.matmul(ST[:, c0 * 128:512], en, ns, start=False, stop=fin),
                             reads=["EONE"] + NSR(hk), writes=[("p", sb_)])
                    if diag:
                        S.op("pe", lambda e, ST=ST, c0=c0: e.matmul(ST[:, c0 * 128:(c0 + 1) * 128], self.identB(), self.cmaskB(),
                                                                   start=False, stop=True), reads=["CSTB"], writes=[("p", sb_)])
                    et = ET(eb_)
                    S.op("act", lambda e, ST=ST, et=et, c0=c0: e.activation(out=et[:, c0 * 128:512], in_=ST[:, c0 * 128:512], func=AF.Exp, scale=scale),
                         reads=[("p", sb_)], writes=ETR(eb_))
                    ets[kt] = (et, eb_, c0)
                info["ets"] = ets

            def s2():
                if Q == 0 and n == 0 and hd + 1 < NH:
                    load_head(hd + 1)
                for kt in sorted(info["ets"]):
                    et, eb_, c0 = info["ets"][kt]
                    ob, rbk = 4 + qp, 6 + qp
                    S.op("pe", lambda e, et=et, c0=c0, vv=VH3[:, kt, 0:128], a=(kt == 0), z=(kt == lastkt), ob=ob: e.matmul(
                        self.PS[:, ob * 512 + c0 * 128:(ob + 1) * 512], vv, et[:, c0 * 128:512], start=a, stop=z),
                        reads=ETR(eb_) + RV(hk), writes=[("p", ob)])
                    S.op("pe", lambda e, et=et, c0=c0, a=(kt == 0), z=(kt == lastkt), rbk=rbk: e.matmul(
                        self.PS[:, rbk * 512 + c0 * 128:(rbk + 1) * 512], self.onesH(), et[:, c0 * 128:512], start=a, stop=z),
                        reads=ETR(eb_) + ["CSTB"], writes=[("p", rbk)])
            return s1, s2

        def mk_qend(hd, Q, qp):
            def s1():
                pass

            def s2():
                ob, rbk = 4 + qp, 6 + qp
                R = RB(qp)
                S.op("act", lambda e: e.activation(out=R, in_=self.PS[:, rbk * 512:(rbk + 1) * 512], func=AF.Ln, scale=128.0),
                     reads=[("p", rbk)], writes=RBR(qp))
                S.op("act", lambda e: e.activation(out=R, in_=R, func=AF.Exp, scale=-1.0), reads=RBR(qp), writes=RBR(qp))
                S.op("dve", lambda e: e.tensor_tensor(OTH[:, Q * 512:(Q + 1) * 512], self.PS[:, ob * 512:(ob + 1) * 512], R, ALU.mult),
                     reads=[("p", ob)] + RBR(qp), writes=OTHR)
                if Q == 3:
                    S.dma("sp", self.OT_s[hd], OTH, reads=OTHR, writes=[("OT_s", hd)])
            return s1, s2

        load_head(0)
        qcount = 0
        for hd in range(NH):
            for Q in range(4):
                qp = qcount % 2
                qcount += 1
                for n in range(2 * Q + 2):
                    tasks.append(mk_block(hd, Q, n, qp))
                tasks.append(mk_qend(hd, Q, qp))
        for idx in range(len(tasks) + DEPTH):
            if idx < len(tasks):
                tasks[idx][0]()
            if idx - DEPTH >= 0:
                tasks[idx - DEPTH][1]()
        S.op("dve", lambda e: e.memset(self.SMALL[:, 192:200], 1.0), reads=["EONE"], writes=["TMP"] + self.bres(0, 16512) + CORE_NAMES)
        for hd in range(NH):
            S.dma("sp", self.BIGA[:, hd * L:(hd + 1) * L], self.OT_s[hd], reads=[("OT_s", hd)], writes=[("A", hd)])
        jobs = []
        for oc in range(NFC):
            def evac_o(ps, psres, oc=oc):
                S.op("dve", lambda e: e.tensor_tensor(self.HT[:, oc * L:(oc + 1) * L], self.HT[:, oc * L:(oc + 1) * L], ps, ALU.add),
                     reads=psres + [("H", oc)], writes=[("H", oc)])
            jobs.append(dict(w2d=self.w_o_d[j], k0=0, KC=8, col0=oc * 128,
                             rhs=lambda kc, tt: self.BIGA[:, kc * L + tt * 512:kc * L + (tt + 1) * 512],
                             rhs_res=lambda kc: [("A", kc)], evac=evac_o))
        self.run_dense(jobs)

    def build(self):
        with ExitStack() as es:
            self.declare(es)
            self.EPS_T = es.enter_context(self.nc.sbuf_tensor("EPS_T", [128, 2], F32))
            self.S.op("pool", lambda e: e.memset(self.EPS_T[:], EPS), writes=["EPS"])
            self.prologue()
            self.body()
            self.epilogue()
            self.S.emit(self.sems, self.dsems)
        return self.nc

    def body(self):
        st = self.stop
        if st == "ffn_only":
            self.ffn(0)
            return
        self.s5_layer(0)
        if st == "mix0":
            return
        self.ffn_hook = lambda: self.s5_layer(1, "params")
        self.ffn(0)
        if st == "ffn0":
            return
        self.s5_layer(1, "main")
        if st == "mix1":
            return
        self.ffn(1)
        if st == "ffn1":
            return
        self.kv_phase()
        self.moba_layer(0)
        if st == "mix2":
            return
        self.ffn(2)
        if st == "ffn2":
            return
        self.moba_layer(1)
        if st == "mix3":
            return
        self.ffn(3)


def _consts():
    c = np.zeros((128, NCST), np.float32)
    p = np.arange(128)
    c[:, CC["ident"]:CC["ident"] + 128] = np.eye(128, dtype=np.float32)
    sw = np.zeros((128, 128), np.float32)
    sw[(p + 64) % 128, p] = 1.0
    c[:, CC["pswap"]:CC["pswap"] + 128] = sw
    c[:, CC["maskQ"]:CC["maskQ"] + 128] = (p[:, None] // 32 == p[None, :] // 32).astype(np.float32)
    c[:, CC["onesD"]:CC["onesD"] + 128] = 1.0 / D
    c[:, CC["onesH"]:CC["onesH"] + 128] = 1.0 / 128
    c[:, CC["cmask"]:CC["cmask"] + 128] = np.where(p[:, None] > p[None, :], NEG, 0.0)
    fm = np.zeros((16, 8), np.float32)
    for qt in range(16):
        fm[qt, (qt // 2):] = -1e30
    c[:, CC["futmask"]:CC["futmask"] + 128] = fm.reshape(1, 128)
    om = np.zeros((16, 8), np.float32)
    for qt in range(16):
        om[qt, qt // 2] = 1.0
    c[:, CC["ownmask"]:CC["ownmask"] + 128] = om.reshape(1, 128)
    rm = np.zeros((128, 4), np.float32)
    for v in range(2):
        rm[:, v] = ((p // 32) % 2 == v)
        rm[:, 2 + v] = -rm[:, v]
    c[:, CC["rowmask"]:CC["rowmask"] + 4] = rm
    c[:, CC["kk"]:CC["kk"] + 9] = np.arange(9, dtype=np.float32)[None, :]
    c[:, CC["kk64"]:CC["kk64"] + 64] = (8.0 * np.arange(1, 65, dtype=np.float32))[None, :]
    return c


def _rope():
    half = 64
    inv = (10000.0 ** (-np.arange(half, dtype=np.float32) * 2.0 / 128)).astype(np.float32)
    ang = np.arange(L, dtype=np.float32)[:, None] * inv[None, :]
    cos, sin = np.cos(ang).T.astype(np.float32), np.sin(ang).T.astype(np.float32)
    r = np.zeros((128, 2 * L), np.float32)
    r[0:64, 0:L] = cos; r[64:128, 0:L] = cos
    r[0:64, L:] = -sin; r[64:128, L:] = sin
    return r


def _fm(v):
    return np.ascontiguousarray(v.reshape(-1, 128).T)


def _host_prep(inp):
    vec = np.zeros((128, NV), np.float32)
    for l in range(2):
        vec[:, VC[("a_norm", l)]:VC[("a_norm", l)] + 8] = _fm(inp["a_norm"][l])
        vec[:, VC[("s5_D", l)]:VC[("s5_D", l)] + 8] = _fm(inp["s5_D"][l])
        vec[:, VC[("b_glu", l)]:VC[("b_glu", l)] + 16] = _fm(inp["b_glu"][l])
    vec[:, VC["kv_norm"]:VC["kv_norm"] + 8] = _fm(inp["kv_norm"])
    vec[:, VC["k_norm"]] = inp["k_norm"]
    for j in range(2):
        vec[:, VC[("b_norm", j)]:VC[("b_norm", j)] + 8] = _fm(inp["b_norm"][j])
        vec[:, VC[("q_norm", j)]] = inp["q_norm"][j]
    for l in range(4):
        vec[:, VC[("ffn_norm", l)]:VC[("ffn_norm", l)] + 8] = _fm(inp["ffn_norm"][l])
        cw = inp["conv_w"][l].reshape(3, NUPC, 128).transpose(2, 0, 1).reshape(128, 3 * NUPC)
        vec[:, VC[("conv_w", l)]:VC[("conv_w", l)] + 3 * NUPC] = cw
        vec[:, VC[("conv_b", l)]:VC[("conv_b", l)] + NUPC] = _fm(inp["conv_b"][l])
    s5p = np.zeros((2, 128, 96 + 4096), np.float32)
    for l in range(2):
        s5p[l, :, 0:32] = inp["s5_A_re"][l].reshape(32, 128).T
        s5p[l, :, 32:64] = inp["s5_A_im"][l].reshape(32, 128).T
        s5p[l, :, 64:96] = np.repeat(inp["s5_log_dt"][l].reshape(32, 2), 64, axis=1).T
        for nm, off in (("s5_B_re", 0), ("s5_B_im", 1024)):
            Bm = np.zeros((2, 64, 32, 2, 16), np.float32)
            Bg = inp[nm][l].reshape(32, 2, 64, 16)
            for m in range(2):
                Bm[m, :, :, m, :] = Bg[:, m].transpose(1, 0, 2)
            s5p[l, :, 96 + off:96 + off + 1024] = Bm.reshape(128, 1024)
        for nm, off in (("s5_C_re", 2048), ("s5_C_im", 3072)):
            Cm = np.zeros((2, 64, 32, 2, 16), np.float32)
            Cg = inp[nm][l].reshape(32, 2, 16, 64)
            for m in range(2):
                Cm[m, :, :, m, :] = Cg[:, m].transpose(2, 0, 1)
            s5p[l, :, 96 + off:96 + off + 1024] = Cm.reshape(128, 1024)
    common = dict(vec=vec, cst=_consts(), rope=_rope(), s5p=s5p,
                  w_glu=np.ascontiguousarray(inp["w_glu"]), w_kv=np.ascontiguousarray(inp["w_kv"]),
                  w_q=np.ascontiguousarray(inp["w_q"]), w_o=np.ascontiguousarray(inp["w_o"]),
                  w_up=np.ascontiguousarray(inp["w_up"]), w_down=np.ascontiguousarray(inp["w_down"]))
    return common


def _x_to_dev(xb):
    return np.ascontiguousarray(xb.T.reshape(NFC, 128, L).transpose(1, 0, 2).reshape(128, NFC * L))


def _dev_to_x(o):
    return np.ascontiguousarray(o.reshape(128, NFC, L).transpose(1, 0, 2).reshape(D, L).T)


def run(inp, stop=None, ncores=8):
    inp = {k: np.asarray(v) for k, v in inp.items()}
    common = _host_prep(inp)
    b = Builder(stop=stop)
    nc = b.build()
    in_maps = []
    for c in range(ncores):
        m = dict(common)
        m["xT"] = _x_to_dev(inp["x"][c])
        in_maps.append(m)
    res = run_bass_kernel_spmd(nc, in_maps, core_ids=list(range(ncores)))
    return np.stack([_dev_to_x(res.results[c]["outT"]) for c in range(ncores)], axis=0)


def kernel(**inputs):
    return run(inputs, stop=None, ncores=8).astype(np.float32)
```
